# Optimizing a Trainium2 kernel written in Bass

```python
import jax, jax.numpy as jnp
from jax import lax
import numpy as np

D_MODEL = 1024
BATCH = 8
SEQ = 4096
DEPTH = 1

CTX_LEN = 256
GRID_W = 64
D_MIX = D_MODEL
D_GMLP = D_MIX // 2
G_A = 4
GA_DIM = D_GMLP // G_A
CHUNK_A = 2 * GRID_W
D_MLSTM = D_MIX - D_GMLP
H_B = 4
HV = D_MLSTM // H_B
HK = HV // 2
D_QK = H_B * HK
CONV_W = 3
CHUNK_B = 128
N_GATES = 4 * H_B
D_IN = 2 * D_GMLP + 2 * D_QK + 2 * D_MLSTM + N_GATES
D_FF = 4 * D_MODEL
ALPHA = (2 * DEPTH) ** 0.25
BETA = (8 * DEPTH) ** -0.25
LN_EPS = 1e-5

kernel_name = "hymba_style_gmlp_mlstm_dit_layer"


def _ln_stats(x):
    xf = x.astype(jnp.float32)
    mu = jnp.mean(xf, axis=-1, keepdims=True)
    var = jnp.mean(jnp.square(xf - mu), axis=-1, keepdims=True)
    return (xf - mu) * lax.rsqrt(var + LN_EPS)


def layer_norm(x, g, b):
    return (_ln_stats(x) * g + b).astype(x.dtype)


def modulate(x, shift, scale):
    return _ln_stats(x).astype(x.dtype) * (1 + scale) + shift


def ada_split(cond, w_ada, b_ada):
    return jnp.split(jax.nn.silu(cond) @ w_ada + b_ada, 6, axis=-1)


def split_proj(p):
    bounds = [D_GMLP, 2 * D_GMLP, 2 * D_GMLP + D_QK, 2 * D_GMLP + 2 * D_QK,
              2 * D_GMLP + 2 * D_QK + D_MLSTM, 2 * D_GMLP + 2 * D_QK + 2 * D_MLSTM]
    return jnp.split(p, bounds, axis=-1)


def spatial_gating(u, v, w_s, b_s, g_v, b_v):
    bsz, s, _ = v.shape
    vc = layer_norm(v, g_v, b_v).reshape(bsz, s // CHUNK_A, CHUNK_A, G_A, GA_DIM)
    mixed = jnp.einsum('gts,bnsgd->bntgd', w_s, vc) + b_s.T[:, :, None]
    return u * mixed.reshape(bsz, s, D_GMLP)


def short_conv(x, w):
    pad = CONV_W // 2
    return lax.conv_general_dilated(
        x, w[:, None, :], window_strides=(1,), padding=[(pad, pad)],
        dimension_numbers=('NWC', 'WIO', 'NWC'), feature_group_count=x.shape[-1])


def mlstm_inputs(q, k, vv, gates, conv_w, b_gates):
    bsz, s, _ = q.shape
    qk = jax.nn.silu(short_conv(jnp.concatenate([q, k], axis=-1), conv_w))
    q, k = jnp.split(qk, 2, axis=-1)
    heads = lambda a, d: a.reshape(bsz, s, H_B, d).transpose(0, 2, 1, 3)
    g = (gates + b_gates).reshape(bsz, s, 4, H_B).transpose(2, 0, 3, 1)
    return heads(q, HK), heads(k, HK), heads(vv, HV), g


def mlstm_chunkwise(q, k, v, ig, fg, state):
    bsz, nh, s, _ = q.shape
    nc = s // CHUNK_B
    f32 = jnp.float32
    q = q.astype(f32) * (HK ** -0.5)
    k = k.astype(f32)
    v = v.astype(f32)
    ig = ig.astype(f32)
    logf = jax.nn.log_sigmoid(fg.astype(f32))

    def to_chunks(a):
        return jnp.moveaxis(a.reshape(bsz, nh, nc, CHUNK_B, *a.shape[3:]), 2, 0)

    tri = jnp.tril(jnp.ones((CHUNK_B, CHUNK_B), dtype=bool))

    def step(carry, xs_c):
        c0, n0, m0 = carry
        qc, kc, vc, ic, lfc = xs_c
        b = jnp.cumsum(lfc, axis=-1)
        dmat = jnp.where(tri, b[..., :, None] - b[..., None, :] + ic[..., None, :], -jnp.inf)
        inter = b + m0[..., None]
        m = jnp.maximum(inter, jnp.max(dmat, axis=-1))
        w_intra = jnp.exp(dmat - m[..., None])
        w_inter = jnp.exp(inter - m)
        scores = jnp.einsum('bhtk,bhsk->bhts', qc, kc) * w_intra
        num = (jnp.einsum('bhts,bhsv->bhtv', scores, vc)
               + w_inter[..., None] * jnp.einsum('bhtk,bhkv->bhtv', qc, c0))
        den = jnp.sum(scores, axis=-1) + w_inter * jnp.einsum('bhtk,bhk->bht', qc, n0)
        h = num / jnp.maximum(jnp.abs(den), jnp.exp(-m))[..., None]
        b_last = b[..., -1]
        dec = b_last[..., None] - b + ic
        m_new = jnp.maximum(b_last + m0, jnp.max(dec, axis=-1))
        w_s = jnp.exp(dec - m_new[..., None])
        carry_w = jnp.exp(b_last + m0 - m_new)
        c_new = carry_w[..., None, None] * c0 + jnp.einsum('bhs,bhsk,bhsv->bhkv', w_s, kc, vc)
        n_new = carry_w[..., None] * n0 + jnp.einsum('bhs,bhsk->bhk', w_s, kc)
        return (c_new, n_new, m_new), h

    xs = (to_chunks(q), to_chunks(k), to_chunks(v), to_chunks(ig), to_chunks(logf))
    state, hs = lax.scan(step, state, xs)
    h = jnp.moveaxis(hs, 0, 2).reshape(bsz, nh, s, HV)
    return h, state


def mlstm_bidirectional(ctx_in, lat_in):
    qc, kc, vc, gc = ctx_in
    ql, kl, vl, gl = lat_in
    bsz = ql.shape[0]
    f32 = jnp.float32
    zero = (jnp.zeros((bsz, H_B, HK, HV), f32), jnp.zeros((bsz, H_B, HK), f32), jnp.zeros((bsz, H_B), f32))
    flip = lambda a: jnp.flip(a, axis=2)
    hc_f, st_f = mlstm_chunkwise(qc, kc, vc, gc[0], gc[1], zero)
    hl_f, _ = mlstm_chunkwise(ql, kl, vl, gl[0], gl[1], st_f)
    hc_b, st_b = mlstm_chunkwise(flip(qc), flip(kc), flip(vc), flip(gc[2]), flip(gc[3]), zero)
    hl_b, _ = mlstm_chunkwise(flip(ql), flip(kl), flip(vl), flip(gl[2]), flip(gl[3]), st_b)
    return hc_f + flip(hc_b), hl_f + flip(hl_b)


def head_norm(h, g):
    bsz, nh, s, dv = h.shape
    y = _ln_stats(h).transpose(0, 2, 1, 3).reshape(bsz, s, nh * dv)
    return y * g


def mixer_output(u, v, h_b, o, w_s, b_s, ln_v_g, ln_v_b, hn_g, w_out):
    y_a = spatial_gating(u, v, w_s, b_s, ln_v_g, ln_v_b)
    y_b = head_norm(h_b, hn_g).astype(o.dtype) * jax.nn.sigmoid(o)
    return jnp.concatenate([y_a, y_b], axis=-1) @ w_out


def channel_mlp(h, w1, b1, w2, b2):
    return jnp.square(jax.nn.relu(h @ w1 + b1)) @ w2 + b2


def deepnorm_update(x, y, gate, g, b):
    return layer_norm(ALPHA * x + gate * y, g, b)


def setup_inputs(seed: int = 0) -> dict:
    key = jax.random.key(seed)
    ks = jax.random.split(key, 28)
    nrm = lambda k, shape, s: jax.random.normal(k, shape, jnp.float32) * s
    L = DEPTH
    f_bias = jnp.linspace(3.0, 6.0, H_B, dtype=jnp.float32)
    b_gates = jnp.concatenate([
        nrm(ks[10], (L, H_B), 0.1), f_bias + nrm(ks[11], (L, H_B), 0.1),
        nrm(ks[12], (L, H_B), 0.1), f_bias + nrm(ks[13], (L, H_B), 0.1)], axis=-1)
    return {
        "x": nrm(ks[0], (BATCH, SEQ, D_MODEL), 1.0),
        "c": nrm(ks[1], (BATCH, D_MODEL), 1.0),
        "ctx": nrm(ks[2], (BATCH, CTX_LEN, D_MODEL), 1.0),
        "c_ctx": nrm(ks[3], (D_MODEL,), 1.0),
        "w_ada": nrm(ks[4], (L, D_MODEL, 6 * D_MODEL), D_MODEL ** -0.5),
        "b_ada": nrm(ks[5], (L, 6 * D_MODEL), 0.01),
        "w_in": nrm(ks[6], (L, D_MODEL, D_IN), D_MODEL ** -0.5),
        "w_s": nrm(ks[7], (L, G_A, CHUNK_A, CHUNK_A), CHUNK_A ** -0.5),
        "b_s": 1.0 + nrm(ks[8], (L, G_A, CHUNK_A), 0.01),
        "ln_v_g": 1.0 + nrm(ks[9], (L, D_GMLP), 0.01),
        "ln_v_b": nrm(ks[14], (L, D_GMLP), 0.01),
        "conv_qk": nrm(ks[15], (L, CONV_W, 2 * D_QK), CONV_W ** -0.5),
        "b_gates": b_gates,
        "hn_g": 1.0 + nrm(ks[16], (L, D_MLSTM), 0.01),
        "w_out": nrm(ks[17], (L, D_MIX, D_MODEL), BETA * D_MIX ** -0.5),
        "ln1_g": 1.0 + nrm(ks[18], (L, D_MODEL), 0.01),
        "ln1_b": nrm(ks[19], (L, D_MODEL), 0.01),
        "w1": nrm(ks[20], (L, D_MODEL, D_FF), D_MODEL ** -0.5),
        "b1": nrm(ks[21], (L, D_FF), 0.01),
        "w2": nrm(ks[22], (L, D_FF, D_MODEL), BETA * D_FF ** -0.5),
        "b2": nrm(ks[23], (L, D_MODEL), 0.01),
        "ln2_g": 1.0 + nrm(ks[24], (L, D_MODEL), 0.01),
        "ln2_b": nrm(ks[25], (L, D_MODEL), 0.01),
    }


def reference(x, c, ctx, c_ctx, w_ada, b_ada, w_in, w_s, b_s, ln_v_g, ln_v_b, conv_qk, b_gates, hn_g,
              w_out, ln1_g, ln1_b, w1, b1, w2, b2, ln2_g, ln2_b):
    xc = ctx
    for l in range(DEPTH):
        sh1, sc1, g1, sh2, sc2, g2 = [m[:, None, :] for m in ada_split(c, w_ada[l], b_ada[l])]
        csh1, csc1, cg1, csh2, csc2, cg2 = ada_split(c_ctx, w_ada[l], b_ada[l])

        u_l, v_l, q_l, k_l, vv_l, o_l, gt_l = split_proj(modulate(x, sh1, sc1) @ w_in[l])
        u_c, v_c, q_c, k_c, vv_c, o_c, gt_c = split_proj(modulate(xc, csh1, csc1) @ w_in[l])

        hb_c, hb_l = mlstm_bidirectional(
            mlstm_inputs(q_c, k_c, vv_c, gt_c, conv_qk[l], b_gates[l]),
            mlstm_inputs(q_l, k_l, vv_l, gt_l, conv_qk[l], b_gates[l]))

        y_l = mixer_output(u_l, v_l, hb_l, o_l, w_s[l], b_s[l], ln_v_g[l], ln_v_b[l], hn_g[l], w_out[l])
        x_new = deepnorm_update(x, y_l, g1, ln1_g[l], ln1_b[l])
        x_new = deepnorm_update(x_new, channel_mlp(modulate(x_new, sh2, sc2), w1[l], b1[l], w2[l], b2[l]),
                                g2, ln2_g[l], ln2_b[l])

        if l < DEPTH - 1:
            y_c = mixer_output(u_c, v_c, hb_c, o_c, w_s[l], b_s[l], ln_v_g[l], ln_v_b[l], hn_g[l], w_out[l])
            xc = deepnorm_update(xc, y_c, cg1, ln1_g[l], ln1_b[l])
            xc = deepnorm_update(xc, channel_mlp(modulate(xc, csh2, csc2), w1[l], b1[l], w2[l], b2[l]),
                                 cg2, ln2_g[l], ln2_b[l])
        x = x_new
    return x
```

```python
import math
from contextlib import ExitStack
import numpy as np
import concourse.bass as bass
import concourse.mybir as mybir
from concourse.bass_utils import run_bass_kernel_spmd

F32 = mybir.dt.float32
BF16 = mybir.dt.bfloat16
AF = mybir.ActivationFunctionType
ALU = mybir.AluOpType

D = 1024
S = 4096
CTX = 256
NT = S // 128
NCT = CTX // 128
NG = NT + NCT
DIN = 2576
DFF = 4096
ALPHA = 2.0 ** 0.25
EPS = 1e-5
SEM_LIMIT = 3000


class Tok:
    __slots__ = ("sem", "val", "key")

    def __init__(self, sem, val, key):
        self.sem, self.val, self.key = sem, val, key


class Buf:
    def __init__(self, t, name):
        self.t = t
        self.name = name
        self.w = None
        self.r = {}
        self.dsem = None
        self.dcount = 0

    def __getitem__(self, idx):
        return self.t[idx]


class Eng:
    def __init__(self, kb, name, h):
        self.kb, self.name, self.h = kb, name, h
        self.seen = {}
        self.epoch = 0
        self.count = 0
        self.sem = kb.new_sem(f"{name}_e0")
        self.pending = False

    def roll(self):
        if self.count >= SEM_LIMIT and not self.pending:
            self.epoch += 1
            self.count = 0
            self.sem = self.kb.new_sem(f"{self.name}_e{self.epoch}")


class KB:
    def __init__(self, nc, es):
        self.nc, self.es = nc, es
        self.nsem = 0
        self.pe = Eng(self, "pe", nc.tensor)
        self.act = Eng(self, "act", nc.scalar)
        self.dve = Eng(self, "dve", nc.vector)
        self.pool = Eng(self, "pool", nc.gpsimd)
        self.sp = Eng(self, "sp", nc.sync)
        self.engs = [self.pe, self.act, self.dve, self.pool, self.sp]
        self.dma_toks = []

    def new_sem(self, name):
        self.nsem += 1
        s = self.es.enter_context(self.nc.semaphore(name))
        return (s, name)

    def sb(self, name, shape, dt, es=None):
        t = (es or self.es).enter_context(self.nc.sbuf_tensor(name, list(shape), dt))
        return Buf(t, name)

    def ps(self, name, shape, dt, es=None):
        t = (es or self.es).enter_context(self.nc.psum_tensor(name, list(shape), dt))
        return Buf(t, name)

    def wait(self, eng, tok):
        if tok is None:
            return
        if eng.name == "pe" and tok.key.startswith("pe_e"):
            return
        if eng.seen.get(tok.key, 0) >= tok.val:
            return
        eng.h.wait_ge(tok.sem, tok.val)
        eng.seen[tok.key] = tok.val

    def _deps(self, eng, reads, writes):
        for b in reads:
            self.wait(eng, b.w)
        for b in writes:
            self.wait(eng, b.w)
            for t in b.r.values():
                self.wait(eng, t)

    def _mark(self, tok, reads, writes):
        for b in reads:
            old = b.r.get(tok.key)
            if old is None or old.val < tok.val:
                b.r[tok.key] = tok
        for b in writes:
            b.w = tok
            b.r = {}

    def op(self, eng, fn, reads=(), writes=(), sig=True):
        if sig:
            eng.roll()
        self._deps(eng, reads, writes)
        inst = fn(eng.h)
        if sig:
            eng.count += 1
            inst.then_inc(eng.sem[0], 1)
            tok = Tok(eng.sem[0], eng.count, eng.sem[1])
            eng.pending = False
        else:
            tok = Tok(eng.sem[0], eng.count + 1, eng.sem[1])
            eng.pending = True
        self._mark(tok, reads, writes)
        return tok

    def dma(self, out_ap, in_ap, reads=(), writes=(), sembuf=None, eng=None):
        eng = eng or self.sp
        sb_ = sembuf or (writes[0] if writes else reads[0])
        if sb_.dsem is None:
            sb_.dsem = self.new_sem(f"d_{sb_.name}")
        self._deps(eng, reads, writes)
        inst = eng.h.dma_start(out=out_ap, in_=in_ap)
        inst.then_inc(sb_.dsem[0], 16)
        sb_.dcount += 16
        tok = Tok(sb_.dsem[0], sb_.dcount, sb_.dsem[1])
        self._mark(tok, reads, writes)
        self.dma_toks.append(tok)
        return tok

    def barrier(self):
        toks = []
        for e in self.engs:
            if e.count > 0:
                assert not e.pending
                toks.append(Tok(e.sem[0], e.count, e.sem[1]))
        toks += self.dma_toks
        self.dma_toks = []
        for e in self.engs:
            for t in toks:
                self.wait(e, t)


class _Stop(Exception):
    pass


def build_program(dbg=False, stage=99):
    try:
        return _build_program(dbg, stage)
    except _Stop as ex:
        return ex.args[0]


def _build_program(dbg, stage):
    nc = bass.Bass("TRN2", target_bir_lowering=False)

    def din(name, shape):
        return nc.dram_tensor(name, list(shape), F32, kind="ExternalInput").ap()

    x_d = din("x", [S, D])
    c_d = din("c", [8, 128])
    ctx_d = din("ctx", [CTX, D])
    cctx_d = din("c_ctx", [8, 128])
    wada_d = din("w_ada", [D, 6 * D])
    bada_d = din("b_ada", [48, 128])
    win_d = din("w_in", [D, DIN])
    ws_d = din("w_s", [4, 128, 128])
    bs_d = din("b_s", [1, 512])
    lnvg_d = din("ln_v_g", [4, 128])
    lnvb_d = din("ln_v_b", [4, 128])
    conv_d = din("conv_qk", [12, 128])
    bg_d = din("b_gates", [1, 16])
    hng_d = din("hn_g", [4, 128])
    wout_d = din("w_out", [D, D])
    ln1g_d = din("ln1_g", [1, D])
    ln1b_d = din("ln1_b", [1, D])
    w1_d = din("w1", [D, DFF])
    b1_d = din("b1", [32, 128])
    w2_d = din("w2", [DFF, D])
    b2_d = din("b2", [1, D])
    ln2g_d = din("ln2_g", [1, D])
    ln2b_d = din("ln2_b", [1, D])
    y_d = nc.dram_tensor("y", [S, D], F32, kind="ExternalOutput").ap()
    dbg_outs = {}

    with ExitStack() as es:
        kb = KB(nc, es)
        pe, act, dve, pool, sp = kb.pe, kb.act, kb.dve, kb.pool, kb.sp

        PB = [kb.ps(f"pb{i}", [128, 512], F32) for i in range(7)]
        PT = kb.ps("pt", [128, 1024], BF16)

        identf = kb.sb("identf", [128, 128], F32)
        identb = kb.sb("identb", [128, 128], BF16)
        LT = kb.sb("LT", [128, 128], F32)
        UT = kb.sb("UT", [128, 128], F32)
        onesf = kb.sb("onesf", [128, 128], F32)
        onesb = kb.sb("onesb", [128, 128], BF16)
        cst = kb.sb("cst", [128, 8], F32)
        modc = kb.sb("modc", [128, 6, 8], F32)
        smallc = kb.sb("smallc", [128, 64], F32)
        rows = kb.sb("rows", [128, 256], F32)
        bgb = kb.sb("bgb", [128, 16], F32)
        bsb = kb.sb("bsb", [128, 512], F32)
        BiasA = kb.sb("BiasA", [128, 4, 128], F32)
        wsT = kb.sb("wsT", [128, 4, 128], BF16)
        setup = Buf(None, "setup")

        def dbg_out(name, buf, ap, shape, dt=F32):
            if not dbg:
                return
            o = nc.dram_tensor("dbg_" + name, list(shape), dt, kind="ExternalOutput").ap()
            dbg_outs[name] = (shape, dt)
            kb.dma(o, ap, reads=[buf], sembuf=buf)

        kb.op(pool, lambda e: e.memset(onesf[:], 1.0), writes=[onesf])
        kb.op(pool, lambda e: e.memset(onesb[:], 1.0), writes=[onesb])
        kb.op(pool, lambda e: e.memset(cst[:, 0:1], -0.5), writes=[cst])
        kb.op(pool, lambda e: e.memset(cst[:, 1:2], EPS), writes=[cst])
        kb.op(pool, lambda e: e.memset(cst[:, 2:3], math.log(8.0)), writes=[cst])
        kb.op(pool, lambda e: e.memset(cst[:, 3:4], 1.0), writes=[cst])
        kb.op(pool, lambda e: e.affine_select(out=identf[:], in_=onesf[:], pattern=[[-1, 128]], compare_op=ALU.is_equal,
                                              fill=0.0, base=0, channel_multiplier=1), reads=[onesf], writes=[identf])
        kb.op(pool, lambda e: e.affine_select(out=LT[:], in_=onesf[:], pattern=[[1, 128]], compare_op=ALU.is_ge,
                                              fill=0.0, base=0, channel_multiplier=-1), reads=[onesf], writes=[LT])
        kb.op(pool, lambda e: e.affine_select(out=UT[:], in_=onesf[:], pattern=[[-1, 128]], compare_op=ALU.is_ge,
                                              fill=0.0, base=0, channel_multiplier=1), reads=[onesf], writes=[UT])
        kb.op(dve, lambda e: e.tensor_copy(out=identb[:], in_=identf[:]), reads=[identf], writes=[identb])

        R_C, R_CC, R_BADA, R_CONV, R_GV, R_BV, R_HNG, R_B1 = 0, 8, 16, 64, 76, 80, 84, 88
        kb.dma(rows[0:8, 0:128], c_d[:, :], writes=[rows])
        kb.dma(rows[0:8, 128:256], cctx_d[:, :], writes=[rows])
        rows2 = kb.sb("rows2", [128, 128], F32)
        kb.dma(rows2[0:48, :], bada_d[:, :], writes=[rows2])
        rows3 = kb.sb("rows3", [128, 128], F32)
        kb.dma(rows3[0:12, :], conv_d[:, :], writes=[rows3])
        kb.dma(rows3[32:36, :], lnvg_d[:, :], writes=[rows3])
        kb.dma(rows3[64:68, :], lnvb_d[:, :], writes=[rows3])
        rows4 = kb.sb("rows4", [128, 128], F32)
        kb.dma(rows4[0:4, :], hng_d[:, :], writes=[rows4])
        kb.dma(rows4[32:64, :], b1_d[:, :], writes=[rows4])
        kb.dma(bgb[:], bg_d[0:1, :].to_broadcast([128, 16]), writes=[bgb])
        kb.dma(bsb[:], bs_d[0:1, :].to_broadcast([128, 512]), writes=[bsb])
        wsr = kb.sb("wsr", [128, 4, 128], F32)
        kb.dma(wsr[:], ws_d.rearrange("g t s -> t g s"), writes=[wsr])

        ccol = kb.sb("ccol", [128, 2, 8], F32)
        badac = kb.sb("badac", [128, 48], F32)

        def tr_f32(dst_ap, dst_buf, src_ap, src_buf, n, bank, p0=0):
            kb.op(pe, lambda e: e.transpose(out=bank[:, 0:n], in_=src_ap, identity=identf[p0:p0 + n, p0:p0 + n]),
                  reads=[src_buf, identf], writes=[bank])
            kb.op(dve, lambda e: e.tensor_copy(out=dst_ap, in_=bank[:, 0:n]), reads=[bank], writes=[dst_buf])

        craw = kb.sb("craw", [128, 2, 8], F32)
        tr_f32(craw[:, 0, :], craw, rows[0:8, 0:128], rows, 8, PB[0])
        tr_f32(craw[:, 1, :], craw, rows[0:8, 128:256], rows, 8, PB[1])
        kb.op(act, lambda e: e.activation(out=ccol[:], in_=craw[:], func=AF.Silu), reads=[craw], writes=[ccol])
        tr_f32(badac[:, :], badac, rows2[0:48, :], rows2, 48, PB[2])
        tr_f32(smallc[:, 12:24], smallc, rows3[0:12, :], rows3, 12, PB[3])
        tr_f32(smallc[:, 0:4], smallc, rows3[32:36, :], rows3, 4, PB[4], p0=32)
        tr_f32(smallc[:, 4:8], smallc, rows3[64:68, :], rows3, 4, PB[5], p0=64)
        tr_f32(smallc[:, 8:12], smallc, rows4[0:4, :], rows4, 4, PB[6])
        tr_f32(smallc[:, 24:56], smallc, rows4[32:64, :], rows4, 32, PB[0], p0=32)
        wsTf = kb.sb("wsTf", [128, 4, 128], F32)
        for g in range(4):
            kb.op(pe, lambda e, g=g: e.transpose(out=PB[1][:, g * 128:(g + 1) * 128], in_=wsr[:, g, :], identity=identf[:]),
                  reads=[wsr, identf], writes=[PB[1]])
        kb.op(dve, lambda e: e.tensor_copy(out=wsTf[:].rearrange("p g t -> p (g t)"), in_=PB[1][:, :]), reads=[PB[1]], writes=[wsTf])
        kb.op(act, lambda e: e.activation(out=wsT[:], in_=wsTf[:], func=AF.Copy), reads=[wsTf], writes=[wsT])
        kb.op(pe, lambda e: e.matmul(PB[2][:, :], lhsT=onesf[:], rhs=wsTf[:].rearrange("p g t -> p (g t)"), start=True, stop=True),
              reads=[onesf, wsTf], writes=[PB[2]])
        for g in range(4):
            kb.op(dve, lambda e, g=g: e.scalar_tensor_tensor(out=BiasA[:, g, :], in0=PB[2][:, g * 128:(g + 1) * 128],
                                                             scalar=smallc[:, 4 + g:5 + g], in1=bsb[:, g * 128:(g + 1) * 128],
                                                             op0=ALU.mult, op1=ALU.add),
                  reads=[PB[2], smallc, bsb], writes=[BiasA])

        if stage == -1:
            dbg_out("smallc", smallc, smallc[:], [128, 64])
            dbg_out("BiasA", BiasA, BiasA[:], [128, 4, 128])
            dbg_out("ccol", ccol, ccol[:], [128, 2, 8])
            dbg_out("LT", LT, LT[:], [128, 128])
            dbg_out("identf", identf, identf[:], [128, 128])
            kb.barrier()
            raise _Stop((nc, dbg_outs))
        es_p = es.enter_context(ExitStack())
        qT = kb.sb("qT", [128, 2, S], BF16, es_p)
        kT = kb.sb("kT", [128, 2, NG * 128], BF16, es_p)
        vaug = kb.sb("vaug", [128, NG, 4, 130], BF16, es_p)
        Gt = kb.sb("Gt", [128, NG, 16], F32, es_p)
        WS = kb.sb("WS", [128, NG, 8], F32, es_p)
        LB = kb.sb("LB", [128, NG, 8], F32, es_p)
        CW = kb.sb("CW", [128, NG, 8], F32, es_p)
        ST = kb.sb("ST", [128, NT, 2, 2, 130], BF16, es_p)
        kb.op(pool, lambda e: e.memset(vaug[:, :, :, 128:130], 1.0), writes=[vaug])

        es0 = es.enter_context(ExitStack())
        ln1gb = kb.sb("ln1gb", [128, D], F32, es0)
        ln1bb = kb.sb("ln1bb", [128, D], F32, es0)
        g1bc = kb.sb("g1bc", [128, D], F32, es0)
        kb.dma(ln1gb[:], ln1g_d[0:1, :].to_broadcast([128, D]), writes=[ln1gb])
        kb.dma(ln1bb[:], ln1b_d[0:1, :].to_broadcast([128, D]), writes=[ln1bb])
        g2row = nc.dram_tensor("g2scratch", [128, D], F32, kind="Internal").ap()

        with ExitStack() as esa:
            stg = [kb.sb(f"astg{i}", [128, 8, 512], F32, esa) for i in range(2)]
            scb = kb.sb("scb", [128, 8, 128], F32, esa)
            badab = kb.sb("badab", [128, 2, D], F32, esa)
            g2bc0 = kb.sb("g2bc0", [128, D], F32, esa)
            kb.dma(badab[:, 0, :], bada_d[16:24, :].rearrange("(o a) b -> o (a b)", o=1).to_broadcast([128, D]), writes=[badab])
            kb.dma(badab[:, 1, :], bada_d[40:48, :].rearrange("(o a) b -> o (a b)", o=1).to_broadcast([128, D]), writes=[badab])
            for kc in range(8):
                kb.op(dve, lambda e, kc=kc: e.tensor_copy(out=scb[:, kc, :], in_=ccol[:, 0, kc:kc + 1].to_broadcast([128, 128])),
                      reads=[ccol], writes=[scb])
            wada_v = wada_d.rearrange("(kc p) n -> p kc n", p=128)
            col_kind = {0: 0, 1: 0, 2: 1, 3: 1, 6: 2, 7: 2, 8: 3, 9: 3}
            for blk in range(12):
                sg = stg[blk % 2]
                kb.dma(sg[:, 0:4, :], wada_v[:, 0:4, blk * 512:(blk + 1) * 512], writes=[sg])
                kb.dma(sg[:, 4:8, :], wada_v[:, 4:8, blk * 512:(blk + 1) * 512], writes=[sg])
                if blk in col_kind:
                    mi = col_kind[blk]
                    for jj in range(4):
                        j = blk * 4 + jj
                        fchunk = j % 8
                        bank = PB[jj % 4]
                        for kc in range(8):
                            kb.op(pe, lambda e, kc=kc, jj=jj, bank=bank: e.matmul(
                                bank[:, 0:2], lhsT=sg[:, kc, jj * 128:(jj + 1) * 128], rhs=ccol[:, :, kc],
                                start=(kc == 0), stop=(kc == 7)), reads=[sg, ccol], writes=[bank], sig=(kc == 7))
                        kb.op(dve, lambda e, bank=bank, mi=mi, fchunk=fchunk, j=j: e.tensor_tensor(
                            out=modc[:, mi, fchunk:fchunk + 1], in0=bank[:, 0:1], in1=badac[:, j:j + 1], op=ALU.add),
                            reads=[bank, badac], writes=[modc])
                        if mi < 2:
                            kb.op(dve, lambda e, bank=bank, mi=mi, fchunk=fchunk, j=j: e.tensor_tensor(
                                out=modc[:, 4 + mi, fchunk:fchunk + 1], in0=bank[:, 1:2], in1=badac[:, j:j + 1], op=ALU.add),
                                reads=[bank, badac], writes=[modc])
                else:
                    which = 0 if blk in (4, 5) else 1
                    half = blk % 2 if which == 1 else blk - 4
                    bank = PB[4 + (blk % 2)]
                    for kc in range(8):
                        kb.op(pe, lambda e, kc=kc, bank=bank: e.matmul(bank[:, :], lhsT=scb[:, kc, :], rhs=sg[:, kc, :],
                                                                      start=(kc == 0), stop=(kc == 7)),
                              reads=[sg, scb], writes=[bank], sig=(kc == 7))
                    dst = g1bc if which == 0 else g2bc0
                    kb.op(dve, lambda e, bank=bank, dst=dst, half=half, which=which: e.tensor_tensor(
                        out=dst[:, half * 512:(half + 1) * 512], in0=bank[:, :], in1=badab[:, which, half * 512:(half + 1) * 512],
                        op=ALU.add), reads=[bank, badab], writes=[dst])
            for mi in (1, 3, 5):
                kb.op(dve, lambda e, mi=mi: e.tensor_scalar(out=modc[:, mi, :], in0=modc[:, mi, :], scalar1=1.0, scalar2=None,
                                                            op0=ALU.add), reads=[modc], writes=[modc])
            g2st = kb.dma(g2row[:, :], g2bc0[:], reads=[g2bc0])
            kb.barrier()
            if stage == 0:
                dbg_out("modc", modc, modc[:], [128, 6, 8])
                dbg_out("g1bc", g1bc, g1bc[:], [128, D])
                dbg_out("smallc", smallc, smallc[:], [128, 64])
                dbg_out("BiasA", BiasA, BiasA[:], [128, 4, 128])
                kb.barrier()
                raise _Stop((nc, dbg_outs))

        def ln_stats(xap, xbuf, width, stats, mv, rstd, nmr=None):
            nchunk = width // 512
            for cidx in range(nchunk):
                kb.op(dve, lambda e, cidx=cidx: e.bn_stats(out=stats[:, cidx, :], in_=xap[:, cidx * 512:(cidx + 1) * 512]),
                      reads=[xbuf], writes=[stats])
            kb.op(dve, lambda e: e.bn_aggr(out=mv[:, 0:2], in_=stats[:, 0:nchunk, :].rearrange("p a b -> p (a b)")),
                  reads=[stats], writes=[mv])
            kb.op(pool, lambda e: e.tensor_tensor(out=mv[:, 2:3], in0=mv[:, 1:2], in1=cst[:, 1:2], op=ALU.add),
                  reads=[mv, cst], writes=[mv])
            kb.op(pool, lambda e: e.tensor_tensor(out=rstd[:, 0:1], in0=mv[:, 2:3], in1=cst[:, 0:1], op=ALU.pow),
                  reads=[mv, cst], writes=[rstd])
            if nmr is not None:
                kb.op(dve, lambda e: e.scalar_tensor_tensor(out=rstd[:, 1:2], in0=mv[:, 0:1], scalar=-1.0, in1=rstd[:, 0:1],
                                                            op0=ALU.mult, op1=ALU.mult), reads=[mv, rstd], writes=[rstd])

        def make_hT(xt, hT, xn, stats, mv, rstd, mi_shift, mi_scale, tok0=0):
            ln_stats(xt[:, :], xt, 1024, stats, mv, rstd, nmr=True)
            kb.op(act, lambda e: e.activation(out=xn[:], in_=xt[:, :], func=AF.Identity, bias=rstd[:, 1:2], scale=rstd[:, 0:1]),
                  reads=[xt, rstd], writes=[xn])
            for kc in range(8):
                kb.op(pe, lambda e, kc=kc: e.transpose(out=PT[:, kc * 128:(kc + 1) * 128], in_=xn[:, kc * 128:(kc + 1) * 128],
                                                      identity=identb[:]), reads=[xn, identb], writes=[PT], sig=(kc == 7))
            for kc in range(8):
                if kc % 2 == 0:
                    kb.op(act, lambda e, kc=kc: e.activation(out=hT[:, kc, tok0:tok0 + 128], in_=PT[:, kc * 128:(kc + 1) * 128],
                                                             func=AF.Identity, bias=modc[:, mi_shift, kc:kc + 1],
                                                             scale=modc[:, mi_scale, kc:kc + 1]),
                          reads=[PT, modc], writes=[hT])
                else:
                    kb.op(dve, lambda e, kc=kc: e.tensor_scalar(out=hT[:, kc, tok0:tok0 + 128], in0=PT[:, kc * 128:(kc + 1) * 128],
                                                                scalar1=modc[:, mi_scale, kc:kc + 1],
                                                                scalar2=modc[:, mi_shift, kc:kc + 1], op0=ALU.mult, op1=ALU.add),
                          reads=[PT, modc], writes=[hT])

        def load_weight_block(dst, dst_ap_fn, src_ap, stg_buf, nk, scale_bc=None, scale_cols=None):
            half = nk // 2
            kb.dma(stg_buf[:, 0:half, :], src_ap[:, 0:half, :], writes=[stg_buf])
            kb.dma(stg_buf[:, half:nk, :], src_ap[:, half:nk, :], writes=[stg_buf])
            if scale_bc is None:
                kb.op(act, lambda e: e.activation(out=dst_ap_fn(slice(0, half)), in_=stg_buf[:, 0:half, :], func=AF.Copy),
                      reads=[stg_buf], writes=[dst])
                kb.op(pool, lambda e: e.tensor_copy(out=dst_ap_fn(slice(half, nk)), in_=stg_buf[:, half:nk, :]),
                      reads=[stg_buf], writes=[dst])
            else:
                for k in range(nk):
                    eng = dve if k % 2 == 0 else pool
                    kb.op(eng, lambda e, k=k: e.tensor_tensor(out=dst_ap_fn(k), in0=stg_buf[:, k, :],
                                                              in1=scale_bc[:, scale_cols], op=ALU.mult),
                          reads=[stg_buf, scale_bc], writes=[dst])

        win_v = win_d.rearrange("(kc p) n -> p kc n", p=128)
        with ExitStack() as es1:
            Win1 = kb.sb("Win1", [128, 8, 1040], BF16, es1)
            with ExitStack() as esw:
                wstg = [kb.sb(f"wstg{i}", [128, 8, 512], F32, esw) for i in range(2)]
                load_weight_block(Win1, lambda ks: Win1[:, ks, 0:512], win_v[:, :, 1024:1536], wstg[0], 8)
                load_weight_block(Win1, lambda ks: Win1[:, ks, 512:1024], win_v[:, :, 1536:2048], wstg[1], 8)
                kb.dma(wstg[0][:, :, 0:16], win_v[:, :, 2560:2576], writes=[wstg[0]])
                kb.op(dve, lambda e: e.tensor_copy(out=Win1[:, :, 1024:1040], in_=wstg[0][:, :, 0:16]), reads=[wstg[0]], writes=[Win1])
                kb.barrier()
            xs = [kb.sb(f"xs{i}", [128, D], F32, es1) for i in range(2)]
            xn = kb.sb("xn", [128, D], BF16, es1)
            hT = kb.sb("hT", [128, 8, 128], BF16, es1)
            stats = kb.sb("stats", [128, 2, 6], F32, es1)
            mv = kb.sb("mv", [128, 4], F32, es1)
            rstd = kb.sb("rstd", [128, 2], F32, es1)
            raw = [kb.sb(f"raw{i}", [128, 4, 130], F32, es1) for i in range(3)]
            cvt = kb.sb("cvt", [128, 4, 128], F32, es1)

            def conv_finish(gc_prev, rb):
                for cc in range(4):
                    kb.op(dve, lambda e, cc=cc: e.tensor_scalar(out=cvt[:, cc, :], in0=rb[:, cc, 0:128],
                                                                scalar1=smallc[:, 12 + 0 * 4 + cc:13 + 0 * 4 + cc], scalar2=None,
                                                                op0=ALU.mult), reads=[rb, smallc], writes=[cvt])
                    kb.op(dve, lambda e, cc=cc: e.scalar_tensor_tensor(out=cvt[:, cc, :], in0=rb[:, cc, 1:129],
                                                                       scalar=smallc[:, 12 + 1 * 4 + cc:13 + 1 * 4 + cc],
                                                                       in1=cvt[:, cc, :], op0=ALU.mult, op1=ALU.add),
                          reads=[rb, smallc, cvt], writes=[cvt])
                    kb.op(dve, lambda e, cc=cc: e.scalar_tensor_tensor(out=cvt[:, cc, :], in0=rb[:, cc, 2:130],
                                                                       scalar=smallc[:, 12 + 2 * 4 + cc:13 + 2 * 4 + cc],
                                                                       in1=cvt[:, cc, :], op0=ALU.mult, op1=ALU.add),
                          reads=[rb, smallc, cvt], writes=[cvt])
                t0 = gc_prev * 128
                kb.op(act, lambda e: e.activation(out=kT[:, :, t0:t0 + 128], in_=cvt[:, 2:4, :], func=AF.Silu),
                      reads=[cvt], writes=[kT])
                if gc_prev >= NCT:
                    l0 = (gc_prev - NCT) * 128
                    kb.op(act, lambda e: e.activation(out=qT[:, :, l0:l0 + 128], in_=cvt[:, 0:2, :], func=AF.Silu),
                          reads=[cvt], writes=[qT])

            for gc in range(NG):
                is_ctx = gc < NCT
                src = ctx_d[gc * 128:(gc + 1) * 128, :] if is_ctx else x_d[(gc - NCT) * 128:(gc - NCT + 1) * 128, :]
                xt = xs[gc % 2]
                kb.dma(xt[:, 0:512], src[:, 0:512], writes=[xt])
                kb.dma(xt[:, 512:1024], src[:, 512:1024], writes=[xt])
                make_hT(xt, hT, xn, stats, mv, rstd, 4 if is_ctx else 0, 5 if is_ctx else 1)
                for cc in range(4):
                    for kc in range(8):
                        kb.op(pe, lambda e, cc=cc, kc=kc: e.matmul(PB[0][:, cc * 128:(cc + 1) * 128],
                                                                   lhsT=Win1[:, kc, cc * 128:(cc + 1) * 128], rhs=hT[:, kc, :],
                                                                   start=(kc == 0), stop=(kc == 7)),
                              reads=[Win1, hT], writes=[PB[0]], sig=(kc == 7 and cc == 3))
                for kc in range(8):
                    kb.op(pe, lambda e, kc=kc: e.matmul(PB[1][:, :], lhsT=hT[:, kc, :], rhs=Win1[:, kc, 512:1024],
                                                        start=(kc == 0), stop=(kc == 7)),
                          reads=[Win1, hT], writes=[PB[1]], sig=(kc == 7))
                for kc in range(8):
                    kb.op(pe, lambda e, kc=kc: e.matmul(PB[2][:, 0:16], lhsT=hT[:, kc, :], rhs=Win1[:, kc, 1024:1040],
                                                        start=(kc == 0), stop=(kc == 7)),
                          reads=[Win1, hT], writes=[PB[2]], sig=(kc == 7))
                rb = raw[gc % 3]
                first = gc in (0, NCT)
                last = gc in (NCT - 1, NG - 1)
                kb.op(act, lambda e: e.activation(out=rb[:, :, 1:129], in_=PB[0][:, :].rearrange("p (c t) -> p c t", c=4), func=AF.Copy),
                      reads=[PB[0]], writes=[rb])
                kb.op(dve, lambda e: e.tensor_copy(out=vaug[:, gc, :, 0:128], in_=PB[1][:, :].rearrange("p (h v) -> p h v", h=4)),
                      reads=[PB[1]], writes=[vaug])
                kb.op(dve, lambda e: e.tensor_tensor(out=Gt[:, gc, :], in0=PB[2][:, 0:16], in1=bgb[:], op=ALU.add),
                      reads=[PB[2], bgb], writes=[Gt])
                if first:
                    kb.op(pool, lambda e: e.memset(rb[:, :, 0:1], 0.0), writes=[rb])
                else:
                    rprev = raw[(gc - 1) % 3]
                    kb.op(pool, lambda e: e.tensor_copy(out=rb[:, :, 0:1], in_=rprev[:, :, 128:129]), reads=[rprev], writes=[rb])
                    kb.op(pool, lambda e: e.tensor_copy(out=rprev[:, :, 129:130], in_=rb[:, :, 1:2]), reads=[rb], writes=[rprev])
                    conv_finish(gc - 1, rprev)
                if last:
                    kb.op(pool, lambda e: e.memset(rb[:, :, 129:130], 0.0), writes=[rb])
                    conv_finish(gc, rb)
            kb.barrier()

        if dbg:
            dbg_out("kT", kT, kT[:], [128, 2, NG * 128], BF16)
            dbg_out("qT", qT, qT[:], [128, 2, S], BF16)
            dbg_out("vaug", vaug, vaug[:], [128, NG, 4, 130], BF16)
            dbg_out("Gt", Gt, Gt[:], [128, NG, 16])
        if stage == 1:
            kb.barrier()
            raise _Stop((nc, dbg_outs))

        with ExitStack() as esg:
            NF = NG * 8
            Gv = Gt[:].rearrange("p c (d t h) -> p c d t h", d=2, t=2, h=4)
            LF = kb.sb("LF", [128, NG, 2, 4], F32, esg)
            T1 = kb.sb("T1", [128, NG, 2, 4], F32, esg)
            T2 = kb.sb("T2", [128, NG, 2, 4], F32, esg)
            Bc = kb.sb("Bc", [128, NG, 2, 4], F32, esg)
            Aa = kb.sb("Aa", [128, NG, 2, 4], F32, esg)
            Mcb = kb.sb("Mcb", [128, NG, 2, 4], F32, esg)
            rowA = kb.sb("rowA", [1, NG, 2, 4], F32, esg)
            rowB = kb.sb("rowB", [1, NG, 2, 4], F32, esg)
            rowM = kb.sb("rowM", [1, NG, 2, 4], F32, esg)
            rowm0 = kb.sb("rowm0", [1, NG, 2, 4], F32, esg)
            colmax = kb.sb("colmax", [128, 3], F32, esg)
            fl = lambda b: b[:].rearrange("p c d h -> p (c d h)")
            FG = Gv[:, :, :, 1, :]
            IG = Gv[:, :, :, 0, :]
            kb.op(dve, lambda e: e.tensor_scalar(out=T1[:], in0=FG, scalar1=-1.0, scalar2=None, op0=ALU.mult), reads=[Gt], writes=[T1])
            kb.op(dve, lambda e: e.tensor_tensor(out=T1[:], in0=T1[:], in1=FG, op=ALU.max), reads=[Gt, T1], writes=[T1])
            kb.op(act, lambda e: e.activation(out=T2[:], in_=T1[:], func=AF.Exp, scale=-1.0), reads=[T1], writes=[T2])
            kb.op(act, lambda e: e.activation(out=T2[:], in_=T2[:], func=AF.Ln, bias=cst[:, 3:4], scale=1.0), reads=[T2, cst], writes=[T2])
            kb.op(dve, lambda e: e.tensor_scalar(out=T1[:], in0=FG, scalar1=0.0, scalar2=None, op0=ALU.min), reads=[Gt], writes=[T1])
            kb.op(dve, lambda e: e.tensor_tensor(out=LF[:], in0=T1[:], in1=T2[:], op=ALU.subtract), reads=[T1, T2], writes=[LF])
            PBv = PB[0][:, 0:NF].rearrange("p (c d h) -> p c d h", c=NG, d=2, h=4)
            kb.op(pe, lambda e: e.matmul(PBv[:, :, 0, :], lhsT=LT[:], rhs=LF[:, :, 0, :], start=True, stop=True),
                  reads=[LT, LF], writes=[PB[0]])
            PBv1 = PB[1][:, 0:NF].rearrange("p (c d h) -> p c d h", c=NG, d=2, h=4)
            kb.op(pe, lambda e: e.matmul(PBv1[:, :, 1, :], lhsT=UT[:], rhs=LF[:, :, 1, :], start=True, stop=True),
                  reads=[UT, LF], writes=[PB[1]])
            kb.op(dve, lambda e: e.tensor_copy(out=Bc[:, :, 0, :], in_=PBv[:, :, 0, :]), reads=[PB[0]], writes=[Bc])
            kb.op(dve, lambda e: e.tensor_copy(out=Bc[:, :, 1, :], in_=PBv1[:, :, 1, :]), reads=[PB[1]], writes=[Bc])
            kb.op(dve, lambda e: e.tensor_tensor(out=Aa[:], in0=IG, in1=Bc[:], op=ALU.subtract), reads=[Gt, Bc], writes=[Aa])
            AaF = fl(Aa)
            segs = [(0, 128), (128, 128), (256, NF - 256)]
            for si, (o, n) in enumerate(segs):
                kb.op(pe, lambda e, o=o, n=n: e.transpose(out=PB[2][0:n, 0:128], in_=AaF[:, o:o + n], identity=identf[:]),
                      reads=[Aa, identf], writes=[PB[2]])
                kb.op(dve, lambda e, si=si, n=n: e.reduce_max(out=colmax[0:n, si:si + 1], in_=PB[2][0:n, 0:128], axis=mybir.AxisListType.X),
                      reads=[PB[2]], writes=[colmax])
                kb.op(pe, lambda e, si=si, n=n, o=o: e.matmul(PB[3][0:1, o:o + n], lhsT=colmax[0:n, si:si + 1], rhs=identf[0:n, 0:n],
                                                               start=True, stop=True), reads=[colmax, identf], writes=[PB[3]])
            kb.op(dve, lambda e: e.tensor_copy(out=fl(rowA), in_=PB[3][0:1, 0:NF]), reads=[PB[3]], writes=[rowA])
            kb.op(pe, lambda e: e.matmul(PB[4][0:1, 0:NF], lhsT=onesf[:, 0:1], rhs=fl(LF), start=True, stop=True),
                  reads=[onesf, LF], writes=[PB[4]])
            kb.op(dve, lambda e: e.tensor_copy(out=fl(rowB), in_=PB[4][0:1, 0:NF]), reads=[PB[4]], writes=[rowB])
            order = [list(range(NG)), [1, 0] + list(range(NG - 1, NCT - 1, -1))]
            for d in range(2):
                g0 = order[d][0]
                kb.op(dve, lambda e, d=d, g0=g0: e.memset(rowm0[0:1, g0, d, :], 0.0), writes=[rowm0])
            for j in range(NG):
                for d in range(2):
                    gcur = order[d][j]
                    kb.op(dve, lambda e, d=d, gcur=gcur: e.tensor_tensor(out=rowM[0:1, gcur, d, :], in0=rowm0[0:1, gcur, d, :],
                                                                         in1=rowA[0:1, gcur, d, :], op=ALU.max),
                          reads=[rowm0, rowA], writes=[rowM])
                    if j + 1 < NG:
                        gn = order[d][j + 1]
                        kb.op(dve, lambda e, d=d, gcur=gcur, gn=gn: e.tensor_tensor(out=rowm0[0:1, gn, d, :], in0=rowM[0:1, gcur, d, :],
                                                                                   in1=rowB[0:1, gcur, d, :], op=ALU.add),
                              reads=[rowM, rowB], writes=[rowm0])
            kb.op(pe, lambda e: e.matmul(PB[5][:, 0:NF], lhsT=onesf[0:1, :], rhs=fl(rowM), start=True, stop=True),
                  reads=[onesf, rowM], writes=[PB[5]])
            kb.op(pe, lambda e: e.matmul(PB[6][:, 0:NF], lhsT=onesf[0:1, :], rhs=fl(rowm0), start=True, stop=True),
                  reads=[onesf, rowm0], writes=[PB[6]])
            kb.op(dve, lambda e: e.tensor_copy(out=fl(Mcb), in_=PB[5][:, 0:NF]), reads=[PB[5]], writes=[Mcb])
            kb.op(dve, lambda e: e.tensor_tensor(out=T1[:], in0=Aa[:], in1=Mcb[:], op=ALU.subtract), reads=[Aa, Mcb], writes=[T1])
            kb.op(act, lambda e: e.activation(out=WS[:].rearrange("p c j -> p (c j)"), in_=fl(T1), func=AF.Exp), reads=[T1], writes=[WS])
            kb.op(dve, lambda e: e.tensor_tensor(out=fl(T2), in0=PB[6][:, 0:NF], in1=fl(Mcb), op=ALU.subtract), reads=[PB[6], Mcb], writes=[T2])
            kb.op(act, lambda e: e.activation(out=CW[:].rearrange("p c j -> p (c j)"), in_=fl(T2), func=AF.Exp), reads=[T2], writes=[CW])
            kb.op(dve, lambda e: e.tensor_tensor(out=T1[:], in0=Bc[:], in1=Mcb[:], op=ALU.add), reads=[Bc, Mcb], writes=[T1])
            kb.op(act, lambda e: e.activation(out=LB[:].rearrange("p c j -> p (c j)"), in_=fl(T1), func=AF.Exp, bias=cst[:, 2:3], scale=-1.0),
                  reads=[T1, cst], writes=[LB])
            kb.barrier()

        if dbg:
            dbg_out("WS", WS, WS[:], [128, NG, 8])
            dbg_out("CW", CW, CW[:], [128, NG, 8])
            dbg_out("LB", LB, LB[:], [128, NG, 8])
        if stage == 2:
            kb.barrier()
            raise _Stop((nc, dbg_outs))

        with ExitStack() as ess:
            Cc = [kb.sb(f"Cc{d}", [128, 2, 130], F32, ess) for d in range(2)]
            kp = [kb.sb(f"kp{i}", [128, 4, 64], BF16, ess) for i in range(2)]
            c0b = [kb.sb(f"c0b{i}", [128, 2, 130], BF16, ess) for i in range(2)]
            for d in range(2):
                kb.op(pool, lambda e, d=d: e.memset(Cc[d][:], 0.0), writes=[Cc[d]])
            it = 0
            for j in range(NG):
                for d in range(2):
                    gcur = order[d][j]
                    C = Cc[d]
                    if j > 0:
                        for h in range(4):
                            p0 = (h % 2) * 64
                            kb.op(dve, lambda e, h=h, p0=p0, d=d, gcur=gcur, C=C: e.tensor_scalar(
                                out=C[p0:p0 + 64, h // 2, :], in0=C[p0:p0 + 64, h // 2, :],
                                scalar1=CW[p0:p0 + 64, gcur, d * 4 + h:d * 4 + h + 1], scalar2=None, op0=ALU.mult),
                                reads=[C, CW], writes=[C])
                    if gcur >= NCT:
                        kb.op(act, lambda e, d=d, gcur=gcur, C=C: e.activation(out=ST[:, gcur - NCT, d, :, :], in_=C[:], func=AF.Copy),
                              reads=[C], writes=[ST])
                    if j == NG - 1:
                        continue
                    kpb = kp[it % 2]
                    bankT = PT
                    for pr in range(2):
                        kb.op(pe, lambda e, pr=pr, gcur=gcur: e.transpose(out=PT[:, pr * 128:(pr + 1) * 128],
                                                                          in_=kT[:, pr, gcur * 128:(gcur + 1) * 128], identity=identb[:]),
                              reads=[kT, identb], writes=[PT], sig=(pr == 1))
                    for h in range(4):
                        eng = dve if h % 2 == 0 else act
                        if eng is dve:
                            kb.op(dve, lambda e, h=h, d=d, gcur=gcur, kpb=kpb: e.tensor_scalar(
                                out=kpb[:, h, :], in0=PT[:, h * 64:(h + 1) * 64], scalar1=WS[:, gcur, d * 4 + h:d * 4 + h + 1],
                                scalar2=None, op0=ALU.mult), reads=[PT, WS], writes=[kpb])
                        else:
                            kb.op(act, lambda e, h=h, d=d, gcur=gcur, kpb=kpb: e.activation(
                                out=kpb[:, h, :], in_=PT[:, h * 64:(h + 1) * 64], func=AF.Identity,
                                scale=WS[:, gcur, d * 4 + h:d * 4 + h + 1]), reads=[PT, WS], writes=[kpb])
                    bank = PB[(it % 2) * 2:(it % 2) * 2 + 2]
                    for h in range(4):
                        bk = bank[h // 2]
                        kb.op(pe, lambda e, h=h, bk=bk, gcur=gcur, kpb=kpb: e.matmul(
                            bk[:, (h % 2) * 130:(h % 2) * 130 + 130], lhsT=kpb[:, (h // 2) * 2:(h // 2) * 2 + 2, :].rearrange("p a b -> p (a b)"),
                            rhs=vaug[:, gcur, h, :], start=True, stop=True), reads=[kpb, vaug], writes=[bk])
                    for h in range(4):
                        p0 = (h % 2) * 64
                        bk = bank[h // 2]
                        kb.op(dve, lambda e, h=h, p0=p0, bk=bk, C=C: e.tensor_tensor(
                            out=C[p0:p0 + 64, h // 2, :], in0=C[p0:p0 + 64, h // 2, :],
                            in1=bk[p0:p0 + 64, (h % 2) * 130:(h % 2) * 130 + 130], op=ALU.add), reads=[C, bk], writes=[C])
                    it += 1
            kb.barrier()

        if dbg:
            dbg_out("ST", ST, ST[:], [128, NT, 2, 2, 130], BF16)
        if stage == 3:
            kb.barrier()
            raise _Stop((nc, dbg_outs))

        wout_v = wout_d.rearrange("(kc p) n -> p kc n", p=128)
        with ExitStack() as es2:
            Win2 = kb.sb("Win2", [128, 8, 1536], BF16, es2)
            Wo = kb.sb("Wo", [128, 8, D], BF16, es2)
            with ExitStack() as esw:
                wstg = [kb.sb(f"wstg2{i}", [128, 8, 256], F32, esw) for i in range(2)]
                nb = 0
                for (wc0, dc0) in ((0, 0), (256, 256), (512, 512), (768, 768), (2048, 1024), (2304, 1280)):
                    load_weight_block(Win2, lambda ks, dc0=dc0: Win2[:, ks, dc0:dc0 + 256], win_v[:, :, wc0:wc0 + 256], wstg[nb % 2], 8)
                    nb += 1
                for c0 in (0, 256, 512, 768):
                    load_weight_block(Wo, lambda k, c0=c0: Wo[:, k, c0:c0 + 256], wout_v[:, :, c0:c0 + 256], wstg[nb % 2], 8,
                                      scale_bc=g1bc, scale_cols=slice(c0, c0 + 256))
                    nb += 1
                kb.barrier()
            xs = [kb.sb(f"x2s{i}", [128, D], F32, es2) for i in range(2)]
            xn = kb.sb("xn2", [128, D], BF16, es2)
            hT = kb.sb("hT2", [128, 8, 128], BF16, es2)
            stats = kb.sb("stats2", [128, 4, 6], F32, es2)
            mv = kb.sb("mv2", [128, 4], F32, es2)
            rstd = kb.sb("rstd2", [128, 2], F32, es2)
            mv4 = kb.sb("mv4", [128, 4, 4], F32, es2)
            rs4 = kb.sb("rs4", [128, 4], F32, es2)
            uT = kb.sb("uT", [128, 4, 128], BF16, es2)
            sgo = kb.sb("sgo", [128, 4, 128], BF16, es2)
            vn = kb.sb("vn", [128, 512], BF16, es2)
            tA = kb.sb("tA", [128, 4, 128], F32, es2)
            yT = kb.sb("yT", [128, 8, 128], BF16, es2)
            sT = kb.sb("sT", [128, 8, 128], BF16, es2)
            dn = kb.sb("dn", [128, 3, 8], F32, es2)
            hs = kb.sb("hs", [128, 4, 128], F32, es2)
            hn = kb.sb("hn", [128, 4, 128], BF16, es2)
            hg = kb.sb("hg", [128, 4], F32, es2)
            Q2 = [kb.sb(f"Q2{i}", [128, 2, 128], BF16, es2) for i in range(2)]
            for pr in range(2):
                kb.op(pool, lambda e, pr=pr: e.memset(Q2[pr][:], 0.0), writes=[Q2[pr]])
            kb.op(dve, lambda e: e.tensor_copy(out=hg[:], in_=smallc[:, 8:12]), reads=[smallc], writes=[hg])

            for i in range(NT):
                gc = i + NCT
                t0k = gc * 128
                t0q = i * 128
                xt = xs[i % 2]
                kb.dma(xt[:, 0:512], x_d[i * 128:(i + 1) * 128, 0:512], writes=[xt])
                kb.dma(xt[:, 512:1024], x_d[i * 128:(i + 1) * 128, 512:1024], writes=[xt])
                make_hT(xt, hT, xn, stats, mv, rstd, 0, 1)
                for (bank, c0) in ((PB[0], 0), (PB[1], 1024)):
                    for cc in range(4):
                        for kc in range(8):
                            kb.op(pe, lambda e, cc=cc, kc=kc, bank=bank, c0=c0: e.matmul(
                                bank[:, cc * 128:(cc + 1) * 128], lhsT=Win2[:, kc, c0 + cc * 128:c0 + (cc + 1) * 128], rhs=hT[:, kc, :],
                                start=(kc == 0), stop=(kc == 7)), reads=[Win2, hT], writes=[bank], sig=(kc == 7 and cc == 3))
                for kc in range(8):
                    kb.op(pe, lambda e, kc=kc: e.matmul(PB[2][:, :], lhsT=hT[:, kc, :], rhs=Win2[:, kc, 512:1024],
                                                        start=(kc == 0), stop=(kc == 7)), reads=[Win2, hT], writes=[PB[2]], sig=(kc == 7))
                kb.op(act, lambda e: e.activation(out=uT[:].rearrange("p c t -> p (c t)"), in_=PB[0][:, :], func=AF.Copy),
                      reads=[PB[0]], writes=[uT])
                kb.op(act, lambda e: e.activation(out=sgo[:].rearrange("p c t -> p (c t)"), in_=PB[1][:, :], func=AF.Sigmoid),
                      reads=[PB[1]], writes=[sgo])
                ln_stats(PB[2][:, :], PB[2], 512, stats, mv, rstd)
                kb.op(dve, lambda e: e.tensor_scalar(out=vn[:], in0=PB[2][:, :], scalar1=mv[:, 0:1], scalar2=rstd[:, 0:1],
                                                     op0=ALU.subtract, op1=ALU.mult), reads=[PB[2], mv, rstd], writes=[vn])
                for g in range(4):
                    kb.op(pe, lambda e, g=g: e.matmul(PB[3][:, g * 128:(g + 1) * 128], lhsT=vn[:, g * 128:(g + 1) * 128], rhs=wsT[:, g, :],
                                                      start=True, stop=True), reads=[vn, wsT], writes=[PB[3]], sig=(g == 3))
                for g in range(4):
                    kb.op(dve, lambda e, g=g: e.scalar_tensor_tensor(out=tA[:, g, :], in0=PB[3][:, g * 128:(g + 1) * 128],
                                                                     scalar=smallc[:, g:g + 1], in1=BiasA[:, g, :], op0=ALU.mult, op1=ALU.add),
                          reads=[PB[3], smallc, BiasA], writes=[tA])
                kb.op(pool, lambda e: e.tensor_tensor(out=yT[:, 0:4, :], in0=tA[:], in1=uT[:], op=ALU.mult), reads=[tA, uT], writes=[yT])
                for pr in range(2):
                    for hh in range(2):
                        kb.op(pool, lambda e, pr=pr, hh=hh: e.tensor_copy(out=Q2[pr][hh * 64:(hh + 1) * 64, hh, :],
                                                                          in_=qT[hh * 64:(hh + 1) * 64, pr, t0q:t0q + 128]),
                              reads=[qT], writes=[Q2[pr]])
                for pr in range(2):
                    kb.op(pe, lambda e, pr=pr: e.matmul(PB[4][:, pr * 256:(pr + 1) * 256], lhsT=kT[:, pr, t0k:t0k + 128],
                                                        rhs=Q2[pr][:].rearrange("p a t -> p (a t)"), start=True, stop=True),
                          reads=[kT, Q2[pr]], writes=[PB[4]], sig=(pr == 1))
                for d in range(2):
                    msk = LT if d == 0 else UT
                    for h in range(4):
                        kb.op(dve, lambda e, d=d, h=h, msk=msk: e.scalar_tensor_tensor(
                            out=sT[:, d * 4 + h, :], in0=PB[4][:, h * 128:(h + 1) * 128], scalar=WS[:, gc, d * 4 + h:d * 4 + h + 1],
                            in1=msk[:], op0=ALU.mult, op1=ALU.mult), reads=[PB[4], WS, msk], writes=[sT])
                for d in range(2):
                    bank = PB[5 + d]
                    for h in range(4):
                        kb.op(pe, lambda e, d=d, h=h, bank=bank: e.matmul(bank[:, h * 128:(h + 1) * 128], lhsT=sT[:, d * 4 + h, :],
                                                                          rhs=vaug[:, gc, h, 0:128], start=True, stop=False),
                              reads=[sT, vaug], writes=[bank], sig=False)
                        kb.op(pe, lambda e, d=d, h=h, bank=bank: e.matmul(
                            bank[:, h * 128:(h + 1) * 128], lhsT=Q2[h // 2][:, h % 2, :],
                            rhs=ST[:, i, d, h // 2, 0:128], start=False, stop=True),
                            reads=[Q2[h // 2], ST], writes=[bank], sig=(h == 3))
                for d in range(2):
                    for h in range(4):
                        jn = d * 4 + h
                        kb.op(pe, lambda e, d=d, h=h, jn=jn: e.matmul(PB[0][:, 2 * jn:2 * jn + 2], lhsT=sT[:, jn, :],
                                                                      rhs=vaug[:, gc, h, 128:130], start=True, stop=False),
                              reads=[sT, vaug], writes=[PB[0]], sig=False)
                        kb.op(pe, lambda e, d=d, h=h, jn=jn: e.matmul(
                            PB[0][:, 2 * jn:2 * jn + 2], lhsT=Q2[h // 2][:, h % 2, :],
                            rhs=ST[:, i, d, h // 2, 128:130], start=False, stop=True),
                            reads=[Q2[h // 2], ST], writes=[PB[0]], sig=(jn == 7))
                den = PB[0][:, 0:16].rearrange("p (j two) -> p j two", two=2)[:, :, 0]
                kb.op(dve, lambda e: e.tensor_scalar(out=dn[:, 0, :], in0=den, scalar1=-1.0, scalar2=None, op0=ALU.mult),
                      reads=[PB[0]], writes=[dn])
                kb.op(dve, lambda e: e.tensor_tensor(out=dn[:, 1, :], in0=dn[:, 0, :], in1=den, op=ALU.max),
                      reads=[PB[0], dn], writes=[dn])
                kb.op(dve, lambda e: e.tensor_tensor(out=dn[:, 0, :], in0=dn[:, 1, :], in1=LB[:, gc, :], op=ALU.max),
                      reads=[dn, LB], writes=[dn])
                kb.op(dve, lambda e: e.reciprocal(out=dn[:, 2, :], in_=dn[:, 0, :]), reads=[dn], writes=[dn])
                for h in range(4):
                    kb.op(act, lambda e, h=h: e.activation(out=hs[:, h, :], in_=PB[5][:, h * 128:(h + 1) * 128], func=AF.Identity,
                                                           scale=dn[:, 2, h:h + 1]), reads=[PB[5], dn], writes=[hs])
                for h in range(4):
                    kb.op(dve, lambda e, h=h: e.scalar_tensor_tensor(out=hs[:, h, :], in0=PB[6][:, h * 128:(h + 1) * 128],
                                                                     scalar=dn[:, 2, 4 + h:5 + h], in1=hs[:, h, :], op0=ALU.mult, op1=ALU.add),
                          reads=[PB[6], dn, hs], writes=[hs])
                for h in range(4):
                    kb.op(dve, lambda e, h=h: e.bn_stats(out=stats[:, h, :], in_=hs[:, h, :]), reads=[hs], writes=[stats])
                for h in range(4):
                    kb.op(dve, lambda e, h=h: e.bn_aggr(out=mv4[:, h, 0:2], in_=stats[:, h, :]), reads=[stats], writes=[mv4])
                kb.op(pool, lambda e: e.tensor_tensor(out=mv4[:, :, 2], in0=mv4[:, :, 1], in1=cst[:, 1:2].to_broadcast([128, 4]), op=ALU.add),
                      reads=[mv4, cst], writes=[mv4])
                kb.op(pool, lambda e: e.tensor_tensor(out=rs4[:], in0=mv4[:, :, 2], in1=cst[:, 0:1].to_broadcast([128, 4]), op=ALU.pow),
                      reads=[mv4, cst], writes=[rs4])
                for h in range(4):
                    kb.op(dve, lambda e, h=h: e.tensor_scalar(out=hn[:, h, :], in0=hs[:, h, :], scalar1=mv4[:, h, 0:1], scalar2=rs4[:, h:h + 1],
                                                              op0=ALU.subtract, op1=ALU.mult), reads=[hs, mv4, rs4], writes=[hn])
                for h in range(4):
                    kb.op(pe, lambda e, h=h: e.transpose(out=PT[:, h * 128:(h + 1) * 128], in_=hn[:, h, :], identity=identb[:]),
                          reads=[hn, identb], writes=[PT], sig=(h == 3))
                for h in range(4):
                    kb.op(dve, lambda e, h=h: e.scalar_tensor_tensor(out=yT[:, 4 + h, :], in0=PT[:, h * 128:(h + 1) * 128],
                                                                     scalar=hg[:, h:h + 1], in1=sgo[:, h, :], op0=ALU.mult, op1=ALU.mult),
                          reads=[PT, hg, sgo], writes=[yT])
                for half in range(2):
                    for kc in range(8):
                        kb.op(pe, lambda e, half=half, kc=kc: e.matmul(PB[1 + half][:, :], lhsT=yT[:, kc, :],
                                                                       rhs=Wo[:, kc, half * 512:(half + 1) * 512],
                                                                       start=(kc == 0), stop=(kc == 7)),
                              reads=[yT, Wo], writes=[PB[1 + half]], sig=(kc == 7))
                for half in range(2):
                    kb.op(dve, lambda e, half=half: e.scalar_tensor_tensor(out=xt[:, half * 512:(half + 1) * 512],
                                                                           in0=xt[:, half * 512:(half + 1) * 512], scalar=ALPHA,
                                                                           in1=PB[1 + half][:, :], op0=ALU.mult, op1=ALU.add),
                          reads=[xt, PB[1 + half]], writes=[xt])
                ln_stats(xt[:, :], xt, 1024, stats, mv, rstd)
                kb.op(dve, lambda e: e.scalar_tensor_tensor(out=xt[:, :], in0=xt[:, :], scalar=mv[:, 0:1], in1=ln1gb[:],
                                                            op0=ALU.subtract, op1=ALU.mult), reads=[xt, mv, ln1gb], writes=[xt])
                kb.op(dve, lambda e: e.scalar_tensor_tensor(out=xt[:, :], in0=xt[:, :], scalar=rstd[:, 0:1], in1=ln1bb[:],
                                                            op0=ALU.mult, op1=ALU.add), reads=[xt, rstd, ln1bb], writes=[xt])
                kb.dma(y_d[i * 128:(i + 1) * 128, :], xt[:, :], reads=[xt])
            kb.barrier()

        if stage == 4:
            raise _Stop((nc, dbg_outs))
        es0.close()
        es_p.close()

        GRP = 2
        w1_v = w1_d.rearrange("(kc p) n -> p kc n", p=128)
        w2_v = w2_d.rearrange("(j p) n -> p j n", p=128)
        with ExitStack() as es3:
            W1b = kb.sb("W1b", [128, 8, DFF], BF16, es3)
            W2b = kb.sb("W2b", [128, 32, D], BF16, es3)
            ln2gb = kb.sb("ln2gb", [128, D], F32, es3)
            ln2bb = kb.sb("ln2bb", [128, D], F32, es3)
            b2h = kb.sb("b2h", [1, 2, D], BF16, es3)
            kb.dma(ln2gb[:], ln2g_d[0:1, :].to_broadcast([128, D]), writes=[ln2gb])
            kb.dma(ln2bb[:], ln2b_d[0:1, :].to_broadcast([128, D]), writes=[ln2bb])
            with ExitStack() as esw:
                g2bc = kb.sb("g2bc", [128, D], F32, esw)
                b2bc = kb.sb("b2bc", [128, D], F32, esw)
                wstg = [kb.sb(f"wstg3{i}", [128, 8, 256], F32, esw) for i in range(2)]
                kb.dma(g2bc[:], g2row[:, :], writes=[g2bc])
                kb.dma(b2bc[0:1, :], b2_d[0:1, :], writes=[b2bc])
                kb.op(dve, lambda e: e.tensor_tensor(out=b2bc[0:1, :], in0=b2bc[0:1, :], in1=g2bc[0:1, :], op=ALU.mult),
                      reads=[b2bc, g2bc], writes=[b2bc])
                kb.op(dve, lambda e: e.tensor_copy(out=b2h[0:1, 0, :], in_=b2bc[0:1, :]), reads=[b2bc], writes=[b2h])
                kb.op(dve, lambda e: e.tensor_tensor(out=b2bc[0:1, :], in0=b2bc[0:1, :], in1=b2h[0:1, 0, :], op=ALU.subtract),
                      reads=[b2bc, b2h], writes=[b2bc])
                kb.op(dve, lambda e: e.tensor_copy(out=b2h[0:1, 1, :], in_=b2bc[0:1, :]), reads=[b2bc], writes=[b2h])
                for blk in range(16):
                    c0 = blk * 256
                    load_weight_block(W1b, lambda ks, c0=c0: W1b[:, ks, c0:c0 + 256], w1_v[:, :, c0:c0 + 256], wstg[blk % 2], 8)
                wstg2 = [Buf(wstg[i].t, f"wstg3b{i}") for i in range(2)]
                kb.barrier()
                for blk in range(16):
                    sgb = wstg2[blk % 2]
                    sv = sgb.t[:].rearrange("p a b -> p (a b)").rearrange("p (j n) -> p j n", j=2)
                    kb.dma(sv[:, 0:1, :], w2_v[:, blk * 2:blk * 2 + 1, :], writes=[sgb])
                    kb.dma(sv[:, 1:2, :], w2_v[:, blk * 2 + 1:blk * 2 + 2, :], writes=[sgb])
                    for jj in range(2):
                        eng = dve if jj == 0 else pool
                        kb.op(eng, lambda e, jj=jj, blk=blk, sv=sv: e.tensor_tensor(out=W2b[:, blk * 2 + jj, :], in0=sv[:, jj, :],
                                                                                   in1=g2bc[:], op=ALU.mult),
                              reads=[sgb, g2bc], writes=[W2b])
                kb.barrier()
            xs = [kb.sb(f"x3s{i}", [128, D], F32, es3) for i in range(2 * GRP)]
            xn = kb.sb("xn3", [128, D], BF16, es3)
            h2T = kb.sb("h2T", [128, 8, GRP * 128], BF16, es3)
            hid = kb.sb("hid", [128, 32, GRP * 128], BF16, es3)
            rl = [kb.sb(f"rl{i}", [128, GRP * 128], BF16, es3) for i in range(2)]
            stats = kb.sb("stats3", [128, 2, 6], F32, es3)
            mv = kb.sb("mv3", [128, 4], F32, es3)
            rstd = kb.sb("rstd3", [128, 2], F32, es3)
            NGRP = NT // GRP
            for gi in range(NGRP):
                tiles = [gi * GRP + a for a in range(GRP)]
                xts = [xs[(gi % 2) * GRP + a] for a in range(GRP)]
                for a, ti in enumerate(tiles):
                    xt = xts[a]
                    kb.dma(xt[:, 0:512], y_d[ti * 128:(ti + 1) * 128, 0:512], writes=[xt])
                    kb.dma(xt[:, 512:1024], y_d[ti * 128:(ti + 1) * 128, 512:1024], writes=[xt])
                    make_hT(xt, h2T, xn, stats, mv, rstd, 2, 3, tok0=a * 128)
                for j in range(32):
                    bank = PB[j % 2]
                    for kc in range(8):
                        kb.op(pe, lambda e, j=j, kc=kc, bank=bank: e.matmul(bank[:, 0:GRP * 128], lhsT=W1b[:, kc, j * 128:(j + 1) * 128],
                                                                            rhs=h2T[:, kc, :], start=(kc == 0), stop=(kc == 7)),
                              reads=[W1b, h2T], writes=[bank], sig=(kc == 7))
                    rb = rl[j % 2]
                    kb.op(act, lambda e, j=j, bank=bank, rb=rb: e.activation(out=rb[:], in_=bank[:, 0:GRP * 128], func=AF.Relu,
                                                                             bias=smallc[:, 24 + j:25 + j], scale=1.0),
                          reads=[bank, smallc], writes=[rb])
                    eng = pool if j % 2 == 0 else dve
                    kb.op(eng, lambda e, j=j, rb=rb: e.tensor_tensor(out=hid[:, j, :], in0=rb[:], in1=rb[:], op=ALU.mult),
                          reads=[rb], writes=[hid])
                for a, ti in enumerate(tiles):
                    xt = xts[a]
                    for half in range(2):
                        bank = PB[2 + 2 * (a % 2) + half]
                        for j in range(32):
                            kb.op(pe, lambda e, j=j, a=a, half=half, bank=bank: e.matmul(
                                bank[:, :], lhsT=hid[:, j, a * 128:(a + 1) * 128], rhs=W2b[:, j, half * 512:(half + 1) * 512],
                                start=(j == 0), stop=False), reads=[hid, W2b], writes=[bank], sig=False)
                        for hl in range(2):
                            kb.op(pe, lambda e, hl=hl, half=half, bank=bank: e.matmul(
                                bank[:, :], lhsT=onesb[0:1, :], rhs=b2h[0:1, hl, half * 512:(half + 1) * 512],
                                start=False, stop=(hl == 1)), reads=[onesb, b2h], writes=[bank], sig=(hl == 1))
                        kb.op(dve, lambda e, half=half, bank=bank, xt=xt: e.scalar_tensor_tensor(
                            out=xt[:, half * 512:(half + 1) * 512], in0=xt[:, half * 512:(half + 1) * 512], scalar=ALPHA,
                            in1=bank[:, :], op0=ALU.mult, op1=ALU.add), reads=[xt, bank], writes=[xt])
                    ln_stats(xt[:, :], xt, 1024, stats, mv, rstd)
                    kb.op(dve, lambda e, xt=xt: e.scalar_tensor_tensor(out=xt[:, :], in0=xt[:, :], scalar=mv[:, 0:1], in1=ln2gb[:],
                                                                       op0=ALU.subtract, op1=ALU.mult), reads=[xt, mv, ln2gb], writes=[xt])
                    kb.op(pool, lambda e, xt=xt: e.tensor_scalar(out=xt[:, :], in0=xt[:, :], scalar1=rstd[:, 0:1], scalar2=None,
                                                                 op0=ALU.mult), reads=[xt, rstd], writes=[xt])
                    kb.op(pool, lambda e, xt=xt: e.tensor_tensor(out=xt[:, :], in0=xt[:, :], in1=ln2bb[:], op=ALU.add),
                          reads=[xt, ln2bb], writes=[xt])
                    kb.dma(y_d[ti * 128:(ti + 1) * 128, :], xt[:, :], reads=[xt])
            kb.barrier()
    return nc, dbg_outs


_CACHE = {}


def make_in_maps(inputs):
    g = lambda k: np.ascontiguousarray(np.asarray(inputs[k], dtype=np.float32))
    shared = {
        "c_ctx": g("c_ctx").reshape(8, 128),
        "w_ada": g("w_ada")[0],
        "b_ada": g("b_ada")[0].reshape(48, 128),
        "w_in": g("w_in")[0],
        "w_s": g("w_s")[0],
        "b_s": g("b_s")[0].reshape(1, 512),
        "ln_v_g": g("ln_v_g")[0].reshape(4, 128),
        "ln_v_b": g("ln_v_b")[0].reshape(4, 128),
        "conv_qk": g("conv_qk")[0].reshape(12, 128),
        "b_gates": g("b_gates")[0].reshape(1, 16),
        "hn_g": g("hn_g")[0].reshape(4, 128),
        "w_out": g("w_out")[0],
        "ln1_g": g("ln1_g")[0].reshape(1, D),
        "ln1_b": g("ln1_b")[0].reshape(1, D),
        "w1": g("w1")[0],
        "b1": g("b1")[0].reshape(32, 128),
        "w2": g("w2")[0],
        "b2": g("b2")[0].reshape(1, D),
        "ln2_g": g("ln2_g")[0].reshape(1, D),
        "ln2_b": g("ln2_b")[0].reshape(1, D),
    }
    x, c, ctx = g("x"), g("c"), g("ctx")
    maps = []
    for b in range(x.shape[0]):
        m = dict(shared)
        m["x"] = x[b]
        m["c"] = c[b].reshape(8, 128)
        m["ctx"] = ctx[b]
        maps.append(m)
    return maps


def kernel(**inputs):
    if "nc" not in _CACHE:
        _CACHE["nc"] = build_program(False)[0]
    nc = _CACHE["nc"]
    maps = make_in_maps(inputs)
    n = len(maps)
    res = run_bass_kernel_spmd(nc, maps, core_ids=list(range(n)))
    out = np.stack([np.asarray(r["y"], dtype=np.float32) for r in res.results], axis=0)
    return out
```

```python
import math
from contextlib import ExitStack
import numpy as np
import concourse.bass as bass
import concourse.mybir as mybir
from concourse.bass_utils import run_bass_kernel_spmd

F32 = mybir.dt.float32
BF16 = mybir.dt.bfloat16
AF = mybir.ActivationFunctionType
ALU = mybir.AluOpType

D = 1024
S = 4096
CTX = 256
NT = S // 128
NCT = CTX // 128
NG = NT + NCT
DIN = 2576
DFF = 4096
ALPHA = 2.0 ** 0.25
EPS = 1e-5
SEM_LIMIT = 3000


class Tok:
    __slots__ = ("sem", "val", "key")

    def __init__(self, sem, val, key):
        self.sem, self.val, self.key = sem, val, key


class Buf:
    def __init__(self, t, name):
        self.t = t
        self.name = name
        self.w = None
        self.r = {}
        self.dsem = None
        self.dcount = 0

    def __getitem__(self, idx):
        return self.t[idx]


class Eng:
    def __init__(self, kb, name, h):
        self.kb, self.name, self.h = kb, name, h
        self.seen = {}
        self.epoch = 0
        self.count = 0
        self.sem = kb.new_sem(f"{name}_e0")
        self.pending = False

    def roll(self):
        if self.count >= SEM_LIMIT and not self.pending:
            self.epoch += 1
            self.count = 0
            self.sem = self.kb.new_sem(f"{self.name}_e{self.epoch}")


class KB:
    def __init__(self, nc, es):
        self.nc, self.es = nc, es
        self.nsem = 0
        self.pe = Eng(self, "pe", nc.tensor)
        self.act = Eng(self, "act", nc.scalar)
        self.dve = Eng(self, "dve", nc.vector)
        self.pool = Eng(self, "pool", nc.gpsimd)
        self.sp = Eng(self, "sp", nc.sync)
        self.engs = [self.pe, self.act, self.dve, self.pool, self.sp]
        self.dma_toks = []

    def new_sem(self, name):
        self.nsem += 1
        s = self.es.enter_context(self.nc.semaphore(name))
        return (s, name)

    def sb(self, name, shape, dt, es=None):
        t = (es or self.es).enter_context(self.nc.sbuf_tensor(name, list(shape), dt))
        return Buf(t, name)

    def ps(self, name, shape, dt, es=None):
        t = (es or self.es).enter_context(self.nc.psum_tensor(name, list(shape), dt))
        return Buf(t, name)

    def wait(self, eng, tok):
        if tok is None:
            return
        if eng.name == "pe" and tok.key.startswith("pe_e"):
            return
        if eng.seen.get(tok.key, 0) >= tok.val:
            return
        eng.h.wait_ge(tok.sem, tok.val)
        eng.seen[tok.key] = tok.val

    def _deps(self, eng, reads, writes):
        for b in reads:
            self.wait(eng, b.w)
        for b in writes:
            self.wait(eng, b.w)
            for t in b.r.values():
                self.wait(eng, t)

    def _mark(self, tok, reads, writes):
        for b in reads:
            old = b.r.get(tok.key)
            if old is None or old.val < tok.val:
                b.r[tok.key] = tok
        for b in writes:
            b.w = tok
            b.r = {}

    def record(self, body, *args):
        self.rec = []
        body(*args)
        r, self.rec = self.rec, None
        return r

    def op(self, eng, fn, reads=(), writes=(), sig=True):
        if getattr(self, "rec", None) is not None:
            self.rec.append(lambda: self._op(eng, fn, reads, writes, sig))
            return None
        return self._op(eng, fn, reads, writes, sig)

    def _op(self, eng, fn, reads=(), writes=(), sig=True):
        if sig:
            eng.roll()
        self._deps(eng, reads, writes)
        inst = fn(eng.h)
        if sig:
            eng.count += 1
            inst.then_inc(eng.sem[0], 1)
            tok = Tok(eng.sem[0], eng.count, eng.sem[1])
            eng.pending = False
        else:
            tok = Tok(eng.sem[0], eng.count + 1, eng.sem[1])
            eng.pending = True
        self._mark(tok, reads, writes)
        return tok

    def dma(self, out_ap, in_ap, reads=(), writes=(), sembuf=None, eng=None):
        if getattr(self, "rec", None) is not None:
            self.rec.append(lambda: self._dma(out_ap, in_ap, reads, writes, sembuf, eng))
            return None
        return self._dma(out_ap, in_ap, reads, writes, sembuf, eng)

    def _dma(self, out_ap, in_ap, reads=(), writes=(), sembuf=None, eng=None):
        eng = eng or self.sp
        sb_ = sembuf or (writes[0] if writes else reads[0])
        if sb_.dsem is None:
            sb_.dsem = self.new_sem(f"d_{sb_.name}")
        self._deps(eng, reads, writes)
        inst = eng.h.dma_start(out=out_ap, in_=in_ap)
        inst.then_inc(sb_.dsem[0], 16)
        sb_.dcount += 16
        tok = Tok(sb_.dsem[0], sb_.dcount, sb_.dsem[1])
        self._mark(tok, reads, writes)
        self.dma_toks.append(tok)
        return tok

    def barrier(self):
        toks = []
        for e in self.engs:
            if e.count > 0:
                assert not e.pending
                toks.append(Tok(e.sem[0], e.count, e.sem[1]))
        toks += self.dma_toks
        self.dma_toks = []
        for e in self.engs:
            for t in toks:
                self.wait(e, t)


class _Stop(Exception):
    pass


def interleave(step_lists, H):
    n = len(step_lists)
    T = max(i * H + len(sl) for i, sl in enumerate(step_lists))
    lo = 0
    for t in range(T):
        while lo < n and t - lo * H >= len(step_lists[lo]):
            lo += 1
        i = lo
        while i < n and t - i * H >= 0:
            k = t - i * H
            if k < len(step_lists[i]):
                step_lists[i][k]()
            i += 1


def build_program(dbg=False, stage=99):
    try:
        return _build_program(dbg, stage)
    except _Stop as ex:
        return ex.args[0]


def _build_program(dbg, stage):
    nc = bass.Bass("TRN2", target_bir_lowering=False)

    def din(name, shape):
        return nc.dram_tensor(name, list(shape), F32, kind="ExternalInput").ap()

    x_d = din("x", [S, D])
    c_d = din("c", [8, 128])
    ctx_d = din("ctx", [CTX, D])
    cctx_d = din("c_ctx", [8, 128])
    wada_d = din("w_ada", [D, 6 * D])
    bada_d = din("b_ada", [48, 128])
    win_d = din("w_in", [D, DIN])
    ws_d = din("w_s", [4, 128, 128])
    bs_d = din("b_s", [1, 512])
    lnvg_d = din("ln_v_g", [4, 128])
    lnvb_d = din("ln_v_b", [4, 128])
    conv_d = din("conv_qk", [12, 128])
    bg_d = din("b_gates", [1, 16])
    hng_d = din("hn_g", [4, 128])
    wout_d = din("w_out", [D, D])
    ln1g_d = din("ln1_g", [1, D])
    ln1b_d = din("ln1_b", [1, D])
    w1_d = din("w1", [D, DFF])
    b1_d = din("b1", [32, 128])
    w2_d = din("w2", [DFF, D])
    b2_d = din("b2", [1, D])
    ln2g_d = din("ln2_g", [1, D])
    ln2b_d = din("ln2_b", [1, D])
    y_d = nc.dram_tensor("y", [S, D], F32, kind="ExternalOutput").ap()
    dbg_outs = {}

    with ExitStack() as es:
        kb = KB(nc, es)
        pe, act, dve, pool, sp = kb.pe, kb.act, kb.dve, kb.pool, kb.sp

        PB = [kb.ps(f"pb{i}", [128, 512], F32) for i in range(7)]
        PT = kb.ps("pt", [128, 1024], BF16)

        identf = kb.sb("identf", [128, 128], F32)
        identb = kb.sb("identb", [128, 128], BF16)
        LT = kb.sb("LT", [128, 128], F32)
        UT = kb.sb("UT", [128, 128], F32)
        onesf = kb.sb("onesf", [128, 128], F32)
        onesb = kb.sb("onesb", [128, 128], BF16)
        cst = kb.sb("cst", [128, 8], F32)
        modc = kb.sb("modc", [128, 6, 8], F32)
        smallc = kb.sb("smallc", [128, 64], F32)
        bgb = kb.sb("bgb", [128, 16], F32)
        BiasA = kb.sb("BiasA", [128, 4, 128], F32)
        wsT = kb.sb("wsT", [128, 4, 128], BF16)
        setup = Buf(None, "setup")
        ccol = kb.sb("ccol", [128, 2, 8], F32)
        badac = kb.sb("badac", [128, 48], F32)

        def dbg_out(name, buf, ap, shape, dt=F32):
            if not dbg:
                return
            o = nc.dram_tensor("dbg_" + name, list(shape), dt, kind="ExternalOutput").ap()
            dbg_outs[name] = (shape, dt)
            kb.dma(o, ap, reads=[buf], sembuf=buf)

        kb.op(pool, lambda e: e.memset(onesf[:], 1.0), writes=[onesf])
        kb.op(pool, lambda e: e.memset(onesb[:], 1.0), writes=[onesb])
        kb.op(pool, lambda e: e.memset(cst[:, 0:1], -0.5), writes=[cst])
        kb.op(pool, lambda e: e.memset(cst[:, 1:2], EPS), writes=[cst])
        kb.op(pool, lambda e: e.memset(cst[:, 2:3], math.log(8.0)), writes=[cst])
        kb.op(pool, lambda e: e.memset(cst[:, 3:4], 1.0), writes=[cst])
        kb.op(pool, lambda e: e.affine_select(out=identf[:], in_=onesf[:], pattern=[[-1, 128]], compare_op=ALU.is_equal,
                                              fill=0.0, base=0, channel_multiplier=1), reads=[onesf], writes=[identf])
        kb.op(pool, lambda e: e.affine_select(out=LT[:], in_=onesf[:], pattern=[[1, 128]], compare_op=ALU.is_ge,
                                              fill=0.0, base=0, channel_multiplier=-1), reads=[onesf], writes=[LT])
        kb.op(pool, lambda e: e.affine_select(out=UT[:], in_=onesf[:], pattern=[[-1, 128]], compare_op=ALU.is_ge,
                                              fill=0.0, base=0, channel_multiplier=1), reads=[onesf], writes=[UT])
        kb.op(dve, lambda e: e.tensor_copy(out=identb[:], in_=identf[:]), reads=[identf], writes=[identb])

        es_p = es.enter_context(ExitStack())
        qT = kb.sb("qT", [128, 2, S], BF16, es_p)
        kT = kb.sb("kT", [128, 2, NG * 128], BF16, es_p)
        vaug = kb.sb("vaug", [128, NG, 4, 130], BF16, es_p)
        Gt = kb.sb("Gt", [128, NG, 16], F32, es_p)
        WS = kb.sb("WS", [128, NG, 8], F32, es_p)
        LB = kb.sb("LB", [128, NG, 8], F32, es_p)
        CW = kb.sb("CW", [128, NG, 8], F32, es_p)
        ST = kb.sb("ST", [128, NT, 2, 2, 130], BF16, es_p)
        kb.op(pool, lambda e: e.memset(vaug[:, :, :, 128:130], 1.0), writes=[vaug])

        es0 = es.enter_context(ExitStack())
        ln1gb = kb.sb("ln1gb", [128, D], F32, es0)
        ln1bb = kb.sb("ln1bb", [128, D], F32, es0)
        es_set = es.enter_context(ExitStack())
        rows = kb.sb("rows", [128, 256], F32, es_set)
        bsb = kb.sb("bsb", [128, 512], F32, es_set)
        R_C, R_CC, R_BADA, R_CONV, R_GV, R_BV, R_HNG, R_B1 = 0, 8, 16, 64, 76, 80, 84, 88
        kb.dma(rows[0:8, 0:128], c_d[:, :], writes=[rows])
        kb.dma(rows[0:8, 128:256], cctx_d[:, :], writes=[rows])
        rows2 = kb.sb("rows2", [128, 128], F32, es_set)
        kb.dma(rows2[0:48, :], bada_d[:, :], writes=[rows2])
        rows3 = kb.sb("rows3", [128, 128], F32, es_set)
        kb.dma(rows3[0:12, :], conv_d[:, :], writes=[rows3])
        kb.dma(rows3[32:36, :], lnvg_d[:, :], writes=[rows3])
        kb.dma(rows3[64:68, :], lnvb_d[:, :], writes=[rows3])
        rows4 = kb.sb("rows4", [128, 128], F32, es_set)
        kb.dma(rows4[0:4, :], hng_d[:, :], writes=[rows4])
        kb.dma(rows4[32:64, :], b1_d[:, :], writes=[rows4])
        kb.dma(bgb[:], bg_d[0:1, :].to_broadcast([128, 16]), writes=[bgb])
        kb.dma(bsb[:], bs_d[0:1, :].to_broadcast([128, 512]), writes=[bsb])
        wsr = kb.sb("wsr", [128, 4, 128], F32, es_set)
        kb.dma(wsr[:], ws_d.rearrange("g t s -> t g s"), writes=[wsr])


        def tr_f32(dst_ap, dst_buf, src_ap, src_buf, n, bank, p0=0):
            kb.op(pe, lambda e: e.transpose(out=bank[:, 0:n], in_=src_ap, identity=identf[p0:p0 + n, p0:p0 + n]),
                  reads=[src_buf, identf], writes=[bank])
            kb.op(dve, lambda e: e.tensor_copy(out=dst_ap, in_=bank[:, 0:n]), reads=[bank], writes=[dst_buf])

        craw = kb.sb("craw", [128, 2, 8], F32, es_set)
        tr_f32(craw[:, 0, :], craw, rows[0:8, 0:128], rows, 8, PB[0])
        tr_f32(craw[:, 1, :], craw, rows[0:8, 128:256], rows, 8, PB[1])
        kb.op(act, lambda e: e.activation(out=ccol[:], in_=craw[:], func=AF.Silu), reads=[craw], writes=[ccol])
        tr_f32(badac[:, :], badac, rows2[0:48, :], rows2, 48, PB[2])
        tr_f32(smallc[:, 12:24], smallc, rows3[0:12, :], rows3, 12, PB[3])
        tr_f32(smallc[:, 0:4], smallc, rows3[32:36, :], rows3, 4, PB[4], p0=32)
        tr_f32(smallc[:, 4:8], smallc, rows3[64:68, :], rows3, 4, PB[5], p0=64)
        tr_f32(smallc[:, 8:12], smallc, rows4[0:4, :], rows4, 4, PB[6])
        tr_f32(smallc[:, 24:56], smallc, rows4[32:64, :], rows4, 32, PB[0], p0=32)
        wsTf = kb.sb("wsTf", [128, 4, 128], F32, es_set)
        for g in range(4):
            kb.op(pe, lambda e, g=g: e.transpose(out=PB[1][:, g * 128:(g + 1) * 128], in_=wsr[:, g, :], identity=identf[:]),
                  reads=[wsr, identf], writes=[PB[1]])
        kb.op(dve, lambda e: e.tensor_copy(out=wsTf[:].rearrange("p g t -> p (g t)"), in_=PB[1][:, :]), reads=[PB[1]], writes=[wsTf])
        kb.op(act, lambda e: e.activation(out=wsT[:], in_=wsTf[:], func=AF.Copy), reads=[wsTf], writes=[wsT])
        kb.op(pe, lambda e: e.matmul(PB[2][:, :], lhsT=onesf[:], rhs=wsTf[:].rearrange("p g t -> p (g t)"), start=True, stop=True),
              reads=[onesf, wsTf], writes=[PB[2]])
        for g in range(4):
            kb.op(dve, lambda e, g=g: e.scalar_tensor_tensor(out=BiasA[:, g, :], in0=PB[2][:, g * 128:(g + 1) * 128],
                                                             scalar=smallc[:, 4 + g:5 + g], in1=bsb[:, g * 128:(g + 1) * 128],
                                                             op0=ALU.mult, op1=ALU.add),
                  reads=[PB[2], smallc, bsb], writes=[BiasA])

        if stage == -1:
            dbg_out("smallc", smallc, smallc[:], [128, 64])
            dbg_out("BiasA", BiasA, BiasA[:], [128, 4, 128])
            dbg_out("ccol", ccol, ccol[:], [128, 2, 8])
            dbg_out("LT", LT, LT[:], [128, 128])
            dbg_out("identf", identf, identf[:], [128, 128])
            kb.barrier()
            raise _Stop((nc, dbg_outs))
        kb.barrier()
        es_set.close()
        kb.dma(ln1gb[:], ln1g_d[0:1, :].to_broadcast([128, D]), writes=[ln1gb])
        kb.dma(ln1bb[:], ln1b_d[0:1, :].to_broadcast([128, D]), writes=[ln1bb])
        g2row = nc.dram_tensor("g2scratch", [128, D], F32, kind="Internal").ap()
        g1row = nc.dram_tensor("g1scratch", [128, D], F32, kind="Internal").ap()

        with ExitStack() as esa:
            stg = [kb.sb(f"astg{i}", [128, 8, 512], F32, esa) for i in range(4)]
            scb = kb.sb("scb", [128, 8, 128], F32, esa)
            badab = kb.sb("badab", [128, 2, D], F32, esa)
            g2bc0 = kb.sb("g2bc0", [128, D], F32, esa)
            g1bc = kb.sb("g1bc0", [128, D], F32, esa)
            kb.dma(badab[:, 0, :], bada_d[16:24, :].rearrange("(o a) b -> o (a b)", o=1).to_broadcast([128, D]), writes=[badab])
            kb.dma(badab[:, 1, :], bada_d[40:48, :].rearrange("(o a) b -> o (a b)", o=1).to_broadcast([128, D]), writes=[badab])
            for kc in range(8):
                kb.op(dve, lambda e, kc=kc: e.tensor_copy(out=scb[:, kc, :], in_=ccol[:, 0, kc:kc + 1].to_broadcast([128, 128])),
                      reads=[ccol], writes=[scb])
            wada_v = wada_d.rearrange("(kc p) n -> p kc n", p=128)
            col_kind = {0: 0, 1: 0, 2: 1, 3: 1, 6: 2, 7: 2, 8: 3, 9: 3}
            for blk in range(12):
                sg = stg[blk % 4]
                kb.dma(sg[:, 0:4, :], wada_v[:, 0:4, blk * 512:(blk + 1) * 512], writes=[sg])
                kb.dma(sg[:, 4:8, :], wada_v[:, 4:8, blk * 512:(blk + 1) * 512], writes=[sg])
                if blk in col_kind:
                    mi = col_kind[blk]
                    for jj in range(4):
                        j = blk * 4 + jj
                        fchunk = j % 8
                        bank = PB[jj % 4]
                        for kc in range(8):
                            kb.op(pe, lambda e, kc=kc, jj=jj, bank=bank: e.matmul(
                                bank[:, 0:2], lhsT=sg[:, kc, jj * 128:(jj + 1) * 128], rhs=ccol[:, :, kc],
                                start=(kc == 0), stop=(kc == 7)), reads=[sg, ccol], writes=[bank], sig=(kc == 7))
                        kb.op(dve, lambda e, bank=bank, mi=mi, fchunk=fchunk, j=j: e.tensor_tensor(
                            out=modc[:, mi, fchunk:fchunk + 1], in0=bank[:, 0:1], in1=badac[:, j:j + 1], op=ALU.add),
                            reads=[bank, badac], writes=[modc])
                        if mi < 2:
                            kb.op(dve, lambda e, bank=bank, mi=mi, fchunk=fchunk, j=j: e.tensor_tensor(
                                out=modc[:, 4 + mi, fchunk:fchunk + 1], in0=bank[:, 1:2], in1=badac[:, j:j + 1], op=ALU.add),
                                reads=[bank, badac], writes=[modc])
                else:
                    which = 0 if blk in (4, 5) else 1
                    half = blk % 2 if which == 1 else blk - 4
                    bank = PB[4 + (blk % 2)]
                    for kc in range(8):
                        kb.op(pe, lambda e, kc=kc, bank=bank: e.matmul(bank[:, :], lhsT=scb[:, kc, :], rhs=sg[:, kc, :],
                                                                      start=(kc == 0), stop=(kc == 7)),
                              reads=[sg, scb], writes=[bank], sig=(kc == 7))
                    dst = g1bc if which == 0 else g2bc0
                    kb.op(dve, lambda e, bank=bank, dst=dst, half=half, which=which: e.tensor_tensor(
                        out=dst[:, half * 512:(half + 1) * 512], in0=bank[:, :], in1=badab[:, which, half * 512:(half + 1) * 512],
                        op=ALU.add), reads=[bank, badab], writes=[dst])
            for mi in (1, 3, 5):
                kb.op(dve, lambda e, mi=mi: e.tensor_scalar(out=modc[:, mi, :], in0=modc[:, mi, :], scalar1=1.0, scalar2=None,
                                                            op0=ALU.add), reads=[modc], writes=[modc])
            g2st = kb.dma(g2row[:, :], g2bc0[:], reads=[g2bc0])
            kb.dma(g1row[:, :], g1bc[:], reads=[g1bc])
            kb.barrier()
            if stage == 0:
                dbg_out("modc", modc, modc[:], [128, 6, 8])
                dbg_out("g1bc", g1bc, g1bc[:], [128, D])
                dbg_out("smallc", smallc, smallc[:], [128, 64])
                dbg_out("BiasA", BiasA, BiasA[:], [128, 4, 128])
                kb.barrier()
                raise _Stop((nc, dbg_outs))

        def ln_stats(xap, xbuf, width, stats, mv, rstd, nmr=None):
            nchunk = width // 512
            for cidx in range(nchunk):
                kb.op(dve, lambda e, cidx=cidx: e.bn_stats(out=stats[:, cidx, :], in_=xap[:, cidx * 512:(cidx + 1) * 512]),
                      reads=[xbuf], writes=[stats])
            kb.op(dve, lambda e: e.bn_aggr(out=mv[:, 0:2], in_=stats[:, 0:nchunk, :].rearrange("p a b -> p (a b)")),
                  reads=[stats], writes=[mv])
            kb.op(pool, lambda e: e.tensor_tensor(out=mv[:, 2:3], in0=mv[:, 1:2], in1=cst[:, 1:2], op=ALU.add),
                  reads=[mv, cst], writes=[mv])
            kb.op(pool, lambda e: e.tensor_tensor(out=rstd[:, 0:1], in0=mv[:, 2:3], in1=cst[:, 0:1], op=ALU.pow),
                  reads=[mv, cst], writes=[rstd])
            if nmr is not None:
                kb.op(dve, lambda e: e.scalar_tensor_tensor(out=rstd[:, 1:2], in0=mv[:, 0:1], scalar=-1.0, in1=rstd[:, 0:1],
                                                            op0=ALU.mult, op1=ALU.mult), reads=[mv, rstd], writes=[rstd])

        def make_hT(xt, hT, xn, stats, mv, rstd, mi_shift, mi_scale, tok0=0):
            ln_stats(xt[:, :], xt, 1024, stats, mv, rstd, nmr=True)
            kb.op(act, lambda e: e.activation(out=xn[:], in_=xt[:, :], func=AF.Identity, bias=rstd[:, 1:2], scale=rstd[:, 0:1]),
                  reads=[xt, rstd], writes=[xn])
            for kc in range(8):
                kb.op(pe, lambda e, kc=kc: e.transpose(out=PT[:, kc * 128:(kc + 1) * 128], in_=xn[:, kc * 128:(kc + 1) * 128],
                                                      identity=identb[:]), reads=[xn, identb], writes=[PT], sig=(kc == 7))
            for kc in range(8):
                if kc % 2 == 0:
                    kb.op(act, lambda e, kc=kc: e.activation(out=hT[:, kc, tok0:tok0 + 128], in_=PT[:, kc * 128:(kc + 1) * 128],
                                                             func=AF.Identity, bias=modc[:, mi_shift, kc:kc + 1],
                                                             scale=modc[:, mi_scale, kc:kc + 1]),
                          reads=[PT, modc], writes=[hT])
                else:
                    kb.op(dve, lambda e, kc=kc: e.tensor_scalar(out=hT[:, kc, tok0:tok0 + 128], in0=PT[:, kc * 128:(kc + 1) * 128],
                                                                scalar1=modc[:, mi_scale, kc:kc + 1],
                                                                scalar2=modc[:, mi_shift, kc:kc + 1], op0=ALU.mult, op1=ALU.add),
                          reads=[PT, modc], writes=[hT])

        def load_weight_block(dst, dst_ap_fn, src_ap, stg_buf, nk, scale_bc=None, scale_cols=None):
            half = nk // 2
            kb.dma(stg_buf[:, 0:half, :], src_ap[:, 0:half, :], writes=[stg_buf])
            kb.dma(stg_buf[:, half:nk, :], src_ap[:, half:nk, :], writes=[stg_buf])
            if scale_bc is None:
                kb.op(act, lambda e: e.activation(out=dst_ap_fn(slice(0, half)), in_=stg_buf[:, 0:half, :], func=AF.Copy),
                      reads=[stg_buf], writes=[dst])
                kb.op(pool, lambda e: e.tensor_copy(out=dst_ap_fn(slice(half, nk)), in_=stg_buf[:, half:nk, :]),
                      reads=[stg_buf], writes=[dst])
            else:
                for k in range(nk):
                    eng = dve if k % 2 == 0 else pool
                    kb.op(eng, lambda e, k=k: e.tensor_tensor(out=dst_ap_fn(k), in0=stg_buf[:, k, :],
                                                              in1=scale_bc[:, scale_cols], op=ALU.mult),
                          reads=[stg_buf, scale_bc], writes=[dst])

        win_v = win_d.rearrange("(kc p) n -> p kc n", p=128)
        with ExitStack() as es1:
            Win1 = kb.sb("Win1", [128, 8, 1040], BF16, es1)
            with ExitStack() as esw:
                wstg = [kb.sb(f"wstg{i}", [128, 8, 512], F32, esw) for i in range(2)]
                load_weight_block(Win1, lambda ks: Win1[:, ks, 0:512], win_v[:, :, 1024:1536], wstg[0], 8)
                load_weight_block(Win1, lambda ks: Win1[:, ks, 512:1024], win_v[:, :, 1536:2048], wstg[1], 8)
                kb.dma(wstg[0][:, :, 0:16], win_v[:, :, 2560:2576], writes=[wstg[0]])
                kb.op(dve, lambda e: e.tensor_copy(out=Win1[:, :, 1024:1040], in_=wstg[0][:, :, 0:16]), reads=[wstg[0]], writes=[Win1])
                kb.barrier()
            xs = [kb.sb(f"xs{i}", [128, D], F32, es1) for i in range(2)]
            xn = kb.sb("xn", [128, D], BF16, es1)
            hT = kb.sb("hT", [128, 8, 128], BF16, es1)
            stats = kb.sb("stats", [128, 2, 6], F32, es1)
            mv = kb.sb("mv", [128, 4], F32, es1)
            rstd = kb.sb("rstd", [128, 2], F32, es1)
            raw = [kb.sb(f"raw{i}", [128, 4, 130], F32, es1) for i in range(3)]
            cvt = kb.sb("cvt", [128, 4, 128], F32, es1)

            def conv_finish(gc_prev, rb):
                for cc in range(4):
                    kb.op(dve, lambda e, cc=cc: e.tensor_scalar(out=cvt[:, cc, :], in0=rb[:, cc, 0:128],
                                                                scalar1=smallc[:, 12 + 0 * 4 + cc:13 + 0 * 4 + cc], scalar2=None,
                                                                op0=ALU.mult), reads=[rb, smallc], writes=[cvt])
                    kb.op(dve, lambda e, cc=cc: e.scalar_tensor_tensor(out=cvt[:, cc, :], in0=rb[:, cc, 1:129],
                                                                       scalar=smallc[:, 12 + 1 * 4 + cc:13 + 1 * 4 + cc],
                                                                       in1=cvt[:, cc, :], op0=ALU.mult, op1=ALU.add),
                          reads=[rb, smallc, cvt], writes=[cvt])
                    kb.op(dve, lambda e, cc=cc: e.scalar_tensor_tensor(out=cvt[:, cc, :], in0=rb[:, cc, 2:130],
                                                                       scalar=smallc[:, 12 + 2 * 4 + cc:13 + 2 * 4 + cc],
                                                                       in1=cvt[:, cc, :], op0=ALU.mult, op1=ALU.add),
                          reads=[rb, smallc, cvt], writes=[cvt])
                t0 = gc_prev * 128
                kb.op(act, lambda e: e.activation(out=kT[:, :, t0:t0 + 128], in_=cvt[:, 2:4, :], func=AF.Silu),
                      reads=[cvt], writes=[kT])
                if gc_prev >= NCT:
                    l0 = (gc_prev - NCT) * 128
                    kb.op(act, lambda e: e.activation(out=qT[:, :, l0:l0 + 128], in_=cvt[:, 0:2, :], func=AF.Silu),
                          reads=[cvt], writes=[qT])

            xn_1 = [xn, kb.sb("xn_b", [128, D], BF16, es1)]
            hT_1 = [hT, kb.sb("hT_b", [128, 8, 128], BF16, es1)]
            stats_1 = [stats, kb.sb("stats_b", [128, 2, 6], F32, es1)]
            mv_1 = [mv, kb.sb("mv_b", [128, 4], F32, es1)]
            rstd_1 = [rstd, kb.sb("rstd_b", [128, 2], F32, es1)]

            def tile1(gc):
                sl = gc % 2
                xn, hT, stats, mv, rstd = xn_1[sl], hT_1[sl], stats_1[sl], mv_1[sl], rstd_1[sl]
                PBq, PBv, PBg = PB[3 * sl], PB[3 * sl + 1], PB[3 * sl + 2]
                is_ctx = gc < NCT
                src = ctx_d[gc * 128:(gc + 1) * 128, :] if is_ctx else x_d[(gc - NCT) * 128:(gc - NCT + 1) * 128, :]
                xt = xs[gc % 2]
                kb.dma(xt[:, 0:512], src[:, 0:512], writes=[xt])
                kb.dma(xt[:, 512:1024], src[:, 512:1024], writes=[xt])
                make_hT(xt, hT, xn, stats, mv, rstd, 4 if is_ctx else 0, 5 if is_ctx else 1)
                for cc in range(4):
                    for kc in range(8):
                        kb.op(pe, lambda e, cc=cc, kc=kc: e.matmul(PBq[:, cc * 128:(cc + 1) * 128],
                                                                   lhsT=Win1[:, kc, cc * 128:(cc + 1) * 128], rhs=hT[:, kc, :],
                                                                   start=(kc == 0), stop=(kc == 7)),
                              reads=[Win1, hT], writes=[PBq], sig=(kc == 7 and cc == 3))
                for kc in range(8):
                    kb.op(pe, lambda e, kc=kc: e.matmul(PBv[:, :], lhsT=hT[:, kc, :], rhs=Win1[:, kc, 512:1024],
                                                        start=(kc == 0), stop=(kc == 7)),
                          reads=[Win1, hT], writes=[PBv], sig=(kc == 7))
                for kc in range(8):
                    kb.op(pe, lambda e, kc=kc: e.matmul(PBg[:, 0:16], lhsT=hT[:, kc, :], rhs=Win1[:, kc, 1024:1040],
                                                        start=(kc == 0), stop=(kc == 7)),
                          reads=[Win1, hT], writes=[PBg], sig=(kc == 7))
                rb = raw[gc % 3]
                first = gc in (0, NCT)
                last = gc in (NCT - 1, NG - 1)
                kb.op(act, lambda e: e.activation(out=rb[:, :, 1:129], in_=PBq[:, :].rearrange("p (c t) -> p c t", c=4), func=AF.Copy),
                      reads=[PBq], writes=[rb])
                kb.op(dve, lambda e: e.tensor_copy(out=vaug[:, gc, :, 0:128], in_=PBv[:, :].rearrange("p (h v) -> p h v", h=4)),
                      reads=[PBv], writes=[vaug])
                kb.op(dve, lambda e: e.tensor_tensor(out=Gt[:, gc, :], in0=PBg[:, 0:16], in1=bgb[:], op=ALU.add),
                      reads=[PBg, bgb], writes=[Gt])
                if first:
                    kb.op(pool, lambda e: e.memset(rb[:, :, 0:1], 0.0), writes=[rb])
                else:
                    rprev = raw[(gc - 1) % 3]
                    kb.op(pool, lambda e: e.tensor_copy(out=rb[:, :, 0:1], in_=rprev[:, :, 128:129]), reads=[rprev], writes=[rb])
                    kb.op(pool, lambda e: e.tensor_copy(out=rprev[:, :, 129:130], in_=rb[:, :, 1:2]), reads=[rb], writes=[rprev])
                    conv_finish(gc - 1, rprev)
                if last:
                    kb.op(pool, lambda e: e.memset(rb[:, :, 129:130], 0.0), writes=[rb])
                    conv_finish(gc, rb)
            lists1 = [kb.record(tile1, gc) for gc in range(NG)]
            interleave(lists1, (len(lists1[2]) * 11) // 20)
            kb.barrier()

        if dbg:
            dbg_out("kT", kT, kT[:], [128, 2, NG * 128], BF16)
            dbg_out("qT", qT, qT[:], [128, 2, S], BF16)
            dbg_out("vaug", vaug, vaug[:], [128, NG, 4, 130], BF16)
            dbg_out("Gt", Gt, Gt[:], [128, NG, 16])
        if stage == 1:
            kb.barrier()
            raise _Stop((nc, dbg_outs))

        with ExitStack() as esg:
            NF = NG * 8
            Gv = Gt[:].rearrange("p c (d t h) -> p c d t h", d=2, t=2, h=4)
            LF = kb.sb("LF", [128, NG, 2, 4], F32, esg)
            T1 = kb.sb("T1", [128, NG, 2, 4], F32, esg)
            T2 = kb.sb("T2", [128, NG, 2, 4], F32, esg)
            Bc = kb.sb("Bc", [128, NG, 2, 4], F32, esg)
            Aa = kb.sb("Aa", [128, NG, 2, 4], F32, esg)
            Mcb = kb.sb("Mcb", [128, NG, 2, 4], F32, esg)
            rowA = kb.sb("rowA", [1, NG, 2, 4], F32, esg)
            rowB = kb.sb("rowB", [1, NG, 2, 4], F32, esg)
            rowM = kb.sb("rowM", [1, NG, 2, 4], F32, esg)
            rowm0 = kb.sb("rowm0", [1, NG, 2, 4], F32, esg)
            colmax = kb.sb("colmax", [128, 3], F32, esg)
            fl = lambda b: b[:].rearrange("p c d h -> p (c d h)")
            FG = Gv[:, :, :, 1, :]
            IG = Gv[:, :, :, 0, :]
            kb.op(dve, lambda e: e.tensor_scalar(out=T1[:], in0=FG, scalar1=-1.0, scalar2=None, op0=ALU.mult), reads=[Gt], writes=[T1])
            kb.op(dve, lambda e: e.tensor_tensor(out=T1[:], in0=T1[:], in1=FG, op=ALU.max), reads=[Gt, T1], writes=[T1])
            kb.op(act, lambda e: e.activation(out=T2[:], in_=T1[:], func=AF.Exp, scale=-1.0), reads=[T1], writes=[T2])
            kb.op(act, lambda e: e.activation(out=T2[:], in_=T2[:], func=AF.Ln, bias=cst[:, 3:4], scale=1.0), reads=[T2, cst], writes=[T2])
            kb.op(dve, lambda e: e.tensor_scalar(out=T1[:], in0=FG, scalar1=0.0, scalar2=None, op0=ALU.min), reads=[Gt], writes=[T1])
            kb.op(dve, lambda e: e.tensor_tensor(out=LF[:], in0=T1[:], in1=T2[:], op=ALU.subtract), reads=[T1, T2], writes=[LF])
            PBv = PB[0][:, 0:NF].rearrange("p (c d h) -> p c d h", c=NG, d=2, h=4)
            kb.op(pe, lambda e: e.matmul(PB[0][:, 0:NF], lhsT=LT[:], rhs=fl(LF), start=True, stop=True),
                  reads=[LT, LF], writes=[PB[0]])
            PBv1 = PB[1][:, 0:NF].rearrange("p (c d h) -> p c d h", c=NG, d=2, h=4)
            kb.op(pe, lambda e: e.matmul(PB[1][:, 0:NF], lhsT=UT[:], rhs=fl(LF), start=True, stop=True),
                  reads=[UT, LF], writes=[PB[1]])
            kb.op(dve, lambda e: e.tensor_copy(out=Bc[:, :, 0, :], in_=PBv[:, :, 0, :]), reads=[PB[0]], writes=[Bc])
            kb.op(dve, lambda e: e.tensor_copy(out=Bc[:, :, 1, :], in_=PBv1[:, :, 1, :]), reads=[PB[1]], writes=[Bc])
            kb.op(dve, lambda e: e.tensor_tensor(out=Aa[:], in0=IG, in1=Bc[:], op=ALU.subtract), reads=[Gt, Bc], writes=[Aa])
            AaF = fl(Aa)
            segs = [(0, 128), (128, 128), (256, NF - 256)]
            for si, (o, n) in enumerate(segs):
                kb.op(pe, lambda e, o=o, n=n: e.transpose(out=PB[2][0:n, 0:128], in_=AaF[:, o:o + n], identity=identf[:]),
                      reads=[Aa, identf], writes=[PB[2]])
                kb.op(dve, lambda e, si=si, n=n: e.reduce_max(out=colmax[0:n, si:si + 1], in_=PB[2][0:n, 0:128], axis=mybir.AxisListType.X),
                      reads=[PB[2]], writes=[colmax])
                kb.op(pe, lambda e, si=si, n=n, o=o: e.matmul(PB[3][0:1, o:o + n], lhsT=colmax[0:n, si:si + 1], rhs=identf[0:n, 0:n],
                                                               start=True, stop=True), reads=[colmax, identf], writes=[PB[3]])
            kb.op(dve, lambda e: e.tensor_copy(out=fl(rowA), in_=PB[3][0:1, 0:NF]), reads=[PB[3]], writes=[rowA])
            kb.op(pe, lambda e: e.matmul(PB[4][0:1, 0:NF], lhsT=onesf[:, 0:1], rhs=fl(LF), start=True, stop=True),
                  reads=[onesf, LF], writes=[PB[4]])
            kb.op(dve, lambda e: e.tensor_copy(out=fl(rowB), in_=PB[4][0:1, 0:NF]), reads=[PB[4]], writes=[rowB])
            order = [list(range(NG)), [1, 0] + list(range(NG - 1, NCT - 1, -1))]
            for d in range(2):
                g0 = order[d][0]
                kb.op(dve, lambda e, d=d, g0=g0: e.memset(rowm0[0:1, g0, d, :], 0.0), writes=[rowm0])
            for j in range(NG):
                for d in range(2):
                    gcur = order[d][j]
                    kb.op(dve, lambda e, d=d, gcur=gcur: e.tensor_tensor(out=rowM[0:1, gcur, d, :], in0=rowm0[0:1, gcur, d, :],
                                                                         in1=rowA[0:1, gcur, d, :], op=ALU.max),
                          reads=[rowm0, rowA], writes=[rowM])
                    if j + 1 < NG:
                        gn = order[d][j + 1]
                        kb.op(dve, lambda e, d=d, gcur=gcur, gn=gn: e.tensor_tensor(out=rowm0[0:1, gn, d, :], in0=rowM[0:1, gcur, d, :],
                                                                                   in1=rowB[0:1, gcur, d, :], op=ALU.add),
                              reads=[rowM, rowB], writes=[rowm0])
            kb.op(pe, lambda e: e.matmul(PB[5][:, 0:NF], lhsT=onesf[0:1, :], rhs=fl(rowM), start=True, stop=True),
                  reads=[onesf, rowM], writes=[PB[5]])
            kb.op(pe, lambda e: e.matmul(PB[6][:, 0:NF], lhsT=onesf[0:1, :], rhs=fl(rowm0), start=True, stop=True),
                  reads=[onesf, rowm0], writes=[PB[6]])
            kb.op(dve, lambda e: e.tensor_copy(out=fl(Mcb), in_=PB[5][:, 0:NF]), reads=[PB[5]], writes=[Mcb])
            kb.op(dve, lambda e: e.tensor_tensor(out=T1[:], in0=Aa[:], in1=Mcb[:], op=ALU.subtract), reads=[Aa, Mcb], writes=[T1])
            kb.op(act, lambda e: e.activation(out=WS[:].rearrange("p c j -> p (c j)"), in_=fl(T1), func=AF.Exp), reads=[T1], writes=[WS])
            kb.op(dve, lambda e: e.tensor_tensor(out=fl(T2), in0=PB[6][:, 0:NF], in1=fl(Mcb), op=ALU.subtract), reads=[PB[6], Mcb], writes=[T2])
            kb.op(act, lambda e: e.activation(out=CW[:].rearrange("p c j -> p (c j)"), in_=fl(T2), func=AF.Exp), reads=[T2], writes=[CW])
            kb.op(dve, lambda e: e.tensor_tensor(out=T1[:], in0=Bc[:], in1=Mcb[:], op=ALU.add), reads=[Bc, Mcb], writes=[T1])
            kb.op(act, lambda e: e.activation(out=LB[:].rearrange("p c j -> p (c j)"), in_=fl(T1), func=AF.Exp, bias=cst[:, 2:3], scale=-1.0),
                  reads=[T1, cst], writes=[LB])
            kb.barrier()

        if dbg:
            dbg_out("WS", WS, WS[:], [128, NG, 8])
            dbg_out("CW", CW, CW[:], [128, NG, 8])
            dbg_out("LB", LB, LB[:], [128, NG, 8])
        if stage == 2:
            kb.barrier()
            raise _Stop((nc, dbg_outs))

        with ExitStack() as ess:
            Cc = [kb.sb(f"Cc{d}", [128, 2, 130], F32, ess) for d in range(2)]
            kp = [kb.sb(f"kp{i}", [128, 4, 64], BF16, ess) for i in range(2)]
            c0b = [kb.sb(f"c0b{i}", [128, 2, 130], BF16, ess) for i in range(2)]
            for d in range(2):
                kb.op(pool, lambda e, d=d: e.memset(Cc[d][:], 0.0), writes=[Cc[d]])
            units = [(j, d) for j in range(NG) for d in range(2)]

            def stepA(u):
                j, d = units[u]
                if j == NG - 1:
                    return
                gcur = order[d][j]
                kpb = kp[u % 2]
                for pr in range(2):
                    kb.op(pe, lambda e, pr=pr: e.transpose(out=PT[:, pr * 128:(pr + 1) * 128],
                                                           in_=kT[:, pr, gcur * 128:(gcur + 1) * 128], identity=identb[:]),
                          reads=[kT, identb], writes=[PT], sig=(pr == 1))
                for h in range(4):
                    if h % 2 == 0:
                        kb.op(dve, lambda e, h=h: e.tensor_scalar(
                            out=kpb[:, h, :], in0=PT[:, h * 64:(h + 1) * 64], scalar1=WS[:, gcur, d * 4 + h:d * 4 + h + 1],
                            scalar2=None, op0=ALU.mult), reads=[PT, WS], writes=[kpb])
                    else:
                        kb.op(act, lambda e, h=h: e.activation(
                            out=kpb[:, h, :], in_=PT[:, h * 64:(h + 1) * 64], func=AF.Identity,
                            scale=WS[:, gcur, d * 4 + h:d * 4 + h + 1]), reads=[PT, WS], writes=[kpb])
                bank = PB[(u % 2) * 2:(u % 2) * 2 + 2]
                for h in range(4):
                    bk = bank[h // 2]
                    kb.op(pe, lambda e, h=h, bk=bk: e.matmul(
                        bk[:, (h % 2) * 130:(h % 2) * 130 + 130], lhsT=kpb[:, (h // 2) * 2:(h // 2) * 2 + 2, :].rearrange("p a b -> p (a b)"),
                        rhs=vaug[:, gcur, h, :], start=True, stop=True), reads=[kpb, vaug], writes=[bk])

            def stepB(u):
                j, d = units[u]
                gcur = order[d][j]
                C = Cc[d]
                if j > 0:
                    for h in range(4):
                        p0 = (h % 2) * 64
                        kb.op(dve, lambda e, h=h, p0=p0: e.tensor_scalar(
                            out=C[p0:p0 + 64, h // 2, :], in0=C[p0:p0 + 64, h // 2, :],
                            scalar1=CW[p0:p0 + 64, gcur, d * 4 + h:d * 4 + h + 1], scalar2=None, op0=ALU.mult),
                            reads=[C, CW], writes=[C])
                if gcur >= NCT:
                    kb.op(act, lambda e: e.activation(out=ST[:, gcur - NCT, d, :, :], in_=C[:], func=AF.Copy),
                          reads=[C], writes=[ST])
                if j == NG - 1:
                    return
                bank = PB[(u % 2) * 2:(u % 2) * 2 + 2]
                for h in range(4):
                    p0 = (h % 2) * 64
                    bk = bank[h // 2]
                    kb.op(dve, lambda e, h=h, p0=p0, bk=bk: e.tensor_tensor(
                        out=C[p0:p0 + 64, h // 2, :], in0=C[p0:p0 + 64, h // 2, :],
                        in1=bk[p0:p0 + 64, (h % 2) * 130:(h % 2) * 130 + 130], op=ALU.add), reads=[C, bk], writes=[C])

            stepA(0)
            for u in range(len(units)):
                if u + 1 < len(units):
                    stepA(u + 1)
                stepB(u)
            kb.barrier()

        if dbg:
            dbg_out("ST", ST, ST[:], [128, NT, 2, 2, 130], BF16)
        if stage == 3:
            kb.barrier()
            raise _Stop((nc, dbg_outs))

        wout_v = wout_d.rearrange("(kc p) n -> p kc n", p=128)
        with ExitStack() as es2:
            Win2 = kb.sb("Win2", [128, 8, 1536], BF16, es2)
            Wo = kb.sb("Wo", [128, 8, D], BF16, es2)
            with ExitStack() as esw:
                wstg = [kb.sb(f"wstg2{i}", [128, 8, 256], F32, esw) for i in range(2)]
                g1bc = kb.sb("g1bc", [128, D], F32, esw)
                kb.dma(g1bc[:], g1row[:, :], writes=[g1bc])
                nb = 0
                for (wc0, dc0) in ((0, 0), (256, 256), (512, 512), (768, 768), (2048, 1024), (2304, 1280)):
                    load_weight_block(Win2, lambda ks, dc0=dc0: Win2[:, ks, dc0:dc0 + 256], win_v[:, :, wc0:wc0 + 256], wstg[nb % 2], 8)
                    nb += 1
                for c0 in (0, 256, 512, 768):
                    load_weight_block(Wo, lambda k, c0=c0: Wo[:, k, c0:c0 + 256], wout_v[:, :, c0:c0 + 256], wstg[nb % 2], 8,
                                      scale_bc=g1bc, scale_cols=slice(c0, c0 + 256))
                    nb += 1
                kb.barrier()
            xs = [kb.sb(f"x2s{i}", [128, D], F32, es2) for i in range(2)]
            def two(name, shape, dt):
                return [kb.sb(f"{name}_{k}", shape, dt, es2) for k in range(2)]
            xn_ = two("xn2", [128, D], BF16)
            hT_ = two("hT2", [128, 8, 128], BF16)
            stats_ = two("stats2", [128, 4, 6], F32)
            mv_ = two("mv2", [128, 4], F32)
            rstd_ = two("rstd2", [128, 2], F32)
            mv4_ = two("mv4", [128, 4, 4], F32)
            rs4_ = two("rs4", [128, 4], F32)
            uT_ = two("uT", [128, 4, 128], BF16)
            sgo_ = two("sgo", [128, 4, 128], BF16)
            vn_ = two("vn", [128, 512], BF16)
            tA_ = two("tA", [128, 4, 128], F32)
            yT_ = two("yT", [128, 8, 128], BF16)
            sT_ = two("sT", [128, 8, 128], BF16)
            dn_ = two("dn", [128, 3, 8], F32)
            hs_ = two("hs", [128, 4, 128], F32)
            hn_ = two("hn", [128, 4, 128], BF16)
            Q2_ = [[kb.sb(f"Q2{k}_{i}", [128, 2, 128], BF16, es2) for i in range(2)] for k in range(2)]
            hg = kb.sb("hg", [128, 4], F32, es2)
            for k in range(2):
                for pr in range(2):
                    kb.op(pool, lambda e, k=k, pr=pr: e.memset(Q2_[k][pr][:], 0.0), writes=[Q2_[k][pr]])
            kb.op(dve, lambda e: e.tensor_copy(out=hg[:], in_=smallc[:, 8:12]), reads=[smallc], writes=[hg])

            def tile2(i):
                sl = i % 2
                xn, hT, stats, mv, rstd, mv4, rs4 = xn_[sl], hT_[sl], stats_[sl], mv_[sl], rstd_[sl], mv4_[sl], rs4_[sl]
                uT, sgo, vn, tA, yT, sT, dn, hs, hn, Q2 = uT_[sl], sgo_[sl], vn_[sl], tA_[sl], yT_[sl], sT_[sl], dn_[sl], hs_[sl], hn_[sl], Q2_[sl]
                gc = i + NCT
                t0k = gc * 128
                t0q = i * 128
                xt = xs[i % 2]
                kb.dma(xt[:, 0:512], x_d[i * 128:(i + 1) * 128, 0:512], writes=[xt])
                kb.dma(xt[:, 512:1024], x_d[i * 128:(i + 1) * 128, 512:1024], writes=[xt])
                make_hT(xt, hT, xn, stats, mv, rstd, 0, 1)
                for (bank, c0) in ((PB[0], 0), (PB[1], 1024)):
                    for cc in range(4):
                        for kc in range(8):
                            kb.op(pe, lambda e, cc=cc, kc=kc, bank=bank, c0=c0: e.matmul(
                                bank[:, cc * 128:(cc + 1) * 128], lhsT=Win2[:, kc, c0 + cc * 128:c0 + (cc + 1) * 128], rhs=hT[:, kc, :],
                                start=(kc == 0), stop=(kc == 7)), reads=[Win2, hT], writes=[bank], sig=(kc == 7 and cc == 3))
                for kc in range(8):
                    kb.op(pe, lambda e, kc=kc: e.matmul(PB[2][:, :], lhsT=hT[:, kc, :], rhs=Win2[:, kc, 512:1024],
                                                        start=(kc == 0), stop=(kc == 7)), reads=[Win2, hT], writes=[PB[2]], sig=(kc == 7))
                kb.op(act, lambda e: e.activation(out=uT[:].rearrange("p c t -> p (c t)"), in_=PB[0][:, :], func=AF.Copy),
                      reads=[PB[0]], writes=[uT])
                kb.op(act, lambda e: e.activation(out=sgo[:].rearrange("p c t -> p (c t)"), in_=PB[1][:, :], func=AF.Sigmoid),
                      reads=[PB[1]], writes=[sgo])
                ln_stats(PB[2][:, :], PB[2], 512, stats, mv, rstd)
                kb.op(dve, lambda e: e.tensor_scalar(out=vn[:], in0=PB[2][:, :], scalar1=mv[:, 0:1], scalar2=rstd[:, 0:1],
                                                     op0=ALU.subtract, op1=ALU.mult), reads=[PB[2], mv, rstd], writes=[vn])
                for g in range(4):
                    kb.op(pe, lambda e, g=g: e.matmul(PB[3][:, g * 128:(g + 1) * 128], lhsT=vn[:, g * 128:(g + 1) * 128], rhs=wsT[:, g, :],
                                                      start=True, stop=True), reads=[vn, wsT], writes=[PB[3]], sig=(g == 3))
                for g in range(4):
                    kb.op(dve, lambda e, g=g: e.scalar_tensor_tensor(out=tA[:, g, :], in0=PB[3][:, g * 128:(g + 1) * 128],
                                                                     scalar=smallc[:, g:g + 1], in1=BiasA[:, g, :], op0=ALU.mult, op1=ALU.add),
                          reads=[PB[3], smallc, BiasA], writes=[tA])
                kb.op(pool, lambda e: e.tensor_tensor(out=yT[:, 0:4, :], in0=tA[:], in1=uT[:], op=ALU.mult), reads=[tA, uT], writes=[yT])
                for pr in range(2):
                    for hh in range(2):
                        kb.op(pool, lambda e, pr=pr, hh=hh: e.tensor_copy(out=Q2[pr][hh * 64:(hh + 1) * 64, hh, :],
                                                                          in_=qT[hh * 64:(hh + 1) * 64, pr, t0q:t0q + 128]),
                              reads=[qT], writes=[Q2[pr]])
                for pr in range(2):
                    kb.op(pe, lambda e, pr=pr: e.matmul(PB[4][:, pr * 256:(pr + 1) * 256], lhsT=kT[:, pr, t0k:t0k + 128],
                                                        rhs=Q2[pr][:].rearrange("p a t -> p (a t)"), start=True, stop=True),
                          reads=[kT, Q2[pr]], writes=[PB[4]], sig=(pr == 1))
                for d in range(2):
                    msk = LT if d == 0 else UT
                    for h in range(4):
                        kb.op(dve, lambda e, d=d, h=h, msk=msk: e.scalar_tensor_tensor(
                            out=sT[:, d * 4 + h, :], in0=PB[4][:, h * 128:(h + 1) * 128], scalar=WS[:, gc, d * 4 + h:d * 4 + h + 1],
                            in1=msk[:], op0=ALU.mult, op1=ALU.mult), reads=[PB[4], WS, msk], writes=[sT])
                for d in range(2):
                    bank = PB[5 + d]
                    for h in range(4):
                        kb.op(pe, lambda e, d=d, h=h, bank=bank: e.matmul(bank[:, h * 128:(h + 1) * 128], lhsT=sT[:, d * 4 + h, :],
                                                                          rhs=vaug[:, gc, h, 0:128], start=True, stop=False),
                              reads=[sT, vaug], writes=[bank], sig=False)
                        kb.op(pe, lambda e, d=d, h=h, bank=bank: e.matmul(
                            bank[:, h * 128:(h + 1) * 128], lhsT=Q2[h // 2][:, h % 2, :],
                            rhs=ST[:, i, d, h // 2, 0:128], start=False, stop=True),
                            reads=[Q2[h // 2], ST], writes=[bank], sig=(h == 3))
                for d in range(2):
                    for h in range(4):
                        jn = d * 4 + h
                        kb.op(pe, lambda e, d=d, h=h, jn=jn: e.matmul(PB[3][:, 2 * jn:2 * jn + 2], lhsT=sT[:, jn, :],
                                                                      rhs=vaug[:, gc, h, 128:130], start=True, stop=False),
                              reads=[sT, vaug], writes=[PB[3]], sig=False)
                        kb.op(pe, lambda e, d=d, h=h, jn=jn: e.matmul(
                            PB[3][:, 2 * jn:2 * jn + 2], lhsT=Q2[h // 2][:, h % 2, :],
                            rhs=ST[:, i, d, h // 2, 128:130], start=False, stop=True),
                            reads=[Q2[h // 2], ST], writes=[PB[3]], sig=(jn == 7))
                den = PB[3][:, 0:16].rearrange("p (j two) -> p j two", two=2)[:, :, 0]
                kb.op(dve, lambda e: e.tensor_scalar(out=dn[:, 0, :], in0=den, scalar1=-1.0, scalar2=None, op0=ALU.mult),
                      reads=[PB[3]], writes=[dn])
                kb.op(dve, lambda e: e.tensor_tensor(out=dn[:, 1, :], in0=dn[:, 0, :], in1=den, op=ALU.max),
                      reads=[PB[3], dn], writes=[dn])
                kb.op(dve, lambda e: e.tensor_tensor(out=dn[:, 0, :], in0=dn[:, 1, :], in1=LB[:, gc, :], op=ALU.max),
                      reads=[dn, LB], writes=[dn])
                kb.op(dve, lambda e: e.reciprocal(out=dn[:, 2, :], in_=dn[:, 0, :]), reads=[dn], writes=[dn])
                for h in range(4):
                    kb.op(act, lambda e, h=h: e.activation(out=hs[:, h, :], in_=PB[5][:, h * 128:(h + 1) * 128], func=AF.Identity,
                                                           scale=dn[:, 2, h:h + 1]), reads=[PB[5], dn], writes=[hs])
                for h in range(4):
                    kb.op(dve, lambda e, h=h: e.scalar_tensor_tensor(out=hs[:, h, :], in0=PB[6][:, h * 128:(h + 1) * 128],
                                                                     scalar=dn[:, 2, 4 + h:5 + h], in1=hs[:, h, :], op0=ALU.mult, op1=ALU.add),
                          reads=[PB[6], dn, hs], writes=[hs])
                for h in range(4):
                    kb.op(dve, lambda e, h=h: e.bn_stats(out=stats[:, h, :], in_=hs[:, h, :]), reads=[hs], writes=[stats])
                for h in range(4):
                    kb.op(dve, lambda e, h=h: e.bn_aggr(out=mv4[:, h, 0:2], in_=stats[:, h, :]), reads=[stats], writes=[mv4])
                kb.op(pool, lambda e: e.tensor_tensor(out=mv4[:, :, 2], in0=mv4[:, :, 1], in1=cst[:, 1:2].to_broadcast([128, 4]), op=ALU.add),
                      reads=[mv4, cst], writes=[mv4])
                kb.op(pool, lambda e: e.tensor_tensor(out=rs4[:], in0=mv4[:, :, 2], in1=cst[:, 0:1].to_broadcast([128, 4]), op=ALU.pow),
                      reads=[mv4, cst], writes=[rs4])
                for h in range(4):
                    kb.op(dve, lambda e, h=h: e.tensor_scalar(out=hn[:, h, :], in0=hs[:, h, :], scalar1=mv4[:, h, 0:1], scalar2=rs4[:, h:h + 1],
                                                              op0=ALU.subtract, op1=ALU.mult), reads=[hs, mv4, rs4], writes=[hn])
                for h in range(4):
                    kb.op(pe, lambda e, h=h: e.transpose(out=PT[:, h * 128:(h + 1) * 128], in_=hn[:, h, :], identity=identb[:]),
                          reads=[hn, identb], writes=[PT], sig=(h == 3))
                for h in range(4):
                    kb.op(dve, lambda e, h=h: e.scalar_tensor_tensor(out=yT[:, 4 + h, :], in0=PT[:, h * 128:(h + 1) * 128],
                                                                     scalar=hg[:, h:h + 1], in1=sgo[:, h, :], op0=ALU.mult, op1=ALU.mult),
                          reads=[PT, hg, sgo], writes=[yT])
                for half in range(2):
                    for kc in range(8):
                        kb.op(pe, lambda e, half=half, kc=kc: e.matmul(PB[3 + half][:, :], lhsT=yT[:, kc, :],
                                                                       rhs=Wo[:, kc, half * 512:(half + 1) * 512],
                                                                       start=(kc == 0), stop=(kc == 7)),
                              reads=[yT, Wo], writes=[PB[3 + half]], sig=(kc == 7))
                for half in range(2):
                    kb.op(dve, lambda e, half=half: e.scalar_tensor_tensor(out=xt[:, half * 512:(half + 1) * 512],
                                                                           in0=xt[:, half * 512:(half + 1) * 512], scalar=ALPHA,
                                                                           in1=PB[3 + half][:, :], op0=ALU.mult, op1=ALU.add),
                          reads=[xt, PB[3 + half]], writes=[xt])
                ln_stats(xt[:, :], xt, 1024, stats, mv, rstd)
                kb.op(dve, lambda e: e.scalar_tensor_tensor(out=xt[:, :], in0=xt[:, :], scalar=mv[:, 0:1], in1=ln1gb[:],
                                                            op0=ALU.subtract, op1=ALU.mult), reads=[xt, mv, ln1gb], writes=[xt])
                kb.op(dve, lambda e: e.scalar_tensor_tensor(out=xt[:, :], in0=xt[:, :], scalar=rstd[:, 0:1], in1=ln1bb[:],
                                                            op0=ALU.mult, op1=ALU.add), reads=[xt, rstd, ln1bb], writes=[xt])
                kb.dma(y_d[i * 128:(i + 1) * 128, :], xt[:, :], reads=[xt])
            lists2 = [kb.record(tile2, i) for i in range(NT)]
            interleave(lists2, (len(lists2[0]) * 11) // 20)
            kb.barrier()

        if stage == 4:
            raise _Stop((nc, dbg_outs))
        es0.close()
        es_p.close()

        GRP = 2
        w1_v = w1_d.rearrange("(kc p) n -> p kc n", p=128)
        w2_v = w2_d.rearrange("(j p) n -> p j n", p=128)
        with ExitStack() as es3:
            W1b = kb.sb("W1b", [128, 8, DFF], BF16, es3)
            W2b = kb.sb("W2b", [128, 32, D], BF16, es3)
            ln2gb = kb.sb("ln2gb", [128, D], F32, es3)
            ln2bb = kb.sb("ln2bb", [128, D], F32, es3)
            b2h = kb.sb("b2h", [1, 2, D], BF16, es3)
            kb.dma(ln2gb[:], ln2g_d[0:1, :].to_broadcast([128, D]), writes=[ln2gb])
            kb.dma(ln2bb[:], ln2b_d[0:1, :].to_broadcast([128, D]), writes=[ln2bb])
            with ExitStack() as esw:
                g2bc = kb.sb("g2bc", [128, D], F32, esw)
                b2bc = kb.sb("b2bc", [128, D], F32, esw)
                NS3 = 6
                wstg = [kb.sb(f"wstg3{i}", [128, 8, 256], F32, esw) for i in range(NS3)]
                kb.dma(g2bc[:], g2row[:, :], writes=[g2bc])
                kb.dma(b2bc[0:1, :], b2_d[0:1, :], writes=[b2bc])
                kb.op(dve, lambda e: e.tensor_tensor(out=b2bc[0:1, :], in0=b2bc[0:1, :], in1=g2bc[0:1, :], op=ALU.mult),
                      reads=[b2bc, g2bc], writes=[b2bc])
                kb.op(dve, lambda e: e.tensor_copy(out=b2h[0:1, 0, :], in_=b2bc[0:1, :]), reads=[b2bc], writes=[b2h])
                kb.op(dve, lambda e: e.tensor_tensor(out=b2bc[0:1, :], in0=b2bc[0:1, :], in1=b2h[0:1, 0, :], op=ALU.subtract),
                      reads=[b2bc, b2h], writes=[b2bc])
                kb.op(dve, lambda e: e.tensor_copy(out=b2h[0:1, 1, :], in_=b2bc[0:1, :]), reads=[b2bc], writes=[b2h])
                for blk in range(16):
                    c0 = blk * 256
                    load_weight_block(W1b, lambda ks, c0=c0: W1b[:, ks, c0:c0 + 256], w1_v[:, :, c0:c0 + 256], wstg[blk % NS3], 8)
                wstg2 = [Buf(wstg[i].t, f"wstg3b{i}") for i in range(NS3)]
                kb.barrier()
                for blk in range(16):
                    sgb = wstg2[blk % NS3]
                    sv = sgb.t[:].rearrange("p a b -> p (a b)").rearrange("p (j n) -> p j n", j=2)
                    kb.dma(sv[:, 0:1, :], w2_v[:, blk * 2:blk * 2 + 1, :], writes=[sgb])
                    kb.dma(sv[:, 1:2, :], w2_v[:, blk * 2 + 1:blk * 2 + 2, :], writes=[sgb])
                    for jj in range(2):
                        eng = dve if jj == 0 else pool
                        kb.op(eng, lambda e, jj=jj, blk=blk, sv=sv: e.tensor_tensor(out=W2b[:, blk * 2 + jj, :], in0=sv[:, jj, :],
                                                                                   in1=g2bc[:], op=ALU.mult),
                              reads=[sgb, g2bc], writes=[W2b])
                kb.barrier()
            xs = [kb.sb(f"x3s{i}", [128, D], F32, es3) for i in range(2 * GRP)]
            xn = kb.sb("xn3", [128, D], BF16, es3)
            h2T_ = [kb.sb(f"h2T{k}", [128, 8, GRP * 128], BF16, es3) for k in range(2)]
            hid = kb.sb("hid", [128, 32, GRP * 128], BF16, es3)
            hidb = [Buf(hid.t, f"hid{j}") for j in range(32)]
            rl = [kb.sb(f"rl{i}", [128, GRP * 128], BF16, es3) for i in range(4)]
            stats_p = kb.sb("stats3p", [128, 2, 6], F32, es3)
            mv_p = kb.sb("mv3p", [128, 4], F32, es3)
            rstd_p = kb.sb("rstd3p", [128, 2], F32, es3)
            stats_e = kb.sb("stats3e", [128, 2, 6], F32, es3)
            mv_e = kb.sb("mv3e", [128, 4], F32, es3)
            rstd_e = kb.sb("rstd3e", [128, 2], F32, es3)
            NGRP = NT // GRP

            def prep3(gi):
                for a in range(GRP):
                    ti = gi * GRP + a
                    xt = xs[(gi % 2) * GRP + a]
                    kb.dma(xt[:, 0:512], y_d[ti * 128:(ti + 1) * 128, 0:512], writes=[xt])
                    kb.dma(xt[:, 512:1024], y_d[ti * 128:(ti + 1) * 128, 512:1024], writes=[xt])
                    make_hT(xt, h2T_[gi % 2], xn, stats_p, mv_p, rstd_p, 2, 3, tok0=a * 128)

            def main3(gi):
                h2T = h2T_[gi % 2]
                for j in range(32):
                    bank = PB[j % 2]
                    for kc in range(8):
                        kb.op(pe, lambda e, j=j, kc=kc, bank=bank: e.matmul(bank[:, 0:GRP * 128], lhsT=W1b[:, kc, j * 128:(j + 1) * 128],
                                                                            rhs=h2T[:, kc, :], start=(kc == 0), stop=(kc == 7)),
                              reads=[W1b, h2T], writes=[bank], sig=(kc == 7))
                    rb = rl[j % 4]
                    kb.op(act, lambda e, j=j, bank=bank, rb=rb: e.activation(out=rb[:], in_=bank[:, 0:GRP * 128], func=AF.Relu,
                                                                             bias=smallc[:, 24 + j:25 + j], scale=1.0),
                          reads=[bank, smallc], writes=[rb])
                    eng = pool if j % 2 == 0 else dve
                    kb.op(eng, lambda e, j=j, rb=rb: e.tensor_tensor(out=hid[:, j, :], in0=rb[:], in1=rb[:], op=ALU.mult),
                          reads=[rb], writes=[hidb[j]])
                for a in range(GRP):
                    ti = gi * GRP + a
                    xt = xs[(gi % 2) * GRP + a]
                    for half in range(2):
                        bank = PB[2 + 2 * (a % 2) + half]
                        for j in range(32):
                            kb.op(pe, lambda e, j=j, a=a, half=half, bank=bank: e.matmul(
                                bank[:, :], lhsT=hid[:, j, a * 128:(a + 1) * 128], rhs=W2b[:, j, half * 512:(half + 1) * 512],
                                start=(j == 0), stop=False), reads=[hidb[j], W2b], writes=[bank], sig=False)
                        for hl in range(2):
                            kb.op(pe, lambda e, hl=hl, half=half, bank=bank: e.matmul(
                                bank[:, :], lhsT=onesb[0:1, :], rhs=b2h[0:1, hl, half * 512:(half + 1) * 512],
                                start=False, stop=(hl == 1)), reads=[onesb, b2h], writes=[bank], sig=(hl == 1))
                        kb.op(dve, lambda e, half=half, bank=bank, xt=xt: e.scalar_tensor_tensor(
                            out=xt[:, half * 512:(half + 1) * 512], in0=xt[:, half * 512:(half + 1) * 512], scalar=ALPHA,
                            in1=bank[:, :], op0=ALU.mult, op1=ALU.add), reads=[xt, bank], writes=[xt])
                    ln_stats(xt[:, :], xt, 1024, stats_e, mv_e, rstd_e)
                    kb.op(dve, lambda e, xt=xt: e.scalar_tensor_tensor(out=xt[:, :], in0=xt[:, :], scalar=mv_e[:, 0:1], in1=ln2gb[:],
                                                                       op0=ALU.subtract, op1=ALU.mult), reads=[xt, mv_e, ln2gb], writes=[xt])
                    kb.op(pool, lambda e, xt=xt: e.tensor_scalar(out=xt[:, :], in0=xt[:, :], scalar1=rstd_e[:, 0:1], scalar2=None,
                                                                 op0=ALU.mult), reads=[xt, rstd_e], writes=[xt])
                    kb.op(pool, lambda e, xt=xt: e.tensor_tensor(out=xt[:, :], in0=xt[:, :], in1=ln2bb[:], op=ALU.add),
                          reads=[xt, ln2bb], writes=[xt])
                    kb.dma(y_d[ti * 128:(ti + 1) * 128, :], xt[:, :], reads=[xt])

            for st in kb.record(prep3, 0):
                st()
            for gi in range(NGRP):
                M = kb.record(main3, gi)
                P = kb.record(prep3, gi + 1) if gi + 1 < NGRP else []
                span = max(1, int(len(M) * 0.55))
                pi = 0
                for k, st in enumerate(M):
                    st()
                    want = min(len(P), ((k + 1) * len(P)) // span)
                    while pi < want:
                        P[pi]()
                        pi += 1
                while pi < len(P):
                    P[pi]()
                    pi += 1
            kb.barrier()
    return nc, dbg_outs


_CACHE = {}


def make_in_maps(inputs):
    g = lambda k: np.ascontiguousarray(np.asarray(inputs[k], dtype=np.float32))
    shared = {
        "c_ctx": g("c_ctx").reshape(8, 128),
        "w_ada": g("w_ada")[0],
        "b_ada": g("b_ada")[0].reshape(48, 128),
        "w_in": g("w_in")[0],
        "w_s": g("w_s")[0],
        "b_s": g("b_s")[0].reshape(1, 512),
        "ln_v_g": g("ln_v_g")[0].reshape(4, 128),
        "ln_v_b": g("ln_v_b")[0].reshape(4, 128),
        "conv_qk": g("conv_qk")[0].reshape(12, 128),
        "b_gates": g("b_gates")[0].reshape(1, 16),
        "hn_g": g("hn_g")[0].reshape(4, 128),
        "w_out": g("w_out")[0],
        "ln1_g": g("ln1_g")[0].reshape(1, D),
        "ln1_b": g("ln1_b")[0].reshape(1, D),
        "w1": g("w1")[0],
        "b1": g("b1")[0].reshape(32, 128),
        "w2": g("w2")[0],
        "b2": g("b2")[0].reshape(1, D),
        "ln2_g": g("ln2_g")[0].reshape(1, D),
        "ln2_b": g("ln2_b")[0].reshape(1, D),
    }
    x, c, ctx = g("x"), g("c"), g("ctx")
    maps = []
    for b in range(x.shape[0]):
        m = dict(shared)
        m["x"] = x[b]
        m["c"] = c[b].reshape(8, 128)
        m["ctx"] = ctx[b]
        maps.append(m)
    return maps


def kernel(**inputs):
    if "nc" not in _CACHE:
        _CACHE["nc"] = build_program(False)[0]
    nc = _CACHE["nc"]
    maps = make_in_maps(inputs)
    n = len(maps)
    res = run_bass_kernel_spmd(nc, maps, core_ids=list(range(n)))
    out = np.stack([np.asarray(r["y"], dtype=np.float32) for r in res.results], axis=0)
    return out
```

```python
import math
from contextlib import ExitStack
import numpy as np
import concourse.bass as bass
import concourse.mybir as mybir
from concourse.bass_utils import run_bass_kernel_spmd

F32 = mybir.dt.float32
BF16 = mybir.dt.bfloat16
AF = mybir.ActivationFunctionType
ALU = mybir.AluOpType

D = 1024
S = 4096
CTX = 256
NT = S // 128
NCT = CTX // 128
NG = NT + NCT
DIN = 2576
DFF = 4096
ALPHA = 2.0 ** 0.25
EPS = 1e-5
SEM_LIMIT = 3000


class Tok:
    __slots__ = ("sem", "val", "key")

    def __init__(self, sem, val, key):
        self.sem, self.val, self.key = sem, val, key


class Buf:
    def __init__(self, t, name, parent=None):
        self.t = t
        self.name = name
        self.w = None
        self.r = {}
        self.dsem = None
        self.dcount = 0
        self.parent = parent
        self.children = {}

    def sub(self, key):
        c = self.children.get(key)
        if c is None:
            c = Buf(self.t, f"{self.name}.{key}", parent=self)
            self.children[key] = c
        return c

    def __getitem__(self, idx):
        return self.t[idx]


class Eng:
    def __init__(self, kb, name, h):
        self.kb, self.name, self.h = kb, name, h
        self.seen = {}
        self.epoch = 0
        self.count = 0
        self.sem = kb.new_sem(f"{name}_e0")
        self.pending = False

    def roll(self):
        if self.count >= SEM_LIMIT and not self.pending:
            self.epoch += 1
            self.count = 0
            self.sem = self.kb.new_sem(f"{self.name}_e{self.epoch}")


class KB:
    def __init__(self, nc, es):
        self.nc, self.es = nc, es
        self.nsem = 0
        self.pe = Eng(self, "pe", nc.tensor)
        self.act = Eng(self, "act", nc.scalar)
        self.dve = Eng(self, "dve", nc.vector)
        self.pool = Eng(self, "pool", nc.gpsimd)
        self.sp = Eng(self, "sp", nc.sync)
        self.engs = [self.pe, self.act, self.dve, self.pool, self.sp]
        self.dma_toks = []

    def new_sem(self, name):
        self.nsem += 1
        s = self.es.enter_context(self.nc.semaphore(name))
        return (s, name)

    def sb(self, name, shape, dt, es=None):
        t = (es or self.es).enter_context(self.nc.sbuf_tensor(name, list(shape), dt))
        return Buf(t, name)

    def ps(self, name, shape, dt, es=None):
        t = (es or self.es).enter_context(self.nc.psum_tensor(name, list(shape), dt))
        b = Buf(t, name)
        b.psum = True
        return b

    def wait(self, eng, tok):
        if tok is None:
            return
        if eng.name == "pe" and tok.key.startswith("pe_e"):
            return
        if eng.seen.get(tok.key, 0) >= tok.val:
            return
        eng.h.wait_ge(tok.sem, tok.val)
        eng.seen[tok.key] = tok.val

    def _deps(self, eng, reads, writes):
        for b in reads:
            self.wait(eng, b.w)
            if getattr(b, "psum", False):
                for k_, t_ in b.r.items():
                    if not k_.startswith(eng.name + "_e"):
                        self.wait(eng, t_)
            if b.parent is not None:
                self.wait(eng, b.parent.w)
            for c in b.children.values():
                self.wait(eng, c.w)
        for b in writes:
            self.wait(eng, b.w)
            for t in b.r.values():
                self.wait(eng, t)
            if b.parent is not None:
                self.wait(eng, b.parent.w)
                for t in b.parent.r.values():
                    self.wait(eng, t)
            for c in b.children.values():
                self.wait(eng, c.w)
                for t in c.r.values():
                    self.wait(eng, t)

    def _mark(self, tok, reads, writes):
        for b in reads:
            old = b.r.get(tok.key)
            if old is None or old.val < tok.val:
                b.r[tok.key] = tok
        for b in writes:
            b.w = tok
            b.r = {}

    def record(self, body, *args):
        self.rec = []
        body(*args)
        r, self.rec = self.rec, None
        return r

    def op(self, eng, fn, reads=(), writes=(), sig=True):
        if getattr(self, "rec", None) is not None:
            self.rec.append(lambda: self._op(eng, fn, reads, writes, sig))
            return None
        return self._op(eng, fn, reads, writes, sig)

    def _op(self, eng, fn, reads=(), writes=(), sig=True):
        if sig:
            eng.roll()
        self._deps(eng, reads, writes)
        inst = fn(eng.h)
        if sig:
            eng.count += 1
            inst.then_inc(eng.sem[0], 1)
            tok = Tok(eng.sem[0], eng.count, eng.sem[1])
            eng.pending = False
        else:
            tok = Tok(eng.sem[0], eng.count + 1, eng.sem[1])
            eng.pending = True
        self._mark(tok, reads, writes)
        return tok

    def dma(self, out_ap, in_ap, reads=(), writes=(), sembuf=None, eng=None):
        if getattr(self, "rec", None) is not None:
            self.rec.append(lambda: self._dma(out_ap, in_ap, reads, writes, sembuf, eng))
            return None
        return self._dma(out_ap, in_ap, reads, writes, sembuf, eng)

    def _dma(self, out_ap, in_ap, reads=(), writes=(), sembuf=None, eng=None):
        eng = eng or self.sp
        sb_ = sembuf or (writes[0] if writes else reads[0])
        if sb_.dsem is None:
            sb_.dsem = self.new_sem(f"d_{sb_.name}")
        self._deps(eng, reads, writes)
        inst = eng.h.dma_start(out=out_ap, in_=in_ap)
        inst.then_inc(sb_.dsem[0], 16)
        sb_.dcount += 16
        tok = Tok(sb_.dsem[0], sb_.dcount, sb_.dsem[1])
        self._mark(tok, reads, writes)
        self.dma_toks.append(tok)
        return tok

    def barrier(self):
        toks = []
        for e in self.engs:
            if e.count > 0:
                assert not e.pending
                toks.append(Tok(e.sem[0], e.count, e.sem[1]))
        toks += self.dma_toks
        self.dma_toks = []
        for e in self.engs:
            for t in toks:
                self.wait(e, t)


class _Stop(Exception):
    pass


def interleave(step_lists, H):
    n = len(step_lists)
    T = max(i * H + len(sl) for i, sl in enumerate(step_lists))
    lo = 0
    for t in range(T):
        while lo < n and t - lo * H >= len(step_lists[lo]):
            lo += 1
        i = lo
        while i < n and t - i * H >= 0:
            k = t - i * H
            if k < len(step_lists[i]):
                step_lists[i][k]()
            i += 1


def build_program(dbg=False, stage=99):
    try:
        return _build_program(dbg, stage)
    except _Stop as ex:
        return ex.args[0]


def _build_program(dbg, stage):
    nc = bass.Bass("TRN2", target_bir_lowering=False)

    def din(name, shape):
        return nc.dram_tensor(name, list(shape), F32, kind="ExternalInput").ap()

    x_d = din("x", [S, D])
    c_d = din("c", [8, 128])
    ctx_d = din("ctx", [CTX, D])
    cctx_d = din("c_ctx", [8, 128])
    wada_d = din("w_ada", [D, 6 * D])
    bada_d = din("b_ada", [48, 128])
    win_d = din("w_in", [D, DIN])
    ws_d = din("w_s", [4, 128, 128])
    bs_d = din("b_s", [1, 512])
    lnvg_d = din("ln_v_g", [4, 128])
    lnvb_d = din("ln_v_b", [4, 128])
    conv_d = din("conv_qk", [12, 128])
    bg_d = din("b_gates", [1, 16])
    hng_d = din("hn_g", [4, 128])
    wout_d = din("w_out", [D, D])
    ln1g_d = din("ln1_g", [1, D])
    ln1b_d = din("ln1_b", [1, D])
    w1_d = din("w1", [D, DFF])
    b1_d = din("b1", [32, 128])
    w2_d = din("w2", [DFF, D])
    b2_d = din("b2", [1, D])
    ln2g_d = din("ln2_g", [1, D])
    ln2b_d = din("ln2_b", [1, D])
    y_d = nc.dram_tensor("y", [S, D], F32, kind="ExternalOutput").ap()
    dbg_outs = {}

    with ExitStack() as es:
        kb = KB(nc, es)
        pe, act, dve, pool, sp = kb.pe, kb.act, kb.dve, kb.pool, kb.sp

        PB = [kb.ps(f"pb{i}", [128, 512], F32) for i in range(7)]
        PT = kb.ps("pt", [128, 1024], BF16)

        identf = kb.sb("identf", [128, 128], F32)
        identb = kb.sb("identb", [128, 128], BF16)
        LT = kb.sb("LT", [128, 128], F32)
        UT = kb.sb("UT", [128, 128], F32)
        onesf = kb.sb("onesf", [128, 128], F32)
        onesb = kb.sb("onesb", [128, 128], BF16)
        cst = kb.sb("cst", [128, 8], F32)
        modc = kb.sb("modc", [128, 6, 8], F32)
        smallc = kb.sb("smallc", [128, 64], F32)
        bgb = kb.sb("bgb", [128, 16], F32)
        BiasA = kb.sb("BiasA", [128, 4, 128], F32)
        wsT = kb.sb("wsT", [128, 4, 128], BF16)
        setup = Buf(None, "setup")
        ccol = kb.sb("ccol", [128, 2, 8], F32)
        badac = kb.sb("badac", [128, 48], F32)

        def dbg_out(name, buf, ap, shape, dt=F32):
            if not dbg:
                return
            o = nc.dram_tensor("dbg_" + name, list(shape), dt, kind="ExternalOutput").ap()
            dbg_outs[name] = (shape, dt)
            kb.dma(o, ap, reads=[buf], sembuf=buf)

        kb.op(pool, lambda e: e.memset(onesf[:], 1.0), writes=[onesf])
        kb.op(pool, lambda e: e.memset(onesb[:], 1.0), writes=[onesb])
        kb.op(pool, lambda e: e.memset(cst[:, 0:1], -0.5), writes=[cst])
        kb.op(pool, lambda e: e.memset(cst[:, 1:2], EPS), writes=[cst])
        kb.op(pool, lambda e: e.memset(cst[:, 2:3], math.log(8.0)), writes=[cst])
        kb.op(pool, lambda e: e.memset(cst[:, 3:4], 1.0), writes=[cst])
        kb.op(pool, lambda e: e.affine_select(out=identf[:], in_=onesf[:], pattern=[[-1, 128]], compare_op=ALU.is_equal,
                                              fill=0.0, base=0, channel_multiplier=1), reads=[onesf], writes=[identf])
        kb.op(pool, lambda e: e.affine_select(out=LT[:], in_=onesf[:], pattern=[[1, 128]], compare_op=ALU.is_ge,
                                              fill=0.0, base=0, channel_multiplier=-1), reads=[onesf], writes=[LT])
        kb.op(pool, lambda e: e.affine_select(out=UT[:], in_=onesf[:], pattern=[[-1, 128]], compare_op=ALU.is_ge,
                                              fill=0.0, base=0, channel_multiplier=1), reads=[onesf], writes=[UT])
        kb.op(dve, lambda e: e.tensor_copy(out=identb[:], in_=identf[:]), reads=[identf], writes=[identb])

        es_p = es.enter_context(ExitStack())
        qT = kb.sb("qT", [128, 2, S], BF16, es_p)
        kT = kb.sb("kT", [128, 2, NG * 128], BF16, es_p)
        vaug = kb.sb("vaug", [128, NG, 4, 130], BF16, es_p)
        Gt = kb.sb("Gt", [128, NG, 16], F32, es_p)
        WS = kb.sb("WS", [128, NG, 8], F32, es_p)
        LB = kb.sb("LB", [128, NG, 8], F32, es_p)
        CW = kb.sb("CW", [128, NG, 8], F32, es_p)
        ST = kb.sb("ST", [128, NT, 2, 2, 130], BF16, es_p)
        kb.op(pool, lambda e: e.memset(vaug[:, :, :, 128:130], 1.0), writes=[vaug])

        es0 = es.enter_context(ExitStack())
        ln1gb = kb.sb("ln1gb", [128, D], F32, es0)
        ln1bb = kb.sb("ln1bb", [128, D], F32, es0)
        es_set = es.enter_context(ExitStack())
        rows = kb.sb("rows", [128, 256], F32, es_set)
        bsb = kb.sb("bsb", [128, 512], F32, es_set)
        R_C, R_CC, R_BADA, R_CONV, R_GV, R_BV, R_HNG, R_B1 = 0, 8, 16, 64, 76, 80, 84, 88
        kb.dma(rows[0:8, 0:128], c_d[:, :], writes=[rows])
        kb.dma(rows[0:8, 128:256], cctx_d[:, :], writes=[rows])
        rows2 = kb.sb("rows2", [128, 128], F32, es_set)
        kb.dma(rows2[0:48, :], bada_d[:, :], writes=[rows2])
        rows3 = kb.sb("rows3", [128, 128], F32, es_set)
        kb.dma(rows3[0:12, :], conv_d[:, :], writes=[rows3])
        kb.dma(rows3[32:36, :], lnvg_d[:, :], writes=[rows3])
        kb.dma(rows3[64:68, :], lnvb_d[:, :], writes=[rows3])
        rows4 = kb.sb("rows4", [128, 128], F32, es_set)
        kb.dma(rows4[0:4, :], hng_d[:, :], writes=[rows4])
        kb.dma(rows4[32:64, :], b1_d[:, :], writes=[rows4])
        kb.dma(bgb[:], bg_d[0:1, :].to_broadcast([128, 16]), writes=[bgb])
        kb.dma(bsb[:], bs_d[0:1, :].to_broadcast([128, 512]), writes=[bsb])
        wsr = kb.sb("wsr", [128, 4, 128], F32, es_set)
        kb.dma(wsr[:], ws_d.rearrange("g t s -> t g s"), writes=[wsr])


        def tr_f32(dst_ap, dst_buf, src_ap, src_buf, n, bank, p0=0):
            kb.op(pe, lambda e: e.transpose(out=bank[:, 0:n], in_=src_ap, identity=identf[p0:p0 + n, p0:p0 + n]),
                  reads=[src_buf, identf], writes=[bank])
            kb.op(dve, lambda e: e.tensor_copy(out=dst_ap, in_=bank[:, 0:n]), reads=[bank], writes=[dst_buf])

        craw = kb.sb("craw", [128, 2, 8], F32, es_set)
        tr_f32(craw[:, 0, :], craw, rows[0:8, 0:128], rows, 8, PB[0])
        tr_f32(craw[:, 1, :], craw, rows[0:8, 128:256], rows, 8, PB[1])
        kb.op(act, lambda e: e.activation(out=ccol[:], in_=craw[:], func=AF.Silu), reads=[craw], writes=[ccol])
        tr_f32(badac[:, :], badac, rows2[0:48, :], rows2, 48, PB[2])
        tr_f32(smallc[:, 12:24], smallc, rows3[0:12, :], rows3, 12, PB[3])
        tr_f32(smallc[:, 0:4], smallc, rows3[32:36, :], rows3, 4, PB[4], p0=32)
        tr_f32(smallc[:, 4:8], smallc, rows3[64:68, :], rows3, 4, PB[5], p0=64)
        tr_f32(smallc[:, 8:12], smallc, rows4[0:4, :], rows4, 4, PB[6])
        tr_f32(smallc[:, 24:56], smallc, rows4[32:64, :], rows4, 32, PB[0], p0=32)
        wsTf = kb.sb("wsTf", [128, 4, 128], F32, es_set)
        for g in range(4):
            kb.op(pe, lambda e, g=g: e.transpose(out=PB[1][:, g * 128:(g + 1) * 128], in_=wsr[:, g, :], identity=identf[:]),
                  reads=[wsr, identf], writes=[PB[1]])
        kb.op(dve, lambda e: e.tensor_copy(out=wsTf[:].rearrange("p g t -> p (g t)"), in_=PB[1][:, :]), reads=[PB[1]], writes=[wsTf])
        kb.op(act, lambda e: e.activation(out=wsT[:], in_=wsTf[:], func=AF.Copy), reads=[wsTf], writes=[wsT])
        kb.op(pe, lambda e: e.matmul(PB[2][:, :], lhsT=onesf[:], rhs=wsTf[:].rearrange("p g t -> p (g t)"), start=True, stop=True),
              reads=[onesf, wsTf], writes=[PB[2]])
        for g in range(4):
            kb.op(dve, lambda e, g=g: e.scalar_tensor_tensor(out=BiasA[:, g, :], in0=PB[2][:, g * 128:(g + 1) * 128],
                                                             scalar=smallc[:, 4 + g:5 + g], in1=bsb[:, g * 128:(g + 1) * 128],
                                                             op0=ALU.mult, op1=ALU.add),
                  reads=[PB[2], smallc, bsb], writes=[BiasA])

        if stage == -1:
            dbg_out("smallc", smallc, smallc[:], [128, 64])
            dbg_out("BiasA", BiasA, BiasA[:], [128, 4, 128])
            dbg_out("ccol", ccol, ccol[:], [128, 2, 8])
            dbg_out("LT", LT, LT[:], [128, 128])
            dbg_out("identf", identf, identf[:], [128, 128])
            kb.barrier()
            raise _Stop((nc, dbg_outs))
        kb.barrier()
        es_set.close()
        kb.dma(ln1gb[:], ln1g_d[0:1, :].to_broadcast([128, D]), writes=[ln1gb])
        kb.dma(ln1bb[:], ln1b_d[0:1, :].to_broadcast([128, D]), writes=[ln1bb])
        g2row = nc.dram_tensor("g2scratch", [128, D], F32, kind="Internal").ap()
        g1row = nc.dram_tensor("g1scratch", [128, D], F32, kind="Internal").ap()

        with ExitStack() as esa:
            stg = [kb.sb(f"astg{i}", [128, 8, 512], F32, esa) for i in range(4)]
            scb = kb.sb("scb", [128, 8, 128], F32, esa)
            badab = kb.sb("badab", [128, 2, D], F32, esa)
            g2bc0 = kb.sb("g2bc0", [128, D], F32, esa)
            g1bc = kb.sb("g1bc0", [128, D], F32, esa)
            kb.dma(badab[:, 0, :], bada_d[16:24, :].rearrange("(o a) b -> o (a b)", o=1).to_broadcast([128, D]), writes=[badab])
            kb.dma(badab[:, 1, :], bada_d[40:48, :].rearrange("(o a) b -> o (a b)", o=1).to_broadcast([128, D]), writes=[badab])
            for kc in range(8):
                kb.op(dve, lambda e, kc=kc: e.tensor_copy(out=scb[:, kc, :], in_=ccol[:, 0, kc:kc + 1].to_broadcast([128, 128])),
                      reads=[ccol], writes=[scb])
            wada_v = wada_d.rearrange("(kc p) n -> p kc n", p=128)
            col_kind = {0: 0, 1: 0, 2: 1, 3: 1, 6: 2, 7: 2, 8: 3, 9: 3}
            for blk in range(12):
                sg = stg[blk % 4]
                kb.dma(sg[:, 0:4, :], wada_v[:, 0:4, blk * 512:(blk + 1) * 512], writes=[sg])
                kb.dma(sg[:, 4:8, :], wada_v[:, 4:8, blk * 512:(blk + 1) * 512], writes=[sg])
                if blk in col_kind:
                    mi = col_kind[blk]
                    for jj in range(4):
                        j = blk * 4 + jj
                        fchunk = j % 8
                        bank = PB[jj % 4]
                        for kc in range(8):
                            kb.op(pe, lambda e, kc=kc, jj=jj, bank=bank: e.matmul(
                                bank[:, 0:2], lhsT=sg[:, kc, jj * 128:(jj + 1) * 128], rhs=ccol[:, :, kc],
                                start=(kc == 0), stop=(kc == 7)), reads=[sg, ccol], writes=[bank], sig=(kc == 7))
                        kb.op(dve, lambda e, bank=bank, mi=mi, fchunk=fchunk, j=j: e.tensor_tensor(
                            out=modc[:, mi, fchunk:fchunk + 1], in0=bank[:, 0:1], in1=badac[:, j:j + 1], op=ALU.add),
                            reads=[bank, badac], writes=[modc.sub((mi, fchunk))])
                        if mi < 2:
                            kb.op(dve, lambda e, bank=bank, mi=mi, fchunk=fchunk, j=j: e.tensor_tensor(
                                out=modc[:, 4 + mi, fchunk:fchunk + 1], in0=bank[:, 1:2], in1=badac[:, j:j + 1], op=ALU.add),
                                reads=[bank, badac], writes=[modc.sub((4 + mi, fchunk))])
                else:
                    which = 0 if blk in (4, 5) else 1
                    half = blk % 2 if which == 1 else blk - 4
                    bank = PB[4 + (blk % 2)]
                    for kc in range(8):
                        kb.op(pe, lambda e, kc=kc, bank=bank: e.matmul(bank[:, :], lhsT=scb[:, kc, :], rhs=sg[:, kc, :],
                                                                      start=(kc == 0), stop=(kc == 7)),
                              reads=[sg, scb], writes=[bank], sig=(kc == 7))
                    dst = g1bc if which == 0 else g2bc0
                    kb.op(dve, lambda e, bank=bank, dst=dst, half=half, which=which: e.tensor_tensor(
                        out=dst[:, half * 512:(half + 1) * 512], in0=bank[:, :], in1=badab[:, which, half * 512:(half + 1) * 512],
                        op=ALU.add), reads=[bank, badab], writes=[dst])
            for mi in (1, 3, 5):
                kb.op(dve, lambda e, mi=mi: e.tensor_scalar(out=modc[:, mi, :], in0=modc[:, mi, :], scalar1=1.0, scalar2=None,
                                                            op0=ALU.add), reads=[modc], writes=[modc])
            g2st = kb.dma(g2row[:, :], g2bc0[:], reads=[g2bc0])
            kb.dma(g1row[:, :], g1bc[:], reads=[g1bc])
            kb.barrier()
            if stage == 0:
                dbg_out("modc", modc, modc[:], [128, 6, 8])
                dbg_out("g1bc", g1bc, g1bc[:], [128, D])
                dbg_out("smallc", smallc, smallc[:], [128, 64])
                dbg_out("BiasA", BiasA, BiasA[:], [128, 4, 128])
                kb.barrier()
                raise _Stop((nc, dbg_outs))

        def ln_stats(xap, xbuf, width, stats, mv, rstd, nmr=None):
            nchunk = width // 512
            for cidx in range(nchunk):
                kb.op(dve, lambda e, cidx=cidx: e.bn_stats(out=stats[:, cidx, :], in_=xap[:, cidx * 512:(cidx + 1) * 512]),
                      reads=[xbuf], writes=[stats.sub(cidx)])
            kb.op(dve, lambda e: e.bn_aggr(out=mv[:, 0:2], in_=stats[:, 0:nchunk, :].rearrange("p a b -> p (a b)")),
                  reads=[stats], writes=[mv])
            kb.op(pool, lambda e: e.tensor_tensor(out=mv[:, 2:3], in0=mv[:, 1:2], in1=cst[:, 1:2], op=ALU.add),
                  reads=[mv, cst], writes=[mv])
            kb.op(pool, lambda e: e.tensor_tensor(out=rstd[:, 0:1], in0=mv[:, 2:3], in1=cst[:, 0:1], op=ALU.pow),
                  reads=[mv, cst], writes=[rstd])
            if nmr is not None:
                kb.op(dve, lambda e: e.scalar_tensor_tensor(out=rstd[:, 1:2], in0=mv[:, 0:1], scalar=-1.0, in1=rstd[:, 0:1],
                                                            op0=ALU.mult, op1=ALU.mult), reads=[mv, rstd], writes=[rstd])

        def make_hT(xt, hT, xn, stats, mv, rstd, mi_shift, mi_scale, tok0=0):
            ln_stats(xt[:, :], xt, 1024, stats, mv, rstd, nmr=True)
            kb.op(act, lambda e: e.activation(out=xn[:], in_=xt[:, :], func=AF.Identity, bias=rstd[:, 1:2], scale=rstd[:, 0:1]),
                  reads=[xt, rstd], writes=[xn])
            for kc in range(8):
                kb.op(pe, lambda e, kc=kc: e.transpose(out=PT[:, kc * 128:(kc + 1) * 128], in_=xn[:, kc * 128:(kc + 1) * 128],
                                                      identity=identb[:]), reads=[xn, identb], writes=[PT], sig=(kc == 7))
            for kc in range(8):
                kb.op(act, lambda e, kc=kc: e.activation(out=hT[:, kc, tok0:tok0 + 128], in_=PT[:, kc * 128:(kc + 1) * 128],
                                                         func=AF.Identity, bias=modc[:, mi_shift, kc:kc + 1],
                                                         scale=modc[:, mi_scale, kc:kc + 1]),
                      reads=[PT, modc], writes=[hT.sub((kc, tok0))])

        def load_weight_block(dst, dst_ap_fn, src_ap, stg_buf, nk, scale_bc=None, scale_cols=None):
            half = nk // 2
            kb.dma(stg_buf[:, 0:half, :], src_ap[:, 0:half, :], writes=[stg_buf])
            kb.dma(stg_buf[:, half:nk, :], src_ap[:, half:nk, :], writes=[stg_buf])
            if scale_bc is None:
                kb.op(act, lambda e: e.activation(out=dst_ap_fn(slice(0, half)), in_=stg_buf[:, 0:half, :], func=AF.Copy),
                      reads=[stg_buf], writes=[dst])
                kb.op(pool, lambda e: e.tensor_copy(out=dst_ap_fn(slice(half, nk)), in_=stg_buf[:, half:nk, :]),
                      reads=[stg_buf], writes=[dst])
            else:
                for k in range(nk):
                    eng = dve if k % 2 == 0 else pool
                    kb.op(eng, lambda e, k=k: e.tensor_tensor(out=dst_ap_fn(k), in0=stg_buf[:, k, :],
                                                              in1=scale_bc[:, scale_cols], op=ALU.mult),
                          reads=[stg_buf, scale_bc], writes=[dst])

        win_v = win_d.rearrange("(kc p) n -> p kc n", p=128)
        with ExitStack() as es1:
            Win1 = kb.sb("Win1", [128, 8, 1040], BF16, es1)
            with ExitStack() as esw:
                wstg = [kb.sb(f"wstg{i}", [128, 8, 512], F32, esw) for i in range(2)]
                load_weight_block(Win1, lambda ks: Win1[:, ks, 0:512], win_v[:, :, 1024:1536], wstg[0], 8)
                load_weight_block(Win1, lambda ks: Win1[:, ks, 512:1024], win_v[:, :, 1536:2048], wstg[1], 8)
                kb.dma(wstg[0][:, :, 0:16], win_v[:, :, 2560:2576], writes=[wstg[0]])
                kb.op(dve, lambda e: e.tensor_copy(out=Win1[:, :, 1024:1040], in_=wstg[0][:, :, 0:16]), reads=[wstg[0]], writes=[Win1])
                kb.barrier()
            xs = [kb.sb(f"xs{i}", [128, D], F32, es1) for i in range(2)]
            xn = kb.sb("xn", [128, D], BF16, es1)
            hT = kb.sb("hT", [128, 8, 128], BF16, es1)
            stats = kb.sb("stats", [128, 2, 6], F32, es1)
            mv = kb.sb("mv", [128, 4], F32, es1)
            rstd = kb.sb("rstd", [128, 2], F32, es1)
            raw = [kb.sb(f"raw{i}", [128, 4, 130], F32, es1) for i in range(3)]
            cvt = kb.sb("cvt", [128, 4, 128], F32, es1)

            def conv_finish(gc_prev, rb):
                for cc in range(4):
                    kb.op(dve, lambda e, cc=cc: e.tensor_scalar(out=cvt[:, cc, :], in0=rb[:, cc, 0:128],
                                                                scalar1=smallc[:, 12 + 0 * 4 + cc:13 + 0 * 4 + cc], scalar2=None,
                                                                op0=ALU.mult), reads=[rb, smallc], writes=[cvt.sub(cc)])
                    kb.op(dve, lambda e, cc=cc: e.scalar_tensor_tensor(out=cvt[:, cc, :], in0=rb[:, cc, 1:129],
                                                                       scalar=smallc[:, 12 + 1 * 4 + cc:13 + 1 * 4 + cc],
                                                                       in1=cvt[:, cc, :], op0=ALU.mult, op1=ALU.add),
                          reads=[rb, smallc, cvt.sub(cc)], writes=[cvt.sub(cc)])
                    kb.op(dve, lambda e, cc=cc: e.scalar_tensor_tensor(out=cvt[:, cc, :], in0=rb[:, cc, 2:130],
                                                                       scalar=smallc[:, 12 + 2 * 4 + cc:13 + 2 * 4 + cc],
                                                                       in1=cvt[:, cc, :], op0=ALU.mult, op1=ALU.add),
                          reads=[rb, smallc, cvt.sub(cc)], writes=[cvt.sub(cc)])
                t0 = gc_prev * 128
                kb.op(act, lambda e: e.activation(out=kT[:, :, t0:t0 + 128], in_=cvt[:, 2:4, :], func=AF.Silu),
                      reads=[cvt.sub(2), cvt.sub(3)], writes=[kT])
                if gc_prev >= NCT:
                    l0 = (gc_prev - NCT) * 128
                    kb.op(act, lambda e: e.activation(out=qT[:, :, l0:l0 + 128], in_=cvt[:, 0:2, :], func=AF.Silu),
                          reads=[cvt.sub(0), cvt.sub(1)], writes=[qT])

            xn_1 = [xn, kb.sb("xn_b", [128, D], BF16, es1)]
            hT_1 = [hT, kb.sb("hT_b", [128, 8, 128], BF16, es1)]
            stats_1 = [stats, kb.sb("stats_b", [128, 2, 6], F32, es1)]
            mv_1 = [mv, kb.sb("mv_b", [128, 4], F32, es1)]
            rstd_1 = [rstd, kb.sb("rstd_b", [128, 2], F32, es1)]

            def tile1(gc):
                sl = gc % 2
                xn, hT, stats, mv, rstd = xn_1[sl], hT_1[sl], stats_1[sl], mv_1[sl], rstd_1[sl]
                PBq, PBv, PBg = PB[3 * sl], PB[3 * sl + 1], PB[3 * sl + 2]
                is_ctx = gc < NCT
                src = ctx_d[gc * 128:(gc + 1) * 128, :] if is_ctx else x_d[(gc - NCT) * 128:(gc - NCT + 1) * 128, :]
                xt = xs[gc % 2]
                kb.dma(xt[:, 0:512], src[:, 0:512], writes=[xt])
                kb.dma(xt[:, 512:1024], src[:, 512:1024], writes=[xt])
                make_hT(xt, hT, xn, stats, mv, rstd, 4 if is_ctx else 0, 5 if is_ctx else 1)
                for cc in range(4):
                    for kc in range(8):
                        kb.op(pe, lambda e, cc=cc, kc=kc: e.matmul(PBq[:, cc * 128:(cc + 1) * 128],
                                                                   lhsT=Win1[:, kc, cc * 128:(cc + 1) * 128], rhs=hT[:, kc, :],
                                                                   start=(kc == 0), stop=(kc == 7)),
                              reads=[Win1, hT.sub((kc, 0))], writes=[PBq], sig=(kc == 7 and cc == 3))
                for kc in range(8):
                    kb.op(pe, lambda e, kc=kc: e.matmul(PBv[:, :], lhsT=hT[:, kc, :], rhs=Win1[:, kc, 512:1024],
                                                        start=(kc == 0), stop=(kc == 7)),
                          reads=[Win1, hT.sub((kc, 0))], writes=[PBv], sig=(kc == 7))
                for kc in range(8):
                    kb.op(pe, lambda e, kc=kc: e.matmul(PBg[:, 0:16], lhsT=hT[:, kc, :], rhs=Win1[:, kc, 1024:1040],
                                                        start=(kc == 0), stop=(kc == 7)),
                          reads=[Win1, hT.sub((kc, 0))], writes=[PBg], sig=(kc == 7))
                rb = raw[gc % 3]
                first = gc in (0, NCT)
                last = gc in (NCT - 1, NG - 1)
                kb.op(act, lambda e: e.activation(out=rb[:, :, 1:129], in_=PBq[:, :].rearrange("p (c t) -> p c t", c=4), func=AF.Copy),
                      reads=[PBq], writes=[rb])
                kb.op(dve, lambda e: e.tensor_copy(out=vaug[:, gc, :, 0:128], in_=PBv[:, :].rearrange("p (h v) -> p h v", h=4)),
                      reads=[PBv], writes=[vaug])
                kb.op(dve, lambda e: e.tensor_tensor(out=Gt[:, gc, :], in0=PBg[:, 0:16], in1=bgb[:], op=ALU.add),
                      reads=[PBg, bgb], writes=[Gt])
                if first:
                    kb.op(pool, lambda e: e.memset(rb[:, :, 0:1], 0.0), writes=[rb])
                else:
                    rprev = raw[(gc - 1) % 3]
                    kb.op(pool, lambda e: e.tensor_copy(out=rb[:, :, 0:1], in_=rprev[:, :, 128:129]), reads=[rprev], writes=[rb])
                    kb.op(pool, lambda e: e.tensor_copy(out=rprev[:, :, 129:130], in_=rb[:, :, 1:2]), reads=[rb], writes=[rprev])
                    conv_finish(gc - 1, rprev)
                if last:
                    kb.op(pool, lambda e: e.memset(rb[:, :, 129:130], 0.0), writes=[rb])
                    conv_finish(gc, rb)
            lists1 = [kb.record(tile1, gc) for gc in range(NG)]
            interleave(lists1, (len(lists1[2]) * 11) // 20)
            kb.barrier()

        if dbg:
            dbg_out("kT", kT, kT[:], [128, 2, NG * 128], BF16)
            dbg_out("qT", qT, qT[:], [128, 2, S], BF16)
            dbg_out("vaug", vaug, vaug[:], [128, NG, 4, 130], BF16)
            dbg_out("Gt", Gt, Gt[:], [128, NG, 16])
        if stage == 1:
            kb.barrier()
            raise _Stop((nc, dbg_outs))

        with ExitStack() as esg:
            NF = NG * 8
            Gv = Gt[:].rearrange("p c (d t h) -> p c d t h", d=2, t=2, h=4)
            LF = kb.sb("LF", [128, NG, 2, 4], F32, esg)
            T1 = kb.sb("T1", [128, NG, 2, 4], F32, esg)
            T2 = kb.sb("T2", [128, NG, 2, 4], F32, esg)
            Bc = kb.sb("Bc", [128, NG, 2, 4], F32, esg)
            Aa = kb.sb("Aa", [128, NG, 2, 4], F32, esg)
            Mcb = kb.sb("Mcb", [128, NG, 2, 4], F32, esg)
            rowA = kb.sb("rowA", [1, NG, 2, 4], F32, esg)
            rowB = kb.sb("rowB", [1, NG, 2, 4], F32, esg)
            rowM = kb.sb("rowM", [1, NG, 2, 4], F32, esg)
            rowm0 = kb.sb("rowm0", [1, NG, 2, 4], F32, esg)
            colmax = kb.sb("colmax", [128, 3], F32, esg)
            fl = lambda b: b[:].rearrange("p c d h -> p (c d h)")
            FG = Gv[:, :, :, 1, :]
            IG = Gv[:, :, :, 0, :]
            kb.op(dve, lambda e: e.tensor_scalar(out=T1[:], in0=FG, scalar1=-1.0, scalar2=None, op0=ALU.mult), reads=[Gt], writes=[T1])
            kb.op(dve, lambda e: e.tensor_tensor(out=T1[:], in0=T1[:], in1=FG, op=ALU.max), reads=[Gt, T1], writes=[T1])
            kb.op(act, lambda e: e.activation(out=T2[:], in_=T1[:], func=AF.Exp, scale=-1.0), reads=[T1], writes=[T2])
            kb.op(act, lambda e: e.activation(out=T2[:], in_=T2[:], func=AF.Ln, bias=cst[:, 3:4], scale=1.0), reads=[T2, cst], writes=[T2])
            kb.op(dve, lambda e: e.tensor_scalar(out=T1[:], in0=FG, scalar1=0.0, scalar2=None, op0=ALU.min), reads=[Gt], writes=[T1])
            kb.op(dve, lambda e: e.tensor_tensor(out=LF[:], in0=T1[:], in1=T2[:], op=ALU.subtract), reads=[T1, T2], writes=[LF])
            PBv = PB[0][:, 0:NF].rearrange("p (c d h) -> p c d h", c=NG, d=2, h=4)
            kb.op(pe, lambda e: e.matmul(PB[0][:, 0:NF], lhsT=LT[:], rhs=fl(LF), start=True, stop=True),
                  reads=[LT, LF], writes=[PB[0]])
            PBv1 = PB[1][:, 0:NF].rearrange("p (c d h) -> p c d h", c=NG, d=2, h=4)
            kb.op(pe, lambda e: e.matmul(PB[1][:, 0:NF], lhsT=UT[:], rhs=fl(LF), start=True, stop=True),
                  reads=[UT, LF], writes=[PB[1]])
            kb.op(dve, lambda e: e.tensor_copy(out=Bc[:, :, 0, :], in_=PBv[:, :, 0, :]), reads=[PB[0]], writes=[Bc])
            kb.op(dve, lambda e: e.tensor_copy(out=Bc[:, :, 1, :], in_=PBv1[:, :, 1, :]), reads=[PB[1]], writes=[Bc])
            kb.op(dve, lambda e: e.tensor_tensor(out=Aa[:], in0=IG, in1=Bc[:], op=ALU.subtract), reads=[Gt, Bc], writes=[Aa])
            AaF = fl(Aa)
            segs = [(0, 128), (128, 128), (256, NF - 256)]
            for si, (o, n) in enumerate(segs):
                kb.op(pe, lambda e, o=o, n=n: e.transpose(out=PB[2][0:n, 0:128], in_=AaF[:, o:o + n], identity=identf[:]),
                      reads=[Aa, identf], writes=[PB[2]])
                kb.op(dve, lambda e, si=si, n=n: e.reduce_max(out=colmax[0:n, si:si + 1], in_=PB[2][0:n, 0:128], axis=mybir.AxisListType.X),
                      reads=[PB[2]], writes=[colmax])
                kb.op(pe, lambda e, si=si, n=n, o=o: e.matmul(PB[3][0:1, o:o + n], lhsT=colmax[0:n, si:si + 1], rhs=identf[0:n, 0:n],
                                                               start=True, stop=True), reads=[colmax, identf], writes=[PB[3]])
            kb.op(dve, lambda e: e.tensor_copy(out=fl(rowA), in_=PB[3][0:1, 0:NF]), reads=[PB[3]], writes=[rowA])
            kb.op(pe, lambda e: e.matmul(PB[4][0:1, 0:NF], lhsT=onesf[:, 0:1], rhs=fl(LF), start=True, stop=True),
                  reads=[onesf, LF], writes=[PB[4]])
            kb.op(dve, lambda e: e.tensor_copy(out=fl(rowB), in_=PB[4][0:1, 0:NF]), reads=[PB[4]], writes=[rowB])
            order = [list(range(NG)), [1, 0] + list(range(NG - 1, NCT - 1, -1))]
            for d in range(2):
                g0 = order[d][0]
                kb.op(dve, lambda e, d=d, g0=g0: e.memset(rowm0[0:1, g0, d, :], 0.0), writes=[rowm0])
            for j in range(NG):
                for d in range(2):
                    gcur = order[d][j]
                    kb.op(dve, lambda e, d=d, gcur=gcur: e.tensor_tensor(out=rowM[0:1, gcur, d, :], in0=rowm0[0:1, gcur, d, :],
                                                                         in1=rowA[0:1, gcur, d, :], op=ALU.max),
                          reads=[rowm0, rowA], writes=[rowM])
                    if j + 1 < NG:
                        gn = order[d][j + 1]
                        kb.op(dve, lambda e, d=d, gcur=gcur, gn=gn: e.tensor_tensor(out=rowm0[0:1, gn, d, :], in0=rowM[0:1, gcur, d, :],
                                                                                   in1=rowB[0:1, gcur, d, :], op=ALU.add),
                              reads=[rowM, rowB], writes=[rowm0])
            kb.op(pe, lambda e: e.matmul(PB[5][:, 0:NF], lhsT=onesf[0:1, :], rhs=fl(rowM), start=True, stop=True),
                  reads=[onesf, rowM], writes=[PB[5]])
            kb.op(pe, lambda e: e.matmul(PB[6][:, 0:NF], lhsT=onesf[0:1, :], rhs=fl(rowm0), start=True, stop=True),
                  reads=[onesf, rowm0], writes=[PB[6]])
            kb.op(dve, lambda e: e.tensor_copy(out=fl(Mcb), in_=PB[5][:, 0:NF]), reads=[PB[5]], writes=[Mcb])
            kb.op(dve, lambda e: e.tensor_tensor(out=T1[:], in0=Aa[:], in1=Mcb[:], op=ALU.subtract), reads=[Aa, Mcb], writes=[T1])
            kb.op(act, lambda e: e.activation(out=WS[:].rearrange("p c j -> p (c j)"), in_=fl(T1), func=AF.Exp), reads=[T1], writes=[WS])
            kb.op(dve, lambda e: e.tensor_tensor(out=fl(T2), in0=PB[6][:, 0:NF], in1=fl(Mcb), op=ALU.subtract), reads=[PB[6], Mcb], writes=[T2])
            kb.op(act, lambda e: e.activation(out=CW[:].rearrange("p c j -> p (c j)"), in_=fl(T2), func=AF.Exp), reads=[T2], writes=[CW])
            kb.op(dve, lambda e: e.tensor_tensor(out=T1[:], in0=Bc[:], in1=Mcb[:], op=ALU.add), reads=[Bc, Mcb], writes=[T1])
            kb.op(act, lambda e: e.activation(out=LB[:].rearrange("p c j -> p (c j)"), in_=fl(T1), func=AF.Exp, bias=cst[:, 2:3], scale=-1.0),
                  reads=[T1, cst], writes=[LB])
            kb.barrier()

        if dbg:
            dbg_out("WS", WS, WS[:], [128, NG, 8])
            dbg_out("CW", CW, CW[:], [128, NG, 8])
            dbg_out("LB", LB, LB[:], [128, NG, 8])
        if stage == 2:
            kb.barrier()
            raise _Stop((nc, dbg_outs))

        with ExitStack() as ess:
            Cc = [kb.sb(f"Cc{d}", [128, 2, 130], F32, ess) for d in range(2)]
            kp = [kb.sb(f"kp{i}", [128, 4, 64], BF16, ess) for i in range(2)]
            c0b = [kb.sb(f"c0b{i}", [128, 2, 130], BF16, ess) for i in range(2)]
            for d in range(2):
                kb.op(pool, lambda e, d=d: e.memset(Cc[d][:], 0.0), writes=[Cc[d]])
            units = [(j, d) for j in range(NG) for d in range(2)]

            def stepA(u):
                j, d = units[u]
                if j == NG - 1:
                    return
                gcur = order[d][j]
                kpb = kp[u % 2]
                for pr in range(2):
                    kb.op(pe, lambda e, pr=pr: e.transpose(out=PT[:, pr * 128:(pr + 1) * 128],
                                                           in_=kT[:, pr, gcur * 128:(gcur + 1) * 128], identity=identb[:]),
                          reads=[kT, identb], writes=[PT], sig=(pr == 1))
                for h in range(4):
                    kb.op(act, lambda e, h=h: e.activation(
                        out=kpb[:, h, :], in_=PT[:, h * 64:(h + 1) * 64], func=AF.Identity,
                        scale=WS[:, gcur, d * 4 + h:d * 4 + h + 1]), reads=[PT, WS], writes=[kpb.sub(h)])
                bank = PB[(u % 2) * 2:(u % 2) * 2 + 2]
                for h in range(4):
                    bk = bank[h // 2]
                    kb.op(pe, lambda e, h=h, bk=bk: e.matmul(
                        bk[:, (h % 2) * 130:(h % 2) * 130 + 130], lhsT=kpb[:, (h // 2) * 2:(h // 2) * 2 + 2, :].rearrange("p a b -> p (a b)"),
                        rhs=vaug[:, gcur, h, :], start=True, stop=True), reads=[kpb.sub((h // 2) * 2), kpb.sub((h // 2) * 2 + 1), vaug], writes=[bk])

            def stepB(u):
                j, d = units[u]
                gcur = order[d][j]
                C = Cc[d]
                if j > 0:
                    for h in range(4):
                        p0 = (h % 2) * 64
                        kb.op(dve, lambda e, h=h, p0=p0: e.tensor_scalar(
                            out=C[p0:p0 + 64, h // 2, :], in0=C[p0:p0 + 64, h // 2, :],
                            scalar1=CW[p0:p0 + 64, gcur, d * 4 + h:d * 4 + h + 1], scalar2=None, op0=ALU.mult),
                            reads=[C.sub(h), CW], writes=[C.sub(h)])
                if gcur >= NCT:
                    kb.op(act, lambda e: e.activation(out=ST[:, gcur - NCT, d, :, :], in_=C[:], func=AF.Copy),
                          reads=[C], writes=[ST])
                if j == NG - 1:
                    return
                bank = PB[(u % 2) * 2:(u % 2) * 2 + 2]
                for h in range(4):
                    p0 = (h % 2) * 64
                    bk = bank[h // 2]
                    kb.op(dve, lambda e, h=h, p0=p0, bk=bk: e.tensor_tensor(
                        out=C[p0:p0 + 64, h // 2, :], in0=C[p0:p0 + 64, h // 2, :],
                        in1=bk[p0:p0 + 64, (h % 2) * 130:(h % 2) * 130 + 130], op=ALU.add), reads=[C.sub(h), bk], writes=[C.sub(h)])

            stepA(0)
            for u in range(len(units)):
                if u + 1 < len(units):
                    stepA(u + 1)
                stepB(u)
            kb.barrier()

        if dbg:
            dbg_out("ST", ST, ST[:], [128, NT, 2, 2, 130], BF16)
        if stage == 3:
            kb.barrier()
            raise _Stop((nc, dbg_outs))

        wout_v = wout_d.rearrange("(kc p) n -> p kc n", p=128)
        with ExitStack() as es2:
            Win2 = kb.sb("Win2", [128, 8, 1536], BF16, es2)
            Wo = kb.sb("Wo", [128, 8, D], BF16, es2)
            with ExitStack() as esw:
                wstg = [kb.sb(f"wstg2{i}", [128, 8, 256], F32, esw) for i in range(2)]
                g1bc = kb.sb("g1bc", [128, D], F32, esw)
                kb.dma(g1bc[:], g1row[:, :], writes=[g1bc])
                nb = 0
                for (wc0, dc0) in ((0, 0), (256, 256), (512, 512), (768, 768), (2048, 1024), (2304, 1280)):
                    load_weight_block(Win2, lambda ks, dc0=dc0: Win2[:, ks, dc0:dc0 + 256], win_v[:, :, wc0:wc0 + 256], wstg[nb % 2], 8)
                    nb += 1
                for c0 in (0, 256, 512, 768):
                    load_weight_block(Wo, lambda k, c0=c0: Wo[:, k, c0:c0 + 256], wout_v[:, :, c0:c0 + 256], wstg[nb % 2], 8,
                                      scale_bc=g1bc, scale_cols=slice(c0, c0 + 256))
                    nb += 1
                kb.barrier()
            xs = [kb.sb(f"x2s{i}", [128, D], F32, es2) for i in range(2)]
            def two(name, shape, dt):
                return [kb.sb(f"{name}_{k}", shape, dt, es2) for k in range(2)]
            xn_ = two("xn2", [128, D], BF16)
            hT_ = two("hT2", [128, 8, 128], BF16)
            stats_ = two("stats2", [128, 4, 6], F32)
            mv_ = two("mv2", [128, 4], F32)
            rstd_ = two("rstd2", [128, 2], F32)
            mv4_ = two("mv4", [128, 4, 4], F32)
            rs4_ = two("rs4", [128, 4], F32)
            uT_ = two("uT", [128, 4, 128], BF16)
            sgo_ = two("sgo", [128, 4, 128], BF16)
            vn_ = two("vn", [128, 512], BF16)
            tA_ = two("tA", [128, 4, 128], F32)
            yT_ = two("yT", [128, 8, 128], BF16)
            sT_ = two("sT", [128, 8, 128], BF16)
            WM_ = two("WM", [128, 8, 128], BF16)
            dn_ = two("dn", [128, 3, 8], F32)
            hs_ = two("hs", [128, 4, 128], F32)
            hn_ = two("hn", [128, 4, 128], BF16)
            Q2_ = [[kb.sb(f"Q2{k}_{i}", [128, 2, 128], BF16, es2) for i in range(2)] for k in range(2)]
            hg = kb.sb("hg", [128, 4], F32, es2)
            for k in range(2):
                for pr in range(2):
                    kb.op(pool, lambda e, k=k, pr=pr: e.memset(Q2_[k][pr][:], 0.0), writes=[Q2_[k][pr]])
            kb.op(dve, lambda e: e.tensor_copy(out=hg[:], in_=smallc[:, 8:12]), reads=[smallc], writes=[hg])

            def tile2(i):
                sl = i % 2
                xn, hT, stats, mv, rstd, mv4, rs4 = xn_[sl], hT_[sl], stats_[sl], mv_[sl], rstd_[sl], mv4_[sl], rs4_[sl]
                uT, sgo, vn, tA, yT, sT, dn, hs, hn, Q2, WM = uT_[sl], sgo_[sl], vn_[sl], tA_[sl], yT_[sl], sT_[sl], dn_[sl], hs_[sl], hn_[sl], Q2_[sl], WM_[sl]
                gc = i + NCT
                t0k = gc * 128
                t0q = i * 128
                xt = xs[i % 2]
                kb.dma(xt[:, 0:512], x_d[i * 128:(i + 1) * 128, 0:512], writes=[xt])
                kb.dma(xt[:, 512:1024], x_d[i * 128:(i + 1) * 128, 512:1024], writes=[xt])
                make_hT(xt, hT, xn, stats, mv, rstd, 0, 1)
                for (bank, c0) in ((PB[0], 0), (PB[1], 1024)):
                    for cc in range(4):
                        for kc in range(8):
                            kb.op(pe, lambda e, cc=cc, kc=kc, bank=bank, c0=c0: e.matmul(
                                bank[:, cc * 128:(cc + 1) * 128], lhsT=Win2[:, kc, c0 + cc * 128:c0 + (cc + 1) * 128], rhs=hT[:, kc, :],
                                start=(kc == 0), stop=(kc == 7)), reads=[Win2, hT.sub((kc, 0))], writes=[bank], sig=(kc == 7 and cc == 3))
                for kc in range(8):
                    kb.op(pe, lambda e, kc=kc: e.matmul(PB[2][:, :], lhsT=hT[:, kc, :], rhs=Win2[:, kc, 512:1024],
                                                        start=(kc == 0), stop=(kc == 7)), reads=[Win2, hT.sub((kc, 0))], writes=[PB[2]], sig=(kc == 7))
                kb.op(act, lambda e: e.activation(out=uT[:].rearrange("p c t -> p (c t)"), in_=PB[0][:, :], func=AF.Copy),
                      reads=[PB[0]], writes=[uT])
                kb.op(act, lambda e: e.activation(out=sgo[:].rearrange("p c t -> p (c t)"), in_=PB[1][:, :], func=AF.Sigmoid),
                      reads=[PB[1]], writes=[sgo])
                ln_stats(PB[2][:, :], PB[2], 512, stats, mv, rstd)
                kb.op(dve, lambda e: e.tensor_scalar(out=vn[:], in0=PB[2][:, :], scalar1=mv[:, 0:1], scalar2=rstd[:, 0:1],
                                                     op0=ALU.subtract, op1=ALU.mult), reads=[PB[2], mv, rstd], writes=[vn])
                for g in range(4):
                    kb.op(pe, lambda e, g=g: e.matmul(PB[3][:, g * 128:(g + 1) * 128], lhsT=vn[:, g * 128:(g + 1) * 128], rhs=wsT[:, g, :],
                                                      start=True, stop=True), reads=[vn, wsT], writes=[PB[3]], sig=(g == 3))
                for g in range(4):
                    kb.op(dve, lambda e, g=g: e.scalar_tensor_tensor(out=tA[:, g, :], in0=PB[3][:, g * 128:(g + 1) * 128],
                                                                     scalar=smallc[:, g:g + 1], in1=BiasA[:, g, :], op0=ALU.mult, op1=ALU.add),
                          reads=[PB[3], smallc, BiasA], writes=[tA.sub(g)])
                kb.op(pool, lambda e: e.tensor_tensor(out=yT[:, 0:4, :], in0=tA[:], in1=uT[:], op=ALU.mult), reads=[tA, uT], writes=[yT.sub('A')])
                for d in range(2):
                    msk = LT if d == 0 else UT
                    for h in range(4):
                        kb.op(pool, lambda e, d=d, h=h, msk=msk: e.tensor_scalar(
                            out=WM[:, d * 4 + h, :], in0=msk[:], scalar1=WS[:, gc, d * 4 + h:d * 4 + h + 1], scalar2=1.0,
                            op0=ALU.mult, op1=ALU.mult), reads=[msk, WS], writes=[WM.sub((d, h))])
                for pr in range(2):
                    for hh in range(2):
                        kb.op(pool, lambda e, pr=pr, hh=hh: e.tensor_copy(out=Q2[pr][hh * 64:(hh + 1) * 64, hh, :],
                                                                          in_=qT[hh * 64:(hh + 1) * 64, pr, t0q:t0q + 128]),
                              reads=[qT], writes=[Q2[pr].sub(hh)])
                for pr in range(2):
                    kb.op(pe, lambda e, pr=pr: e.matmul(PB[4][:, pr * 256:(pr + 1) * 256], lhsT=kT[:, pr, t0k:t0k + 128],
                                                        rhs=Q2[pr][:].rearrange("p a t -> p (a t)"), start=True, stop=True),
                          reads=[kT, Q2[pr]], writes=[PB[4]], sig=(pr == 1))
                for d in range(2):
                    kb.op(dve, lambda e, d=d: e.tensor_tensor(out=sT[:, d * 4:(d + 1) * 4, :].rearrange("p h t -> p (h t)"), in0=PB[4][:, :],
                                                              in1=WM[:, d * 4:(d + 1) * 4, :].rearrange("p h t -> p (h t)"), op=ALU.mult),
                          reads=[PB[4]] + [WM.sub((d, hh_)) for hh_ in range(4)], writes=[sT.sub(d)])
                for d in range(2):
                    bank = PB[5 + d]
                    for h in range(4):
                        kb.op(pe, lambda e, d=d, h=h, bank=bank: e.matmul(bank[:, h * 128:(h + 1) * 128], lhsT=sT[:, d * 4 + h, :],
                                                                          rhs=vaug[:, gc, h, 0:128], start=True, stop=False),
                              reads=[sT.sub(d), vaug], writes=[bank], sig=False)
                        kb.op(pe, lambda e, d=d, h=h, bank=bank: e.matmul(
                            bank[:, h * 128:(h + 1) * 128], lhsT=Q2[h // 2][:, h % 2, :],
                            rhs=ST[:, i, d, h // 2, 0:128], start=False, stop=True),
                            reads=[Q2[h // 2], ST], writes=[bank], sig=(h == 3))
                for d in range(2):
                    for h in range(4):
                        jn = d * 4 + h
                        kb.op(pe, lambda e, d=d, h=h, jn=jn: e.matmul(PB[3][:, 2 * jn:2 * jn + 2], lhsT=sT[:, jn, :],
                                                                      rhs=vaug[:, gc, h, 128:130], start=True, stop=False),
                              reads=[sT.sub(d), vaug], writes=[PB[3]], sig=False)
                        kb.op(pe, lambda e, d=d, h=h, jn=jn: e.matmul(
                            PB[3][:, 2 * jn:2 * jn + 2], lhsT=Q2[h // 2][:, h % 2, :],
                            rhs=ST[:, i, d, h // 2, 128:130], start=False, stop=True),
                            reads=[Q2[h // 2], ST], writes=[PB[3]], sig=(jn == 7))
                den = PB[3][:, 0:16].rearrange("p (j two) -> p j two", two=2)[:, :, 0]
                kb.op(dve, lambda e: e.tensor_scalar(out=dn[:, 0, :], in0=den, scalar1=-1.0, scalar2=None, op0=ALU.mult),
                      reads=[PB[3]], writes=[dn])
                kb.op(dve, lambda e: e.tensor_tensor(out=dn[:, 1, :], in0=dn[:, 0, :], in1=den, op=ALU.max),
                      reads=[PB[3], dn], writes=[dn])
                kb.op(dve, lambda e: e.tensor_tensor(out=dn[:, 0, :], in0=dn[:, 1, :], in1=LB[:, gc, :], op=ALU.max),
                      reads=[dn, LB], writes=[dn])
                kb.op(dve, lambda e: e.reciprocal(out=dn[:, 2, :], in_=dn[:, 0, :]), reads=[dn], writes=[dn])
                for h in range(4):
                    kb.op(act, lambda e, h=h: e.activation(out=hs[:, h, :], in_=PB[5][:, h * 128:(h + 1) * 128], func=AF.Identity,
                                                           scale=dn[:, 2, h:h + 1]), reads=[PB[5], dn], writes=[hs.sub(h)])
                for h in range(4):
                    kb.op(dve, lambda e, h=h: e.scalar_tensor_tensor(out=hs[:, h, :], in0=PB[6][:, h * 128:(h + 1) * 128],
                                                                     scalar=dn[:, 2, 4 + h:5 + h], in1=hs[:, h, :], op0=ALU.mult, op1=ALU.add),
                          reads=[PB[6], dn, hs.sub(h)], writes=[hs.sub(h)])
                for h in range(4):
                    kb.op(dve, lambda e, h=h: e.bn_stats(out=stats[:, h, :], in_=hs[:, h, :]), reads=[hs.sub(h)], writes=[stats.sub(h)])
                for h in range(4):
                    kb.op(dve, lambda e, h=h: e.bn_aggr(out=mv4[:, h, 0:2], in_=stats[:, h, :]), reads=[stats.sub(h)], writes=[mv4.sub(h)])
                kb.op(pool, lambda e: e.tensor_tensor(out=mv4[:, :, 2], in0=mv4[:, :, 1], in1=cst[:, 1:2].to_broadcast([128, 4]), op=ALU.add),
                      reads=[mv4, cst], writes=[mv4])
                kb.op(pool, lambda e: e.tensor_tensor(out=rs4[:], in0=mv4[:, :, 2], in1=cst[:, 0:1].to_broadcast([128, 4]), op=ALU.pow),
                      reads=[mv4, cst], writes=[rs4])
                for h in range(4):
                    kb.op(dve, lambda e, h=h: e.tensor_scalar(out=hn[:, h, :], in0=hs[:, h, :], scalar1=mv4[:, h, 0:1], scalar2=rs4[:, h:h + 1],
                                                              op0=ALU.subtract, op1=ALU.mult), reads=[hs.sub(h), mv4, rs4], writes=[hn.sub(h)])
                for h in range(4):
                    kb.op(pe, lambda e, h=h: e.transpose(out=PT[:, h * 128:(h + 1) * 128], in_=hn[:, h, :], identity=identb[:]),
                          reads=[hn.sub(h), identb], writes=[PT], sig=(h == 3))
                for h in range(4):
                    kb.op(dve, lambda e, h=h: e.scalar_tensor_tensor(out=yT[:, 4 + h, :], in0=PT[:, h * 128:(h + 1) * 128],
                                                                     scalar=hg[:, h:h + 1], in1=sgo[:, h, :], op0=ALU.mult, op1=ALU.mult),
                          reads=[PT, hg, sgo], writes=[yT.sub(4 + h)])
                for half in range(2):
                    for kc in range(8):
                        kb.op(pe, lambda e, half=half, kc=kc: e.matmul(PB[3 + half][:, :], lhsT=yT[:, kc, :],
                                                                       rhs=Wo[:, kc, half * 512:(half + 1) * 512],
                                                                       start=(kc == 0), stop=(kc == 7)),
                              reads=[yT, Wo], writes=[PB[3 + half]], sig=(kc == 7))
                for half in range(2):
                    kb.op(dve, lambda e, half=half: e.scalar_tensor_tensor(out=xt[:, half * 512:(half + 1) * 512],
                                                                           in0=xt[:, half * 512:(half + 1) * 512], scalar=ALPHA,
                                                                           in1=PB[3 + half][:, :], op0=ALU.mult, op1=ALU.add),
                          reads=[xt.sub(half), PB[3 + half]], writes=[xt.sub(half)])
                ln_stats(xt[:, :], xt, 1024, stats, mv, rstd)
                kb.op(dve, lambda e: e.scalar_tensor_tensor(out=xt[:, :], in0=xt[:, :], scalar=mv[:, 0:1], in1=ln1gb[:],
                                                            op0=ALU.subtract, op1=ALU.mult), reads=[xt, mv, ln1gb], writes=[xt])
                kb.op(dve, lambda e: e.scalar_tensor_tensor(out=xt[:, :], in0=xt[:, :], scalar=rstd[:, 0:1], in1=ln1bb[:],
                                                            op0=ALU.mult, op1=ALU.add), reads=[xt, rstd, ln1bb], writes=[xt])
                kb.dma(y_d[i * 128:(i + 1) * 128, :], xt[:, :], reads=[xt])
            lists2 = [kb.record(tile2, i) for i in range(NT)]
            interleave(lists2, (len(lists2[0]) * 11) // 20)
            kb.barrier()

        if stage == 4:
            raise _Stop((nc, dbg_outs))
        es0.close()
        es_p.close()

        GRP = 2
        w1_v = w1_d.rearrange("(kc p) n -> p kc n", p=128)
        w2_v = w2_d.rearrange("(j p) n -> p j n", p=128)
        with ExitStack() as es3:
            W1b = kb.sb("W1b", [128, 8, DFF], BF16, es3)
            W2b = kb.sb("W2b", [128, 32, D], BF16, es3)
            ln2gb = kb.sb("ln2gb", [128, D], F32, es3)
            ln2bb = kb.sb("ln2bb", [128, D], F32, es3)
            b2h = kb.sb("b2h", [1, 2, D], BF16, es3)
            kb.dma(ln2gb[:], ln2g_d[0:1, :].to_broadcast([128, D]), writes=[ln2gb])
            kb.dma(ln2bb[:], ln2b_d[0:1, :].to_broadcast([128, D]), writes=[ln2bb])
            with ExitStack() as esw:
                g2bc = kb.sb("g2bc", [128, D], F32, esw)
                b2bc = kb.sb("b2bc", [128, D], F32, esw)
                NS3 = 6
                wstg = [kb.sb(f"wstg3{i}", [128, 8, 256], F32, esw) for i in range(NS3)]
                kb.dma(g2bc[:], g2row[:, :], writes=[g2bc])
                kb.dma(b2bc[0:1, :], b2_d[0:1, :], writes=[b2bc])
                kb.op(dve, lambda e: e.tensor_tensor(out=b2bc[0:1, :], in0=b2bc[0:1, :], in1=g2bc[0:1, :], op=ALU.mult),
                      reads=[b2bc, g2bc], writes=[b2bc])
                kb.op(dve, lambda e: e.tensor_copy(out=b2h[0:1, 0, :], in_=b2bc[0:1, :]), reads=[b2bc], writes=[b2h])
                kb.op(dve, lambda e: e.tensor_tensor(out=b2bc[0:1, :], in0=b2bc[0:1, :], in1=b2h[0:1, 0, :], op=ALU.subtract),
                      reads=[b2bc, b2h], writes=[b2bc])
                kb.op(dve, lambda e: e.tensor_copy(out=b2h[0:1, 1, :], in_=b2bc[0:1, :]), reads=[b2bc], writes=[b2h])
                for blk in range(16):
                    c0 = blk * 256
                    load_weight_block(W1b, lambda ks, c0=c0: W1b[:, ks, c0:c0 + 256], w1_v[:, :, c0:c0 + 256], wstg[blk % NS3], 8)
                wstg2 = [Buf(wstg[i].t, f"wstg3b{i}") for i in range(NS3)]
                kb.barrier()
                for blk in range(16):
                    sgb = wstg2[blk % NS3]
                    sv = sgb.t[:].rearrange("p a b -> p (a b)").rearrange("p (j n) -> p j n", j=2)
                    kb.dma(sv[:, 0:1, :], w2_v[:, blk * 2:blk * 2 + 1, :], writes=[sgb])
                    kb.dma(sv[:, 1:2, :], w2_v[:, blk * 2 + 1:blk * 2 + 2, :], writes=[sgb])
                    for jj in range(2):
                        eng = dve if jj == 0 else pool
                        kb.op(eng, lambda e, jj=jj, blk=blk, sv=sv: e.tensor_tensor(out=W2b[:, blk * 2 + jj, :], in0=sv[:, jj, :],
                                                                                   in1=g2bc[:], op=ALU.mult),
                              reads=[sgb, g2bc], writes=[W2b])
                kb.barrier()
            xs = [kb.sb(f"x3s{i}", [128, D], F32, es3) for i in range(2 * GRP)]
            xn = kb.sb("xn3", [128, D], BF16, es3)
            h2T_ = [kb.sb(f"h2T{k}", [128, 8, GRP * 128], BF16, es3) for k in range(2)]
            hid = kb.sb("hid", [128, 32, GRP * 128], BF16, es3)
            hidb = [Buf(hid.t, f"hid{j}") for j in range(32)]
            rl = [kb.sb(f"rl{i}", [128, GRP * 128], BF16, es3) for i in range(4)]
            stats_p = kb.sb("stats3p", [128, 2, 6], F32, es3)
            mv_p = kb.sb("mv3p", [128, 4], F32, es3)
            rstd_p = kb.sb("rstd3p", [128, 2], F32, es3)
            stats_e = kb.sb("stats3e", [128, 2, 6], F32, es3)
            mv_e = kb.sb("mv3e", [128, 4], F32, es3)
            rstd_e = kb.sb("rstd3e", [128, 2], F32, es3)
            NGRP = NT // GRP

            def prep3(gi):
                for a in range(GRP):
                    ti = gi * GRP + a
                    xt = xs[(gi % 2) * GRP + a]
                    kb.dma(xt[:, 0:512], y_d[ti * 128:(ti + 1) * 128, 0:512], writes=[xt])
                    kb.dma(xt[:, 512:1024], y_d[ti * 128:(ti + 1) * 128, 512:1024], writes=[xt])
                    make_hT(xt, h2T_[gi % 2], xn, stats_p, mv_p, rstd_p, 2, 3, tok0=a * 128)

            def main3(gi):
                h2T = h2T_[gi % 2]
                for j in range(32):
                    bank = PB[j % 2]
                    for kc in range(8):
                        kb.op(pe, lambda e, j=j, kc=kc, bank=bank: e.matmul(bank[:, 0:GRP * 128], lhsT=W1b[:, kc, j * 128:(j + 1) * 128],
                                                                            rhs=h2T[:, kc, :], start=(kc == 0), stop=(kc == 7)),
                              reads=[W1b] + [h2T.sub((kc, a_ * 128)) for a_ in range(GRP)], writes=[bank], sig=(kc == 7))
                    rb = rl[j % 4]
                    kb.op(act, lambda e, j=j, bank=bank, rb=rb: e.activation(out=rb[:], in_=bank[:, 0:GRP * 128], func=AF.Relu,
                                                                             bias=smallc[:, 24 + j:25 + j], scale=1.0),
                          reads=[bank, smallc], writes=[rb])
                    eng = pool if j % 4 == 3 else dve
                    kb.op(eng, lambda e, j=j, rb=rb: e.tensor_tensor(out=hid[:, j, :], in0=rb[:], in1=rb[:], op=ALU.mult),
                          reads=[rb], writes=[hidb[j]])
                for a in range(GRP):
                    ti = gi * GRP + a
                    xt = xs[(gi % 2) * GRP + a]
                    for half in range(2):
                        bank = PB[2 + 2 * (a % 2) + half]
                        for j in range(32):
                            kb.op(pe, lambda e, j=j, a=a, half=half, bank=bank: e.matmul(
                                bank[:, :], lhsT=hid[:, j, a * 128:(a + 1) * 128], rhs=W2b[:, j, half * 512:(half + 1) * 512],
                                start=(j == 0), stop=False), reads=[hidb[j], W2b], writes=[bank], sig=False)
                        for hl in range(2):
                            kb.op(pe, lambda e, hl=hl, half=half, bank=bank: e.matmul(
                                bank[:, :], lhsT=onesb[0:1, :], rhs=b2h[0:1, hl, half * 512:(half + 1) * 512],
                                start=False, stop=(hl == 1)), reads=[onesb, b2h], writes=[bank], sig=(hl == 1))
                        kb.op(dve, lambda e, half=half, bank=bank, xt=xt: e.scalar_tensor_tensor(
                            out=xt[:, half * 512:(half + 1) * 512], in0=xt[:, half * 512:(half + 1) * 512], scalar=ALPHA,
                            in1=bank[:, :], op0=ALU.mult, op1=ALU.add), reads=[xt.sub(half), bank], writes=[xt.sub(half)])
                    ln_stats(xt[:, :], xt, 1024, stats_e, mv_e, rstd_e)
                    kb.op(dve, lambda e, xt=xt: e.scalar_tensor_tensor(out=xt[:, :], in0=xt[:, :], scalar=mv_e[:, 0:1], in1=ln2gb[:],
                                                                       op0=ALU.subtract, op1=ALU.mult), reads=[xt, mv_e, ln2gb], writes=[xt])
                    kb.op(dve, lambda e, xt=xt: e.scalar_tensor_tensor(out=xt[:, :], in0=xt[:, :], scalar=rstd_e[:, 0:1], in1=ln2bb[:],
                                                                       op0=ALU.mult, op1=ALU.add), reads=[xt, rstd_e, ln2bb], writes=[xt])
                    kb.dma(y_d[ti * 128:(ti + 1) * 128, :], xt[:, :], reads=[xt])

            for st in kb.record(prep3, 0):
                st()
            for gi in range(NGRP):
                M = kb.record(main3, gi)
                P = kb.record(prep3, gi + 1) if gi + 1 < NGRP else []
                span = max(1, int(len(M) * 0.55))
                pi = 0
                for k, st in enumerate(M):
                    st()
                    want = min(len(P), ((k + 1) * len(P)) // span)
                    while pi < want:
                        P[pi]()
                        pi += 1
                while pi < len(P):
                    P[pi]()
                    pi += 1
            kb.barrier()
    return nc, dbg_outs


_CACHE = {}


def make_in_maps(inputs):
    g = lambda k: np.ascontiguousarray(np.asarray(inputs[k], dtype=np.float32))
    shared = {
        "c_ctx": g("c_ctx").reshape(8, 128),
        "w_ada": g("w_ada")[0],
        "b_ada": g("b_ada")[0].reshape(48, 128),
        "w_in": g("w_in")[0],
        "w_s": g("w_s")[0],
        "b_s": g("b_s")[0].reshape(1, 512),
        "ln_v_g": g("ln_v_g")[0].reshape(4, 128),
        "ln_v_b": g("ln_v_b")[0].reshape(4, 128),
        "conv_qk": g("conv_qk")[0].reshape(12, 128),
        "b_gates": g("b_gates")[0].reshape(1, 16),
        "hn_g": g("hn_g")[0].reshape(4, 128),
        "w_out": g("w_out")[0],
        "ln1_g": g("ln1_g")[0].reshape(1, D),
        "ln1_b": g("ln1_b")[0].reshape(1, D),
        "w1": g("w1")[0],
        "b1": g("b1")[0].reshape(32, 128),
        "w2": g("w2")[0],
        "b2": g("b2")[0].reshape(1, D),
        "ln2_g": g("ln2_g")[0].reshape(1, D),
        "ln2_b": g("ln2_b")[0].reshape(1, D),
    }
    x, c, ctx = g("x"), g("c"), g("ctx")
    maps = []
    for b in range(x.shape[0]):
        m = dict(shared)
        m["x"] = x[b]
        m["c"] = c[b].reshape(8, 128)
        m["ctx"] = ctx[b]
        maps.append(m)
    return maps


def kernel(**inputs):
    if "nc" not in _CACHE:
        _CACHE["nc"] = build_program(False)[0]
    nc = _CACHE["nc"]
    maps = make_in_maps(inputs)
    n = len(maps)
    res = run_bass_kernel_spmd(nc, maps, core_ids=list(range(n)))
    out = np.stack([np.asarray(r["y"], dtype=np.float32) for r in res.results], axis=0)
    return out
```

```python
import math
from contextlib import ExitStack
import numpy as np
import concourse.bass as bass
import concourse.mybir as mybir
from concourse.bass_utils import run_bass_kernel_spmd

F32 = mybir.dt.float32
BF16 = mybir.dt.bfloat16
AF = mybir.ActivationFunctionType
ALU = mybir.AluOpType

D = 1024
S = 4096
CTX = 256
NT = S // 128
NCT = CTX // 128
NG = NT + NCT
DIN = 2576
DFF = 4096
ALPHA = 2.0 ** 0.25
EPS = 1e-5
SEM_LIMIT = 3000


class Tok:
    __slots__ = ("sem", "val", "key")

    def __init__(self, sem, val, key):
        self.sem, self.val, self.key = sem, val, key


class Buf:
    def __init__(self, t, name, parent=None):
        self.t = t
        self.name = name
        self.w = None
        self.r = {}
        self.dsem = None
        self.dcount = 0
        self.parent = parent
        self.children = {}

    def sub(self, key):
        c = self.children.get(key)
        if c is None:
            c = Buf(self.t, f"{self.name}.{key}", parent=self)
            self.children[key] = c
        return c

    def __getitem__(self, idx):
        return self.t[idx]


class Eng:
    def __init__(self, kb, name, h):
        self.kb, self.name, self.h = kb, name, h
        self.seen = {}
        self.epoch = 0
        self.count = 0
        self.sem = kb.new_sem(f"{name}_e0")
        self.pending = False

    def roll(self):
        if self.count >= SEM_LIMIT and not self.pending:
            self.epoch += 1
            self.count = 0
            self.sem = self.kb.new_sem(f"{self.name}_e{self.epoch}")


class KB:
    def __init__(self, nc, es):
        self.nc, self.es = nc, es
        self.nsem = 0
        self.pe = Eng(self, "pe", nc.tensor)
        self.act = Eng(self, "act", nc.scalar)
        self.dve = Eng(self, "dve", nc.vector)
        self.pool = Eng(self, "pool", nc.gpsimd)
        self.sp = Eng(self, "sp", nc.sync)
        self.engs = [self.pe, self.act, self.dve, self.pool, self.sp]
        self.dma_toks = []

    def new_sem(self, name):
        self.nsem += 1
        s = self.es.enter_context(self.nc.semaphore(name))
        return (s, name)

    def sb(self, name, shape, dt, es=None):
        t = (es or self.es).enter_context(self.nc.sbuf_tensor(name, list(shape), dt))
        return Buf(t, name)

    def ps(self, name, shape, dt, es=None):
        t = (es or self.es).enter_context(self.nc.psum_tensor(name, list(shape), dt))
        b = Buf(t, name)
        b.psum = True
        return b

    def wait(self, eng, tok):
        if tok is None:
            return
        if eng.name == "pe" and tok.key.startswith("pe_e"):
            return
        if eng.seen.get(tok.key, 0) >= tok.val:
            return
        eng.h.wait_ge(tok.sem, tok.val)
        eng.seen[tok.key] = tok.val

    def _deps(self, eng, reads, writes):
        for b in reads:
            self.wait(eng, b.w)
            if getattr(b, "psum", False):
                for k_, t_ in b.r.items():
                    if not k_.startswith(eng.name + "_e"):
                        self.wait(eng, t_)
            if b.parent is not None:
                self.wait(eng, b.parent.w)
            for c in b.children.values():
                self.wait(eng, c.w)
        for b in writes:
            self.wait(eng, b.w)
            for t in b.r.values():
                self.wait(eng, t)
            if b.parent is not None:
                self.wait(eng, b.parent.w)
                for t in b.parent.r.values():
                    self.wait(eng, t)
            for c in b.children.values():
                self.wait(eng, c.w)
                for t in c.r.values():
                    self.wait(eng, t)

    def _mark(self, tok, reads, writes):
        for b in reads:
            old = b.r.get(tok.key)
            if old is None or old.val < tok.val:
                b.r[tok.key] = tok
        for b in writes:
            b.w = tok
            b.r = {}

    def record(self, body, *args):
        self.rec = []
        body(*args)
        r, self.rec = self.rec, None
        return r

    def op(self, eng, fn, reads=(), writes=(), sig=True):
        if getattr(self, "rec", None) is not None:
            self.rec.append(lambda: self._op(eng, fn, reads, writes, sig))
            return None
        return self._op(eng, fn, reads, writes, sig)

    def _op(self, eng, fn, reads=(), writes=(), sig=True):
        if sig:
            eng.roll()
        self._deps(eng, reads, writes)
        inst = fn(eng.h)
        if sig:
            eng.count += 1
            inst.then_inc(eng.sem[0], 1)
            tok = Tok(eng.sem[0], eng.count, eng.sem[1])
            eng.pending = False
        else:
            tok = Tok(eng.sem[0], eng.count + 1, eng.sem[1])
            eng.pending = True
        self._mark(tok, reads, writes)
        return tok

    def dma(self, out_ap, in_ap, reads=(), writes=(), sembuf=None, eng=None):
        if getattr(self, "rec", None) is not None:
            self.rec.append(lambda: self._dma(out_ap, in_ap, reads, writes, sembuf, eng))
            return None
        return self._dma(out_ap, in_ap, reads, writes, sembuf, eng)

    def _dma(self, out_ap, in_ap, reads=(), writes=(), sembuf=None, eng=None):
        eng = eng or self.sp
        sb_ = sembuf or (writes[0] if writes else reads[0])
        if sb_.dsem is None:
            sb_.dsem = self.new_sem(f"d_{sb_.name}")
        self._deps(eng, reads, writes)
        inst = eng.h.dma_start(out=out_ap, in_=in_ap)
        inst.then_inc(sb_.dsem[0], 16)
        sb_.dcount += 16
        tok = Tok(sb_.dsem[0], sb_.dcount, sb_.dsem[1])
        self._mark(tok, reads, writes)
        self.dma_toks.append(tok)
        return tok

    def barrier(self):
        toks = []
        for e in self.engs:
            if e.count > 0:
                assert not e.pending
                toks.append(Tok(e.sem[0], e.count, e.sem[1]))
        toks += self.dma_toks
        self.dma_toks = []
        for e in self.engs:
            for t in toks:
                self.wait(e, t)


class _Stop(Exception):
    pass


def interleave(step_lists, H):
    n = len(step_lists)
    T = max(i * H + len(sl) for i, sl in enumerate(step_lists))
    lo = 0
    for t in range(T):
        while lo < n and t - lo * H >= len(step_lists[lo]):
            lo += 1
        i = lo
        while i < n and t - i * H >= 0:
            k = t - i * H
            if k < len(step_lists[i]):
                step_lists[i][k]()
            i += 1


def build_program(dbg=False, stage=99):
    try:
        return _build_program(dbg, stage)
    except _Stop as ex:
        return ex.args[0]


def _build_program(dbg, stage):
    nc = bass.Bass("TRN2", target_bir_lowering=False)

    def din(name, shape):
        return nc.dram_tensor(name, list(shape), F32, kind="ExternalInput").ap()

    x_d = din("x", [S, D])
    c_d = din("c", [8, 128])
    ctx_d = din("ctx", [CTX, D])
    cctx_d = din("c_ctx", [8, 128])
    wada_d = din("w_ada", [D, 6 * D])
    bada_d = din("b_ada", [48, 128])
    win_d = din("w_in", [D, DIN])
    ws_d = din("w_s", [4, 128, 128])
    bs_d = din("b_s", [1, 512])
    lnvg_d = din("ln_v_g", [4, 128])
    lnvb_d = din("ln_v_b", [4, 128])
    conv_d = din("conv_qk", [12, 128])
    bg_d = din("b_gates", [1, 16])
    hng_d = din("hn_g", [4, 128])
    wout_d = din("w_out", [D, D])
    ln1g_d = din("ln1_g", [1, D])
    ln1b_d = din("ln1_b", [1, D])
    w1_d = din("w1", [D, DFF])
    b1_d = din("b1", [32, 128])
    w2_d = din("w2", [DFF, D])
    b2_d = din("b2", [1, D])
    ln2g_d = din("ln2_g", [1, D])
    ln2b_d = din("ln2_b", [1, D])
    y_d = nc.dram_tensor("y", [S, D], F32, kind="ExternalOutput").ap()
    dbg_outs = {}

    with ExitStack() as es:
        kb = KB(nc, es)
        pe, act, dve, pool, sp = kb.pe, kb.act, kb.dve, kb.pool, kb.sp

        PB = [kb.ps(f"pb{i}", [128, 512], F32) for i in range(7)]
        PT = kb.ps("pt", [128, 1024], BF16)

        identf = kb.sb("identf", [128, 128], F32)
        identb = kb.sb("identb", [128, 128], BF16)
        LT = kb.sb("LT", [128, 128], F32)
        UT = kb.sb("UT", [128, 128], F32)
        onesf = kb.sb("onesf", [128, 128], F32)
        onesb = kb.sb("onesb", [128, 128], BF16)
        cst = kb.sb("cst", [128, 8], F32)
        modc = kb.sb("modc", [128, 6, 8], F32)
        smallc = kb.sb("smallc", [128, 64], F32)
        bgb = kb.sb("bgb", [128, 16], F32)
        BiasA = kb.sb("BiasA", [128, 4, 128], F32)
        wsT = kb.sb("wsT", [128, 4, 128], BF16)
        setup = Buf(None, "setup")
        ccol = kb.sb("ccol", [128, 2, 8], F32)
        badac = kb.sb("badac", [128, 48], F32)

        def dbg_out(name, buf, ap, shape, dt=F32):
            if not dbg:
                return
            o = nc.dram_tensor("dbg_" + name, list(shape), dt, kind="ExternalOutput").ap()
            dbg_outs[name] = (shape, dt)
            kb.dma(o, ap, reads=[buf], sembuf=buf)

        kb.op(pool, lambda e: e.memset(onesf[:], 1.0), writes=[onesf])
        kb.op(pool, lambda e: e.memset(onesb[:], 1.0), writes=[onesb])
        kb.op(pool, lambda e: e.memset(cst[:, 0:1], -0.5), writes=[cst])
        kb.op(pool, lambda e: e.memset(cst[:, 1:2], EPS), writes=[cst])
        kb.op(pool, lambda e: e.memset(cst[:, 2:3], math.log(8.0)), writes=[cst])
        kb.op(pool, lambda e: e.memset(cst[:, 3:4], 1.0), writes=[cst])
        kb.op(pool, lambda e: e.affine_select(out=identf[:], in_=onesf[:], pattern=[[-1, 128]], compare_op=ALU.is_equal,
                                              fill=0.0, base=0, channel_multiplier=1), reads=[onesf], writes=[identf])
        kb.op(pool, lambda e: e.affine_select(out=LT[:], in_=onesf[:], pattern=[[1, 128]], compare_op=ALU.is_ge,
                                              fill=0.0, base=0, channel_multiplier=-1), reads=[onesf], writes=[LT])
        kb.op(pool, lambda e: e.affine_select(out=UT[:], in_=onesf[:], pattern=[[-1, 128]], compare_op=ALU.is_ge,
                                              fill=0.0, base=0, channel_multiplier=1), reads=[onesf], writes=[UT])
        kb.op(dve, lambda e: e.tensor_copy(out=identb[:], in_=identf[:]), reads=[identf], writes=[identb])

        es_p = es.enter_context(ExitStack())
        qT = kb.sb("qT", [128, 2, S], BF16, es_p)
        kT = kb.sb("kT", [128, 2, NG * 128], BF16, es_p)
        vaug = kb.sb("vaug", [128, NG, 4, 130], BF16, es_p)
        Gt = kb.sb("Gt", [128, NG, 16], F32, es_p)
        WS = kb.sb("WS", [128, NG, 8], F32, es_p)
        LB = kb.sb("LB", [128, NG, 8], F32, es_p)
        CW = kb.sb("CW", [128, NG, 8], F32, es_p)
        ST = kb.sb("ST", [128, NT, 2, 2, 130], BF16, es_p)
        kb.op(pool, lambda e: e.memset(vaug[:, :, :, 128:130], 1.0), writes=[vaug])

        es0 = es.enter_context(ExitStack())
        ln1gb = kb.sb("ln1gb", [128, D], F32, es0)
        ln1bb = kb.sb("ln1bb", [128, D], F32, es0)
        es_set = es.enter_context(ExitStack())
        rows = kb.sb("rows", [128, 256], F32, es_set)
        bsb = kb.sb("bsb", [128, 512], F32, es_set)
        R_C, R_CC, R_BADA, R_CONV, R_GV, R_BV, R_HNG, R_B1 = 0, 8, 16, 64, 76, 80, 84, 88
        kb.dma(rows[0:8, 0:128], c_d[:, :], writes=[rows])
        kb.dma(rows[0:8, 128:256], cctx_d[:, :], writes=[rows])
        rows2 = kb.sb("rows2", [128, 128], F32, es_set)
        kb.dma(rows2[0:48, :], bada_d[:, :], writes=[rows2])
        rows3 = kb.sb("rows3", [128, 128], F32, es_set)
        kb.dma(rows3[0:12, :], conv_d[:, :], writes=[rows3])
        kb.dma(rows3[32:36, :], lnvg_d[:, :], writes=[rows3])
        kb.dma(rows3[64:68, :], lnvb_d[:, :], writes=[rows3])
        rows4 = kb.sb("rows4", [128, 128], F32, es_set)
        kb.dma(rows4[0:4, :], hng_d[:, :], writes=[rows4])
        kb.dma(rows4[32:64, :], b1_d[:, :], writes=[rows4])
        kb.dma(bgb[:], bg_d[0:1, :].to_broadcast([128, 16]), writes=[bgb])
        kb.dma(bsb[:], bs_d[0:1, :].to_broadcast([128, 512]), writes=[bsb])
        wsr = kb.sb("wsr", [128, 4, 128], F32, es_set)
        kb.dma(wsr[:], ws_d.rearrange("g t s -> t g s"), writes=[wsr])


        def tr_f32(dst_ap, dst_buf, src_ap, src_buf, n, bank, p0=0):
            kb.op(pe, lambda e: e.transpose(out=bank[:, 0:n], in_=src_ap, identity=identf[p0:p0 + n, p0:p0 + n]),
                  reads=[src_buf, identf], writes=[bank])
            kb.op(dve, lambda e: e.tensor_copy(out=dst_ap, in_=bank[:, 0:n]), reads=[bank], writes=[dst_buf])

        craw = kb.sb("craw", [128, 2, 8], F32, es_set)
        tr_f32(craw[:, 0, :], craw, rows[0:8, 0:128], rows, 8, PB[0])
        tr_f32(craw[:, 1, :], craw, rows[0:8, 128:256], rows, 8, PB[1])
        kb.op(act, lambda e: e.activation(out=ccol[:], in_=craw[:], func=AF.Silu), reads=[craw], writes=[ccol])
        tr_f32(badac[:, :], badac, rows2[0:48, :], rows2, 48, PB[2])
        tr_f32(smallc[:, 12:24], smallc, rows3[0:12, :], rows3, 12, PB[3])
        tr_f32(smallc[:, 0:4], smallc, rows3[32:36, :], rows3, 4, PB[4], p0=32)
        tr_f32(smallc[:, 4:8], smallc, rows3[64:68, :], rows3, 4, PB[5], p0=64)
        tr_f32(smallc[:, 8:12], smallc, rows4[0:4, :], rows4, 4, PB[6])
        tr_f32(smallc[:, 24:56], smallc, rows4[32:64, :], rows4, 32, PB[0], p0=32)
        wsTf = kb.sb("wsTf", [128, 4, 128], F32, es_set)
        for g in range(4):
            kb.op(pe, lambda e, g=g: e.transpose(out=PB[1][:, g * 128:(g + 1) * 128], in_=wsr[:, g, :], identity=identf[:]),
                  reads=[wsr, identf], writes=[PB[1]])
        kb.op(dve, lambda e: e.tensor_copy(out=wsTf[:].rearrange("p g t -> p (g t)"), in_=PB[1][:, :]), reads=[PB[1]], writes=[wsTf])
        kb.op(act, lambda e: e.activation(out=wsT[:], in_=wsTf[:], func=AF.Copy), reads=[wsTf], writes=[wsT])
        kb.op(pe, lambda e: e.matmul(PB[2][:, :], lhsT=onesf[:], rhs=wsTf[:].rearrange("p g t -> p (g t)"), start=True, stop=True),
              reads=[onesf, wsTf], writes=[PB[2]])
        for g in range(4):
            kb.op(dve, lambda e, g=g: e.scalar_tensor_tensor(out=BiasA[:, g, :], in0=PB[2][:, g * 128:(g + 1) * 128],
                                                             scalar=smallc[:, 4 + g:5 + g], in1=bsb[:, g * 128:(g + 1) * 128],
                                                             op0=ALU.mult, op1=ALU.add),
                  reads=[PB[2], smallc, bsb], writes=[BiasA])

        if stage == -1:
            dbg_out("smallc", smallc, smallc[:], [128, 64])
            dbg_out("BiasA", BiasA, BiasA[:], [128, 4, 128])
            dbg_out("ccol", ccol, ccol[:], [128, 2, 8])
            dbg_out("LT", LT, LT[:], [128, 128])
            dbg_out("identf", identf, identf[:], [128, 128])
            kb.barrier()
            raise _Stop((nc, dbg_outs))
        kb.barrier()
        es_set.close()
        kb.dma(ln1gb[:], ln1g_d[0:1, :].to_broadcast([128, D]), writes=[ln1gb])
        kb.dma(ln1bb[:], ln1b_d[0:1, :].to_broadcast([128, D]), writes=[ln1bb])
        g2row = nc.dram_tensor("g2scratch", [128, D], F32, kind="Internal").ap()
        g1row = nc.dram_tensor("g1scratch", [128, D], F32, kind="Internal").ap()

        with ExitStack() as esa:
            stg = [kb.sb(f"astg{i}", [128, 8, 512], F32, esa) for i in range(4)]
            scb = kb.sb("scb", [128, 8, 128], F32, esa)
            badab = kb.sb("badab", [128, 2, D], F32, esa)
            g2bc0 = kb.sb("g2bc0", [128, D], F32, esa)
            g1bc = kb.sb("g1bc0", [128, D], F32, esa)
            kb.dma(badab[:, 0, :], bada_d[16:24, :].rearrange("(o a) b -> o (a b)", o=1).to_broadcast([128, D]), writes=[badab])
            kb.dma(badab[:, 1, :], bada_d[40:48, :].rearrange("(o a) b -> o (a b)", o=1).to_broadcast([128, D]), writes=[badab])
            for kc in range(8):
                kb.op(dve, lambda e, kc=kc: e.tensor_copy(out=scb[:, kc, :], in_=ccol[:, 0, kc:kc + 1].to_broadcast([128, 128])),
                      reads=[ccol], writes=[scb])
            wada_v = wada_d.rearrange("(kc p) n -> p kc n", p=128)
            col_kind = {0: 0, 1: 0, 2: 1, 3: 1, 6: 2, 7: 2, 8: 3, 9: 3}
            for blk in range(12):
                sg = stg[blk % 4]
                kb.dma(sg[:, 0:4, :], wada_v[:, 0:4, blk * 512:(blk + 1) * 512], writes=[sg])
                kb.dma(sg[:, 4:8, :], wada_v[:, 4:8, blk * 512:(blk + 1) * 512], writes=[sg])
                if blk in col_kind:
                    mi = col_kind[blk]
                    for jj in range(4):
                        j = blk * 4 + jj
                        fchunk = j % 8
                        bank = PB[jj % 4]
                        for kc in range(8):
                            kb.op(pe, lambda e, kc=kc, jj=jj, bank=bank: e.matmul(
                                bank[:, 0:2], lhsT=sg[:, kc, jj * 128:(jj + 1) * 128], rhs=ccol[:, :, kc],
                                start=(kc == 0), stop=(kc == 7)), reads=[sg, ccol], writes=[bank], sig=(kc == 7))
                        kb.op(dve, lambda e, bank=bank, mi=mi, fchunk=fchunk, j=j: e.tensor_tensor(
                            out=modc[:, mi, fchunk:fchunk + 1], in0=bank[:, 0:1], in1=badac[:, j:j + 1], op=ALU.add),
                            reads=[bank, badac], writes=[modc.sub((mi, fchunk))])
                        if mi < 2:
                            kb.op(dve, lambda e, bank=bank, mi=mi, fchunk=fchunk, j=j: e.tensor_tensor(
                                out=modc[:, 4 + mi, fchunk:fchunk + 1], in0=bank[:, 1:2], in1=badac[:, j:j + 1], op=ALU.add),
                                reads=[bank, badac], writes=[modc.sub((4 + mi, fchunk))])
                else:
                    which = 0 if blk in (4, 5) else 1
                    half = blk % 2 if which == 1 else blk - 4
                    bank = PB[4 + (blk % 2)]
                    for kc in range(8):
                        kb.op(pe, lambda e, kc=kc, bank=bank: e.matmul(bank[:, :], lhsT=scb[:, kc, :], rhs=sg[:, kc, :],
                                                                      start=(kc == 0), stop=(kc == 7)),
                              reads=[sg, scb], writes=[bank], sig=(kc == 7))
                    dst = g1bc if which == 0 else g2bc0
                    kb.op(dve, lambda e, bank=bank, dst=dst, half=half, which=which: e.tensor_tensor(
                        out=dst[:, half * 512:(half + 1) * 512], in0=bank[:, :], in1=badab[:, which, half * 512:(half + 1) * 512],
                        op=ALU.add), reads=[bank, badab], writes=[dst])
            for mi in (1, 3, 5):
                kb.op(dve, lambda e, mi=mi: e.tensor_scalar(out=modc[:, mi, :], in0=modc[:, mi, :], scalar1=1.0, scalar2=None,
                                                            op0=ALU.add), reads=[modc], writes=[modc])
            g2st = kb.dma(g2row[:, :], g2bc0[:], reads=[g2bc0])
            kb.dma(g1row[:, :], g1bc[:], reads=[g1bc])
            kb.barrier()
            if stage == 0:
                dbg_out("modc", modc, modc[:], [128, 6, 8])
                dbg_out("g1bc", g1bc, g1bc[:], [128, D])
                dbg_out("smallc", smallc, smallc[:], [128, 64])
                dbg_out("BiasA", BiasA, BiasA[:], [128, 4, 128])
                kb.barrier()
                raise _Stop((nc, dbg_outs))

        def ln_stats(xap, xbuf, width, stats, mv, rstd, nmr=None):
            nchunk = width // 512
            for cidx in range(nchunk):
                kb.op(dve, lambda e, cidx=cidx: e.bn_stats(out=stats[:, cidx, :], in_=xap[:, cidx * 512:(cidx + 1) * 512]),
                      reads=[xbuf], writes=[stats.sub(cidx)])
            kb.op(dve, lambda e: e.bn_aggr(out=mv[:, 0:2], in_=stats[:, 0:nchunk, :].rearrange("p a b -> p (a b)")),
                  reads=[stats], writes=[mv])
            kb.op(pool, lambda e: e.tensor_tensor(out=mv[:, 2:3], in0=mv[:, 1:2], in1=cst[:, 1:2], op=ALU.add),
                  reads=[mv, cst], writes=[mv])
            kb.op(pool, lambda e: e.tensor_tensor(out=rstd[:, 0:1], in0=mv[:, 2:3], in1=cst[:, 0:1], op=ALU.pow),
                  reads=[mv, cst], writes=[rstd])
            if nmr is not None:
                kb.op(dve, lambda e: e.scalar_tensor_tensor(out=rstd[:, 1:2], in0=mv[:, 0:1], scalar=-1.0, in1=rstd[:, 0:1],
                                                            op0=ALU.mult, op1=ALU.mult), reads=[mv, rstd], writes=[rstd])

        def make_hT(xt, hT, xn, stats, mv, rstd, mi_shift, mi_scale, tok0=0):
            make_hT_A(xt, xn, stats, mv, rstd)
            make_hT_B(hT, xn, mi_shift, mi_scale, tok0)

        def make_hT_A(xt, xn, stats, mv, rstd):
            ln_stats(xt[:, :], xt, 1024, stats, mv, rstd, nmr=True)
            kb.op(act, lambda e: e.activation(out=xn[:], in_=xt[:, :], func=AF.Identity, bias=rstd[:, 1:2], scale=rstd[:, 0:1]),
                  reads=[xt, rstd], writes=[xn])

        def make_hT_B(hT, xn, mi_shift, mi_scale, tok0=0):
            for kc in range(8):
                kb.op(pe, lambda e, kc=kc: e.transpose(out=PT[:, kc * 128:(kc + 1) * 128], in_=xn[:, kc * 128:(kc + 1) * 128],
                                                      identity=identb[:]), reads=[xn, identb], writes=[PT], sig=(kc == 7))
            for kc in range(8):
                kb.op(act, lambda e, kc=kc: e.activation(out=hT[:, kc, tok0:tok0 + 128], in_=PT[:, kc * 128:(kc + 1) * 128],
                                                         func=AF.Identity, bias=modc[:, mi_shift, kc:kc + 1],
                                                         scale=modc[:, mi_scale, kc:kc + 1]),
                      reads=[PT, modc], writes=[hT.sub((kc, tok0))])

        def load_weight_block(dst, dst_ap_fn, src_ap, stg_buf, nk, scale_bc=None, scale_cols=None):
            half = nk // 2
            kb.dma(stg_buf[:, 0:half, :], src_ap[:, 0:half, :], writes=[stg_buf])
            kb.dma(stg_buf[:, half:nk, :], src_ap[:, half:nk, :], writes=[stg_buf])
            if scale_bc is None:
                kb.op(act, lambda e: e.activation(out=dst_ap_fn(slice(0, half)), in_=stg_buf[:, 0:half, :], func=AF.Copy),
                      reads=[stg_buf], writes=[dst])
                kb.op(pool, lambda e: e.tensor_copy(out=dst_ap_fn(slice(half, nk)), in_=stg_buf[:, half:nk, :]),
                      reads=[stg_buf], writes=[dst])
            else:
                for k in range(nk):
                    eng = dve if k % 2 == 0 else pool
                    kb.op(eng, lambda e, k=k: e.tensor_tensor(out=dst_ap_fn(k), in0=stg_buf[:, k, :],
                                                              in1=scale_bc[:, scale_cols], op=ALU.mult),
                          reads=[stg_buf, scale_bc], writes=[dst])

        win_v = win_d.rearrange("(kc p) n -> p kc n", p=128)
        with ExitStack() as es1:
            Win1 = kb.sb("Win1", [128, 8, 1040], BF16, es1)
            kb.dma(Win1[:, :, 0:512], win_v[:, :, 1024:1536], writes=[Win1], eng=pool)
            kb.dma(Win1[:, :, 512:1024], win_v[:, :, 1536:2048], writes=[Win1], eng=pool)
            kb.dma(Win1[:, :, 1024:1040], win_v[:, :, 2560:2576], writes=[Win1], eng=pool)
            xs = [kb.sb(f"xs{i}", [128, D], F32, es1) for i in range(2)]
            xn = kb.sb("xn", [128, D], BF16, es1)
            hT = kb.sb("hT", [128, 8, 128], BF16, es1)
            stats = kb.sb("stats", [128, 2, 6], F32, es1)
            mv = kb.sb("mv", [128, 4], F32, es1)
            rstd = kb.sb("rstd", [128, 2], F32, es1)
            raw = [kb.sb(f"raw{i}", [128, 4, 130], F32, es1) for i in range(3)]
            cvt = kb.sb("cvt", [128, 4, 128], F32, es1)

            def conv_finish(gc_prev, rb):
                for cc in range(4):
                    kb.op(dve, lambda e, cc=cc: e.tensor_scalar(out=cvt[:, cc, :], in0=rb[:, cc, 0:128],
                                                                scalar1=smallc[:, 12 + 0 * 4 + cc:13 + 0 * 4 + cc], scalar2=None,
                                                                op0=ALU.mult), reads=[rb, smallc], writes=[cvt.sub(cc)])
                    kb.op(dve, lambda e, cc=cc: e.scalar_tensor_tensor(out=cvt[:, cc, :], in0=rb[:, cc, 1:129],
                                                                       scalar=smallc[:, 12 + 1 * 4 + cc:13 + 1 * 4 + cc],
                                                                       in1=cvt[:, cc, :], op0=ALU.mult, op1=ALU.add),
                          reads=[rb, smallc, cvt.sub(cc)], writes=[cvt.sub(cc)])
                    kb.op(dve, lambda e, cc=cc: e.scalar_tensor_tensor(out=cvt[:, cc, :], in0=rb[:, cc, 2:130],
                                                                       scalar=smallc[:, 12 + 2 * 4 + cc:13 + 2 * 4 + cc],
                                                                       in1=cvt[:, cc, :], op0=ALU.mult, op1=ALU.add),
                          reads=[rb, smallc, cvt.sub(cc)], writes=[cvt.sub(cc)])
                t0 = gc_prev * 128
                kb.op(act, lambda e: e.activation(out=kT[:, :, t0:t0 + 128], in_=cvt[:, 2:4, :], func=AF.Silu),
                      reads=[cvt.sub(2), cvt.sub(3)], writes=[kT])
                if gc_prev >= NCT:
                    l0 = (gc_prev - NCT) * 128
                    kb.op(act, lambda e: e.activation(out=qT[:, :, l0:l0 + 128], in_=cvt[:, 0:2, :], func=AF.Silu),
                          reads=[cvt.sub(0), cvt.sub(1)], writes=[qT])

            xn_1 = [xn, kb.sb("xn_b", [128, D], BF16, es1)]
            hT_1 = [hT, kb.sb("hT_b", [128, 8, 128], BF16, es1)]
            stats_1 = [stats, kb.sb("stats_b", [128, 2, 6], F32, es1)]
            mv_1 = [mv, kb.sb("mv_b", [128, 4], F32, es1)]
            rstd_1 = [rstd, kb.sb("rstd_b", [128, 2], F32, es1)]

            def tile1(gc):
                sl = gc % 2
                xn, hT, stats, mv, rstd = xn_1[sl], hT_1[sl], stats_1[sl], mv_1[sl], rstd_1[sl]
                PBq, PBv, PBg = PB[3 * sl], PB[3 * sl + 1], PB[3 * sl + 2]
                is_ctx = gc < NCT
                src = ctx_d[gc * 128:(gc + 1) * 128, :] if is_ctx else x_d[(gc - NCT) * 128:(gc - NCT + 1) * 128, :]
                xt = xs[gc % 2]
                kb.dma(xt[:, 0:512], src[:, 0:512], writes=[xt])
                kb.dma(xt[:, 512:1024], src[:, 512:1024], writes=[xt])
                make_hT(xt, hT, xn, stats, mv, rstd, 4 if is_ctx else 0, 5 if is_ctx else 1)
                for cc in range(4):
                    for kc in range(8):
                        kb.op(pe, lambda e, cc=cc, kc=kc: e.matmul(PBq[:, cc * 128:(cc + 1) * 128],
                                                                   lhsT=Win1[:, kc, cc * 128:(cc + 1) * 128], rhs=hT[:, kc, :],
                                                                   start=(kc == 0), stop=(kc == 7)),
                              reads=[Win1, hT.sub((kc, 0))], writes=[PBq], sig=(kc == 7 and cc == 3))
                for kc in range(8):
                    kb.op(pe, lambda e, kc=kc: e.matmul(PBv[:, :], lhsT=hT[:, kc, :], rhs=Win1[:, kc, 512:1024],
                                                        start=(kc == 0), stop=(kc == 7)),
                          reads=[Win1, hT.sub((kc, 0))], writes=[PBv], sig=(kc == 7))
                for kc in range(8):
                    kb.op(pe, lambda e, kc=kc: e.matmul(PBg[:, 0:16], lhsT=hT[:, kc, :], rhs=Win1[:, kc, 1024:1040],
                                                        start=(kc == 0), stop=(kc == 7)),
                          reads=[Win1, hT.sub((kc, 0))], writes=[PBg], sig=(kc == 7))
                rb = raw[gc % 3]
                first = gc in (0, NCT)
                last = gc in (NCT - 1, NG - 1)
                kb.op(act, lambda e: e.activation(out=rb[:, :, 1:129], in_=PBq[:, :].rearrange("p (c t) -> p c t", c=4), func=AF.Copy),
                      reads=[PBq], writes=[rb])
                kb.op(dve, lambda e: e.tensor_copy(out=vaug[:, gc, :, 0:128], in_=PBv[:, :].rearrange("p (h v) -> p h v", h=4)),
                      reads=[PBv], writes=[vaug])
                kb.op(dve, lambda e: e.tensor_tensor(out=Gt[:, gc, :], in0=PBg[:, 0:16], in1=bgb[:], op=ALU.add),
                      reads=[PBg, bgb], writes=[Gt])
                if first:
                    kb.op(pool, lambda e: e.memset(rb[:, :, 0:1], 0.0), writes=[rb])
                else:
                    rprev = raw[(gc - 1) % 3]
                    kb.op(pool, lambda e: e.tensor_copy(out=rb[:, :, 0:1], in_=rprev[:, :, 128:129]), reads=[rprev], writes=[rb])
                    kb.op(pool, lambda e: e.tensor_copy(out=rprev[:, :, 129:130], in_=rb[:, :, 1:2]), reads=[rb], writes=[rprev])
                    conv_finish(gc - 1, rprev)
                if last:
                    kb.op(pool, lambda e: e.memset(rb[:, :, 129:130], 0.0), writes=[rb])
                    conv_finish(gc, rb)
            lists1 = [kb.record(tile1, gc) for gc in range(NG)]
            interleave(lists1, (len(lists1[2]) * 11) // 20)
            kb.barrier()

        if dbg:
            dbg_out("kT", kT, kT[:], [128, 2, NG * 128], BF16)
            dbg_out("qT", qT, qT[:], [128, 2, S], BF16)
            dbg_out("vaug", vaug, vaug[:], [128, NG, 4, 130], BF16)
            dbg_out("Gt", Gt, Gt[:], [128, NG, 16])
        if stage == 1:
            kb.barrier()
            raise _Stop((nc, dbg_outs))

        with ExitStack() as esg:
            NF = NG * 8
            Gv = Gt[:].rearrange("p c (d t h) -> p c d t h", d=2, t=2, h=4)
            LF = kb.sb("LF", [128, NG, 2, 4], F32, esg)
            T1 = kb.sb("T1", [128, NG, 2, 4], F32, esg)
            T2 = kb.sb("T2", [128, NG, 2, 4], F32, esg)
            Bc = kb.sb("Bc", [128, NG, 2, 4], F32, esg)
            Aa = kb.sb("Aa", [128, NG, 2, 4], F32, esg)
            Mcb = kb.sb("Mcb", [128, NG, 2, 4], F32, esg)
            rowA = kb.sb("rowA", [1, NG, 2, 4], F32, esg)
            rowB = kb.sb("rowB", [1, NG, 2, 4], F32, esg)
            rowM = kb.sb("rowM", [1, NG, 2, 4], F32, esg)
            rowm0 = kb.sb("rowm0", [1, NG, 2, 4], F32, esg)
            colmax = kb.sb("colmax", [128, 3], F32, esg)
            fl = lambda b: b[:].rearrange("p c d h -> p (c d h)")
            FG = Gv[:, :, :, 1, :]
            IG = Gv[:, :, :, 0, :]
            kb.op(dve, lambda e: e.tensor_scalar(out=T1[:], in0=FG, scalar1=-1.0, scalar2=None, op0=ALU.mult), reads=[Gt], writes=[T1])
            kb.op(dve, lambda e: e.tensor_tensor(out=T1[:], in0=T1[:], in1=FG, op=ALU.max), reads=[Gt, T1], writes=[T1])
            kb.op(act, lambda e: e.activation(out=T2[:], in_=T1[:], func=AF.Exp, scale=-1.0), reads=[T1], writes=[T2])
            kb.op(act, lambda e: e.activation(out=T2[:], in_=T2[:], func=AF.Ln, bias=cst[:, 3:4], scale=1.0), reads=[T2, cst], writes=[T2])
            kb.op(dve, lambda e: e.tensor_scalar(out=T1[:], in0=FG, scalar1=0.0, scalar2=None, op0=ALU.min), reads=[Gt], writes=[T1])
            kb.op(dve, lambda e: e.tensor_tensor(out=LF[:], in0=T1[:], in1=T2[:], op=ALU.subtract), reads=[T1, T2], writes=[LF])
            PBv = PB[0][:, 0:NF].rearrange("p (c d h) -> p c d h", c=NG, d=2, h=4)
            kb.op(pe, lambda e: e.matmul(PB[0][:, 0:NF], lhsT=LT[:], rhs=fl(LF), start=True, stop=True),
                  reads=[LT, LF], writes=[PB[0]])
            PBv1 = PB[1][:, 0:NF].rearrange("p (c d h) -> p c d h", c=NG, d=2, h=4)
            kb.op(pe, lambda e: e.matmul(PB[1][:, 0:NF], lhsT=UT[:], rhs=fl(LF), start=True, stop=True),
                  reads=[UT, LF], writes=[PB[1]])
            kb.op(dve, lambda e: e.tensor_copy(out=Bc[:, :, 0, :], in_=PBv[:, :, 0, :]), reads=[PB[0]], writes=[Bc])
            kb.op(dve, lambda e: e.tensor_copy(out=Bc[:, :, 1, :], in_=PBv1[:, :, 1, :]), reads=[PB[1]], writes=[Bc])
            kb.op(dve, lambda e: e.tensor_tensor(out=Aa[:], in0=IG, in1=Bc[:], op=ALU.subtract), reads=[Gt, Bc], writes=[Aa])
            AaF = fl(Aa)
            segs = [(0, 128), (128, 128), (256, NF - 256)]
            for si, (o, n) in enumerate(segs):
                kb.op(pe, lambda e, o=o, n=n: e.transpose(out=PB[2][0:n, 0:128], in_=AaF[:, o:o + n], identity=identf[:]),
                      reads=[Aa, identf], writes=[PB[2]])
                kb.op(dve, lambda e, si=si, n=n: e.reduce_max(out=colmax[0:n, si:si + 1], in_=PB[2][0:n, 0:128], axis=mybir.AxisListType.X),
                      reads=[PB[2]], writes=[colmax])
                kb.op(pe, lambda e, si=si, n=n, o=o: e.matmul(PB[3][0:1, o:o + n], lhsT=colmax[0:n, si:si + 1], rhs=identf[0:n, 0:n],
                                                               start=True, stop=True), reads=[colmax, identf], writes=[PB[3]])
            kb.op(dve, lambda e: e.tensor_copy(out=fl(rowA), in_=PB[3][0:1, 0:NF]), reads=[PB[3]], writes=[rowA])
            kb.op(pe, lambda e: e.matmul(PB[4][0:1, 0:NF], lhsT=onesf[:, 0:1], rhs=fl(LF), start=True, stop=True),
                  reads=[onesf, LF], writes=[PB[4]])
            kb.op(dve, lambda e: e.tensor_copy(out=fl(rowB), in_=PB[4][0:1, 0:NF]), reads=[PB[4]], writes=[rowB])
            order = [list(range(NG)), [1, 0] + list(range(NG - 1, NCT - 1, -1))]
            for d in range(2):
                g0 = order[d][0]
                kb.op(dve, lambda e, d=d, g0=g0: e.memset(rowm0[0:1, g0, d, :], 0.0), writes=[rowm0])
            for j in range(NG):
                for d in range(2):
                    gcur = order[d][j]
                    kb.op(dve, lambda e, d=d, gcur=gcur: e.tensor_tensor(out=rowM[0:1, gcur, d, :], in0=rowm0[0:1, gcur, d, :],
                                                                         in1=rowA[0:1, gcur, d, :], op=ALU.max),
                          reads=[rowm0, rowA], writes=[rowM])
                    if j + 1 < NG:
                        gn = order[d][j + 1]
                        kb.op(dve, lambda e, d=d, gcur=gcur, gn=gn: e.tensor_tensor(out=rowm0[0:1, gn, d, :], in0=rowM[0:1, gcur, d, :],
                                                                                   in1=rowB[0:1, gcur, d, :], op=ALU.add),
                              reads=[rowM, rowB], writes=[rowm0])
            kb.op(pe, lambda e: e.matmul(PB[5][:, 0:NF], lhsT=onesf[0:1, :], rhs=fl(rowM), start=True, stop=True),
                  reads=[onesf, rowM], writes=[PB[5]])
            kb.op(pe, lambda e: e.matmul(PB[6][:, 0:NF], lhsT=onesf[0:1, :], rhs=fl(rowm0), start=True, stop=True),
                  reads=[onesf, rowm0], writes=[PB[6]])
            kb.op(dve, lambda e: e.tensor_copy(out=fl(Mcb), in_=PB[5][:, 0:NF]), reads=[PB[5]], writes=[Mcb])
            kb.op(dve, lambda e: e.tensor_tensor(out=T1[:], in0=Aa[:], in1=Mcb[:], op=ALU.subtract), reads=[Aa, Mcb], writes=[T1])
            kb.op(act, lambda e: e.activation(out=WS[:].rearrange("p c j -> p (c j)"), in_=fl(T1), func=AF.Exp), reads=[T1], writes=[WS])
            kb.op(dve, lambda e: e.tensor_tensor(out=fl(T2), in0=PB[6][:, 0:NF], in1=fl(Mcb), op=ALU.subtract), reads=[PB[6], Mcb], writes=[T2])
            kb.op(act, lambda e: e.activation(out=CW[:].rearrange("p c j -> p (c j)"), in_=fl(T2), func=AF.Exp), reads=[T2], writes=[CW])
            kb.op(dve, lambda e: e.tensor_tensor(out=T1[:], in0=Bc[:], in1=Mcb[:], op=ALU.add), reads=[Bc, Mcb], writes=[T1])
            kb.op(act, lambda e: e.activation(out=LB[:].rearrange("p c j -> p (c j)"), in_=fl(T1), func=AF.Exp, bias=cst[:, 2:3], scale=-1.0),
                  reads=[T1, cst], writes=[LB])
            kb.barrier()

        if dbg:
            dbg_out("WS", WS, WS[:], [128, NG, 8])
            dbg_out("CW", CW, CW[:], [128, NG, 8])
            dbg_out("LB", LB, LB[:], [128, NG, 8])
        if stage == 2:
            kb.barrier()
            raise _Stop((nc, dbg_outs))

        with ExitStack() as ess:
            Cc = [kb.sb(f"Cc{d}", [128, 2, 130], F32, ess) for d in range(2)]
            kp = [kb.sb(f"kp{i}", [128, 4, 64], BF16, ess) for i in range(2)]
            c0b = [kb.sb(f"c0b{i}", [128, 2, 130], BF16, ess) for i in range(2)]
            for d in range(2):
                kb.op(pool, lambda e, d=d: e.memset(Cc[d][:], 0.0), writes=[Cc[d]])
            units = [(j, d) for j in range(NG) for d in range(2)]

            def stepA(u):
                j, d = units[u]
                if j == NG - 1:
                    return
                gcur = order[d][j]
                kpb = kp[u % 2]
                for pr in range(2):
                    kb.op(pe, lambda e, pr=pr: e.transpose(out=PT[:, pr * 128:(pr + 1) * 128],
                                                           in_=kT[:, pr, gcur * 128:(gcur + 1) * 128], identity=identb[:]),
                          reads=[kT, identb], writes=[PT], sig=(pr == 1))
                for h in range(4):
                    kb.op(act, lambda e, h=h: e.activation(
                        out=kpb[:, h, :], in_=PT[:, h * 64:(h + 1) * 64], func=AF.Identity,
                        scale=WS[:, gcur, d * 4 + h:d * 4 + h + 1]), reads=[PT, WS], writes=[kpb.sub(h)])
                bank = PB[(u % 2) * 2:(u % 2) * 2 + 2]
                for h in range(4):
                    bk = bank[h // 2]
                    kb.op(pe, lambda e, h=h, bk=bk: e.matmul(
                        bk[:, (h % 2) * 130:(h % 2) * 130 + 130], lhsT=kpb[:, (h // 2) * 2:(h // 2) * 2 + 2, :].rearrange("p a b -> p (a b)"),
                        rhs=vaug[:, gcur, h, :], start=True, stop=True), reads=[kpb.sub((h // 2) * 2), kpb.sub((h // 2) * 2 + 1), vaug], writes=[bk])

            def stepB(u):
                j, d = units[u]
                gcur = order[d][j]
                C = Cc[d]
                if j > 0:
                    for h in range(4):
                        p0 = (h % 2) * 64
                        kb.op(dve, lambda e, h=h, p0=p0: e.tensor_scalar(
                            out=C[p0:p0 + 64, h // 2, :], in0=C[p0:p0 + 64, h // 2, :],
                            scalar1=CW[p0:p0 + 64, gcur, d * 4 + h:d * 4 + h + 1], scalar2=None, op0=ALU.mult),
                            reads=[C.sub(h), CW], writes=[C.sub(h)])
                if gcur >= NCT:
                    kb.op(act, lambda e: e.activation(out=ST[:, gcur - NCT, d, :, :], in_=C[:], func=AF.Copy),
                          reads=[C], writes=[ST])
                if j == NG - 1:
                    return
                bank = PB[(u % 2) * 2:(u % 2) * 2 + 2]
                for h in range(4):
                    p0 = (h % 2) * 64
                    bk = bank[h // 2]
                    kb.op(dve, lambda e, h=h, p0=p0, bk=bk: e.tensor_tensor(
                        out=C[p0:p0 + 64, h // 2, :], in0=C[p0:p0 + 64, h // 2, :],
                        in1=bk[p0:p0 + 64, (h % 2) * 130:(h % 2) * 130 + 130], op=ALU.add), reads=[C.sub(h), bk], writes=[C.sub(h)])

            stepA(0)
            for u in range(len(units)):
                if u + 1 < len(units):
                    stepA(u + 1)
                stepB(u)
            kb.barrier()

        if dbg:
            dbg_out("ST", ST, ST[:], [128, NT, 2, 2, 130], BF16)
        if stage == 3:
            kb.barrier()
            raise _Stop((nc, dbg_outs))

        wout_v = wout_d.rearrange("(kc p) n -> p kc n", p=128)
        with ExitStack() as es2:
            Win2 = kb.sb("Win2", [128, 8, 1536], BF16, es2)
            Wo = kb.sb("Wo", [128, 8, D], BF16, es2)
            with ExitStack() as esw:
                wstg = [kb.sb(f"wstg2{i}", [128, 8, 256], F32, esw) for i in range(2)]
                g1bc = kb.sb("g1bc", [128, D], F32, esw)
                kb.dma(g1bc[:], g1row[:, :], writes=[g1bc])
                nb = 0
                kb.dma(Win2[:, :, 0:512], win_v[:, :, 0:512], writes=[Win2], eng=pool)
                kb.dma(Win2[:, :, 512:1024], win_v[:, :, 512:1024], writes=[Win2], eng=pool)
                kb.dma(Win2[:, :, 1024:1536], win_v[:, :, 2048:2560], writes=[Win2], eng=pool)
                for c0 in (0, 256, 512, 768):
                    load_weight_block(Wo, lambda k, c0=c0: Wo[:, k, c0:c0 + 256], wout_v[:, :, c0:c0 + 256], wstg[nb % 2], 8,
                                      scale_bc=g1bc, scale_cols=slice(c0, c0 + 256))
                    nb += 1
                kb.barrier()
            xs = [kb.sb(f"x2s{i}", [128, D], F32, es2) for i in range(2)]
            def two(name, shape, dt):
                return [kb.sb(f"{name}_{k}", shape, dt, es2) for k in range(2)]
            xn_ = two("xn2", [128, D], BF16)
            hT_ = two("hT2", [128, 8, 128], BF16)
            stats_ = two("stats2", [128, 4, 6], F32)
            mv_ = two("mv2", [128, 4], F32)
            rstd_ = two("rstd2", [128, 2], F32)
            mv4_ = two("mv4", [128, 4, 4], F32)
            rs4_ = two("rs4", [128, 4], F32)
            uT_ = two("uT", [128, 4, 128], BF16)
            sgo_ = two("sgo", [128, 4, 128], BF16)
            vn_ = two("vn", [128, 512], BF16)
            tA_ = two("tA", [128, 4, 128], F32)
            yT_ = two("yT", [128, 8, 128], BF16)
            sT_ = two("sT", [128, 8, 128], BF16)
            WM_ = two("WM", [128, 8, 128], BF16)
            dn_ = two("dn", [128, 3, 8], F32)
            hs_ = two("hs", [128, 4, 128], F32)
            hn_ = two("hn", [128, 4, 128], BF16)
            Q2_ = [[kb.sb(f"Q2{k}_{i}", [128, 2, 128], BF16, es2) for i in range(2)] for k in range(2)]
            hg = kb.sb("hg", [128, 4], F32, es2)
            for k in range(2):
                for pr in range(2):
                    kb.op(pool, lambda e, k=k, pr=pr: e.memset(Q2_[k][pr][:], 0.0), writes=[Q2_[k][pr]])
            kb.op(dve, lambda e: e.tensor_copy(out=hg[:], in_=smallc[:, 8:12]), reads=[smallc], writes=[hg])

            def tile2(i):
                sl = i % 2
                xn, hT, stats, mv, rstd, mv4, rs4 = xn_[sl], hT_[sl], stats_[sl], mv_[sl], rstd_[sl], mv4_[sl], rs4_[sl]
                uT, sgo, vn, tA, yT, sT, dn, hs, hn, Q2, WM = uT_[sl], sgo_[sl], vn_[sl], tA_[sl], yT_[sl], sT_[sl], dn_[sl], hs_[sl], hn_[sl], Q2_[sl], WM_[sl]
                gc = i + NCT
                t0k = gc * 128
                t0q = i * 128
                xt = xs[i % 2]
                kb.dma(xt[:, 0:512], x_d[i * 128:(i + 1) * 128, 0:512], writes=[xt])
                kb.dma(xt[:, 512:1024], x_d[i * 128:(i + 1) * 128, 512:1024], writes=[xt])
                make_hT(xt, hT, xn, stats, mv, rstd, 0, 1)
                for (bank, c0) in ((PB[0], 0), (PB[1], 1024)):
                    for cc in range(4):
                        for kc in range(8):
                            kb.op(pe, lambda e, cc=cc, kc=kc, bank=bank, c0=c0: e.matmul(
                                bank[:, cc * 128:(cc + 1) * 128], lhsT=Win2[:, kc, c0 + cc * 128:c0 + (cc + 1) * 128], rhs=hT[:, kc, :],
                                start=(kc == 0), stop=(kc == 7)), reads=[Win2, hT.sub((kc, 0))], writes=[bank], sig=(kc == 7 and cc == 3))
                for kc in range(8):
                    kb.op(pe, lambda e, kc=kc: e.matmul(PB[2][:, :], lhsT=hT[:, kc, :], rhs=Win2[:, kc, 512:1024],
                                                        start=(kc == 0), stop=(kc == 7)), reads=[Win2, hT.sub((kc, 0))], writes=[PB[2]], sig=(kc == 7))
                kb.op(act, lambda e: e.activation(out=uT[:].rearrange("p c t -> p (c t)"), in_=PB[0][:, :], func=AF.Copy),
                      reads=[PB[0]], writes=[uT])
                kb.op(act, lambda e: e.activation(out=sgo[:].rearrange("p c t -> p (c t)"), in_=PB[1][:, :], func=AF.Sigmoid),
                      reads=[PB[1]], writes=[sgo])
                ln_stats(PB[2][:, :], PB[2], 512, stats, mv, rstd)
                kb.op(dve, lambda e: e.tensor_scalar(out=vn[:], in0=PB[2][:, :], scalar1=mv[:, 0:1], scalar2=rstd[:, 0:1],
                                                     op0=ALU.subtract, op1=ALU.mult), reads=[PB[2], mv, rstd], writes=[vn])
                for g in range(4):
                    kb.op(pe, lambda e, g=g: e.matmul(PB[3][:, g * 128:(g + 1) * 128], lhsT=vn[:, g * 128:(g + 1) * 128], rhs=wsT[:, g, :],
                                                      start=True, stop=True), reads=[vn, wsT], writes=[PB[3]], sig=(g == 3))
                for g in range(4):
                    kb.op(dve, lambda e, g=g: e.scalar_tensor_tensor(out=tA[:, g, :], in0=PB[3][:, g * 128:(g + 1) * 128],
                                                                     scalar=smallc[:, g:g + 1], in1=BiasA[:, g, :], op0=ALU.mult, op1=ALU.add),
                          reads=[PB[3], smallc, BiasA], writes=[tA.sub(g)])
                kb.op(pool, lambda e: e.tensor_tensor(out=yT[:, 0:4, :], in0=tA[:], in1=uT[:], op=ALU.mult), reads=[tA, uT], writes=[yT.sub('A')])
                for d in range(2):
                    msk = LT if d == 0 else UT
                    for h in range(4):
                        kb.op(pool, lambda e, d=d, h=h, msk=msk: e.tensor_scalar(
                            out=WM[:, d * 4 + h, :], in0=msk[:], scalar1=WS[:, gc, d * 4 + h:d * 4 + h + 1], scalar2=1.0,
                            op0=ALU.mult, op1=ALU.mult), reads=[msk, WS], writes=[WM.sub((d, h))])
                for pr in range(2):
                    for hh in range(2):
                        kb.op(pool, lambda e, pr=pr, hh=hh: e.tensor_copy(out=Q2[pr][hh * 64:(hh + 1) * 64, hh, :],
                                                                          in_=qT[hh * 64:(hh + 1) * 64, pr, t0q:t0q + 128]),
                              reads=[qT], writes=[Q2[pr].sub(hh)])
                for pr in range(2):
                    kb.op(pe, lambda e, pr=pr: e.matmul(PB[4][:, pr * 256:(pr + 1) * 256], lhsT=kT[:, pr, t0k:t0k + 128],
                                                        rhs=Q2[pr][:].rearrange("p a t -> p (a t)"), start=True, stop=True),
                          reads=[kT, Q2[pr]], writes=[PB[4]], sig=(pr == 1))
                for d in range(2):
                    kb.op(dve, lambda e, d=d: e.tensor_tensor(out=sT[:, d * 4:(d + 1) * 4, :].rearrange("p h t -> p (h t)"), in0=PB[4][:, :],
                                                              in1=WM[:, d * 4:(d + 1) * 4, :].rearrange("p h t -> p (h t)"), op=ALU.mult),
                          reads=[PB[4]] + [WM.sub((d, hh_)) for hh_ in range(4)], writes=[sT.sub(d)])
                for d in range(2):
                    bank = PB[5 + d]
                    for h in range(4):
                        kb.op(pe, lambda e, d=d, h=h, bank=bank: e.matmul(bank[:, h * 128:(h + 1) * 128], lhsT=sT[:, d * 4 + h, :],
                                                                          rhs=vaug[:, gc, h, 0:128], start=True, stop=False),
                              reads=[sT.sub(d), vaug], writes=[bank], sig=False)
                        kb.op(pe, lambda e, d=d, h=h, bank=bank: e.matmul(
                            bank[:, h * 128:(h + 1) * 128], lhsT=Q2[h // 2][:, h % 2, :],
                            rhs=ST[:, i, d, h // 2, 0:128], start=False, stop=True),
                            reads=[Q2[h // 2], ST], writes=[bank], sig=(h == 3))
                for d in range(2):
                    for h in range(4):
                        jn = d * 4 + h
                        kb.op(pe, lambda e, d=d, h=h, jn=jn: e.matmul(PB[3][:, 2 * jn:2 * jn + 2], lhsT=sT[:, jn, :],
                                                                      rhs=vaug[:, gc, h, 128:130], start=True, stop=False),
                              reads=[sT.sub(d), vaug], writes=[PB[3]], sig=False)
                        kb.op(pe, lambda e, d=d, h=h, jn=jn: e.matmul(
                            PB[3][:, 2 * jn:2 * jn + 2], lhsT=Q2[h // 2][:, h % 2, :],
                            rhs=ST[:, i, d, h // 2, 128:130], start=False, stop=True),
                            reads=[Q2[h // 2], ST], writes=[PB[3]], sig=(jn == 7))
                den = PB[3][:, 0:16].rearrange("p (j two) -> p j two", two=2)[:, :, 0]
                kb.op(dve, lambda e: e.tensor_scalar(out=dn[:, 0, :], in0=den, scalar1=-1.0, scalar2=None, op0=ALU.mult),
                      reads=[PB[3]], writes=[dn])
                kb.op(dve, lambda e: e.tensor_tensor(out=dn[:, 1, :], in0=dn[:, 0, :], in1=den, op=ALU.max),
                      reads=[PB[3], dn], writes=[dn])
                kb.op(dve, lambda e: e.tensor_tensor(out=dn[:, 0, :], in0=dn[:, 1, :], in1=LB[:, gc, :], op=ALU.max),
                      reads=[dn, LB], writes=[dn])
                kb.op(dve, lambda e: e.reciprocal(out=dn[:, 2, :], in_=dn[:, 0, :]), reads=[dn], writes=[dn])
                for h in range(4):
                    kb.op(act, lambda e, h=h: e.activation(out=hs[:, h, :], in_=PB[5][:, h * 128:(h + 1) * 128], func=AF.Identity,
                                                           scale=dn[:, 2, h:h + 1]), reads=[PB[5], dn], writes=[hs.sub(h)])
                for h in range(4):
                    kb.op(dve, lambda e, h=h: e.scalar_tensor_tensor(out=hs[:, h, :], in0=PB[6][:, h * 128:(h + 1) * 128],
                                                                     scalar=dn[:, 2, 4 + h:5 + h], in1=hs[:, h, :], op0=ALU.mult, op1=ALU.add),
                          reads=[PB[6], dn, hs.sub(h)], writes=[hs.sub(h)])
                for h in range(4):
                    kb.op(dve, lambda e, h=h: e.bn_stats(out=stats[:, h, :], in_=hs[:, h, :]), reads=[hs.sub(h)], writes=[stats.sub(h)])
                for h in range(4):
                    kb.op(dve, lambda e, h=h: e.bn_aggr(out=mv4[:, h, 0:2], in_=stats[:, h, :]), reads=[stats.sub(h)], writes=[mv4.sub(h)])
                kb.op(pool, lambda e: e.tensor_tensor(out=mv4[:, :, 2], in0=mv4[:, :, 1], in1=cst[:, 1:2].to_broadcast([128, 4]), op=ALU.add),
                      reads=[mv4, cst], writes=[mv4])
                kb.op(pool, lambda e: e.tensor_tensor(out=rs4[:], in0=mv4[:, :, 2], in1=cst[:, 0:1].to_broadcast([128, 4]), op=ALU.pow),
                      reads=[mv4, cst], writes=[rs4])
                for h in range(4):
                    kb.op(dve, lambda e, h=h: e.tensor_scalar(out=hn[:, h, :], in0=hs[:, h, :], scalar1=mv4[:, h, 0:1], scalar2=rs4[:, h:h + 1],
                                                              op0=ALU.subtract, op1=ALU.mult), reads=[hs.sub(h), mv4, rs4], writes=[hn.sub(h)])
                for h in range(4):
                    kb.op(pe, lambda e, h=h: e.transpose(out=PT[:, h * 128:(h + 1) * 128], in_=hn[:, h, :], identity=identb[:]),
                          reads=[hn.sub(h), identb], writes=[PT], sig=(h == 3))
                for h in range(4):
                    kb.op(dve, lambda e, h=h: e.scalar_tensor_tensor(out=yT[:, 4 + h, :], in0=PT[:, h * 128:(h + 1) * 128],
                                                                     scalar=hg[:, h:h + 1], in1=sgo[:, h, :], op0=ALU.mult, op1=ALU.mult),
                          reads=[PT, hg, sgo], writes=[yT.sub(4 + h)])
                for half in range(2):
                    for kc in range(8):
                        kb.op(pe, lambda e, half=half, kc=kc: e.matmul(PB[3 + half][:, :], lhsT=yT[:, kc, :],
                                                                       rhs=Wo[:, kc, half * 512:(half + 1) * 512],
                                                                       start=(kc == 0), stop=(kc == 7)),
                              reads=[yT, Wo], writes=[PB[3 + half]], sig=(kc == 7))
                for half in range(2):
                    kb.op(dve, lambda e, half=half: e.scalar_tensor_tensor(out=xt[:, half * 512:(half + 1) * 512],
                                                                           in0=xt[:, half * 512:(half + 1) * 512], scalar=ALPHA,
                                                                           in1=PB[3 + half][:, :], op0=ALU.mult, op1=ALU.add),
                          reads=[xt.sub(half), PB[3 + half]], writes=[xt.sub(half)])
                ln_stats(xt[:, :], xt, 1024, stats, mv, rstd)
                kb.op(dve, lambda e: e.scalar_tensor_tensor(out=xt[:, :], in0=xt[:, :], scalar=mv[:, 0:1], in1=ln1gb[:],
                                                            op0=ALU.subtract, op1=ALU.mult), reads=[xt, mv, ln1gb], writes=[xt])
                kb.op(dve, lambda e: e.scalar_tensor_tensor(out=xt[:, :], in0=xt[:, :], scalar=rstd[:, 0:1], in1=ln1bb[:],
                                                            op0=ALU.mult, op1=ALU.add), reads=[xt, rstd, ln1bb], writes=[xt])
                kb.dma(y_d[i * 128:(i + 1) * 128, :], xt[:, :], reads=[xt])
            lists2 = [kb.record(tile2, i) for i in range(NT)]
            interleave(lists2, (len(lists2[0]) * 11) // 20)
            kb.barrier()

        if stage == 4:
            raise _Stop((nc, dbg_outs))
        es0.close()
        es_p.close()

        GRP = 2
        w1_v = w1_d.rearrange("(kc p) n -> p kc n", p=128)
        w2_v = w2_d.rearrange("(j p) n -> p j n", p=128)
        with ExitStack() as es3:
            W1b = kb.sb("W1b", [128, 8, DFF], BF16, es3)
            W2b = kb.sb("W2b", [128, 32, D], BF16, es3)
            ln2gb = kb.sb("ln2gb", [128, D], F32, es3)
            ln2bb = kb.sb("ln2bb", [128, D], F32, es3)
            b2h = kb.sb("b2h", [1, 2, D], BF16, es3)
            kb.dma(ln2gb[:], ln2g_d[0:1, :].to_broadcast([128, D]), writes=[ln2gb])
            kb.dma(ln2bb[:], ln2b_d[0:1, :].to_broadcast([128, D]), writes=[ln2bb])
            g2bc = kb.sb("g2bc", [128, D], F32, es3)
            kb.dma(g2bc[:], g2row[:, :], writes=[g2bc])
            for blk in range(8):
                kb.dma(W1b[:, :, blk * 512:(blk + 1) * 512], w1_v[:, :, blk * 512:(blk + 1) * 512], writes=[W1b.sub(blk)], sembuf=W1b, eng=pool)
            for blk in range(8):
                kb.dma(W2b[:, blk * 4:(blk + 1) * 4, :], w2_v[:, blk * 4:(blk + 1) * 4, :], writes=[W2b.sub(blk)], sembuf=W2b, eng=pool)
            with ExitStack() as esw:
                b2bc = kb.sb("b2bc", [128, D], F32, esw)
                kb.dma(b2bc[0:1, :], b2_d[0:1, :], writes=[b2bc])
                kb.op(dve, lambda e: e.tensor_copy(out=b2h[0:1, 0, :], in_=b2bc[0:1, :]), reads=[b2bc], writes=[b2h])
                kb.op(dve, lambda e: e.tensor_tensor(out=b2bc[0:1, :], in0=b2bc[0:1, :], in1=b2h[0:1, 0, :], op=ALU.subtract),
                      reads=[b2bc, b2h], writes=[b2bc])
                kb.op(dve, lambda e: e.tensor_copy(out=b2h[0:1, 1, :], in_=b2bc[0:1, :]), reads=[b2bc], writes=[b2h])
                kb.barrier()
            tmpo = [kb.sb(f"tmpo{i}", [128, 512], F32, es3) for i in range(2)]
            xs = [kb.sb(f"x3s{i}", [128, D], F32, es3) for i in range(2 * GRP)]
            xn = kb.sb("xn3", [128, D], BF16, es3)
            h2T_ = [kb.sb(f"h2T{k}", [128, 8, GRP * 128], BF16, es3) for k in range(2)]
            hid = kb.sb("hid", [128, 32, GRP * 128], BF16, es3)
            hidb = [Buf(hid.t, f"hid{j}") for j in range(32)]
            rl = [kb.sb(f"rl{i}", [128, GRP * 128], BF16, es3) for i in range(4)]
            stats_p = kb.sb("stats3p", [128, 2, 6], F32, es3)
            mv_p = kb.sb("mv3p", [128, 4], F32, es3)
            rstd_p = kb.sb("rstd3p", [128, 2], F32, es3)
            stats_e = kb.sb("stats3e", [128, 2, 6], F32, es3)
            mv_e = kb.sb("mv3e", [128, 4], F32, es3)
            rstd_e = kb.sb("rstd3e", [128, 2], F32, es3)
            NGRP = NT // GRP

            xn_3 = [xn] + [kb.sb(f"xn3_{a}", [128, D], BF16, es3) for a in range(1, GRP)]

            def prep3A(gi):
                for a in range(GRP):
                    ti = gi * GRP + a
                    xt = xs[(gi % 2) * GRP + a]
                    kb.dma(xt[:, 0:512], y_d[ti * 128:(ti + 1) * 128, 0:512], writes=[xt.sub(0)], sembuf=xt)
                    kb.dma(xt[:, 512:1024], y_d[ti * 128:(ti + 1) * 128, 512:1024], writes=[xt.sub(1)], sembuf=xt)
                    make_hT_A(xt, xn_3[a], stats_p, mv_p, rstd_p)

            def prep3B(gi):
                for a in range(GRP):
                    make_hT_B(h2T_[gi % 2], xn_3[a], 2, 3, tok0=a * 128)

            def main3(gi):
                h2T = h2T_[gi % 2]
                for j in range(32):
                    bank = PB[j % 2]
                    for kc in range(8):
                        kb.op(pe, lambda e, j=j, kc=kc, bank=bank: e.matmul(bank[:, 0:GRP * 128], lhsT=W1b[:, kc, j * 128:(j + 1) * 128],
                                                                            rhs=h2T[:, kc, :], start=(kc == 0), stop=(kc == 7)),
                              reads=[W1b] + [h2T.sub((kc, a_ * 128)) for a_ in range(GRP)], writes=[bank], sig=(kc == 7))
                    rb = rl[j % 4]
                    kb.op(act, lambda e, j=j, bank=bank, rb=rb: e.activation(out=rb[:], in_=bank[:, 0:GRP * 128], func=AF.Relu,
                                                                             bias=smallc[:, 24 + j:25 + j], scale=1.0),
                          reads=[bank, smallc], writes=[rb])
                    eng = pool if j % 4 == 3 else dve
                    kb.op(eng, lambda e, j=j, rb=rb: e.tensor_tensor(out=hid[:, j, :], in0=rb[:], in1=rb[:], op=ALU.mult),
                          reads=[rb], writes=[hidb[j]])
                for a in range(GRP):
                    ti = gi * GRP + a
                    xt = xs[(gi % 2) * GRP + a]
                    for half in range(2):
                        bank = PB[2 + 2 * (a % 2) + half]
                        for j in range(32):
                            kb.op(pe, lambda e, j=j, a=a, half=half, bank=bank: e.matmul(
                                bank[:, :], lhsT=hid[:, j, a * 128:(a + 1) * 128], rhs=W2b[:, j, half * 512:(half + 1) * 512],
                                start=(j == 0), stop=False), reads=[hidb[j], W2b], writes=[bank], sig=False)
                        for hl in range(2):
                            kb.op(pe, lambda e, hl=hl, half=half, bank=bank: e.matmul(
                                bank[:, :], lhsT=onesb[0:1, :], rhs=b2h[0:1, hl, half * 512:(half + 1) * 512],
                                start=False, stop=(hl == 1)), reads=[onesb, b2h], writes=[bank], sig=(hl == 1))
                        tm = tmpo[half]
                        kb.op(dve, lambda e, half=half, bank=bank, tm=tm: e.tensor_tensor(
                            out=tm[:], in0=bank[:, :], in1=g2bc[:, half * 512:(half + 1) * 512], op=ALU.mult),
                            reads=[bank, g2bc], writes=[tm])
                        kb.op(dve, lambda e, half=half, tm=tm, xt=xt: e.scalar_tensor_tensor(
                            out=xt[:, half * 512:(half + 1) * 512], in0=xt[:, half * 512:(half + 1) * 512], scalar=ALPHA,
                            in1=tm[:], op0=ALU.mult, op1=ALU.add), reads=[xt.sub(half), tm], writes=[xt.sub(half)])
                    ln_stats(xt[:, :], xt, 1024, stats_e, mv_e, rstd_e)
                    kb.op(dve, lambda e, xt=xt: e.scalar_tensor_tensor(out=xt[:, :], in0=xt[:, :], scalar=mv_e[:, 0:1], in1=ln2gb[:],
                                                                       op0=ALU.subtract, op1=ALU.mult), reads=[xt, mv_e, ln2gb], writes=[xt])
                    kb.op(dve, lambda e, xt=xt: e.scalar_tensor_tensor(out=xt[:, :], in0=xt[:, :], scalar=rstd_e[:, 0:1], in1=ln2bb[:],
                                                                       op0=ALU.mult, op1=ALU.add), reads=[xt, rstd_e, ln2bb], writes=[xt])
                    kb.dma(y_d[ti * 128:(ti + 1) * 128, :], xt[:, :], reads=[xt])

            for st in kb.record(prep3A, 0) + kb.record(prep3B, 0):
                st()
            for gi in range(NGRP):
                M = kb.record(main3, gi)
                PA = kb.record(prep3A, gi + 1) if gi + 1 < NGRP else []
                PB_ = kb.record(prep3B, gi + 1) if gi + 1 < NGRP else []
                nM = len(M)
                a0, a1 = int(nM * 0.02), int(nM * 0.35)
                b0, b1 = int(nM * 0.55), int(nM * 0.90)
                ia = ib = 0
                for k, st in enumerate(M):
                    st()
                    if k >= a0 and PA:
                        want = min(len(PA), ((k - a0 + 1) * len(PA)) // max(1, a1 - a0))
                        while ia < want:
                            PA[ia]()
                            ia += 1
                    if k >= b0 and PB_:
                        want = min(len(PB_), ((k - b0 + 1) * len(PB_)) // max(1, b1 - b0))
                        while ib < want:
                            PB_[ib]()
                            ib += 1
                for st in PA[ia:] + PB_[ib:]:
                    st()
            kb.barrier()
    return nc, dbg_outs


_CACHE = {}


def make_in_maps(inputs):
    g = lambda k: np.ascontiguousarray(np.asarray(inputs[k], dtype=np.float32))
    shared = {
        "c_ctx": g("c_ctx").reshape(8, 128),
        "w_ada": g("w_ada")[0],
        "b_ada": g("b_ada")[0].reshape(48, 128),
        "w_in": g("w_in")[0],
        "w_s": g("w_s")[0],
        "b_s": g("b_s")[0].reshape(1, 512),
        "ln_v_g": g("ln_v_g")[0].reshape(4, 128),
        "ln_v_b": g("ln_v_b")[0].reshape(4, 128),
        "conv_qk": g("conv_qk")[0].reshape(12, 128),
        "b_gates": g("b_gates")[0].reshape(1, 16),
        "hn_g": g("hn_g")[0].reshape(4, 128),
        "w_out": g("w_out")[0],
        "ln1_g": g("ln1_g")[0].reshape(1, D),
        "ln1_b": g("ln1_b")[0].reshape(1, D),
        "w1": g("w1")[0],
        "b1": g("b1")[0].reshape(32, 128),
        "w2": g("w2")[0],
        "b2": g("b2")[0].reshape(1, D),
        "ln2_g": g("ln2_g")[0].reshape(1, D),
        "ln2_b": g("ln2_b")[0].reshape(1, D),
    }
    x, c, ctx = g("x"), g("c"), g("ctx")
    maps = []
    for b in range(x.shape[0]):
        m = dict(shared)
        m["x"] = x[b]
        m["c"] = c[b].reshape(8, 128)
        m["ctx"] = ctx[b]
        maps.append(m)
    return maps


def kernel(**inputs):
    if "nc" not in _CACHE:
        _CACHE["nc"] = build_program(False)[0]
    nc = _CACHE["nc"]
    maps = make_in_maps(inputs)
    n = len(maps)
    res = run_bass_kernel_spmd(nc, maps, core_ids=list(range(n)))
    out = np.stack([np.asarray(r["y"], dtype=np.float32) for r in res.results], axis=0)
    return out
```

```python
import math
from contextlib import ExitStack
import numpy as np
import concourse.bass as bass
import concourse.mybir as mybir
from concourse.bass_utils import run_bass_kernel_spmd

F32 = mybir.dt.float32
BF16 = mybir.dt.bfloat16
AF = mybir.ActivationFunctionType
ALU = mybir.AluOpType

D = 1024
S = 4096
CTX = 256
NT = S // 128
NCT = CTX // 128
NG = NT + NCT
DIN = 2576
DFF = 4096
ALPHA = 2.0 ** 0.25
EPS = 1e-5
SEM_LIMIT = 3000
SKEW1 = 0.55
MERGE2 = False
SKEW2 = 0.55


class Tok:
    __slots__ = ("sem", "val", "key")

    def __init__(self, sem, val, key):
        self.sem, self.val, self.key = sem, val, key


class Buf:
    def __init__(self, t, name, parent=None):
        self.t = t
        self.name = name
        self.w = None
        self.r = {}
        self.dsem = None
        self.dcount = 0
        self.parent = parent
        self.children = {}

    def sub(self, key):
        c = self.children.get(key)
        if c is None:
            c = Buf(self.t, f"{self.name}.{key}", parent=self)
            self.children[key] = c
        return c

    def __getitem__(self, idx):
        return self.t[idx]


class Eng:
    def __init__(self, kb, name, h):
        self.kb, self.name, self.h = kb, name, h
        self.seen = {}
        self.epoch = 0
        self.count = 0
        self.sem = kb.new_sem(f"{name}_e0")
        self.pending = False

    def roll(self):
        if self.count >= SEM_LIMIT and not self.pending:
            self.epoch += 1
            self.count = 0
            self.sem = self.kb.new_sem(f"{self.name}_e{self.epoch}")


class KB:
    def __init__(self, nc, es):
        self.nc, self.es = nc, es
        self.nsem = 0
        self.pe = Eng(self, "pe", nc.tensor)
        self.act = Eng(self, "act", nc.scalar)
        self.dve = Eng(self, "dve", nc.vector)
        self.pool = Eng(self, "pool", nc.gpsimd)
        self.sp = Eng(self, "sp", nc.sync)
        self.engs = [self.pe, self.act, self.dve, self.pool, self.sp]
        self.dma_toks = []

    def new_sem(self, name):
        self.nsem += 1
        s = self.es.enter_context(self.nc.semaphore(name))
        return (s, name)

    def sb(self, name, shape, dt, es=None):
        t = (es or self.es).enter_context(self.nc.sbuf_tensor(name, list(shape), dt))
        return Buf(t, name)

    def ps(self, name, shape, dt, es=None):
        t = (es or self.es).enter_context(self.nc.psum_tensor(name, list(shape), dt))
        b = Buf(t, name)
        b.psum = True
        return b

    def wait(self, eng, tok):
        if tok is None:
            return
        if eng.name == "pe" and tok.key.startswith("pe_e"):
            return
        if eng.seen.get(tok.key, 0) >= tok.val:
            return
        eng.h.wait_ge(tok.sem, tok.val)
        eng.seen[tok.key] = tok.val

    def _deps(self, eng, reads, writes):
        for b in reads:
            self.wait(eng, b.w)
            if getattr(b, "psum", False):
                for k_, t_ in b.r.items():
                    if not k_.startswith(eng.name + "_e"):
                        self.wait(eng, t_)
            if b.parent is not None:
                self.wait(eng, b.parent.w)
            for c in b.children.values():
                self.wait(eng, c.w)
        for b in writes:
            self.wait(eng, b.w)
            for t in b.r.values():
                self.wait(eng, t)
            if b.parent is not None:
                self.wait(eng, b.parent.w)
                for t in b.parent.r.values():
                    self.wait(eng, t)
            for c in b.children.values():
                self.wait(eng, c.w)
                for t in c.r.values():
                    self.wait(eng, t)

    def _mark(self, tok, reads, writes):
        for b in reads:
            old = b.r.get(tok.key)
            if old is None or old.val < tok.val:
                b.r[tok.key] = tok
        for b in writes:
            b.w = tok
            b.r = {}

    def record(self, body, *args):
        outer = getattr(self, "rec", None)
        self.rec = []
        body(*args)
        r = self.rec
        self.rec = outer
        return r

    def op(self, eng, fn, reads=(), writes=(), sig=True):
        if getattr(self, "rec", None) is not None:
            self.rec.append(lambda: self._op(eng, fn, reads, writes, sig))
            return None
        return self._op(eng, fn, reads, writes, sig)

    def _op(self, eng, fn, reads=(), writes=(), sig=True):
        if sig:
            eng.roll()
        self._deps(eng, reads, writes)
        inst = fn(eng.h)
        if sig:
            eng.count += 1
            inst.then_inc(eng.sem[0], 1)
            tok = Tok(eng.sem[0], eng.count, eng.sem[1])
            eng.pending = False
        else:
            tok = Tok(eng.sem[0], eng.count + 1, eng.sem[1])
            eng.pending = True
        self._mark(tok, reads, writes)
        return tok

    def dma(self, out_ap, in_ap, reads=(), writes=(), sembuf=None, eng=None, after=()):
        if getattr(self, "rec", None) is not None:
            self.rec.append(lambda: self._dma(out_ap, in_ap, reads, writes, sembuf, eng, after))
            return None
        return self._dma(out_ap, in_ap, reads, writes, sembuf, eng, after)

    def cast_load(self, dst_buf, pieces, depth=2):
        toks = []
        for (out_ap, in_ap, wr) in pieces:
            after = [toks[-depth]] if len(toks) >= depth else []
            toks.append(self._dma(out_ap, in_ap, (), [wr], dst_buf, self.pool, after))
        return toks

    def _dma(self, out_ap, in_ap, reads=(), writes=(), sembuf=None, eng=None, after=()):
        eng = eng or self.sp
        for t_ in after:
            self.wait(eng, t_)
        sb_ = sembuf or (writes[0] if writes else reads[0])
        if sb_.dsem is None:
            sb_.dsem = self.new_sem(f"d_{sb_.name}")
        self._deps(eng, reads, writes)
        inst = eng.h.dma_start(out=out_ap, in_=in_ap)
        inst.then_inc(sb_.dsem[0], 16)
        sb_.dcount += 16
        tok = Tok(sb_.dsem[0], sb_.dcount, sb_.dsem[1])
        self._mark(tok, reads, writes)
        self.dma_toks.append(tok)
        return tok

    def barrier(self):
        toks = []
        for e in self.engs:
            if e.count > 0:
                assert not e.pending
                toks.append(Tok(e.sem[0], e.count, e.sem[1]))
        toks += self.dma_toks
        self.dma_toks = []
        for e in self.engs:
            for t in toks:
                self.wait(e, t)


class _Stop(Exception):
    pass


def interleave(step_lists, H):
    n = len(step_lists)
    T = max(i * H + len(sl) for i, sl in enumerate(step_lists))
    lo = 0
    for t in range(T):
        while lo < n and t - lo * H >= len(step_lists[lo]):
            lo += 1
        i = lo
        while i < n and t - i * H >= 0:
            k = t - i * H
            if k < len(step_lists[i]):
                step_lists[i][k]()
            i += 1


def build_program(dbg=False, stage=99):
    try:
        return _build_program(dbg, stage)
    except _Stop as ex:
        return ex.args[0]


def _build_program(dbg, stage):
    nc = bass.Bass("TRN2", target_bir_lowering=False)

    def din(name, shape):
        return nc.dram_tensor(name, list(shape), F32, kind="ExternalInput").ap()

    x_d = din("x", [S, D])
    c_d = din("c", [8, 128])
    ctx_d = din("ctx", [CTX, D])
    cctx_d = din("c_ctx", [8, 128])
    wada_d = din("w_ada", [D, 6 * D])
    bada_d = din("b_ada", [48, 128])
    win_d = din("w_in", [D, DIN])
    ws_d = din("w_s", [4, 128, 128])
    bs_d = din("b_s", [1, 512])
    lnvg_d = din("ln_v_g", [4, 128])
    lnvb_d = din("ln_v_b", [4, 128])
    conv_d = din("conv_qk", [12, 128])
    bg_d = din("b_gates", [1, 16])
    hng_d = din("hn_g", [4, 128])
    wout_d = din("w_out", [D, D])
    ln1g_d = din("ln1_g", [1, D])
    ln1b_d = din("ln1_b", [1, D])
    w1_d = din("w1", [D, DFF])
    b1_d = din("b1", [32, 128])
    w2_d = din("w2", [DFF, D])
    b2_d = din("b2", [1, D])
    ln2g_d = din("ln2_g", [1, D])
    ln2b_d = din("ln2_b", [1, D])
    y_d = nc.dram_tensor("y", [S, D], F32, kind="ExternalOutput").ap()
    dbg_outs = {}

    with ExitStack() as es:
        kb = KB(nc, es)
        pe, act, dve, pool, sp = kb.pe, kb.act, kb.dve, kb.pool, kb.sp

        PB = [kb.ps(f"pb{i}", [128, 512], F32) for i in range(7)]
        PT = kb.ps("pt", [128, 1024], BF16)

        identf = kb.sb("identf", [128, 128], F32)
        identb = kb.sb("identb", [128, 128], BF16)
        LT = kb.sb("LT", [128, 128], F32)
        UT = kb.sb("UT", [128, 128], F32)
        onesf = kb.sb("onesf", [128, 128], F32)
        onesb = kb.sb("onesb", [128, 128], BF16)
        cst = kb.sb("cst", [128, 8], F32)
        modc = kb.sb("modc", [128, 6, 8], F32)
        smallc = kb.sb("smallc", [128, 64], F32)
        bgb = kb.sb("bgb", [128, 16], F32)
        BiasA = kb.sb("BiasA", [128, 4, 128], F32)
        wsT = kb.sb("wsT", [128, 4, 128], BF16)
        setup = Buf(None, "setup")
        ccol = kb.sb("ccol", [128, 2, 8], F32)
        badac = kb.sb("badac", [128, 48], F32)

        def dbg_out(name, buf, ap, shape, dt=F32):
            if not dbg:
                return
            o = nc.dram_tensor("dbg_" + name, list(shape), dt, kind="ExternalOutput").ap()
            dbg_outs[name] = (shape, dt)
            kb.dma(o, ap, reads=[buf], sembuf=buf)

        kb.op(pool, lambda e: e.memset(onesf[:], 1.0), writes=[onesf])
        kb.op(pool, lambda e: e.memset(onesb[:], 1.0), writes=[onesb])
        kb.op(pool, lambda e: e.memset(cst[:, 0:1], -0.5), writes=[cst])
        kb.op(pool, lambda e: e.memset(cst[:, 1:2], EPS), writes=[cst])
        kb.op(pool, lambda e: e.memset(cst[:, 2:3], math.log(8.0)), writes=[cst])
        kb.op(pool, lambda e: e.memset(cst[:, 3:4], 1.0), writes=[cst])
        kb.op(pool, lambda e: e.affine_select(out=identf[:], in_=onesf[:], pattern=[[-1, 128]], compare_op=ALU.is_equal,
                                              fill=0.0, base=0, channel_multiplier=1), reads=[onesf], writes=[identf])
        kb.op(pool, lambda e: e.affine_select(out=LT[:], in_=onesf[:], pattern=[[1, 128]], compare_op=ALU.is_ge,
                                              fill=0.0, base=0, channel_multiplier=-1), reads=[onesf], writes=[LT])
        kb.op(pool, lambda e: e.affine_select(out=UT[:], in_=onesf[:], pattern=[[-1, 128]], compare_op=ALU.is_ge,
                                              fill=0.0, base=0, channel_multiplier=1), reads=[onesf], writes=[UT])
        kb.op(dve, lambda e: e.tensor_copy(out=identb[:], in_=identf[:]), reads=[identf], writes=[identb])

        es_p = es.enter_context(ExitStack())
        qT = kb.sb("qT", [128, 2, S], BF16, es_p)
        kT = kb.sb("kT", [128, 2, NG * 128], BF16, es_p)
        vaug = kb.sb("vaug", [128, NG, 4, 130], BF16, es_p)
        WS = kb.sb("WS", [128, NG, 8], F32, es_p)
        LB = kb.sb("LB", [128, NG, 8], F32, es_p)
        ST = kb.sb("ST", [128, NT, 2, 2, 130], BF16, es_p)
        kb.op(pool, lambda e: e.memset(vaug[:, :, :, 128:130], 1.0), writes=[vaug])

        es0 = es.enter_context(ExitStack())
        ln1gb = kb.sb("ln1gb", [128, D], F32, es0)
        ln1bb = kb.sb("ln1bb", [128, D], F32, es0)
        es_gc = es.enter_context(ExitStack())
        Gt = kb.sb("Gt", [128, NG, 16], F32, es_gc)
        CW = kb.sb("CW", [128, NG, 8], F32, es_gc)
        es_set = es.enter_context(ExitStack())
        rows = kb.sb("rows", [128, 256], F32, es_set)
        bsb = kb.sb("bsb", [128, 512], F32, es_set)
        R_C, R_CC, R_BADA, R_CONV, R_GV, R_BV, R_HNG, R_B1 = 0, 8, 16, 64, 76, 80, 84, 88
        kb.dma(rows[0:8, 0:128], c_d[:, :], writes=[rows])
        kb.dma(rows[0:8, 128:256], cctx_d[:, :], writes=[rows])
        rows2 = kb.sb("rows2", [128, 128], F32, es_set)
        kb.dma(rows2[0:48, :], bada_d[:, :], writes=[rows2])
        rows3 = kb.sb("rows3", [128, 128], F32, es_set)
        kb.dma(rows3[0:12, :], conv_d[:, :], writes=[rows3])
        kb.dma(rows3[32:36, :], lnvg_d[:, :], writes=[rows3])
        kb.dma(rows3[64:68, :], lnvb_d[:, :], writes=[rows3])
        rows4 = kb.sb("rows4", [128, 128], F32, es_set)
        kb.dma(rows4[0:4, :], hng_d[:, :], writes=[rows4])
        kb.dma(rows4[32:64, :], b1_d[:, :], writes=[rows4])
        kb.dma(bgb[:], bg_d[0:1, :].to_broadcast([128, 16]), writes=[bgb])
        kb.dma(bsb[:], bs_d[0:1, :].to_broadcast([128, 512]), writes=[bsb])
        wsr = kb.sb("wsr", [128, 4, 128], F32, es_set)
        kb.dma(wsr[:], ws_d.rearrange("g t s -> t g s"), writes=[wsr])


        def tr_f32(dst_ap, dst_buf, src_ap, src_buf, n, bank, p0=0):
            kb.op(pe, lambda e: e.transpose(out=bank[:, 0:n], in_=src_ap, identity=identf[p0:p0 + n, p0:p0 + n]),
                  reads=[src_buf, identf], writes=[bank])
            kb.op(dve, lambda e: e.tensor_copy(out=dst_ap, in_=bank[:, 0:n]), reads=[bank], writes=[dst_buf])

        craw = kb.sb("craw", [128, 2, 8], F32, es_set)
        tr_f32(craw[:, 0, :], craw, rows[0:8, 0:128], rows, 8, PB[0])
        tr_f32(craw[:, 1, :], craw, rows[0:8, 128:256], rows, 8, PB[1])
        kb.op(act, lambda e: e.activation(out=ccol[:], in_=craw[:], func=AF.Silu), reads=[craw], writes=[ccol])
        tr_f32(badac[:, :], badac, rows2[0:48, :], rows2, 48, PB[2])
        tr_f32(smallc[:, 12:24], smallc, rows3[0:12, :], rows3, 12, PB[3])
        tr_f32(smallc[:, 0:4], smallc, rows3[32:36, :], rows3, 4, PB[4], p0=32)
        tr_f32(smallc[:, 4:8], smallc, rows3[64:68, :], rows3, 4, PB[5], p0=64)
        tr_f32(smallc[:, 8:12], smallc, rows4[0:4, :], rows4, 4, PB[6])
        tr_f32(smallc[:, 24:56], smallc, rows4[32:64, :], rows4, 32, PB[0], p0=32)
        wsTf = kb.sb("wsTf", [128, 4, 128], F32, es_set)
        for g in range(4):
            kb.op(pe, lambda e, g=g: e.transpose(out=PB[1][:, g * 128:(g + 1) * 128], in_=wsr[:, g, :], identity=identf[:]),
                  reads=[wsr, identf], writes=[PB[1]])
        kb.op(dve, lambda e: e.tensor_copy(out=wsTf[:].rearrange("p g t -> p (g t)"), in_=PB[1][:, :]), reads=[PB[1]], writes=[wsTf])
        kb.op(act, lambda e: e.activation(out=wsT[:], in_=wsTf[:], func=AF.Copy), reads=[wsTf], writes=[wsT])
        kb.op(pe, lambda e: e.matmul(PB[2][:, :], lhsT=onesf[:], rhs=wsTf[:].rearrange("p g t -> p (g t)"), start=True, stop=True),
              reads=[onesf, wsTf], writes=[PB[2]])
        for g in range(4):
            kb.op(dve, lambda e, g=g: e.scalar_tensor_tensor(out=BiasA[:, g, :], in0=PB[2][:, g * 128:(g + 1) * 128],
                                                             scalar=smallc[:, 4 + g:5 + g], in1=bsb[:, g * 128:(g + 1) * 128],
                                                             op0=ALU.mult, op1=ALU.add),
                  reads=[PB[2], smallc, bsb], writes=[BiasA])

        if stage == -1:
            dbg_out("smallc", smallc, smallc[:], [128, 64])
            dbg_out("BiasA", BiasA, BiasA[:], [128, 4, 128])
            dbg_out("ccol", ccol, ccol[:], [128, 2, 8])
            dbg_out("LT", LT, LT[:], [128, 128])
            dbg_out("identf", identf, identf[:], [128, 128])
            kb.barrier()
            raise _Stop((nc, dbg_outs))
        kb.barrier()
        es_set.close()
        kb.dma(ln1gb[:], ln1g_d[0:1, :].to_broadcast([128, D]), writes=[ln1gb])
        kb.dma(ln1bb[:], ln1b_d[0:1, :].to_broadcast([128, D]), writes=[ln1bb])
        g2row = nc.dram_tensor("g2scratch", [128, D], F32, kind="Internal").ap()
        g1row = nc.dram_tensor("g1scratch", [128, D], F32, kind="Internal").ap()

        with ExitStack() as esa:
            stg = [kb.sb(f"astg{i}", [128, 8, 512], BF16, esa) for i in range(4)]
            scb = kb.sb("scb", [128, 8, 128], BF16, esa)
            ccolb = kb.sb("ccolb", [128, 2, 8], BF16, esa)
            kb.op(dve, lambda e: e.tensor_copy(out=ccolb[:], in_=ccol[:]), reads=[ccol], writes=[ccolb])
            badab = kb.sb("badab", [128, 2, D], F32, esa)
            g2bc0 = kb.sb("g2bc0", [128, D], F32, esa)
            g1bc = kb.sb("g1bc0", [128, D], F32, esa)
            kb.dma(badab[:, 0, :], bada_d[16:24, :].rearrange("(o a) b -> o (a b)", o=1).to_broadcast([128, D]), writes=[badab])
            kb.dma(badab[:, 1, :], bada_d[40:48, :].rearrange("(o a) b -> o (a b)", o=1).to_broadcast([128, D]), writes=[badab])
            for kc in range(8):
                kb.op(dve, lambda e, kc=kc: e.tensor_copy(out=scb[:, kc, :], in_=ccol[:, 0, kc:kc + 1].to_broadcast([128, 128])),
                      reads=[ccol], writes=[scb])
            wada_v = wada_d.rearrange("(kc p) n -> p kc n", p=128)
            col_kind = {0: 0, 1: 0, 2: 1, 3: 1, 6: 2, 7: 2, 8: 3, 9: 3}
            for blk in range(12):
                sg = stg[blk % 4]
                kb.dma(sg[:, 0:4, :], wada_v[:, 0:4, blk * 512:(blk + 1) * 512], writes=[sg], eng=pool)
                kb.dma(sg[:, 4:8, :], wada_v[:, 4:8, blk * 512:(blk + 1) * 512], writes=[sg], eng=pool)
                if blk in col_kind:
                    mi = col_kind[blk]
                    for jj in range(4):
                        j = blk * 4 + jj
                        fchunk = j % 8
                        bank = PB[jj % 4]
                        for kc in range(8):
                            kb.op(pe, lambda e, kc=kc, jj=jj, bank=bank: e.matmul(
                                bank[:, 0:2], lhsT=sg[:, kc, jj * 128:(jj + 1) * 128], rhs=ccolb[:, :, kc],
                                start=(kc == 0), stop=(kc == 7)), reads=[sg, ccolb], writes=[bank], sig=(kc == 7))
                        kb.op(dve, lambda e, bank=bank, mi=mi, fchunk=fchunk, j=j: e.tensor_tensor(
                            out=modc[:, mi, fchunk:fchunk + 1], in0=bank[:, 0:1], in1=badac[:, j:j + 1], op=ALU.add),
                            reads=[bank, badac], writes=[modc.sub((mi, fchunk))])
                        if mi < 2:
                            kb.op(dve, lambda e, bank=bank, mi=mi, fchunk=fchunk, j=j: e.tensor_tensor(
                                out=modc[:, 4 + mi, fchunk:fchunk + 1], in0=bank[:, 1:2], in1=badac[:, j:j + 1], op=ALU.add),
                                reads=[bank, badac], writes=[modc.sub((4 + mi, fchunk))])
                else:
                    which = 0 if blk in (4, 5) else 1
                    half = blk % 2 if which == 1 else blk - 4
                    bank = PB[4 + (blk % 2)]
                    for kc in range(8):
                        kb.op(pe, lambda e, kc=kc, bank=bank: e.matmul(bank[:, :], lhsT=scb[:, kc, :], rhs=sg[:, kc, :],
                                                                      start=(kc == 0), stop=(kc == 7)),
                              reads=[sg, scb], writes=[bank], sig=(kc == 7))
                    dst = g1bc if which == 0 else g2bc0
                    kb.op(dve, lambda e, bank=bank, dst=dst, half=half, which=which: e.tensor_tensor(
                        out=dst[:, half * 512:(half + 1) * 512], in0=bank[:, :], in1=badab[:, which, half * 512:(half + 1) * 512],
                        op=ALU.add), reads=[bank, badab], writes=[dst])
            for mi in (1, 3, 5):
                kb.op(dve, lambda e, mi=mi: e.tensor_scalar(out=modc[:, mi, :], in0=modc[:, mi, :], scalar1=1.0, scalar2=None,
                                                            op0=ALU.add), reads=[modc], writes=[modc])
            g2st = kb.dma(g2row[:, :], g2bc0[:], reads=[g2bc0])
            kb.dma(g1row[:, :], g1bc[:], reads=[g1bc])
            kb.barrier()
            if stage == 0:
                dbg_out("modc", modc, modc[:], [128, 6, 8])
                dbg_out("g1bc", g1bc, g1bc[:], [128, D])
                dbg_out("smallc", smallc, smallc[:], [128, 64])
                dbg_out("BiasA", BiasA, BiasA[:], [128, 4, 128])
                kb.barrier()
                raise _Stop((nc, dbg_outs))

        def ln_stats(xap, xbuf, width, stats, mv, rstd, nmr=None):
            nchunk = width // 512
            for cidx in range(nchunk):
                kb.op(dve, lambda e, cidx=cidx: e.bn_stats(out=stats[:, cidx, :], in_=xap[:, cidx * 512:(cidx + 1) * 512]),
                      reads=[xbuf], writes=[stats.sub(cidx)])
            kb.op(dve, lambda e: e.bn_aggr(out=mv[:, 0:2], in_=stats[:, 0:nchunk, :].rearrange("p a b -> p (a b)")),
                  reads=[stats], writes=[mv])
            kb.op(pool, lambda e: e.tensor_tensor(out=mv[:, 2:3], in0=mv[:, 1:2], in1=cst[:, 1:2], op=ALU.add),
                  reads=[mv, cst], writes=[mv])
            kb.op(pool, lambda e: e.tensor_tensor(out=rstd[:, 0:1], in0=mv[:, 2:3], in1=cst[:, 0:1], op=ALU.pow),
                  reads=[mv, cst], writes=[rstd])
            if nmr is not None:
                kb.op(dve, lambda e: e.scalar_tensor_tensor(out=rstd[:, 1:2], in0=mv[:, 0:1], scalar=-1.0, in1=rstd[:, 0:1],
                                                            op0=ALU.mult, op1=ALU.mult), reads=[mv, rstd], writes=[rstd])

        def make_hT(xt, hT, xn, stats, mv, rstd, mi_shift, mi_scale, tok0=0):
            make_hT_A(xt, xn, stats, mv, rstd)
            make_hT_B(hT, xn, mi_shift, mi_scale, tok0)

        def make_hT_A(xt, xn, stats, mv, rstd):
            ln_stats(xt[:, :], xt, 1024, stats, mv, rstd, nmr=True)
            kb.op(act, lambda e: e.activation(out=xn[:], in_=xt[:, :], func=AF.Identity, bias=rstd[:, 1:2], scale=rstd[:, 0:1]),
                  reads=[xt, rstd], writes=[xn])

        def make_hT_B(hT, xn, mi_shift, mi_scale, tok0=0):
            for kc in range(8):
                kb.op(pe, lambda e, kc=kc: e.transpose(out=PT[:, kc * 128:(kc + 1) * 128], in_=xn[:, kc * 128:(kc + 1) * 128],
                                                      identity=identb[:]), reads=[xn, identb], writes=[PT], sig=(kc == 7))
            for kc in range(8):
                kb.op(act, lambda e, kc=kc: e.activation(out=hT[:, kc, tok0:tok0 + 128], in_=PT[:, kc * 128:(kc + 1) * 128],
                                                         func=AF.Identity, bias=modc[:, mi_shift, kc:kc + 1],
                                                         scale=modc[:, mi_scale, kc:kc + 1]),
                      reads=[PT, modc], writes=[hT.sub((kc, tok0))])

        def load_weight_block(dst, dst_ap_fn, src_ap, stg_buf, nk, scale_bc=None, scale_cols=None):
            half = nk // 2
            kb.dma(stg_buf[:, 0:half, :], src_ap[:, 0:half, :], writes=[stg_buf])
            kb.dma(stg_buf[:, half:nk, :], src_ap[:, half:nk, :], writes=[stg_buf])
            if scale_bc is None:
                kb.op(act, lambda e: e.activation(out=dst_ap_fn(slice(0, half)), in_=stg_buf[:, 0:half, :], func=AF.Copy),
                      reads=[stg_buf], writes=[dst])
                kb.op(pool, lambda e: e.tensor_copy(out=dst_ap_fn(slice(half, nk)), in_=stg_buf[:, half:nk, :]),
                      reads=[stg_buf], writes=[dst])
            else:
                for k in range(nk):
                    eng = dve if k % 2 == 0 else pool
                    kb.op(eng, lambda e, k=k: e.tensor_tensor(out=dst_ap_fn(k), in0=stg_buf[:, k, :],
                                                              in1=scale_bc[:, scale_cols], op=ALU.mult),
                          reads=[stg_buf, scale_bc], writes=[dst])

        win_v = win_d.rearrange("(kc p) n -> p kc n", p=128)
        with ExitStack() as es1:
            Win1 = kb.sb("Win1", [128, 8, 1040], BF16, es1)
            kb.cast_load(Win1, [(Win1[:, k0:k0 + 4, c0:c0 + 512], win_v[:, k0:k0 + 4, w0:w0 + 512], Win1.sub((k0, c0)))
                                for (c0, w0) in ((0, 1024), (512, 1536)) for k0 in (0, 4)]
                         + [(Win1[:, :, 1024:1040], win_v[:, :, 2560:2576], Win1.sub("g"))])
            xs = [kb.sb(f"xs{i}", [128, D], F32, es1) for i in range(2)]
            xn = kb.sb("xn", [128, D], BF16, es1)
            hT = kb.sb("hT", [128, 8, 128], BF16, es1)
            stats = kb.sb("stats", [128, 2, 6], F32, es1)
            mv = kb.sb("mv", [128, 4], F32, es1)
            rstd = kb.sb("rstd", [128, 2], F32, es1)
            raw = [kb.sb(f"raw{i}", [128, 4, 130], F32, es1) for i in range(3)]
            cvt = kb.sb("cvt", [128, 4, 128], F32, es1)

            def conv_finish(gc_prev, rb):
                for cc in range(4):
                    kb.op(dve, lambda e, cc=cc: e.tensor_scalar(out=cvt[:, cc, :], in0=rb[:, cc, 0:128],
                                                                scalar1=smallc[:, 12 + 0 * 4 + cc:13 + 0 * 4 + cc], scalar2=None,
                                                                op0=ALU.mult), reads=[rb, smallc], writes=[cvt.sub(cc)])
                    kb.op(dve, lambda e, cc=cc: e.scalar_tensor_tensor(out=cvt[:, cc, :], in0=rb[:, cc, 1:129],
                                                                       scalar=smallc[:, 12 + 1 * 4 + cc:13 + 1 * 4 + cc],
                                                                       in1=cvt[:, cc, :], op0=ALU.mult, op1=ALU.add),
                          reads=[rb, smallc, cvt.sub(cc)], writes=[cvt.sub(cc)])
                    kb.op(dve, lambda e, cc=cc: e.scalar_tensor_tensor(out=cvt[:, cc, :], in0=rb[:, cc, 2:130],
                                                                       scalar=smallc[:, 12 + 2 * 4 + cc:13 + 2 * 4 + cc],
                                                                       in1=cvt[:, cc, :], op0=ALU.mult, op1=ALU.add),
                          reads=[rb, smallc, cvt.sub(cc)], writes=[cvt.sub(cc)])
                t0 = gc_prev * 128
                kb.op(act, lambda e: e.activation(out=kT[:, :, t0:t0 + 128], in_=cvt[:, 2:4, :], func=AF.Silu),
                      reads=[cvt.sub(2), cvt.sub(3)], writes=[kT])
                if gc_prev >= NCT:
                    l0 = (gc_prev - NCT) * 128
                    kb.op(act, lambda e: e.activation(out=qT[:, :, l0:l0 + 128], in_=cvt[:, 0:2, :], func=AF.Silu),
                          reads=[cvt.sub(0), cvt.sub(1)], writes=[qT])

            xn_1 = [xn, kb.sb("xn_b", [128, D], BF16, es1)]
            hT_1 = [hT, kb.sb("hT_b", [128, 8, 128], BF16, es1)]
            stats_1 = [stats, kb.sb("stats_b", [128, 2, 6], F32, es1)]
            mv_1 = [mv, kb.sb("mv_b", [128, 4], F32, es1)]
            rstd_1 = [rstd, kb.sb("rstd_b", [128, 2], F32, es1)]

            xs3 = xs + [kb.sb("xs_c", [128, D], F32, es1)]
            xn3 = xn_1 + [kb.sb("xn_c", [128, D], BF16, es1)]
            stA = stats_1 + [kb.sb("stats_c", [128, 2, 6], F32, es1)]
            mvA = mv_1 + [kb.sb("mv_c", [128, 4], F32, es1)]
            rsA = rstd_1 + [kb.sb("rstd_c", [128, 2], F32, es1)]

            def tile1A(gc):
                if gc >= NG:
                    return
                is_ctx = gc < NCT
                src = ctx_d[gc * 128:(gc + 1) * 128, :] if is_ctx else x_d[(gc - NCT) * 128:(gc - NCT + 1) * 128, :]
                xt = xs3[gc % 3]
                kb.dma(xt[:, 0:512], src[:, 0:512], writes=[xt.sub(0)], sembuf=xt)
                kb.dma(xt[:, 512:1024], src[:, 512:1024], writes=[xt.sub(1)], sembuf=xt)
                make_hT_A(xt, xn3[gc % 3], stA[gc % 3], mvA[gc % 3], rsA[gc % 3])

            def tile1(gc):
                tile1A(gc + 1)
                sl = gc % 2
                hT = hT_1[sl]
                PBq, PBv, PBg = PB[3 * sl], PB[3 * sl + 1], PB[3 * sl + 2]
                is_ctx = gc < NCT
                make_hT_B(hT, xn3[gc % 3], 4 if is_ctx else 0, 5 if is_ctx else 1)
                for cc in range(4):
                    for kc in range(8):
                        kb.op(pe, lambda e, cc=cc, kc=kc: e.matmul(PBq[:, cc * 128:(cc + 1) * 128],
                                                                   lhsT=Win1[:, kc, cc * 128:(cc + 1) * 128], rhs=hT[:, kc, :],
                                                                   start=(kc == 0), stop=(kc == 7)),
                              reads=[Win1, hT.sub((kc, 0))], writes=[PBq], sig=(kc == 7 and cc == 3))
                for kc in range(8):
                    kb.op(pe, lambda e, kc=kc: e.matmul(PBv[:, :], lhsT=hT[:, kc, :], rhs=Win1[:, kc, 512:1024],
                                                        start=(kc == 0), stop=(kc == 7)),
                          reads=[Win1, hT.sub((kc, 0))], writes=[PBv], sig=(kc == 7))
                for kc in range(8):
                    kb.op(pe, lambda e, kc=kc: e.matmul(PBg[:, 0:16], lhsT=hT[:, kc, :], rhs=Win1[:, kc, 1024:1040],
                                                        start=(kc == 0), stop=(kc == 7)),
                          reads=[Win1, hT.sub((kc, 0))], writes=[PBg], sig=(kc == 7))
                rb = raw[gc % 3]
                first = gc in (0, NCT)
                last = gc in (NCT - 1, NG - 1)
                kb.op(act, lambda e: e.activation(out=rb[:, :, 1:129], in_=PBq[:, :].rearrange("p (c t) -> p c t", c=4), func=AF.Copy),
                      reads=[PBq], writes=[rb])
                kb.op(dve, lambda e: e.tensor_copy(out=vaug[:, gc, :, 0:128], in_=PBv[:, :].rearrange("p (h v) -> p h v", h=4)),
                      reads=[PBv], writes=[vaug])
                kb.op(dve, lambda e: e.tensor_tensor(out=Gt[:, gc, :], in0=PBg[:, 0:16], in1=bgb[:], op=ALU.add),
                      reads=[PBg, bgb], writes=[Gt])
                if first:
                    kb.op(pool, lambda e: e.memset(rb[:, :, 0:1], 0.0), writes=[rb])
                else:
                    rprev = raw[(gc - 1) % 3]
                    kb.op(pool, lambda e: e.tensor_copy(out=rb[:, :, 0:1], in_=rprev[:, :, 128:129]), reads=[rprev], writes=[rb])
                    kb.op(pool, lambda e: e.tensor_copy(out=rprev[:, :, 129:130], in_=rb[:, :, 1:2]), reads=[rb], writes=[rprev])
                    conv_finish(gc - 1, rprev)
                if last:
                    kb.op(pool, lambda e: e.memset(rb[:, :, 129:130], 0.0), writes=[rb])
                    conv_finish(gc, rb)
            tile1A(0)
            lists1 = [kb.record(tile1, gc) for gc in range(NG)]
            interleave(lists1, int(len(lists1[2]) * SKEW1))
            kb.barrier()

        if dbg:
            dbg_out("kT", kT, kT[:], [128, 2, NG * 128], BF16)
            dbg_out("qT", qT, qT[:], [128, 2, S], BF16)
            dbg_out("vaug", vaug, vaug[:], [128, NG, 4, 130], BF16)
            dbg_out("Gt", Gt, Gt[:], [128, NG, 16])
        if stage == 1:
            kb.barrier()
            raise _Stop((nc, dbg_outs))

        with ExitStack() as esg:
            NF = NG * 8
            Gv = Gt[:].rearrange("p c (d t h) -> p c d t h", d=2, t=2, h=4)
            LF = kb.sb("LF", [128, NG, 2, 4], F32, esg)
            T1 = kb.sb("T1", [128, NG, 2, 4], F32, esg)
            T2 = kb.sb("T2", [128, NG, 2, 4], F32, esg)
            Bc = kb.sb("Bc", [128, NG, 2, 4], F32, esg)
            Aa = kb.sb("Aa", [128, NG, 2, 4], F32, esg)
            Mcb = kb.sb("Mcb", [128, NG, 2, 4], F32, esg)
            rowA = kb.sb("rowA", [1, NG, 2, 4], F32, esg)
            rowB = kb.sb("rowB", [1, NG, 2, 4], F32, esg)
            rowM = kb.sb("rowM", [1, NG, 2, 4], F32, esg)
            rowm0 = kb.sb("rowm0", [1, NG, 2, 4], F32, esg)
            colmax = kb.sb("colmax", [128, 3], F32, esg)
            fl = lambda b: b[:].rearrange("p c d h -> p (c d h)")
            FG = Gv[:, :, :, 1, :]
            IG = Gv[:, :, :, 0, :]
            kb.op(dve, lambda e: e.tensor_scalar(out=T1[:], in0=FG, scalar1=-1.0, scalar2=None, op0=ALU.mult), reads=[Gt], writes=[T1])
            kb.op(dve, lambda e: e.tensor_tensor(out=T1[:], in0=T1[:], in1=FG, op=ALU.max), reads=[Gt, T1], writes=[T1])
            kb.op(act, lambda e: e.activation(out=T2[:], in_=T1[:], func=AF.Exp, scale=-1.0), reads=[T1], writes=[T2])
            kb.op(act, lambda e: e.activation(out=T2[:], in_=T2[:], func=AF.Ln, bias=cst[:, 3:4], scale=1.0), reads=[T2, cst], writes=[T2])
            kb.op(dve, lambda e: e.tensor_scalar(out=T1[:], in0=FG, scalar1=0.0, scalar2=None, op0=ALU.min), reads=[Gt], writes=[T1])
            kb.op(dve, lambda e: e.tensor_tensor(out=LF[:], in0=T1[:], in1=T2[:], op=ALU.subtract), reads=[T1, T2], writes=[LF])
            PBv = PB[0][:, 0:NF].rearrange("p (c d h) -> p c d h", c=NG, d=2, h=4)
            kb.op(pe, lambda e: e.matmul(PB[0][:, 0:NF], lhsT=LT[:], rhs=fl(LF), start=True, stop=True),
                  reads=[LT, LF], writes=[PB[0]])
            PBv1 = PB[1][:, 0:NF].rearrange("p (c d h) -> p c d h", c=NG, d=2, h=4)
            kb.op(pe, lambda e: e.matmul(PB[1][:, 0:NF], lhsT=UT[:], rhs=fl(LF), start=True, stop=True),
                  reads=[UT, LF], writes=[PB[1]])
            kb.op(dve, lambda e: e.tensor_copy(out=Bc[:, :, 0, :], in_=PBv[:, :, 0, :]), reads=[PB[0]], writes=[Bc])
            kb.op(dve, lambda e: e.tensor_copy(out=Bc[:, :, 1, :], in_=PBv1[:, :, 1, :]), reads=[PB[1]], writes=[Bc])
            kb.op(dve, lambda e: e.tensor_tensor(out=Aa[:], in0=IG, in1=Bc[:], op=ALU.subtract), reads=[Gt, Bc], writes=[Aa])
            AaF = fl(Aa)
            segs = [(0, 128), (128, 128), (256, NF - 256)]
            for si, (o, n) in enumerate(segs):
                kb.op(pe, lambda e, o=o, n=n: e.transpose(out=PB[2][0:n, 0:128], in_=AaF[:, o:o + n], identity=identf[:]),
                      reads=[Aa, identf], writes=[PB[2]])
                kb.op(dve, lambda e, si=si, n=n: e.reduce_max(out=colmax[0:n, si:si + 1], in_=PB[2][0:n, 0:128], axis=mybir.AxisListType.X),
                      reads=[PB[2]], writes=[colmax])
                kb.op(pe, lambda e, si=si, n=n, o=o: e.matmul(PB[3][0:1, o:o + n], lhsT=colmax[0:n, si:si + 1], rhs=identf[0:n, 0:n],
                                                               start=True, stop=True), reads=[colmax, identf], writes=[PB[3]])
            kb.op(dve, lambda e: e.tensor_copy(out=fl(rowA), in_=PB[3][0:1, 0:NF]), reads=[PB[3]], writes=[rowA])
            kb.op(pe, lambda e: e.matmul(PB[4][0:1, 0:NF], lhsT=onesf[:, 0:1], rhs=fl(LF), start=True, stop=True),
                  reads=[onesf, LF], writes=[PB[4]])
            kb.op(dve, lambda e: e.tensor_copy(out=fl(rowB), in_=PB[4][0:1, 0:NF]), reads=[PB[4]], writes=[rowB])
            order = [list(range(NG)), [1, 0] + list(range(NG - 1, NCT - 1, -1))]
            for d in range(2):
                g0 = order[d][0]
                kb.op(dve, lambda e, d=d, g0=g0: e.memset(rowm0[0:1, g0, d, :], 0.0), writes=[rowm0])
            for j in range(NG):
                for d in range(2):
                    gcur = order[d][j]
                    kb.op(dve, lambda e, d=d, gcur=gcur: e.tensor_tensor(out=rowM[0:1, gcur, d, :], in0=rowm0[0:1, gcur, d, :],
                                                                         in1=rowA[0:1, gcur, d, :], op=ALU.max),
                          reads=[rowm0, rowA], writes=[rowM])
                    if j + 1 < NG:
                        gn = order[d][j + 1]
                        kb.op(dve, lambda e, d=d, gcur=gcur, gn=gn: e.tensor_tensor(out=rowm0[0:1, gn, d, :], in0=rowM[0:1, gcur, d, :],
                                                                                   in1=rowB[0:1, gcur, d, :], op=ALU.add),
                              reads=[rowM, rowB], writes=[rowm0])
            kb.op(pe, lambda e: e.matmul(PB[5][:, 0:NF], lhsT=onesf[0:1, :], rhs=fl(rowM), start=True, stop=True),
                  reads=[onesf, rowM], writes=[PB[5]])
            kb.op(pe, lambda e: e.matmul(PB[6][:, 0:NF], lhsT=onesf[0:1, :], rhs=fl(rowm0), start=True, stop=True),
                  reads=[onesf, rowm0], writes=[PB[6]])
            kb.op(dve, lambda e: e.tensor_copy(out=fl(Mcb), in_=PB[5][:, 0:NF]), reads=[PB[5]], writes=[Mcb])
            kb.op(dve, lambda e: e.tensor_tensor(out=T1[:], in0=Aa[:], in1=Mcb[:], op=ALU.subtract), reads=[Aa, Mcb], writes=[T1])
            kb.op(act, lambda e: e.activation(out=WS[:].rearrange("p c j -> p (c j)"), in_=fl(T1), func=AF.Exp), reads=[T1], writes=[WS])
            kb.op(dve, lambda e: e.tensor_tensor(out=fl(T2), in0=PB[6][:, 0:NF], in1=fl(Mcb), op=ALU.subtract), reads=[PB[6], Mcb], writes=[T2])
            kb.op(act, lambda e: e.activation(out=CW[:].rearrange("p c j -> p (c j)"), in_=fl(T2), func=AF.Exp), reads=[T2], writes=[CW])
            kb.op(dve, lambda e: e.tensor_tensor(out=T1[:], in0=Bc[:], in1=Mcb[:], op=ALU.add), reads=[Bc, Mcb], writes=[T1])
            kb.op(act, lambda e: e.activation(out=LB[:].rearrange("p c j -> p (c j)"), in_=fl(T1), func=AF.Exp, bias=cst[:, 2:3], scale=-1.0),
                  reads=[T1, cst], writes=[LB])
            kb.barrier()

        if dbg:
            dbg_out("WS", WS, WS[:], [128, NG, 8])
            dbg_out("CW", CW, CW[:], [128, NG, 8])
            dbg_out("LB", LB, LB[:], [128, NG, 8])
        if stage == 2:
            kb.barrier()
            raise _Stop((nc, dbg_outs))

        with ExitStack() as ess:
            Cc = [kb.sb(f"Cc{d}", [128, 2, 130], F32, ess) for d in range(2)]
            kp = [kb.sb(f"kp{i}", [128, 4, 64], BF16, ess) for i in range(2)]
            c0b = [kb.sb(f"c0b{i}", [128, 2, 130], BF16, ess) for i in range(2)]
            for d in range(2):
                kb.op(pool, lambda e, d=d: e.memset(Cc[d][:], 0.0), writes=[Cc[d]])
            units = [(j, d) for j in range(NG) for d in range(2)]

            def stepA(u):
                j, d = units[u]
                if j == NG - 1:
                    return
                gcur = order[d][j]
                kpb = kp[u % 2]
                for pr in range(2):
                    kb.op(pe, lambda e, pr=pr: e.transpose(out=PT[:, pr * 128:(pr + 1) * 128],
                                                           in_=kT[:, pr, gcur * 128:(gcur + 1) * 128], identity=identb[:]),
                          reads=[kT, identb], writes=[PT], sig=(pr == 1))
                for h in range(4):
                    kb.op(act, lambda e, h=h: e.activation(
                        out=kpb[:, h, :], in_=PT[:, h * 64:(h + 1) * 64], func=AF.Identity,
                        scale=WS[:, gcur, d * 4 + h:d * 4 + h + 1]), reads=[PT, WS], writes=[kpb.sub(h)])
                bank = PB[(u % 2) * 2:(u % 2) * 2 + 2]
                for h in range(4):
                    bk = bank[h // 2]
                    kb.op(pe, lambda e, h=h, bk=bk: e.matmul(
                        bk[:, (h % 2) * 130:(h % 2) * 130 + 130], lhsT=kpb[:, (h // 2) * 2:(h // 2) * 2 + 2, :].rearrange("p a b -> p (a b)"),
                        rhs=vaug[:, gcur, h, :], start=True, stop=True), reads=[kpb.sub((h // 2) * 2), kpb.sub((h // 2) * 2 + 1), vaug], writes=[bk])

            Cnext = [kb.sb(f"Cn{d}", [128, 2, 130], F32, ess) for d in range(2)]
            for d in range(2):
                kb.op(pool, lambda e, d=d: e.memset(Cnext[d][:], 0.0), writes=[Cnext[d]])
            Cpp = [[Cc[d], Cnext[d]] for d in range(2)]

            def stepB(u):
                j, d = units[u]
                gcur = order[d][j]
                Ccur = Cpp[d][j % 2]
                Cnew = Cpp[d][(j + 1) % 2]
                if gcur >= NCT:
                    for h in range(4):
                        p0 = (h % 2) * 64
                        kb.op(pool, lambda e, h=h, p0=p0: e.tensor_scalar(
                            out=ST[p0:p0 + 64, gcur - NCT, d, h // 2, :], in0=Ccur[p0:p0 + 64, h // 2, :],
                            scalar1=CW[p0:p0 + 64, gcur, d * 4 + h:d * 4 + h + 1], scalar2=1.0, op0=ALU.mult, op1=ALU.mult),
                            reads=[Ccur.sub(h), CW], writes=[ST])
                if j == NG - 1:
                    return
                bank = PB[(u % 2) * 2:(u % 2) * 2 + 2]
                for h in range(4):
                    p0 = (h % 2) * 64
                    bk = bank[h // 2]
                    kb.op(dve, lambda e, h=h, p0=p0, bk=bk: e.scalar_tensor_tensor(
                        out=Cnew[p0:p0 + 64, h // 2, :], in0=Ccur[p0:p0 + 64, h // 2, :],
                        scalar=CW[p0:p0 + 64, gcur, d * 4 + h:d * 4 + h + 1],
                        in1=bk[p0:p0 + 64, (h % 2) * 130:(h % 2) * 130 + 130], op0=ALU.mult, op1=ALU.add),
                        reads=[Ccur.sub(h), CW, bk], writes=[Cnew.sub(h)])

            stepA(0)
            for u in range(len(units)):
                if u + 1 < len(units):
                    stepA(u + 1)
                stepB(u)
            kb.barrier()

        if dbg:
            dbg_out("ST", ST, ST[:], [128, NT, 2, 2, 130], BF16)
        if stage == 3:
            kb.barrier()
            raise _Stop((nc, dbg_outs))

        es_gc.close()
        wout_v = wout_d.rearrange("(kc p) n -> p kc n", p=128)
        with ExitStack() as es2:
            Win2 = kb.sb("Win2", [128, 8, 1536], BF16, es2)
            Wo = kb.sb("Wo", [128, 8, D], BF16, es2)
            with ExitStack() as esw:
                wstg = [kb.sb(f"wstg2{i}", [128, 8, 256], F32, esw) for i in range(2)]
                g1bc = kb.sb("g1bc", [128, D], F32, esw)
                kb.dma(g1bc[:], g1row[:, :], writes=[g1bc])
                nb = 0
                kb.cast_load(Win2, [(Win2[:, k0:k0 + 4, c0:c0 + 512], win_v[:, k0:k0 + 4, w0:w0 + 512], Win2.sub((k0, c0)))
                                    for (c0, w0) in ((0, 0), (512, 512), (1024, 2048)) for k0 in (0, 4)])
                for c0 in (0, 256, 512, 768):
                    load_weight_block(Wo, lambda k, c0=c0: Wo[:, k, c0:c0 + 256], wout_v[:, :, c0:c0 + 256], wstg[nb % 2], 8,
                                      scale_bc=g1bc, scale_cols=slice(c0, c0 + 256))
                    nb += 1
                kb.barrier()
            xs = [kb.sb(f"x2s{i}", [128, D], F32, es2) for i in range(2)]
            def two(name, shape, dt):
                return [kb.sb(f"{name}_{k}", shape, dt, es2) for k in range(2)]
            xn_ = two("xn2", [128, D], BF16)
            hT_ = two("hT2", [128, 8, 128], BF16)
            stats_ = two("stats2", [128, 4, 6], F32)
            mv_ = two("mv2", [128, 4], F32)
            rstd_ = two("rstd2", [128, 2], F32)
            mv4_ = two("mv4", [128, 4, 4], F32)
            rs4_ = two("rs4", [128, 4], F32)
            uT_ = two("uT", [128, 4, 128], BF16)
            sgo_ = two("sgo", [128, 4, 128], BF16)
            vn_ = two("vn", [128, 512], BF16)
            tA_1 = kb.sb("tA", [128, 4, 128], F32, es2)
            tA_ = [tA_1, tA_1]
            yT_ = two("yT", [128, 8, 128], BF16)
            sT_ = two("sT", [128, 8, 128], BF16)
            WM_1 = kb.sb("WM", [128, 8, 128], BF16, es2)
            WM_ = [WM_1, WM_1]
            dn_ = two("dn", [128, 3, 8], F32)
            hs_ = two("hs", [128, 4, 128], F32)
            hn_ = two("hn", [128, 4, 128], BF16)
            Q2_ = [[kb.sb(f"Q2{k}_{i}", [128, 2, 128], BF16, es2) for i in range(2)] for k in range(2)]
            hg = kb.sb("hg", [128, 4], F32, es2)
            for k in range(2):
                for pr in range(2):
                    kb.op(pool, lambda e, k=k, pr=pr: e.memset(Q2_[k][pr][:], 0.0), writes=[Q2_[k][pr]])
            kb.op(dve, lambda e: e.tensor_copy(out=hg[:], in_=smallc[:, 8:12]), reads=[smallc], writes=[hg])

            xs3 = xs + [kb.sb("x2s_c", [128, D], F32, es2)]
            xn3 = xn_ + [kb.sb("xn2_c", [128, D], BF16, es2)]
            stA = [kb.sb(f"stA{k}", [128, 2, 6], F32, es2) for k in range(3)]
            mvA = [kb.sb(f"mvA{k}", [128, 4], F32, es2) for k in range(3)]
            rsA = [kb.sb(f"rsA{k}", [128, 2], F32, es2) for k in range(3)]

            def tile2A(i):
                if i >= NT:
                    return
                xt = xs3[i % 3]
                kb.dma(xt[:, 0:512], x_d[i * 128:(i + 1) * 128, 0:512], writes=[xt.sub(0)], sembuf=xt)
                kb.dma(xt[:, 512:1024], x_d[i * 128:(i + 1) * 128, 512:1024], writes=[xt.sub(1)], sembuf=xt)
                make_hT_A(xt, xn3[i % 3], stA[i % 3], mvA[i % 3], rsA[i % 3])

            def tile2(i):
                tile2A(i + 1)
                sl = i % 2
                xn, hT, stats, mv, rstd, mv4, rs4 = xn3[i % 3], hT_[sl], stats_[sl], mv_[sl], rstd_[sl], mv4_[sl], rs4_[sl]
                uT, sgo, vn, tA, yT, sT, dn, hs, hn, Q2, WM = uT_[sl], sgo_[sl], vn_[sl], tA_[sl], yT_[sl], sT_[sl], dn_[sl], hs_[sl], hn_[sl], Q2_[sl], WM_[sl]
                gc = i + NCT
                t0k = gc * 128
                t0q = i * 128
                xt = xs3[i % 3]
                def branchP():
                    make_hT_B(hT, xn, 0, 1)
                    for (bank, c0) in ((PB[0], 0), (PB[1], 1024)):
                        for cc in range(4):
                            for kc in range(8):
                                kb.op(pe, lambda e, cc=cc, kc=kc, bank=bank, c0=c0: e.matmul(
                                    bank[:, cc * 128:(cc + 1) * 128], lhsT=Win2[:, kc, c0 + cc * 128:c0 + (cc + 1) * 128], rhs=hT[:, kc, :],
                                    start=(kc == 0), stop=(kc == 7)), reads=[Win2, hT.sub((kc, 0))], writes=[bank], sig=(kc == 7 and cc == 3))
                    for kc in range(8):
                        kb.op(pe, lambda e, kc=kc: e.matmul(PB[2][:, :], lhsT=hT[:, kc, :], rhs=Win2[:, kc, 512:1024],
                                                            start=(kc == 0), stop=(kc == 7)), reads=[Win2, hT.sub((kc, 0))], writes=[PB[2]], sig=(kc == 7))
                    kb.op(act, lambda e: e.activation(out=uT[:].rearrange("p c t -> p (c t)"), in_=PB[0][:, :], func=AF.Copy),
                          reads=[PB[0]], writes=[uT])
                    kb.op(act, lambda e: e.activation(out=sgo[:].rearrange("p c t -> p (c t)"), in_=PB[1][:, :], func=AF.Sigmoid),
                          reads=[PB[1]], writes=[sgo])
                    ln_stats(PB[2][:, :], PB[2], 512, stA[i % 3], mvA[i % 3], rsA[i % 3])
                    kb.op(dve, lambda e: e.tensor_scalar(out=vn[:], in0=PB[2][:, :], scalar1=mvA[i % 3][:, 0:1], scalar2=rsA[i % 3][:, 0:1],
                                                         op0=ALU.subtract, op1=ALU.mult), reads=[PB[2], mvA[i % 3], rsA[i % 3]], writes=[vn])
                    for g in range(4):
                        kb.op(pe, lambda e, g=g: e.matmul(PB[3][:, g * 128:(g + 1) * 128], lhsT=vn[:, g * 128:(g + 1) * 128], rhs=wsT[:, g, :],
                                                          start=True, stop=True), reads=[vn, wsT], writes=[PB[3]], sig=(g == 3))
                    for g in range(4):
                        kb.op(dve, lambda e, g=g: e.scalar_tensor_tensor(out=tA[:, g, :], in0=PB[3][:, g * 128:(g + 1) * 128],
                                                                         scalar=smallc[:, g:g + 1], in1=BiasA[:, g, :], op0=ALU.mult, op1=ALU.add),
                              reads=[PB[3], smallc, BiasA], writes=[tA.sub(g)])
                    kb.op(pool, lambda e: e.tensor_tensor(out=yT[:, 0:4, :], in0=tA[:], in1=uT[:], op=ALU.mult), reads=[tA, uT], writes=[yT.sub('A')])

                def branchM():
                    for d in range(2):
                        msk = LT if d == 0 else UT
                        for h in range(4):
                            kb.op(pool, lambda e, d=d, h=h, msk=msk: e.tensor_scalar(
                                out=WM[:, d * 4 + h, :], in0=msk[:], scalar1=WS[:, gc, d * 4 + h:d * 4 + h + 1], scalar2=1.0,
                                op0=ALU.mult, op1=ALU.mult), reads=[msk, WS], writes=[WM.sub((d, h))])
                    for pr in range(2):
                        for hh in range(2):
                            kb.op(pool, lambda e, pr=pr, hh=hh: e.tensor_copy(out=Q2[pr][hh * 64:(hh + 1) * 64, hh, :],
                                                                              in_=qT[hh * 64:(hh + 1) * 64, pr, t0q:t0q + 128]),
                                  reads=[qT], writes=[Q2[pr].sub(hh)])
                    for pr in range(2):
                        kb.op(pe, lambda e, pr=pr: e.matmul(PB[4][:, pr * 256:(pr + 1) * 256], lhsT=kT[:, pr, t0k:t0k + 128],
                                                            rhs=Q2[pr][:].rearrange("p a t -> p (a t)"), start=True, stop=True),
                              reads=[kT, Q2[pr]], writes=[PB[4]], sig=(pr == 1))
                    for d in range(2):
                        kb.op(dve, lambda e, d=d: e.tensor_tensor(out=sT[:, d * 4:(d + 1) * 4, :].rearrange("p h t -> p (h t)"), in0=PB[4][:, :],
                                                                  in1=WM[:, d * 4:(d + 1) * 4, :].rearrange("p h t -> p (h t)"), op=ALU.mult),
                              reads=[PB[4]] + [WM.sub((d, hh_)) for hh_ in range(4)], writes=[sT.sub(d)])
                    for d in range(2):
                        bank = PB[5 + d]
                        for h in range(4):
                            kb.op(pe, lambda e, d=d, h=h, bank=bank: e.matmul(bank[:, h * 128:(h + 1) * 128], lhsT=sT[:, d * 4 + h, :],
                                                                              rhs=vaug[:, gc, h, 0:128], start=True, stop=False),
                                  reads=[sT.sub(d), vaug], writes=[bank], sig=False)
                            kb.op(pe, lambda e, d=d, h=h, bank=bank: e.matmul(
                                bank[:, h * 128:(h + 1) * 128], lhsT=Q2[h // 2][:, h % 2, :],
                                rhs=ST[:, i, d, h // 2, 0:128], start=False, stop=True),
                                reads=[Q2[h // 2], ST], writes=[bank], sig=(h == 3))
                    for d in range(2):
                        for h in range(4):
                            jn = d * 4 + h
                            kb.op(pe, lambda e, d=d, h=h, jn=jn: e.matmul(PB[4][:, 2 * jn:2 * jn + 2], lhsT=sT[:, jn, :],
                                                                          rhs=vaug[:, gc, h, 128:130], start=True, stop=False),
                                  reads=[sT.sub(d), vaug], writes=[PB[4]], sig=False)
                            kb.op(pe, lambda e, d=d, h=h, jn=jn: e.matmul(
                                PB[4][:, 2 * jn:2 * jn + 2], lhsT=Q2[h // 2][:, h % 2, :],
                                rhs=ST[:, i, d, h // 2, 128:130], start=False, stop=True),
                                reads=[Q2[h // 2], ST], writes=[PB[4]], sig=(jn == 7))
                    den = PB[4][:, 0:16].rearrange("p (j two) -> p j two", two=2)[:, :, 0]
                    kb.op(dve, lambda e: e.tensor_scalar(out=dn[:, 0, :], in0=den, scalar1=-1.0, scalar2=None, op0=ALU.mult),
                          reads=[PB[4]], writes=[dn])
                    kb.op(dve, lambda e: e.tensor_tensor(out=dn[:, 1, :], in0=dn[:, 0, :], in1=den, op=ALU.max),
                          reads=[PB[4], dn], writes=[dn])
                    kb.op(dve, lambda e: e.tensor_tensor(out=dn[:, 0, :], in0=dn[:, 1, :], in1=LB[:, gc, :], op=ALU.max),
                          reads=[dn, LB], writes=[dn])
                    kb.op(dve, lambda e: e.reciprocal(out=dn[:, 2, :], in_=dn[:, 0, :]), reads=[dn], writes=[dn])
                    for h in range(4):
                        kb.op(act, lambda e, h=h: e.activation(out=hs[:, h, :], in_=PB[5][:, h * 128:(h + 1) * 128], func=AF.Identity,
                                                               scale=dn[:, 2, h:h + 1]), reads=[PB[5], dn], writes=[hs.sub(h)])
                    for h in range(4):
                        kb.op(dve, lambda e, h=h: e.scalar_tensor_tensor(out=hs[:, h, :], in0=PB[6][:, h * 128:(h + 1) * 128],
                                                                         scalar=dn[:, 2, 4 + h:5 + h], in1=hs[:, h, :], op0=ALU.mult, op1=ALU.add),
                              reads=[PB[6], dn, hs.sub(h)], writes=[hs.sub(h)])
                    for h in range(4):
                        kb.op(dve, lambda e, h=h: e.bn_stats(out=stats[:, h, :], in_=hs[:, h, :]), reads=[hs.sub(h)], writes=[stats.sub(h)])
                    for h in range(4):
                        kb.op(dve, lambda e, h=h: e.bn_aggr(out=mv4[:, h, 0:2], in_=stats[:, h, :]), reads=[stats.sub(h)], writes=[mv4.sub(h)])
                    kb.op(pool, lambda e: e.tensor_tensor(out=mv4[:, :, 2], in0=mv4[:, :, 1], in1=cst[:, 1:2].to_broadcast([128, 4]), op=ALU.add),
                          reads=[mv4, cst], writes=[mv4])
                    kb.op(pool, lambda e: e.tensor_tensor(out=rs4[:], in0=mv4[:, :, 2], in1=cst[:, 0:1].to_broadcast([128, 4]), op=ALU.pow),
                          reads=[mv4, cst], writes=[rs4])
                    for h in range(4):
                        kb.op(dve, lambda e, h=h: e.tensor_scalar(out=hn[:, h, :], in0=hs[:, h, :], scalar1=mv4[:, h, 0:1], scalar2=rs4[:, h:h + 1],
                                                                  op0=ALU.subtract, op1=ALU.mult), reads=[hs.sub(h), mv4, rs4], writes=[hn.sub(h)])

                Pl = kb.record(branchP)
                Ml = kb.record(branchM)
                ip = im = 0
                tot = len(Pl) + len(Ml)
                for k in range(tot):
                    if MERGE2 and im * len(Pl) <= ip * len(Ml) and im < len(Ml):
                        kb.rec.append(Ml[im]); im += 1
                    elif ip < len(Pl):
                        kb.rec.append(Pl[ip]); ip += 1
                    else:
                        kb.rec.append(Ml[im]); im += 1
                for h in range(4):
                    kb.op(pe, lambda e, h=h: e.transpose(out=PT[:, h * 128:(h + 1) * 128], in_=hn[:, h, :], identity=identb[:]),
                          reads=[hn.sub(h), identb], writes=[PT], sig=(h == 3))
                for h in range(4):
                    kb.op(dve, lambda e, h=h: e.scalar_tensor_tensor(out=yT[:, 4 + h, :], in0=PT[:, h * 128:(h + 1) * 128],
                                                                     scalar=hg[:, h:h + 1], in1=sgo[:, h, :], op0=ALU.mult, op1=ALU.mult),
                          reads=[PT, hg, sgo], writes=[yT.sub(4 + h)])
                for half in range(2):
                    for kc in range(8):
                        kb.op(pe, lambda e, half=half, kc=kc: e.matmul(PB[3 + half][:, :], lhsT=yT[:, kc, :],
                                                                       rhs=Wo[:, kc, half * 512:(half + 1) * 512],
                                                                       start=(kc == 0), stop=(kc == 7)),
                              reads=[yT, Wo], writes=[PB[3 + half]], sig=(kc == 7))
                for half in range(2):
                    kb.op(dve, lambda e, half=half: e.scalar_tensor_tensor(out=xt[:, half * 512:(half + 1) * 512],
                                                                           in0=xt[:, half * 512:(half + 1) * 512], scalar=ALPHA,
                                                                           in1=PB[3 + half][:, :], op0=ALU.mult, op1=ALU.add),
                          reads=[xt.sub(half), PB[3 + half]], writes=[xt.sub(half)])
                ln_stats(xt[:, :], xt, 1024, stats, mv, rstd)
                kb.op(dve, lambda e: e.scalar_tensor_tensor(out=xt[:, :], in0=xt[:, :], scalar=mv[:, 0:1], in1=ln1gb[:],
                                                            op0=ALU.subtract, op1=ALU.mult), reads=[xt, mv, ln1gb], writes=[xt])
                kb.op(dve, lambda e: e.scalar_tensor_tensor(out=xt[:, :], in0=xt[:, :], scalar=rstd[:, 0:1], in1=ln1bb[:],
                                                            op0=ALU.mult, op1=ALU.add), reads=[xt, rstd, ln1bb], writes=[xt])
                kb.dma(y_d[i * 128:(i + 1) * 128, :], xt[:, :], reads=[xt])
            tile2A(0)
            lists2 = [kb.record(tile2, i) for i in range(NT)]
            interleave(lists2, int(len(lists2[0]) * SKEW2))
            kb.barrier()

        if stage == 4:
            raise _Stop((nc, dbg_outs))
        es0.close()
        es_p.close()

        GRP = 2
        w1_v = w1_d.rearrange("(kc p) n -> p kc n", p=128)
        w2_v = w2_d.rearrange("(j p) n -> p j n", p=128)
        with ExitStack() as es3:
            W1b = kb.sb("W1b", [128, 8, DFF], BF16, es3)
            W2b = kb.sb("W2b", [128, 32, D], BF16, es3)
            ln2gb = kb.sb("ln2gb", [128, D], F32, es3)
            ln2bb = kb.sb("ln2bb", [128, D], F32, es3)
            b2h = kb.sb("b2h", [1, 2, D], BF16, es3)
            kb.dma(ln2gb[:], ln2g_d[0:1, :].to_broadcast([128, D]), writes=[ln2gb])
            kb.dma(ln2bb[:], ln2b_d[0:1, :].to_broadcast([128, D]), writes=[ln2bb])
            g2bc = kb.sb("g2bc", [128, D], F32, es3)
            kb.dma(g2bc[:], g2row[:, :], writes=[g2bc])
            kb.cast_load(W1b, [(W1b[:, k0:k0 + 4, blk * 512:(blk + 1) * 512], w1_v[:, k0:k0 + 4, blk * 512:(blk + 1) * 512], W1b.sub(blk).sub(k0))
                               for blk in range(8) for k0 in (0, 4)], depth=3)
            kb.cast_load(W2b, [(W2b[:, blk * 4:(blk + 1) * 4, :], w2_v[:, blk * 4:(blk + 1) * 4, :], W2b.sub(blk))
                               for blk in range(8)], depth=3)
            with ExitStack() as esw:
                b2bc = kb.sb("b2bc", [128, D], F32, esw)
                kb.dma(b2bc[0:1, :], b2_d[0:1, :], writes=[b2bc])
                kb.op(dve, lambda e: e.tensor_copy(out=b2h[0:1, 0, :], in_=b2bc[0:1, :]), reads=[b2bc], writes=[b2h])
                kb.op(dve, lambda e: e.tensor_tensor(out=b2bc[0:1, :], in0=b2bc[0:1, :], in1=b2h[0:1, 0, :], op=ALU.subtract),
                      reads=[b2bc, b2h], writes=[b2bc])
                kb.op(dve, lambda e: e.tensor_copy(out=b2h[0:1, 1, :], in_=b2bc[0:1, :]), reads=[b2bc], writes=[b2h])
                kb.barrier()
            tmpo = [kb.sb(f"tmpo{i}", [128, 512], F32, es3) for i in range(2)]
            xs = [kb.sb(f"x3s{i}", [128, D], F32, es3) for i in range(2 * GRP)]
            xn = kb.sb("xn3", [128, D], BF16, es3)
            h2T_ = [kb.sb(f"h2T{k}", [128, 8, GRP * 128], BF16, es3) for k in range(2)]
            hid = kb.sb("hid", [128, 32, GRP * 128], BF16, es3)
            hidb = [Buf(hid.t, f"hid{j}") for j in range(32)]
            rl = [kb.sb(f"rl{i}", [128, GRP * 128], BF16, es3) for i in range(4)]
            stats_p = kb.sb("stats3p", [128, 2, 6], F32, es3)
            mv_p = kb.sb("mv3p", [128, 4], F32, es3)
            rstd_p = kb.sb("rstd3p", [128, 2], F32, es3)
            stats_e = kb.sb("stats3e", [128, 2, 6], F32, es3)
            mv_e = kb.sb("mv3e", [128, 4], F32, es3)
            rstd_e = kb.sb("rstd3e", [128, 2], F32, es3)
            NGRP = NT // GRP

            xn_3 = [xn] + [kb.sb(f"xn3_{a}", [128, D], BF16, es3) for a in range(1, GRP)]

            def prep3A(gi):
                for a in range(GRP):
                    ti = gi * GRP + a
                    xt = xs[(gi % 2) * GRP + a]
                    kb.dma(xt[:, 0:512], y_d[ti * 128:(ti + 1) * 128, 0:512], writes=[xt.sub(0)], sembuf=xt)
                    kb.dma(xt[:, 512:1024], y_d[ti * 128:(ti + 1) * 128, 512:1024], writes=[xt.sub(1)], sembuf=xt)
                    make_hT_A(xt, xn_3[a], stats_p, mv_p, rstd_p)

            def prep3B(gi):
                for a in range(GRP):
                    make_hT_B(h2T_[gi % 2], xn_3[a], 2, 3, tok0=a * 128)

            def main3(gi):
                h2T = h2T_[gi % 2]
                for j in range(32):
                    bank = PB[j % 2]
                    for kc in range(8):
                        kb.op(pe, lambda e, j=j, kc=kc, bank=bank: e.matmul(bank[:, 0:GRP * 128], lhsT=W1b[:, kc, j * 128:(j + 1) * 128],
                                                                            rhs=h2T[:, kc, :], start=(kc == 0), stop=(kc == 7)),
                              reads=[W1b.sub(j // 4)] + [h2T.sub((kc, a_ * 128)) for a_ in range(GRP)], writes=[bank], sig=(kc == 7))
                    rb = rl[j % 4]
                    kb.op(act, lambda e, j=j, bank=bank, rb=rb: e.activation(out=rb[:], in_=bank[:, 0:GRP * 128], func=AF.Relu,
                                                                             bias=smallc[:, 24 + j:25 + j], scale=1.0),
                          reads=[bank, smallc], writes=[rb])
                    eng = pool if j % 4 == 3 else dve
                    kb.op(eng, lambda e, j=j, rb=rb: e.tensor_tensor(out=hid[:, j, :], in0=rb[:], in1=rb[:], op=ALU.mult),
                          reads=[rb], writes=[hidb[j]])
                for a in range(GRP):
                    ti = gi * GRP + a
                    xt = xs[(gi % 2) * GRP + a]
                    for half in range(2):
                        bank = PB[2 + 2 * (a % 2) + half]
                        for j in range(32):
                            kb.op(pe, lambda e, j=j, a=a, half=half, bank=bank: e.matmul(
                                bank[:, :], lhsT=hid[:, j, a * 128:(a + 1) * 128], rhs=W2b[:, j, half * 512:(half + 1) * 512],
                                start=(j == 0), stop=False), reads=[hidb[j], W2b.sub(j // 4)], writes=[bank], sig=False)
                        for hl in range(2):
                            kb.op(pe, lambda e, hl=hl, half=half, bank=bank: e.matmul(
                                bank[:, :], lhsT=onesb[0:1, :], rhs=b2h[0:1, hl, half * 512:(half + 1) * 512],
                                start=False, stop=(hl == 1)), reads=[onesb, b2h], writes=[bank], sig=(hl == 1))
                        tm = tmpo[half]
                        kb.op(dve, lambda e, half=half, bank=bank, tm=tm: e.tensor_tensor(
                            out=tm[:], in0=bank[:, :], in1=g2bc[:, half * 512:(half + 1) * 512], op=ALU.mult),
                            reads=[bank, g2bc], writes=[tm])
                        kb.op(dve, lambda e, half=half, tm=tm, xt=xt: e.scalar_tensor_tensor(
                            out=xt[:, half * 512:(half + 1) * 512], in0=xt[:, half * 512:(half + 1) * 512], scalar=ALPHA,
                            in1=tm[:], op0=ALU.mult, op1=ALU.add), reads=[xt.sub(half), tm], writes=[xt.sub(half)])
                    ln_stats(xt[:, :], xt, 1024, stats_e, mv_e, rstd_e)
                    kb.op(dve, lambda e, xt=xt: e.scalar_tensor_tensor(out=xt[:, :], in0=xt[:, :], scalar=mv_e[:, 0:1], in1=ln2gb[:],
                                                                       op0=ALU.subtract, op1=ALU.mult), reads=[xt, mv_e, ln2gb], writes=[xt])
                    kb.op(dve, lambda e, xt=xt: e.scalar_tensor_tensor(out=xt[:, :], in0=xt[:, :], scalar=rstd_e[:, 0:1], in1=ln2bb[:],
                                                                       op0=ALU.mult, op1=ALU.add), reads=[xt, rstd_e, ln2bb], writes=[xt])
                    kb.dma(y_d[ti * 128:(ti + 1) * 128, :], xt[:, :], reads=[xt])

            for st in kb.record(prep3A, 0) + kb.record(prep3B, 0):
                st()
            for gi in range(NGRP):
                M = kb.record(main3, gi)
                PA = kb.record(prep3A, gi + 1) if gi + 1 < NGRP else []
                PB_ = kb.record(prep3B, gi + 1) if gi + 1 < NGRP else []
                nM = len(M)
                a0, a1 = int(nM * 0.02), int(nM * 0.35)
                b0, b1 = int(nM * 0.55), int(nM * 0.90)
                ia = ib = 0
                for k, st in enumerate(M):
                    st()
                    if k >= a0 and PA:
                        want = min(len(PA), ((k - a0 + 1) * len(PA)) // max(1, a1 - a0))
                        while ia < want:
                            PA[ia]()
                            ia += 1
                    if k >= b0 and PB_:
                        want = min(len(PB_), ((k - b0 + 1) * len(PB_)) // max(1, b1 - b0))
                        while ib < want:
                            PB_[ib]()
                            ib += 1
                for st in PA[ia:] + PB_[ib:]:
                    st()
            kb.barrier()
    return nc, dbg_outs


_CACHE = {}


def make_in_maps(inputs):
    g = lambda k: np.ascontiguousarray(np.asarray(inputs[k], dtype=np.float32))
    shared = {
        "c_ctx": g("c_ctx").reshape(8, 128),
        "w_ada": g("w_ada")[0],
        "b_ada": g("b_ada")[0].reshape(48, 128),
        "w_in": g("w_in")[0],
        "w_s": g("w_s")[0],
        "b_s": g("b_s")[0].reshape(1, 512),
        "ln_v_g": g("ln_v_g")[0].reshape(4, 128),
        "ln_v_b": g("ln_v_b")[0].reshape(4, 128),
        "conv_qk": g("conv_qk")[0].reshape(12, 128),
        "b_gates": g("b_gates")[0].reshape(1, 16),
        "hn_g": g("hn_g")[0].reshape(4, 128),
        "w_out": g("w_out")[0],
        "ln1_g": g("ln1_g")[0].reshape(1, D),
        "ln1_b": g("ln1_b")[0].reshape(1, D),
        "w1": g("w1")[0],
        "b1": g("b1")[0].reshape(32, 128),
        "w2": g("w2")[0],
        "b2": g("b2")[0].reshape(1, D),
        "ln2_g": g("ln2_g")[0].reshape(1, D),
        "ln2_b": g("ln2_b")[0].reshape(1, D),
    }
    x, c, ctx = g("x"), g("c"), g("ctx")
    maps = []
    for b in range(x.shape[0]):
        m = dict(shared)
        m["x"] = x[b]
        m["c"] = c[b].reshape(8, 128)
        m["ctx"] = ctx[b]
        maps.append(m)
    return maps


def kernel(**inputs):
    if "nc" not in _CACHE:
        _CACHE["nc"] = build_program(False)[0]
    nc = _CACHE["nc"]
    maps = make_in_maps(inputs)
    n = len(maps)
    res = run_bass_kernel_spmd(nc, maps, core_ids=list(range(n)))
    out = np.stack([np.asarray(r["y"], dtype=np.float32) for r in res.results], axis=0)
    return out
```

```python
import math
from contextlib import ExitStack
import numpy as np
import concourse.bass as bass
import concourse.mybir as mybir
from concourse.bass_utils import run_bass_kernel_spmd

F32 = mybir.dt.float32
BF16 = mybir.dt.bfloat16
AF = mybir.ActivationFunctionType
ALU = mybir.AluOpType

D = 1024
S = 4096
CTX = 256
NT = S // 128
NCT = CTX // 128
NG = NT + NCT
DIN = 2576
DFF = 4096
ALPHA = 2.0 ** 0.25
EPS = 1e-5
SEM_LIMIT = 3000
SKEW1 = 0.55
MERGE2 = 2
SKEW2 = 0.55


class Tok:
    __slots__ = ("sem", "val", "key")

    def __init__(self, sem, val, key):
        self.sem, self.val, self.key = sem, val, key


class Buf:
    def __init__(self, t, name, parent=None):
        self.t = t
        self.name = name
        self.w = None
        self.r = {}
        self.dsem = None
        self.dcount = 0
        self.parent = parent
        self.children = {}

    def sub(self, key):
        c = self.children.get(key)
        if c is None:
            c = Buf(self.t, f"{self.name}.{key}", parent=self)
            self.children[key] = c
        return c

    def __getitem__(self, idx):
        return self.t[idx]


class Eng:
    def __init__(self, kb, name, h):
        self.kb, self.name, self.h = kb, name, h
        self.seen = {}
        self.epoch = 0
        self.count = 0
        self.sem = kb.new_sem(f"{name}_e0")
        self.pending = False

    def roll(self):
        if self.count >= SEM_LIMIT and not self.pending:
            self.epoch += 1
            self.count = 0
            self.sem = self.kb.new_sem(f"{self.name}_e{self.epoch}")


class KB:
    def __init__(self, nc, es):
        self.nc, self.es = nc, es
        self.nsem = 0
        self.pe = Eng(self, "pe", nc.tensor)
        self.act = Eng(self, "act", nc.scalar)
        self.dve = Eng(self, "dve", nc.vector)
        self.pool = Eng(self, "pool", nc.gpsimd)
        self.sp = Eng(self, "sp", nc.sync)
        self.engs = [self.pe, self.act, self.dve, self.pool, self.sp]
        self.dma_toks = []

    def new_sem(self, name):
        self.nsem += 1
        s = self.es.enter_context(self.nc.semaphore(name))
        return (s, name)

    def sb(self, name, shape, dt, es=None):
        t = (es or self.es).enter_context(self.nc.sbuf_tensor(name, list(shape), dt))
        return Buf(t, name)

    def ps(self, name, shape, dt, es=None):
        t = (es or self.es).enter_context(self.nc.psum_tensor(name, list(shape), dt))
        b = Buf(t, name)
        b.psum = True
        return b

    def wait(self, eng, tok):
        if tok is None:
            return
        if eng.name == "pe" and tok.key.startswith("pe_e"):
            return
        if eng.seen.get(tok.key, 0) >= tok.val:
            return
        eng.h.wait_ge(tok.sem, tok.val)
        eng.seen[tok.key] = tok.val

    def _deps(self, eng, reads, writes):
        need = {}

        def add(t):
            if t is None:
                return
            o = need.get(t.key)
            if o is None or o.val < t.val:
                need[t.key] = t

        for b in reads:
            add(b.w)
            if getattr(b, "psum", False):
                for k_, t_ in b.r.items():
                    if not k_.startswith(eng.name + "_e"):
                        add(t_)
            if b.parent is not None:
                add(b.parent.w)
            for c in b.children.values():
                add(c.w)
                for c2 in c.children.values():
                    add(c2.w)
        for b in writes:
            add(b.w)
            for t in b.r.values():
                add(t)
            if b.parent is not None:
                add(b.parent.w)
                for t in b.parent.r.values():
                    add(t)
            for c in b.children.values():
                add(c.w)
                for t in c.r.values():
                    add(t)
                for c2 in c.children.values():
                    add(c2.w)
                    for t in c2.r.values():
                        add(t)
        for t in need.values():
            self.wait(eng, t)

    def _mark(self, tok, reads, writes):
        for b in reads:
            old = b.r.get(tok.key)
            if old is None or old.val < tok.val:
                b.r[tok.key] = tok
        for b in writes:
            b.w = tok
            b.r = {}

    def record(self, body, *args):
        outer = getattr(self, "rec", None)
        self.rec = []
        body(*args)
        r = self.rec
        self.rec = outer
        return r

    def op(self, eng, fn, reads=(), writes=(), sig=True):
        if getattr(self, "rec", None) is not None:
            self.rec.append(lambda: self._op(eng, fn, reads, writes, sig))
            return None
        return self._op(eng, fn, reads, writes, sig)

    def _op(self, eng, fn, reads=(), writes=(), sig=True):
        if sig:
            eng.roll()
        self._deps(eng, reads, writes)
        inst = fn(eng.h)
        if sig:
            eng.count += 1
            inst.then_inc(eng.sem[0], 1)
            tok = Tok(eng.sem[0], eng.count, eng.sem[1])
            eng.pending = False
        else:
            tok = Tok(eng.sem[0], eng.count + 1, eng.sem[1])
            eng.pending = True
        self._mark(tok, reads, writes)
        return tok

    def dma(self, out_ap, in_ap, reads=(), writes=(), sembuf=None, eng=None, after=()):
        if getattr(self, "rec", None) is not None:
            self.rec.append(lambda: self._dma(out_ap, in_ap, reads, writes, sembuf, eng, after))
            return None
        return self._dma(out_ap, in_ap, reads, writes, sembuf, eng, after)

    def cast_load(self, dst_buf, pieces, depth=2):
        toks = []
        for piece in pieces:
            out_ap, in_ap, wr = piece[:3]
            semb = piece[3] if len(piece) > 3 else dst_buf
            after = [toks[-depth]] if len(toks) >= depth else []
            toks.append(self._dma(out_ap, in_ap, (), [wr], semb, self.pool, after))
        return toks

    def _dma(self, out_ap, in_ap, reads=(), writes=(), sembuf=None, eng=None, after=()):
        eng = eng or self.sp
        for t_ in after:
            self.wait(eng, t_)
        sb_ = sembuf or (writes[0] if writes else reads[0])
        if sb_.dsem is None:
            sb_.dsem = self.new_sem(f"d_{sb_.name}")
        self._deps(eng, reads, writes)
        inst = eng.h.dma_start(out=out_ap, in_=in_ap)
        inst.then_inc(sb_.dsem[0], 16)
        sb_.dcount += 16
        tok = Tok(sb_.dsem[0], sb_.dcount, sb_.dsem[1])
        self._mark(tok, reads, writes)
        self.dma_toks.append(tok)
        return tok

    def barrier(self):
        toks = []
        for e in self.engs:
            if e.count > 0:
                assert not e.pending
                toks.append(Tok(e.sem[0], e.count, e.sem[1]))
        toks += self.dma_toks
        self.dma_toks = []
        for e in self.engs:
            for t in toks:
                self.wait(e, t)


class _Stop(Exception):
    pass


def interleave(step_lists, H):
    n = len(step_lists)
    T = max(i * H + len(sl) for i, sl in enumerate(step_lists))
    lo = 0
    for t in range(T):
        while lo < n and t - lo * H >= len(step_lists[lo]):
            lo += 1
        i = lo
        while i < n and t - i * H >= 0:
            k = t - i * H
            if k < len(step_lists[i]):
                step_lists[i][k]()
            i += 1


def build_program(dbg=False, stage=99):
    try:
        return _build_program(dbg, stage)
    except _Stop as ex:
        return ex.args[0]


def _build_program(dbg, stage):
    nc = bass.Bass("TRN2", target_bir_lowering=False)

    def din(name, shape):
        return nc.dram_tensor(name, list(shape), F32, kind="ExternalInput").ap()

    x_d = din("x", [S, D])
    c_d = din("c", [8, 128])
    ctx_d = din("ctx", [CTX, D])
    cctx_d = din("c_ctx", [8, 128])
    wada_d = din("w_ada", [D, 6 * D])
    bada_d = din("b_ada", [48, 128])
    win_d = din("w_in", [D, DIN])
    ws_d = din("w_s", [4, 128, 128])
    bs_d = din("b_s", [1, 512])
    lnvg_d = din("ln_v_g", [4, 128])
    lnvb_d = din("ln_v_b", [4, 128])
    conv_d = din("conv_qk", [12, 128])
    bg_d = din("b_gates", [1, 16])
    hng_d = din("hn_g", [4, 128])
    wout_d = din("w_out", [D, D])
    ln1g_d = din("ln1_g", [1, D])
    ln1b_d = din("ln1_b", [1, D])
    w1_d = din("w1", [D, DFF])
    b1_d = din("b1", [32, 128])
    w2_d = din("w2", [DFF, D])
    b2_d = din("b2", [1, D])
    ln2g_d = din("ln2_g", [1, D])
    ln2b_d = din("ln2_b", [1, D])
    y_d = nc.dram_tensor("y", [S, D], F32, kind="ExternalOutput").ap()
    dbg_outs = {}

    with ExitStack() as es:
        kb = KB(nc, es)
        pe, act, dve, pool, sp = kb.pe, kb.act, kb.dve, kb.pool, kb.sp

        PB = [kb.ps(f"pb{i}", [128, 512], F32) for i in range(7)]
        PT = kb.ps("pt", [128, 1024], BF16)

        identf = kb.sb("identf", [128, 128], F32)
        identb = kb.sb("identb", [128, 128], BF16)
        LT = kb.sb("LT", [128, 128], F32)
        UT = kb.sb("UT", [128, 128], F32)
        onesf = kb.sb("onesf", [128, 128], F32)
        onesb = kb.sb("onesb", [128, 128], BF16)
        cst = kb.sb("cst", [128, 8], F32)
        modc = kb.sb("modc", [128, 6, 8], F32)
        smallc = kb.sb("smallc", [128, 64], F32)
        bgb = kb.sb("bgb", [128, 16], F32)
        BiasA = kb.sb("BiasA", [128, 4, 128], F32)
        wsT = kb.sb("wsT", [128, 4, 128], BF16)
        setup = Buf(None, "setup")
        ccol = kb.sb("ccol", [128, 2, 8], F32)
        badac = kb.sb("badac", [128, 48], F32)

        def dbg_out(name, buf, ap, shape, dt=F32):
            if not dbg:
                return
            o = nc.dram_tensor("dbg_" + name, list(shape), dt, kind="ExternalOutput").ap()
            dbg_outs[name] = (shape, dt)
            kb.dma(o, ap, reads=[buf], sembuf=buf)

        kb.op(pool, lambda e: e.memset(onesf[:], 1.0), writes=[onesf])
        kb.op(pool, lambda e: e.memset(onesb[:], 1.0), writes=[onesb])
        kb.op(pool, lambda e: e.memset(cst[:, 0:1], -0.5), writes=[cst])
        kb.op(pool, lambda e: e.memset(cst[:, 1:2], EPS), writes=[cst])
        kb.op(pool, lambda e: e.memset(cst[:, 2:3], math.log(8.0)), writes=[cst])
        kb.op(pool, lambda e: e.memset(cst[:, 3:4], 1.0), writes=[cst])
        kb.op(pool, lambda e: e.affine_select(out=identf[:], in_=onesf[:], pattern=[[-1, 128]], compare_op=ALU.is_equal,
                                              fill=0.0, base=0, channel_multiplier=1), reads=[onesf], writes=[identf])
        kb.op(pool, lambda e: e.affine_select(out=LT[:], in_=onesf[:], pattern=[[1, 128]], compare_op=ALU.is_ge,
                                              fill=0.0, base=0, channel_multiplier=-1), reads=[onesf], writes=[LT])
        kb.op(pool, lambda e: e.affine_select(out=UT[:], in_=onesf[:], pattern=[[-1, 128]], compare_op=ALU.is_ge,
                                              fill=0.0, base=0, channel_multiplier=1), reads=[onesf], writes=[UT])
        kb.op(dve, lambda e: e.tensor_copy(out=identb[:], in_=identf[:]), reads=[identf], writes=[identb])

        es_p = es.enter_context(ExitStack())
        qT = kb.sb("qT", [128, 2, S], BF16, es_p)
        kT = kb.sb("kT", [128, 2, NG * 128], BF16, es_p)
        vaug = kb.sb("vaug", [128, NG, 4, 130], BF16, es_p)
        WS = kb.sb("WS", [128, NG, 8], F32, es_p)
        LB = kb.sb("LB", [128, NG, 8], F32, es_p)
        ST = kb.sb("ST", [128, NT, 2, 2, 130], BF16, es_p)
        kb.op(pool, lambda e: e.memset(vaug[:, :, :, 128:130], 1.0), writes=[vaug])

        es0 = es.enter_context(ExitStack())
        ln1gb = kb.sb("ln1gb", [128, D], F32, es0)
        ln1bb = kb.sb("ln1bb", [128, D], F32, es0)
        es_gc = es.enter_context(ExitStack())
        Gt = kb.sb("Gt", [128, NG, 16], F32, es_gc)
        CW = kb.sb("CW", [128, NG, 8], F32, es_gc)
        es_set = es.enter_context(ExitStack())
        rows = kb.sb("rows", [128, 256], F32, es_set)
        bsb = kb.sb("bsb", [128, 512], F32, es_set)
        R_C, R_CC, R_BADA, R_CONV, R_GV, R_BV, R_HNG, R_B1 = 0, 8, 16, 64, 76, 80, 84, 88
        kb.dma(rows[0:8, 0:128], c_d[:, :], writes=[rows])
        kb.dma(rows[0:8, 128:256], cctx_d[:, :], writes=[rows])
        rows2 = kb.sb("rows2", [128, 128], F32, es_set)
        kb.dma(rows2[0:48, :], bada_d[:, :], writes=[rows2])
        rows3 = kb.sb("rows3", [128, 128], F32, es_set)
        kb.dma(rows3[0:12, :], conv_d[:, :], writes=[rows3])
        kb.dma(rows3[32:36, :], lnvg_d[:, :], writes=[rows3])
        kb.dma(rows3[64:68, :], lnvb_d[:, :], writes=[rows3])
        rows4 = kb.sb("rows4", [128, 128], F32, es_set)
        kb.dma(rows4[0:4, :], hng_d[:, :], writes=[rows4])
        kb.dma(rows4[32:64, :], b1_d[:, :], writes=[rows4])
        kb.dma(bgb[:], bg_d[0:1, :].to_broadcast([128, 16]), writes=[bgb])
        kb.dma(bsb[:], bs_d[0:1, :].to_broadcast([128, 512]), writes=[bsb])
        wsr = kb.sb("wsr", [128, 4, 128], F32, es_set)
        kb.dma(wsr[:], ws_d.rearrange("g t s -> t g s"), writes=[wsr])


        def tr_f32(dst_ap, dst_buf, src_ap, src_buf, n, bank, p0=0):
            kb.op(pe, lambda e: e.transpose(out=bank[:, 0:n], in_=src_ap, identity=identf[p0:p0 + n, p0:p0 + n]),
                  reads=[src_buf, identf], writes=[bank])
            kb.op(dve, lambda e: e.tensor_copy(out=dst_ap, in_=bank[:, 0:n]), reads=[bank], writes=[dst_buf])

        craw = kb.sb("craw", [128, 2, 8], F32, es_set)
        tr_f32(craw[:, 0, :], craw, rows[0:8, 0:128], rows, 8, PB[0])
        tr_f32(craw[:, 1, :], craw, rows[0:8, 128:256], rows, 8, PB[1])
        kb.op(act, lambda e: e.activation(out=ccol[:], in_=craw[:], func=AF.Silu), reads=[craw], writes=[ccol])
        tr_f32(badac[:, :], badac, rows2[0:48, :], rows2, 48, PB[2])
        tr_f32(smallc[:, 12:24], smallc, rows3[0:12, :], rows3, 12, PB[3])
        tr_f32(smallc[:, 0:4], smallc, rows3[32:36, :], rows3, 4, PB[4], p0=32)
        tr_f32(smallc[:, 4:8], smallc, rows3[64:68, :], rows3, 4, PB[5], p0=64)
        tr_f32(smallc[:, 8:12], smallc, rows4[0:4, :], rows4, 4, PB[6])
        tr_f32(smallc[:, 24:56], smallc, rows4[32:64, :], rows4, 32, PB[0], p0=32)
        wsTf = kb.sb("wsTf", [128, 4, 128], F32, es_set)
        for g in range(4):
            kb.op(pe, lambda e, g=g: e.transpose(out=PB[1][:, g * 128:(g + 1) * 128], in_=wsr[:, g, :], identity=identf[:]),
                  reads=[wsr, identf], writes=[PB[1]])
        kb.op(dve, lambda e: e.tensor_copy(out=wsTf[:].rearrange("p g t -> p (g t)"), in_=PB[1][:, :]), reads=[PB[1]], writes=[wsTf])
        kb.op(act, lambda e: e.activation(out=wsT[:], in_=wsTf[:], func=AF.Copy), reads=[wsTf], writes=[wsT])
        kb.op(pe, lambda e: e.matmul(PB[2][:, :], lhsT=onesf[:], rhs=wsTf[:].rearrange("p g t -> p (g t)"), start=True, stop=True),
              reads=[onesf, wsTf], writes=[PB[2]])
        for g in range(4):
            kb.op(dve, lambda e, g=g: e.scalar_tensor_tensor(out=BiasA[:, g, :], in0=PB[2][:, g * 128:(g + 1) * 128],
                                                             scalar=smallc[:, 4 + g:5 + g], in1=bsb[:, g * 128:(g + 1) * 128],
                                                             op0=ALU.mult, op1=ALU.add),
                  reads=[PB[2], smallc, bsb], writes=[BiasA])

        if stage == -1:
            dbg_out("smallc", smallc, smallc[:], [128, 64])
            dbg_out("BiasA", BiasA, BiasA[:], [128, 4, 128])
            dbg_out("ccol", ccol, ccol[:], [128, 2, 8])
            dbg_out("LT", LT, LT[:], [128, 128])
            dbg_out("identf", identf, identf[:], [128, 128])
            kb.barrier()
            raise _Stop((nc, dbg_outs))
        kb.barrier()
        es_set.close()
        kb.dma(ln1gb[:], ln1g_d[0:1, :].to_broadcast([128, D]), writes=[ln1gb])
        kb.dma(ln1bb[:], ln1b_d[0:1, :].to_broadcast([128, D]), writes=[ln1bb])
        g2row = nc.dram_tensor("g2scratch", [128, D], F32, kind="Internal").ap()
        g1row = nc.dram_tensor("g1scratch", [128, D], F32, kind="Internal").ap()

        with ExitStack() as esa:
            stg = [kb.sb(f"astg{i}", [128, 8, 512], BF16, esa) for i in range(4)]
            scb = kb.sb("scb", [128, 8, 128], BF16, esa)
            ccolb = kb.sb("ccolb", [128, 2, 8], BF16, esa)
            kb.op(dve, lambda e: e.tensor_copy(out=ccolb[:], in_=ccol[:]), reads=[ccol], writes=[ccolb])
            badab = kb.sb("badab", [128, 2, D], F32, esa)
            g2bc0 = kb.sb("g2bc0", [128, D], F32, esa)
            g1bc = kb.sb("g1bc0", [128, D], F32, esa)
            kb.dma(badab[:, 0, :], bada_d[16:24, :].rearrange("(o a) b -> o (a b)", o=1).to_broadcast([128, D]), writes=[badab])
            kb.dma(badab[:, 1, :], bada_d[40:48, :].rearrange("(o a) b -> o (a b)", o=1).to_broadcast([128, D]), writes=[badab])
            for kc in range(8):
                kb.op(dve, lambda e, kc=kc: e.tensor_copy(out=scb[:, kc, :], in_=ccol[:, 0, kc:kc + 1].to_broadcast([128, 128])),
                      reads=[ccol], writes=[scb])
            wada_v = wada_d.rearrange("(kc p) n -> p kc n", p=128)
            col_kind = {0: 0, 1: 0, 2: 1, 3: 1, 6: 2, 7: 2, 8: 3, 9: 3}
            for blk in range(12):
                sg = stg[blk % 4]
                kb.dma(sg[:, 0:4, :], wada_v[:, 0:4, blk * 512:(blk + 1) * 512], writes=[sg], eng=pool)
                kb.dma(sg[:, 4:8, :], wada_v[:, 4:8, blk * 512:(blk + 1) * 512], writes=[sg], eng=pool)
                if blk in col_kind:
                    mi = col_kind[blk]
                    for jj in range(4):
                        j = blk * 4 + jj
                        fchunk = j % 8
                        bank = PB[jj % 4]
                        for kc in range(8):
                            kb.op(pe, lambda e, kc=kc, jj=jj, bank=bank: e.matmul(
                                bank[:, 0:2], lhsT=sg[:, kc, jj * 128:(jj + 1) * 128], rhs=ccolb[:, :, kc],
                                start=(kc == 0), stop=(kc == 7)), reads=[sg, ccolb], writes=[bank], sig=(kc == 7))
                        kb.op(dve, lambda e, bank=bank, mi=mi, fchunk=fchunk, j=j: e.tensor_tensor(
                            out=modc[:, mi, fchunk:fchunk + 1], in0=bank[:, 0:1], in1=badac[:, j:j + 1], op=ALU.add),
                            reads=[bank, badac], writes=[modc.sub((mi, fchunk))])
                        if mi < 2:
                            kb.op(dve, lambda e, bank=bank, mi=mi, fchunk=fchunk, j=j: e.tensor_tensor(
                                out=modc[:, 4 + mi, fchunk:fchunk + 1], in0=bank[:, 1:2], in1=badac[:, j:j + 1], op=ALU.add),
                                reads=[bank, badac], writes=[modc.sub((4 + mi, fchunk))])
                else:
                    which = 0 if blk in (4, 5) else 1
                    half = blk % 2 if which == 1 else blk - 4
                    bank = PB[4 + (blk % 2)]
                    for kc in range(8):
                        kb.op(pe, lambda e, kc=kc, bank=bank: e.matmul(bank[:, :], lhsT=scb[:, kc, :], rhs=sg[:, kc, :],
                                                                      start=(kc == 0), stop=(kc == 7)),
                              reads=[sg, scb], writes=[bank], sig=(kc == 7))
                    dst = g1bc if which == 0 else g2bc0
                    kb.op(dve, lambda e, bank=bank, dst=dst, half=half, which=which: e.tensor_tensor(
                        out=dst[:, half * 512:(half + 1) * 512], in0=bank[:, :], in1=badab[:, which, half * 512:(half + 1) * 512],
                        op=ALU.add), reads=[bank, badab], writes=[dst])
            for mi in (1, 3, 5):
                kb.op(dve, lambda e, mi=mi: e.tensor_scalar(out=modc[:, mi, :], in0=modc[:, mi, :], scalar1=1.0, scalar2=None,
                                                            op0=ALU.add), reads=[modc], writes=[modc])
            g2st = kb.dma(g2row[:, :], g2bc0[:], reads=[g2bc0])
            kb.dma(g1row[:, :], g1bc[:], reads=[g1bc])
            kb.barrier()
            if stage == 0:
                dbg_out("modc", modc, modc[:], [128, 6, 8])
                dbg_out("g1bc", g1bc, g1bc[:], [128, D])
                dbg_out("smallc", smallc, smallc[:], [128, 64])
                dbg_out("BiasA", BiasA, BiasA[:], [128, 4, 128])
                kb.barrier()
                raise _Stop((nc, dbg_outs))

        def ln_stats(xap, xbuf, width, stats, mv, rstd, nmr=None):
            nchunk = width // 512
            for cidx in range(nchunk):
                kb.op(dve, lambda e, cidx=cidx: e.bn_stats(out=stats[:, cidx, :], in_=xap[:, cidx * 512:(cidx + 1) * 512]),
                      reads=[xbuf], writes=[stats.sub(cidx)])
            kb.op(dve, lambda e: e.bn_aggr(out=mv[:, 0:2], in_=stats[:, 0:nchunk, :].rearrange("p a b -> p (a b)")),
                  reads=[stats], writes=[mv])
            kb.op(pool, lambda e: e.tensor_tensor(out=mv[:, 2:3], in0=mv[:, 1:2], in1=cst[:, 1:2], op=ALU.add),
                  reads=[mv, cst], writes=[mv])
            kb.op(pool, lambda e: e.tensor_tensor(out=rstd[:, 0:1], in0=mv[:, 2:3], in1=cst[:, 0:1], op=ALU.pow),
                  reads=[mv, cst], writes=[rstd])
            if nmr is not None:
                kb.op(dve, lambda e: e.scalar_tensor_tensor(out=rstd[:, 1:2], in0=mv[:, 0:1], scalar=-1.0, in1=rstd[:, 0:1],
                                                            op0=ALU.mult, op1=ALU.mult), reads=[mv, rstd], writes=[rstd])

        def make_hT(xt, hT, xn, stats, mv, rstd, mi_shift, mi_scale, tok0=0):
            make_hT_A(xt, xn, stats, mv, rstd)
            make_hT_B(hT, xn, mi_shift, mi_scale, tok0)

        def make_hT_A(xt, xn, stats, mv, rstd):
            ln_stats(xt[:, :], xt, 1024, stats, mv, rstd, nmr=True)
            kb.op(act, lambda e: e.activation(out=xn[:], in_=xt[:, :], func=AF.Identity, bias=rstd[:, 1:2], scale=rstd[:, 0:1]),
                  reads=[xt, rstd], writes=[xn])

        def make_hT_B(hT, xn, mi_shift, mi_scale, tok0=0):
            for kc in range(8):
                kb.op(pe, lambda e, kc=kc: e.transpose(out=PT[:, kc * 128:(kc + 1) * 128], in_=xn[:, kc * 128:(kc + 1) * 128],
                                                      identity=identb[:]), reads=[xn, identb], writes=[PT], sig=(kc == 7))
            for kc in range(8):
                kb.op(act, lambda e, kc=kc: e.activation(out=hT[:, kc, tok0:tok0 + 128], in_=PT[:, kc * 128:(kc + 1) * 128],
                                                         func=AF.Identity, bias=modc[:, mi_shift, kc:kc + 1],
                                                         scale=modc[:, mi_scale, kc:kc + 1]),
                      reads=[PT, modc], writes=[hT.sub((kc, tok0))])

        def load_weight_block(dst, dst_ap_fn, src_ap, stg_buf, nk, scale_bc=None, scale_cols=None):
            half = nk // 2
            kb.dma(stg_buf[:, 0:half, :], src_ap[:, 0:half, :], writes=[stg_buf])
            kb.dma(stg_buf[:, half:nk, :], src_ap[:, half:nk, :], writes=[stg_buf])
            if scale_bc is None:
                kb.op(act, lambda e: e.activation(out=dst_ap_fn(slice(0, half)), in_=stg_buf[:, 0:half, :], func=AF.Copy),
                      reads=[stg_buf], writes=[dst])
                kb.op(pool, lambda e: e.tensor_copy(out=dst_ap_fn(slice(half, nk)), in_=stg_buf[:, half:nk, :]),
                      reads=[stg_buf], writes=[dst])
            else:
                for k in range(nk):
                    eng = dve if k % 2 == 0 else pool
                    kb.op(eng, lambda e, k=k: e.tensor_tensor(out=dst_ap_fn(k), in0=stg_buf[:, k, :],
                                                              in1=scale_bc[:, scale_cols], op=ALU.mult),
                          reads=[stg_buf, scale_bc], writes=[dst])

        win_v = win_d.rearrange("(kc p) n -> p kc n", p=128)
        with ExitStack() as es1:
            Win1 = kb.sb("Win1", [128, 8, 1040], BF16, es1)
            kb.cast_load(Win1, [(Win1[:, k0:k0 + 4, c0:c0 + 512], win_v[:, k0:k0 + 4, w0:w0 + 512], Win1.sub((k0, c0)))
                                for (c0, w0) in ((0, 1024), (512, 1536)) for k0 in (0, 4)]
                         + [(Win1[:, :, 1024:1040], win_v[:, :, 2560:2576], Win1.sub("g"))])
            xs = [kb.sb(f"xs{i}", [128, D], F32, es1) for i in range(2)]
            xn = kb.sb("xn", [128, D], BF16, es1)
            hT = kb.sb("hT", [128, 8, 128], BF16, es1)
            stats = kb.sb("stats", [128, 2, 6], F32, es1)
            mv = kb.sb("mv", [128, 4], F32, es1)
            rstd = kb.sb("rstd", [128, 2], F32, es1)
            raw = [kb.sb(f"raw{i}", [128, 4, 130], F32, es1) for i in range(3)]
            cvt = kb.sb("cvt", [128, 4, 128], F32, es1)

            def conv_finish(gc_prev, rb):
                for cc in range(4):
                    kb.op(dve, lambda e, cc=cc: e.tensor_scalar(out=cvt[:, cc, :], in0=rb[:, cc, 0:128],
                                                                scalar1=smallc[:, 12 + 0 * 4 + cc:13 + 0 * 4 + cc], scalar2=None,
                                                                op0=ALU.mult), reads=[rb, smallc], writes=[cvt.sub(cc)])
                    kb.op(dve, lambda e, cc=cc: e.scalar_tensor_tensor(out=cvt[:, cc, :], in0=rb[:, cc, 1:129],
                                                                       scalar=smallc[:, 12 + 1 * 4 + cc:13 + 1 * 4 + cc],
                                                                       in1=cvt[:, cc, :], op0=ALU.mult, op1=ALU.add),
                          reads=[rb, smallc, cvt.sub(cc)], writes=[cvt.sub(cc)])
                    kb.op(dve, lambda e, cc=cc: e.scalar_tensor_tensor(out=cvt[:, cc, :], in0=rb[:, cc, 2:130],
                                                                       scalar=smallc[:, 12 + 2 * 4 + cc:13 + 2 * 4 + cc],
                                                                       in1=cvt[:, cc, :], op0=ALU.mult, op1=ALU.add),
                          reads=[rb, smallc, cvt.sub(cc)], writes=[cvt.sub(cc)])
                t0 = gc_prev * 128
                kb.op(act, lambda e: e.activation(out=kT[:, :, t0:t0 + 128], in_=cvt[:, 2:4, :], func=AF.Silu),
                      reads=[cvt.sub(2), cvt.sub(3)], writes=[kT])
                if gc_prev >= NCT:
                    l0 = (gc_prev - NCT) * 128
                    kb.op(act, lambda e: e.activation(out=qT[:, :, l0:l0 + 128], in_=cvt[:, 0:2, :], func=AF.Silu),
                          reads=[cvt.sub(0), cvt.sub(1)], writes=[qT])

            xn_1 = [xn, kb.sb("xn_b", [128, D], BF16, es1)]
            hT_1 = [hT, kb.sb("hT_b", [128, 8, 128], BF16, es1)]
            stats_1 = [stats, kb.sb("stats_b", [128, 2, 6], F32, es1)]
            mv_1 = [mv, kb.sb("mv_b", [128, 4], F32, es1)]
            rstd_1 = [rstd, kb.sb("rstd_b", [128, 2], F32, es1)]

            xs3 = xs + [kb.sb("xs_c", [128, D], F32, es1)]
            xn3 = xn_1 + [kb.sb("xn_c", [128, D], BF16, es1)]
            stA = stats_1 + [kb.sb("stats_c", [128, 2, 6], F32, es1)]
            mvA = mv_1 + [kb.sb("mv_c", [128, 4], F32, es1)]
            rsA = rstd_1 + [kb.sb("rstd_c", [128, 2], F32, es1)]

            def tile1A(gc):
                if gc >= NG:
                    return
                is_ctx = gc < NCT
                src = ctx_d[gc * 128:(gc + 1) * 128, :] if is_ctx else x_d[(gc - NCT) * 128:(gc - NCT + 1) * 128, :]
                xt = xs3[gc % 3]
                kb.dma(xt[:, :], src[:, :], writes=[xt])
                make_hT_A(xt, xn3[gc % 3], stA[gc % 3], mvA[gc % 3], rsA[gc % 3])

            def tile1(gc):
                tile1A(gc + 1)
                sl = gc % 2
                hT = hT_1[sl]
                PBq, PBv, PBg = PB[3 * sl], PB[3 * sl + 1], PB[3 * sl + 2]
                is_ctx = gc < NCT
                make_hT_B(hT, xn3[gc % 3], 4 if is_ctx else 0, 5 if is_ctx else 1)
                for cc in range(4):
                    for kc in range(8):
                        kb.op(pe, lambda e, cc=cc, kc=kc: e.matmul(PBq[:, cc * 128:(cc + 1) * 128],
                                                                   lhsT=Win1[:, kc, cc * 128:(cc + 1) * 128], rhs=hT[:, kc, :],
                                                                   start=(kc == 0), stop=(kc == 7)),
                              reads=[Win1, hT.sub((kc, 0))], writes=[PBq], sig=(kc == 7 and cc == 3))
                for kc in range(8):
                    kb.op(pe, lambda e, kc=kc: e.matmul(PBv[:, :], lhsT=hT[:, kc, :], rhs=Win1[:, kc, 512:1024],
                                                        start=(kc == 0), stop=(kc == 7)),
                          reads=[Win1, hT.sub((kc, 0))], writes=[PBv], sig=(kc == 7))
                for kc in range(8):
                    kb.op(pe, lambda e, kc=kc: e.matmul(PBg[:, 0:16], lhsT=hT[:, kc, :], rhs=Win1[:, kc, 1024:1040],
                                                        start=(kc == 0), stop=(kc == 7)),
                          reads=[Win1, hT.sub((kc, 0))], writes=[PBg], sig=(kc == 7))
                rb = raw[gc % 3]
                first = gc in (0, NCT)
                last = gc in (NCT - 1, NG - 1)
                kb.op(act, lambda e: e.activation(out=rb[:, :, 1:129], in_=PBq[:, :].rearrange("p (c t) -> p c t", c=4), func=AF.Copy),
                      reads=[PBq], writes=[rb])
                kb.op(dve, lambda e: e.tensor_copy(out=vaug[:, gc, :, 0:128], in_=PBv[:, :].rearrange("p (h v) -> p h v", h=4)),
                      reads=[PBv], writes=[vaug])
                kb.op(dve, lambda e: e.tensor_tensor(out=Gt[:, gc, :], in0=PBg[:, 0:16], in1=bgb[:], op=ALU.add),
                      reads=[PBg, bgb], writes=[Gt])
                if first:
                    kb.op(pool, lambda e: e.memset(rb[:, :, 0:1], 0.0), writes=[rb])
                else:
                    rprev = raw[(gc - 1) % 3]
                    kb.op(pool, lambda e: e.tensor_copy(out=rb[:, :, 0:1], in_=rprev[:, :, 128:129]), reads=[rprev], writes=[rb])
                    kb.op(pool, lambda e: e.tensor_copy(out=rprev[:, :, 129:130], in_=rb[:, :, 1:2]), reads=[rb], writes=[rprev])
                    conv_finish(gc - 1, rprev)
                if last:
                    kb.op(pool, lambda e: e.memset(rb[:, :, 129:130], 0.0), writes=[rb])
                    conv_finish(gc, rb)
            tile1A(0)
            lists1 = [kb.record(tile1, gc) for gc in range(NG)]
            interleave(lists1, int(len(lists1[2]) * SKEW1))
            kb.barrier()

        if dbg:
            dbg_out("kT", kT, kT[:], [128, 2, NG * 128], BF16)
            dbg_out("qT", qT, qT[:], [128, 2, S], BF16)
            dbg_out("vaug", vaug, vaug[:], [128, NG, 4, 130], BF16)
            dbg_out("Gt", Gt, Gt[:], [128, NG, 16])
        if stage == 1:
            kb.barrier()
            raise _Stop((nc, dbg_outs))

        with ExitStack() as esg:
            NF = NG * 8
            Gv = Gt[:].rearrange("p c (d t h) -> p c d t h", d=2, t=2, h=4)
            LF = kb.sb("LF", [128, NG, 2, 4], F32, esg)
            T1 = kb.sb("T1", [128, NG, 2, 4], F32, esg)
            T2 = kb.sb("T2", [128, NG, 2, 4], F32, esg)
            Bc = kb.sb("Bc", [128, NG, 2, 4], F32, esg)
            Aa = kb.sb("Aa", [128, NG, 2, 4], F32, esg)
            Mcb = kb.sb("Mcb", [128, NG, 2, 4], F32, esg)
            rowA = kb.sb("rowA", [1, NG, 2, 4], F32, esg)
            rowB = kb.sb("rowB", [1, NG, 2, 4], F32, esg)
            rowM = kb.sb("rowM", [1, NG, 2, 4], F32, esg)
            rowm0 = kb.sb("rowm0", [1, NG, 2, 4], F32, esg)
            colmax = kb.sb("colmax", [128, 3], F32, esg)
            fl = lambda b: b[:].rearrange("p c d h -> p (c d h)")
            FG = Gv[:, :, :, 1, :]
            IG = Gv[:, :, :, 0, :]
            kb.op(dve, lambda e: e.tensor_scalar(out=T1[:], in0=FG, scalar1=-1.0, scalar2=None, op0=ALU.mult), reads=[Gt], writes=[T1])
            kb.op(dve, lambda e: e.tensor_tensor(out=T1[:], in0=T1[:], in1=FG, op=ALU.max), reads=[Gt, T1], writes=[T1])
            kb.op(act, lambda e: e.activation(out=T2[:], in_=T1[:], func=AF.Exp, scale=-1.0), reads=[T1], writes=[T2])
            kb.op(act, lambda e: e.activation(out=T2[:], in_=T2[:], func=AF.Ln, bias=cst[:, 3:4], scale=1.0), reads=[T2, cst], writes=[T2])
            kb.op(dve, lambda e: e.tensor_scalar(out=T1[:], in0=FG, scalar1=0.0, scalar2=None, op0=ALU.min), reads=[Gt], writes=[T1])
            kb.op(dve, lambda e: e.tensor_tensor(out=LF[:], in0=T1[:], in1=T2[:], op=ALU.subtract), reads=[T1, T2], writes=[LF])
            PBv = PB[0][:, 0:NF].rearrange("p (c d h) -> p c d h", c=NG, d=2, h=4)
            kb.op(pe, lambda e: e.matmul(PB[0][:, 0:NF], lhsT=LT[:], rhs=fl(LF), start=True, stop=True),
                  reads=[LT, LF], writes=[PB[0]])
            PBv1 = PB[1][:, 0:NF].rearrange("p (c d h) -> p c d h", c=NG, d=2, h=4)
            kb.op(pe, lambda e: e.matmul(PB[1][:, 0:NF], lhsT=UT[:], rhs=fl(LF), start=True, stop=True),
                  reads=[UT, LF], writes=[PB[1]])
            kb.op(dve, lambda e: e.tensor_copy(out=Bc[:, :, 0, :], in_=PBv[:, :, 0, :]), reads=[PB[0]], writes=[Bc])
            kb.op(dve, lambda e: e.tensor_copy(out=Bc[:, :, 1, :], in_=PBv1[:, :, 1, :]), reads=[PB[1]], writes=[Bc])
            kb.op(dve, lambda e: e.tensor_tensor(out=Aa[:], in0=IG, in1=Bc[:], op=ALU.subtract), reads=[Gt, Bc], writes=[Aa])
            AaF = fl(Aa)
            segs = [(0, 128), (128, 128), (256, NF - 256)]
            for si, (o, n) in enumerate(segs):
                kb.op(pe, lambda e, o=o, n=n: e.transpose(out=PB[2][0:n, 0:128], in_=AaF[:, o:o + n], identity=identf[:]),
                      reads=[Aa, identf], writes=[PB[2]])
                kb.op(dve, lambda e, si=si, n=n: e.reduce_max(out=colmax[0:n, si:si + 1], in_=PB[2][0:n, 0:128], axis=mybir.AxisListType.X),
                      reads=[PB[2]], writes=[colmax])
                kb.op(pe, lambda e, si=si, n=n, o=o: e.matmul(PB[3][0:1, o:o + n], lhsT=colmax[0:n, si:si + 1], rhs=identf[0:n, 0:n],
                                                               start=True, stop=True), reads=[colmax, identf], writes=[PB[3]])
            kb.op(dve, lambda e: e.tensor_copy(out=fl(rowA), in_=PB[3][0:1, 0:NF]), reads=[PB[3]], writes=[rowA])
            kb.op(pe, lambda e: e.matmul(PB[4][0:1, 0:NF], lhsT=onesf[:, 0:1], rhs=fl(LF), start=True, stop=True),
                  reads=[onesf, LF], writes=[PB[4]])
            kb.op(dve, lambda e: e.tensor_copy(out=fl(rowB), in_=PB[4][0:1, 0:NF]), reads=[PB[4]], writes=[rowB])
            order = [list(range(NG)), [1, 0] + list(range(NG - 1, NCT - 1, -1))]
            for d in range(2):
                g0 = order[d][0]
                kb.op(dve, lambda e, d=d, g0=g0: e.memset(rowm0[0:1, g0, d, :], 0.0), writes=[rowm0])
            for j in range(NG):
                for d in range(2):
                    gcur = order[d][j]
                    kb.op(dve, lambda e, d=d, gcur=gcur: e.tensor_tensor(out=rowM[0:1, gcur, d, :], in0=rowm0[0:1, gcur, d, :],
                                                                         in1=rowA[0:1, gcur, d, :], op=ALU.max),
                          reads=[rowm0, rowA], writes=[rowM])
                    if j + 1 < NG:
                        gn = order[d][j + 1]
                        kb.op(dve, lambda e, d=d, gcur=gcur, gn=gn: e.tensor_tensor(out=rowm0[0:1, gn, d, :], in0=rowM[0:1, gcur, d, :],
                                                                                   in1=rowB[0:1, gcur, d, :], op=ALU.add),
                              reads=[rowM, rowB], writes=[rowm0])
            kb.op(pe, lambda e: e.matmul(PB[5][:, 0:NF], lhsT=onesf[0:1, :], rhs=fl(rowM), start=True, stop=True),
                  reads=[onesf, rowM], writes=[PB[5]])
            kb.op(pe, lambda e: e.matmul(PB[6][:, 0:NF], lhsT=onesf[0:1, :], rhs=fl(rowm0), start=True, stop=True),
                  reads=[onesf, rowm0], writes=[PB[6]])
            kb.op(dve, lambda e: e.tensor_copy(out=fl(Mcb), in_=PB[5][:, 0:NF]), reads=[PB[5]], writes=[Mcb])
            kb.op(dve, lambda e: e.tensor_tensor(out=T1[:], in0=Aa[:], in1=Mcb[:], op=ALU.subtract), reads=[Aa, Mcb], writes=[T1])
            kb.op(act, lambda e: e.activation(out=WS[:].rearrange("p c j -> p (c j)"), in_=fl(T1), func=AF.Exp), reads=[T1], writes=[WS])
            kb.op(dve, lambda e: e.tensor_tensor(out=fl(T2), in0=PB[6][:, 0:NF], in1=fl(Mcb), op=ALU.subtract), reads=[PB[6], Mcb], writes=[T2])
            kb.op(act, lambda e: e.activation(out=CW[:].rearrange("p c j -> p (c j)"), in_=fl(T2), func=AF.Exp), reads=[T2], writes=[CW])
            kb.op(dve, lambda e: e.tensor_tensor(out=T1[:], in0=Bc[:], in1=Mcb[:], op=ALU.add), reads=[Bc, Mcb], writes=[T1])
            kb.op(act, lambda e: e.activation(out=LB[:].rearrange("p c j -> p (c j)"), in_=fl(T1), func=AF.Exp, bias=cst[:, 2:3], scale=-1.0),
                  reads=[T1, cst], writes=[LB])
            kb.barrier()

        if dbg:
            dbg_out("WS", WS, WS[:], [128, NG, 8])
            dbg_out("CW", CW, CW[:], [128, NG, 8])
            dbg_out("LB", LB, LB[:], [128, NG, 8])
        if stage == 2:
            kb.barrier()
            raise _Stop((nc, dbg_outs))

        with ExitStack() as ess:
            Cc = [kb.sb(f"Cc{d}", [128, 2, 130], F32, ess) for d in range(2)]
            kp = [kb.sb(f"kp{i}", [128, 4, 64], BF16, ess) for i in range(2)]
            c0b = [kb.sb(f"c0b{i}", [128, 2, 130], BF16, ess) for i in range(2)]
            for d in range(2):
                kb.op(pool, lambda e, d=d: e.memset(Cc[d][:], 0.0), writes=[Cc[d]])
            units = [(j, d) for j in range(NG) for d in range(2)]

            def stepA(u):
                j, d = units[u]
                if j == NG - 1:
                    return
                gcur = order[d][j]
                kpb = kp[u % 2]
                for pr in range(2):
                    kb.op(pe, lambda e, pr=pr: e.transpose(out=PT[:, pr * 128:(pr + 1) * 128],
                                                           in_=kT[:, pr, gcur * 128:(gcur + 1) * 128], identity=identb[:]),
                          reads=[kT, identb], writes=[PT], sig=(pr == 1))
                for h in range(4):
                    kb.op(act, lambda e, h=h: e.activation(
                        out=kpb[:, h, :], in_=PT[:, h * 64:(h + 1) * 64], func=AF.Identity,
                        scale=WS[:, gcur, d * 4 + h:d * 4 + h + 1]), reads=[PT, WS], writes=[kpb.sub(h)])
                bank = PB[(u % 2) * 2:(u % 2) * 2 + 2]
                for h in range(4):
                    bk = bank[h // 2]
                    kb.op(pe, lambda e, h=h, bk=bk: e.matmul(
                        bk[:, (h % 2) * 130:(h % 2) * 130 + 130], lhsT=kpb[:, (h // 2) * 2:(h // 2) * 2 + 2, :].rearrange("p a b -> p (a b)"),
                        rhs=vaug[:, gcur, h, :], start=True, stop=True), reads=[kpb.sub((h // 2) * 2), kpb.sub((h // 2) * 2 + 1), vaug], writes=[bk])

            Cnext = [kb.sb(f"Cn{d}", [128, 2, 130], F32, ess) for d in range(2)]
            for d in range(2):
                kb.op(pool, lambda e, d=d: e.memset(Cnext[d][:], 0.0), writes=[Cnext[d]])
            Cpp = [[Cc[d], Cnext[d]] for d in range(2)]

            def stepB(u):
                j, d = units[u]
                gcur = order[d][j]
                Ccur = Cpp[d][j % 2]
                Cnew = Cpp[d][(j + 1) % 2]
                if gcur >= NCT:
                    for h in range(4):
                        p0 = (h % 2) * 64
                        kb.op(pool, lambda e, h=h, p0=p0: e.tensor_scalar(
                            out=ST[p0:p0 + 64, gcur - NCT, d, h // 2, :], in0=Ccur[p0:p0 + 64, h // 2, :],
                            scalar1=CW[p0:p0 + 64, gcur, d * 4 + h:d * 4 + h + 1], scalar2=1.0, op0=ALU.mult, op1=ALU.mult),
                            reads=[Ccur.sub(h), CW], writes=[ST])
                if j == NG - 1:
                    return
                bank = PB[(u % 2) * 2:(u % 2) * 2 + 2]
                for h in range(4):
                    p0 = (h % 2) * 64
                    bk = bank[h // 2]
                    kb.op(dve, lambda e, h=h, p0=p0, bk=bk: e.scalar_tensor_tensor(
                        out=Cnew[p0:p0 + 64, h // 2, :], in0=Ccur[p0:p0 + 64, h // 2, :],
                        scalar=CW[p0:p0 + 64, gcur, d * 4 + h:d * 4 + h + 1],
                        in1=bk[p0:p0 + 64, (h % 2) * 130:(h % 2) * 130 + 130], op0=ALU.mult, op1=ALU.add),
                        reads=[Ccur.sub(h), CW, bk], writes=[Cnew.sub(h)])

            stepA(0)
            for u in range(len(units)):
                if u + 1 < len(units):
                    stepA(u + 1)
                stepB(u)
            kb.barrier()

        if dbg:
            dbg_out("ST", ST, ST[:], [128, NT, 2, 2, 130], BF16)
        if stage == 3:
            kb.barrier()
            raise _Stop((nc, dbg_outs))

        es_gc.close()
        wout_v = wout_d.rearrange("(kc p) n -> p kc n", p=128)
        with ExitStack() as es2:
            Win2 = kb.sb("Win2", [128, 8, 1536], BF16, es2)
            Wo = kb.sb("Wo", [128, 8, D], BF16, es2)
            with ExitStack() as esw:
                wstg = [kb.sb(f"wstg2{i}", [128, 8, 256], F32, esw) for i in range(2)]
                g1bc = kb.sb("g1bc", [128, D], F32, esw)
                kb.dma(g1bc[:], g1row[:, :], writes=[g1bc])
                nb = 0
                kb.cast_load(Win2, [(Win2[:, k0:k0 + 4, c0:c0 + 512], win_v[:, k0:k0 + 4, w0:w0 + 512], Win2.sub((k0, c0)))
                                    for (c0, w0) in ((0, 0), (512, 512), (1024, 2048)) for k0 in (0, 4)])
                for c0 in (0, 256, 512, 768):
                    load_weight_block(Wo, lambda k, c0=c0: Wo[:, k, c0:c0 + 256], wout_v[:, :, c0:c0 + 256], wstg[nb % 2], 8,
                                      scale_bc=g1bc, scale_cols=slice(c0, c0 + 256))
                    nb += 1
                kb.barrier()
            xs = [kb.sb(f"x2s{i}", [128, D], F32, es2) for i in range(2)]
            def two(name, shape, dt):
                return [kb.sb(f"{name}_{k}", shape, dt, es2) for k in range(2)]
            xn_ = two("xn2", [128, D], BF16)
            hT_ = two("hT2", [128, 8, 128], BF16)
            stats_ = two("stats2", [128, 4, 6], F32)
            mv_ = two("mv2", [128, 4], F32)
            rstd_ = two("rstd2", [128, 2], F32)
            mv4_ = two("mv4", [128, 4, 4], F32)
            rs4_ = two("rs4", [128, 4], F32)
            uT_ = two("uT", [128, 4, 128], BF16)
            sgo_ = two("sgo", [128, 4, 128], BF16)
            vn_ = two("vn", [128, 512], BF16)
            tA_1 = kb.sb("tA", [128, 4, 128], F32, es2)
            tA_ = [tA_1, tA_1]
            yT_ = two("yT", [128, 8, 128], BF16)
            sT_ = two("sT", [128, 8, 128], BF16)
            WM_1 = kb.sb("WM", [128, 8, 128], BF16, es2)
            WM_ = [WM_1, WM_1]
            dn_ = two("dn", [128, 3, 8], F32)
            hs_ = two("hs", [128, 4, 128], F32)
            hn_ = two("hn", [128, 4, 128], BF16)
            Q2_ = [[kb.sb(f"Q2{k}_{i}", [128, 2, 128], BF16, es2) for i in range(2)] for k in range(2)]
            hg = kb.sb("hg", [128, 4], F32, es2)
            for k in range(2):
                for pr in range(2):
                    kb.op(pool, lambda e, k=k, pr=pr: e.memset(Q2_[k][pr][:], 0.0), writes=[Q2_[k][pr]])
            kb.op(dve, lambda e: e.tensor_copy(out=hg[:], in_=smallc[:, 8:12]), reads=[smallc], writes=[hg])

            xs3 = xs + [kb.sb("x2s_c", [128, D], F32, es2)]
            xn3 = xn_ + [kb.sb("xn2_c", [128, D], BF16, es2)]
            stA = [kb.sb(f"stA{k}", [128, 2, 6], F32, es2) for k in range(3)]
            mvA = [kb.sb(f"mvA{k}", [128, 4], F32, es2) for k in range(3)]
            rsA = [kb.sb(f"rsA{k}", [128, 2], F32, es2) for k in range(3)]

            def tile2A(i):
                if i >= NT:
                    return
                xt = xs3[i % 3]
                kb.dma(xt[:, :], x_d[i * 128:(i + 1) * 128, :], writes=[xt])
                make_hT_A(xt, xn3[i % 3], stA[i % 3], mvA[i % 3], rsA[i % 3])

            def tile2(i):
                tile2A(i + 1)
                sl = i % 2
                xn, hT, stats, mv, rstd, mv4, rs4 = xn3[i % 3], hT_[sl], stats_[sl], mv_[sl], rstd_[sl], mv4_[sl], rs4_[sl]
                uT, sgo, vn, tA, yT, sT, dn, hs, hn, Q2, WM = uT_[sl], sgo_[sl], vn_[sl], tA_[sl], yT_[sl], sT_[sl], dn_[sl], hs_[sl], hn_[sl], Q2_[sl], WM_[sl]
                gc = i + NCT
                t0k = gc * 128
                t0q = i * 128
                xt = xs3[i % 3]
                def branchP():
                    make_hT_B(hT, xn, 0, 1)
                    for (bank, c0) in ((PB[0], 0), (PB[1], 1024)):
                        for cc in range(4):
                            for kc in range(8):
                                kb.op(pe, lambda e, cc=cc, kc=kc, bank=bank, c0=c0: e.matmul(
                                    bank[:, cc * 128:(cc + 1) * 128], lhsT=Win2[:, kc, c0 + cc * 128:c0 + (cc + 1) * 128], rhs=hT[:, kc, :],
                                    start=(kc == 0), stop=(kc == 7)), reads=[Win2, hT.sub((kc, 0))], writes=[bank], sig=(kc == 7 and cc == 3))
                    for kc in range(8):
                        kb.op(pe, lambda e, kc=kc: e.matmul(PB[2][:, :], lhsT=hT[:, kc, :], rhs=Win2[:, kc, 512:1024],
                                                            start=(kc == 0), stop=(kc == 7)), reads=[Win2, hT.sub((kc, 0))], writes=[PB[2]], sig=(kc == 7))
                    kb.op(act, lambda e: e.activation(out=uT[:].rearrange("p c t -> p (c t)"), in_=PB[0][:, :], func=AF.Copy),
                          reads=[PB[0]], writes=[uT])
                    kb.op(act, lambda e: e.activation(out=sgo[:].rearrange("p c t -> p (c t)"), in_=PB[1][:, :], func=AF.Sigmoid),
                          reads=[PB[1]], writes=[sgo])
                    ln_stats(PB[2][:, :], PB[2], 512, stA[i % 3], mvA[i % 3], rsA[i % 3])
                    kb.op(dve, lambda e: e.tensor_scalar(out=vn[:], in0=PB[2][:, :], scalar1=mvA[i % 3][:, 0:1], scalar2=rsA[i % 3][:, 0:1],
                                                         op0=ALU.subtract, op1=ALU.mult), reads=[PB[2], mvA[i % 3], rsA[i % 3]], writes=[vn])
                    for g in range(4):
                        kb.op(pe, lambda e, g=g: e.matmul(PB[3][:, g * 128:(g + 1) * 128], lhsT=vn[:, g * 128:(g + 1) * 128], rhs=wsT[:, g, :],
                                                          start=True, stop=True), reads=[vn, wsT], writes=[PB[3]], sig=(g == 3))
                    for g in range(4):
                        kb.op(dve, lambda e, g=g: e.scalar_tensor_tensor(out=tA[:, g, :], in0=PB[3][:, g * 128:(g + 1) * 128],
                                                                         scalar=smallc[:, g:g + 1], in1=BiasA[:, g, :], op0=ALU.mult, op1=ALU.add),
                              reads=[PB[3], smallc, BiasA], writes=[tA.sub(g)])
                    kb.op(pool, lambda e: e.tensor_tensor(out=yT[:, 0:4, :], in0=tA[:], in1=uT[:], op=ALU.mult), reads=[tA, uT], writes=[yT.sub('A')])

                def branchM():
                    for d in range(2):
                        msk = LT if d == 0 else UT
                        for h in range(4):
                            kb.op(pool, lambda e, d=d, h=h, msk=msk: e.tensor_scalar(
                                out=WM[:, d * 4 + h, :], in0=msk[:], scalar1=WS[:, gc, d * 4 + h:d * 4 + h + 1], scalar2=1.0,
                                op0=ALU.mult, op1=ALU.mult), reads=[msk, WS], writes=[WM.sub((d, h))])
                    for pr in range(2):
                        for hh in range(2):
                            kb.op(pool, lambda e, pr=pr, hh=hh: e.tensor_copy(out=Q2[pr][hh * 64:(hh + 1) * 64, hh, :],
                                                                              in_=qT[hh * 64:(hh + 1) * 64, pr, t0q:t0q + 128]),
                                  reads=[qT], writes=[Q2[pr].sub(hh)])
                    for pr in range(2):
                        kb.op(pe, lambda e, pr=pr: e.matmul(PB[4][:, pr * 256:(pr + 1) * 256], lhsT=kT[:, pr, t0k:t0k + 128],
                                                            rhs=Q2[pr][:].rearrange("p a t -> p (a t)"), start=True, stop=True),
                              reads=[kT, Q2[pr]], writes=[PB[4]], sig=(pr == 1))
                    for d in range(2):
                        kb.op(dve, lambda e, d=d: e.tensor_tensor(out=sT[:, d * 4:(d + 1) * 4, :].rearrange("p h t -> p (h t)"), in0=PB[4][:, :],
                                                                  in1=WM[:, d * 4:(d + 1) * 4, :].rearrange("p h t -> p (h t)"), op=ALU.mult),
                              reads=[PB[4]] + [WM.sub((d, hh_)) for hh_ in range(4)], writes=[sT.sub(d)])
                    for d in range(2):
                        bank = PB[5 + d]
                        for h in range(4):
                            kb.op(pe, lambda e, d=d, h=h, bank=bank: e.matmul(bank[:, h * 128:(h + 1) * 128], lhsT=sT[:, d * 4 + h, :],
                                                                              rhs=vaug[:, gc, h, 0:128], start=True, stop=False),
                                  reads=[sT.sub(d), vaug], writes=[bank], sig=False)
                            kb.op(pe, lambda e, d=d, h=h, bank=bank: e.matmul(
                                bank[:, h * 128:(h + 1) * 128], lhsT=Q2[h // 2][:, h % 2, :],
                                rhs=ST[:, i, d, h // 2, 0:128], start=False, stop=True),
                                reads=[Q2[h // 2], ST], writes=[bank], sig=(h == 3))
                    for d in range(2):
                        for h in range(4):
                            jn = d * 4 + h
                            kb.op(pe, lambda e, d=d, h=h, jn=jn: e.matmul(PB[4][:, 2 * jn:2 * jn + 2], lhsT=sT[:, jn, :],
                                                                          rhs=vaug[:, gc, h, 128:130], start=True, stop=False),
                                  reads=[sT.sub(d), vaug], writes=[PB[4]], sig=False)
                            kb.op(pe, lambda e, d=d, h=h, jn=jn: e.matmul(
                                PB[4][:, 2 * jn:2 * jn + 2], lhsT=Q2[h // 2][:, h % 2, :],
                                rhs=ST[:, i, d, h // 2, 128:130], start=False, stop=True),
                                reads=[Q2[h // 2], ST], writes=[PB[4]], sig=(jn == 7))
                    den = PB[4][:, 0:16].rearrange("p (j two) -> p j two", two=2)[:, :, 0]
                    kb.op(dve, lambda e: e.tensor_scalar(out=dn[:, 0, :], in0=den, scalar1=-1.0, scalar2=None, op0=ALU.mult),
                          reads=[PB[4]], writes=[dn])
                    kb.op(dve, lambda e: e.tensor_tensor(out=dn[:, 1, :], in0=dn[:, 0, :], in1=den, op=ALU.max),
                          reads=[PB[4], dn], writes=[dn])
                    kb.op(dve, lambda e: e.tensor_tensor(out=dn[:, 0, :], in0=dn[:, 1, :], in1=LB[:, gc, :], op=ALU.max),
                          reads=[dn, LB], writes=[dn])
                    kb.op(dve, lambda e: e.reciprocal(out=dn[:, 2, :], in_=dn[:, 0, :]), reads=[dn], writes=[dn])
                    for h in range(4):
                        kb.op(act, lambda e, h=h: e.activation(out=hs[:, h, :], in_=PB[5][:, h * 128:(h + 1) * 128], func=AF.Identity,
                                                               scale=dn[:, 2, h:h + 1]), reads=[PB[5], dn], writes=[hs.sub(h)])
                    for h in range(4):
                        kb.op(dve, lambda e, h=h: e.scalar_tensor_tensor(out=hs[:, h, :], in0=PB[6][:, h * 128:(h + 1) * 128],
                                                                         scalar=dn[:, 2, 4 + h:5 + h], in1=hs[:, h, :], op0=ALU.mult, op1=ALU.add),
                              reads=[PB[6], dn, hs.sub(h)], writes=[hs.sub(h)])
                    for h in range(4):
                        kb.op(dve, lambda e, h=h: e.bn_stats(out=stats[:, h, :], in_=hs[:, h, :]), reads=[hs.sub(h)], writes=[stats.sub(h)])
                    for h in range(4):
                        kb.op(dve, lambda e, h=h: e.bn_aggr(out=mv4[:, h, 0:2], in_=stats[:, h, :]), reads=[stats.sub(h)], writes=[mv4.sub(h)])
                    kb.op(pool, lambda e: e.tensor_tensor(out=mv4[:, :, 2], in0=mv4[:, :, 1], in1=cst[:, 1:2].to_broadcast([128, 4]), op=ALU.add),
                          reads=[mv4, cst], writes=[mv4])
                    kb.op(pool, lambda e: e.tensor_tensor(out=rs4[:], in0=mv4[:, :, 2], in1=cst[:, 0:1].to_broadcast([128, 4]), op=ALU.pow),
                          reads=[mv4, cst], writes=[rs4])
                    for h in range(4):
                        kb.op(dve, lambda e, h=h: e.tensor_scalar(out=hn[:, h, :], in0=hs[:, h, :], scalar1=mv4[:, h, 0:1], scalar2=rs4[:, h:h + 1],
                                                                  op0=ALU.subtract, op1=ALU.mult), reads=[hs.sub(h), mv4, rs4], writes=[hn.sub(h)])

                Pl = kb.record(branchP)
                Ml = kb.record(branchM)
                ip = im = 0
                tot = len(Pl) + len(Ml)
                if MERGE2 == 2:
                    kgm = len(Pl) - 9
                    kb.rec.extend(Ml[:12] + Pl[:kgm] + Ml[12:] + Pl[kgm:])
                    tot = 0
                for k in range(tot):
                    if MERGE2 and im * len(Pl) <= ip * len(Ml) and im < len(Ml):
                        kb.rec.append(Ml[im]); im += 1
                    elif ip < len(Pl):
                        kb.rec.append(Pl[ip]); ip += 1
                    else:
                        kb.rec.append(Ml[im]); im += 1
                for h in range(4):
                    kb.op(pe, lambda e, h=h: e.transpose(out=PT[:, h * 128:(h + 1) * 128], in_=hn[:, h, :], identity=identb[:]),
                          reads=[hn.sub(h), identb], writes=[PT], sig=(h == 3))
                for h in range(4):
                    kb.op(dve, lambda e, h=h: e.scalar_tensor_tensor(out=yT[:, 4 + h, :], in0=PT[:, h * 128:(h + 1) * 128],
                                                                     scalar=hg[:, h:h + 1], in1=sgo[:, h, :], op0=ALU.mult, op1=ALU.mult),
                          reads=[PT, hg, sgo], writes=[yT.sub(4 + h)])
                for half in range(2):
                    for kc in range(8):
                        kb.op(pe, lambda e, half=half, kc=kc: e.matmul(PB[3 + half][:, :], lhsT=yT[:, kc, :],
                                                                       rhs=Wo[:, kc, half * 512:(half + 1) * 512],
                                                                       start=(kc == 0), stop=(kc == 7)),
                              reads=[yT, Wo], writes=[PB[3 + half]], sig=(kc == 7))
                for half in range(2):
                    kb.op(dve, lambda e, half=half: e.scalar_tensor_tensor(out=xt[:, half * 512:(half + 1) * 512],
                                                                           in0=xt[:, half * 512:(half + 1) * 512], scalar=ALPHA,
                                                                           in1=PB[3 + half][:, :], op0=ALU.mult, op1=ALU.add),
                          reads=[xt.sub(half), PB[3 + half]], writes=[xt.sub(half)])
                ln_stats(xt[:, :], xt, 1024, stats, mv, rstd)
                kb.op(dve, lambda e: e.scalar_tensor_tensor(out=xt[:, :], in0=xt[:, :], scalar=mv[:, 0:1], in1=ln1gb[:],
                                                            op0=ALU.subtract, op1=ALU.mult), reads=[xt, mv, ln1gb], writes=[xt])
                kb.op(dve, lambda e: e.scalar_tensor_tensor(out=xt[:, :], in0=xt[:, :], scalar=rstd[:, 0:1], in1=ln1bb[:],
                                                            op0=ALU.mult, op1=ALU.add), reads=[xt, rstd, ln1bb], writes=[xt])
                kb.dma(y_d[i * 128:(i + 1) * 128, :], xt[:, :], reads=[xt])
            tile2A(0)
            lists2 = [kb.record(tile2, i) for i in range(NT)]
            interleave(lists2, int(len(lists2[0]) * SKEW2))
            kb.barrier()

        if stage == 4:
            raise _Stop((nc, dbg_outs))
        es0.close()
        es_p.close()

        GRP = 2
        w1_v = w1_d.rearrange("(kc p) n -> p kc n", p=128)
        w2_v = w2_d.rearrange("(j p) n -> p j n", p=128)
        with ExitStack() as es3:
            W1b = kb.sb("W1b", [128, 8, DFF], BF16, es3)
            W2b = kb.sb("W2b", [128, 32, D], BF16, es3)
            ln2gb = kb.sb("ln2gb", [128, D], F32, es3)
            ln2bb = kb.sb("ln2bb", [128, D], F32, es3)
            b2h = kb.sb("b2h", [1, 2, D], BF16, es3)
            kb.dma(ln2gb[:], ln2g_d[0:1, :].to_broadcast([128, D]), writes=[ln2gb])
            kb.dma(ln2bb[:], ln2b_d[0:1, :].to_broadcast([128, D]), writes=[ln2bb])
            g2bc = kb.sb("g2bc", [128, D], F32, es3)
            kb.dma(g2bc[:], g2row[:, :], writes=[g2bc])
            kb.cast_load(W1b, [(W1b[:, k0:k0 + 4, blk * 512:(blk + 1) * 512], w1_v[:, k0:k0 + 4, blk * 512:(blk + 1) * 512], W1b.sub(blk).sub(k0), W1b.sub(blk))
                               for blk in range(8) for k0 in (0, 4)], depth=3)
            kb.cast_load(W2b, [(W2b[:, blk * 4:(blk + 1) * 4, :], w2_v[:, blk * 4:(blk + 1) * 4, :], W2b.sub(blk), W2b.sub(blk))
                               for blk in range(8)], depth=3)
            with ExitStack() as esw:
                b2bc = kb.sb("b2bc", [128, D], F32, esw)
                kb.dma(b2bc[0:1, :], b2_d[0:1, :], writes=[b2bc])
                kb.op(dve, lambda e: e.tensor_copy(out=b2h[0:1, 0, :], in_=b2bc[0:1, :]), reads=[b2bc], writes=[b2h])
                kb.op(dve, lambda e: e.tensor_tensor(out=b2bc[0:1, :], in0=b2bc[0:1, :], in1=b2h[0:1, 0, :], op=ALU.subtract),
                      reads=[b2bc, b2h], writes=[b2bc])
                kb.op(dve, lambda e: e.tensor_copy(out=b2h[0:1, 1, :], in_=b2bc[0:1, :]), reads=[b2bc], writes=[b2h])
                kb.barrier()
            tmpo = [kb.sb(f"tmpo{i}", [128, 512], F32, es3) for i in range(2)]
            xs = [kb.sb(f"x3s{i}", [128, D], F32, es3) for i in range(2 * GRP)]
            xn = kb.sb("xn3", [128, D], BF16, es3)
            h2T_ = [kb.sb(f"h2T{k}", [128, 8, GRP * 128], BF16, es3) for k in range(2)]
            hid = kb.sb("hid", [128, 32, GRP * 128], BF16, es3)
            hidb = [Buf(hid.t, f"hid{j}") for j in range(32)]
            rl = [kb.sb(f"rl{i}", [128, GRP * 128], BF16, es3) for i in range(4)]
            stats_p = kb.sb("stats3p", [128, 2, 6], F32, es3)
            mv_p = kb.sb("mv3p", [128, 4], F32, es3)
            rstd_p = kb.sb("rstd3p", [128, 2], F32, es3)
            stats_e = kb.sb("stats3e", [128, 2, 6], F32, es3)
            mv_e = kb.sb("mv3e", [128, 4], F32, es3)
            rstd_e = kb.sb("rstd3e", [128, 2], F32, es3)
            NGRP = NT // GRP

            xn_3 = [xn] + [kb.sb(f"xn3_{a}", [128, D], BF16, es3) for a in range(1, GRP)]

            def prep3A(gi):
                for a in range(GRP):
                    ti = gi * GRP + a
                    xt = xs[(gi % 2) * GRP + a]
                    kb.dma(xt[:, :], y_d[ti * 128:(ti + 1) * 128, :], writes=[xt])
                    make_hT_A(xt, xn_3[a], stats_p, mv_p, rstd_p)

            def prep3B(gi):
                for a in range(GRP):
                    make_hT_B(h2T_[gi % 2], xn_3[a], 2, 3, tok0=a * 128)

            def main3(gi):
                h2T = h2T_[gi % 2]
                for j in range(32):
                    bank = PB[j % 2]
                    for kc in range(8):
                        kb.op(pe, lambda e, j=j, kc=kc, bank=bank: e.matmul(bank[:, 0:GRP * 128], lhsT=W1b[:, kc, j * 128:(j + 1) * 128],
                                                                            rhs=h2T[:, kc, :], start=(kc == 0), stop=(kc == 7)),
                              reads=[W1b.sub(j // 4)] + [h2T.sub((kc, a_ * 128)) for a_ in range(GRP)], writes=[bank], sig=(kc == 7))
                    rb = rl[j % 4]
                    kb.op(act, lambda e, j=j, bank=bank, rb=rb: e.activation(out=rb[:], in_=bank[:, 0:GRP * 128], func=AF.Relu,
                                                                             bias=smallc[:, 24 + j:25 + j], scale=1.0),
                          reads=[bank, smallc], writes=[rb])
                    eng = pool if j % 4 == 3 else dve
                    kb.op(eng, lambda e, j=j, rb=rb: e.tensor_tensor(out=hid[:, j, :], in0=rb[:], in1=rb[:], op=ALU.mult),
                          reads=[rb], writes=[hidb[j]])
                for a in range(GRP):
                    ti = gi * GRP + a
                    xt = xs[(gi % 2) * GRP + a]
                    for half in range(2):
                        bank = PB[2 + 2 * (a % 2) + half]
                        for j in range(32):
                            kb.op(pe, lambda e, j=j, a=a, half=half, bank=bank: e.matmul(
                                bank[:, :], lhsT=hid[:, j, a * 128:(a + 1) * 128], rhs=W2b[:, j, half * 512:(half + 1) * 512],
                                start=(j == 0), stop=False), reads=[hidb[j], W2b.sub(j // 4)], writes=[bank], sig=False)
                        for hl in range(2):
                            kb.op(pe, lambda e, hl=hl, half=half, bank=bank: e.matmul(
                                bank[:, :], lhsT=onesb[0:1, :], rhs=b2h[0:1, hl, half * 512:(half + 1) * 512],
                                start=False, stop=(hl == 1)), reads=[onesb, b2h], writes=[bank], sig=(hl == 1))
                        tm = tmpo[half]
                        kb.op(dve, lambda e, half=half, bank=bank, tm=tm: e.tensor_tensor(
                            out=tm[:], in0=bank[:, :], in1=g2bc[:, half * 512:(half + 1) * 512], op=ALU.mult),
                            reads=[bank, g2bc], writes=[tm])
                        kb.op(dve, lambda e, half=half, tm=tm, xt=xt: e.scalar_tensor_tensor(
                            out=xt[:, half * 512:(half + 1) * 512], in0=xt[:, half * 512:(half + 1) * 512], scalar=ALPHA,
                            in1=tm[:], op0=ALU.mult, op1=ALU.add), reads=[xt.sub(half), tm], writes=[xt.sub(half)])
                    ln_stats(xt[:, :], xt, 1024, stats_e, mv_e, rstd_e)
                    kb.op(dve, lambda e, xt=xt: e.scalar_tensor_tensor(out=xt[:, :], in0=xt[:, :], scalar=mv_e[:, 0:1], in1=ln2gb[:],
                                                                       op0=ALU.subtract, op1=ALU.mult), reads=[xt, mv_e, ln2gb], writes=[xt])
                    kb.op(dve, lambda e, xt=xt: e.scalar_tensor_tensor(out=xt[:, :], in0=xt[:, :], scalar=rstd_e[:, 0:1], in1=ln2bb[:],
                                                                       op0=ALU.mult, op1=ALU.add), reads=[xt, rstd_e, ln2bb], writes=[xt])
                    kb.dma(y_d[ti * 128:(ti + 1) * 128, :], xt[:, :], reads=[xt])

            for st in kb.record(prep3A, 0) + kb.record(prep3B, 0):
                st()
            for gi in range(NGRP):
                M = kb.record(main3, gi)
                PA = kb.record(prep3A, gi + 1) if gi + 1 < NGRP else []
                PB_ = kb.record(prep3B, gi + 1) if gi + 1 < NGRP else []
                nM = len(M)
                a0, a1 = int(nM * 0.02), int(nM * 0.35)
                b0, b1 = int(nM * 0.55), int(nM * 0.90)
                ia = ib = 0
                for k, st in enumerate(M):
                    st()
                    if k >= a0 and PA:
                        want = min(len(PA), ((k - a0 + 1) * len(PA)) // max(1, a1 - a0))
                        while ia < want:
                            PA[ia]()
                            ia += 1
                    if k >= b0 and PB_:
                        want = min(len(PB_), ((k - b0 + 1) * len(PB_)) // max(1, b1 - b0))
                        while ib < want:
                            PB_[ib]()
                            ib += 1
                for st in PA[ia:] + PB_[ib:]:
                    st()
            kb.barrier()
    return nc, dbg_outs


_CACHE = {}


def make_in_maps(inputs):
    g = lambda k: np.ascontiguousarray(np.asarray(inputs[k], dtype=np.float32))
    shared = {
        "c_ctx": g("c_ctx").reshape(8, 128),
        "w_ada": g("w_ada")[0],
        "b_ada": g("b_ada")[0].reshape(48, 128),
        "w_in": g("w_in")[0],
        "w_s": g("w_s")[0],
        "b_s": g("b_s")[0].reshape(1, 512),
        "ln_v_g": g("ln_v_g")[0].reshape(4, 128),
        "ln_v_b": g("ln_v_b")[0].reshape(4, 128),
        "conv_qk": g("conv_qk")[0].reshape(12, 128),
        "b_gates": g("b_gates")[0].reshape(1, 16),
        "hn_g": g("hn_g")[0].reshape(4, 128),
        "w_out": g("w_out")[0],
        "ln1_g": g("ln1_g")[0].reshape(1, D),
        "ln1_b": g("ln1_b")[0].reshape(1, D),
        "w1": g("w1")[0],
        "b1": g("b1")[0].reshape(32, 128),
        "w2": g("w2")[0],
        "b2": g("b2")[0].reshape(1, D),
        "ln2_g": g("ln2_g")[0].reshape(1, D),
        "ln2_b": g("ln2_b")[0].reshape(1, D),
    }
    x, c, ctx = g("x"), g("c"), g("ctx")
    maps = []
    for b in range(x.shape[0]):
        m = dict(shared)
        m["x"] = x[b]
        m["c"] = c[b].reshape(8, 128)
        m["ctx"] = ctx[b]
        maps.append(m)
    return maps


def kernel(**inputs):
    if "nc" not in _CACHE:
        _CACHE["nc"] = build_program(False)[0]
    nc = _CACHE["nc"]
    maps = make_in_maps(inputs)
    n = len(maps)
    res = run_bass_kernel_spmd(nc, maps, core_ids=list(range(n)))
    out = np.stack([np.asarray(r["y"], dtype=np.float32) for r in res.results], axis=0)
    return out
```

```python
import math
from contextlib import ExitStack
import numpy as np
import concourse.bass as bass
import concourse.mybir as mybir
from concourse.bass_utils import run_bass_kernel_spmd

F32 = mybir.dt.float32
BF16 = mybir.dt.bfloat16
AF = mybir.ActivationFunctionType
ALU = mybir.AluOpType

D = 1024
S = 4096
CTX = 256
NT = S // 128
NCT = CTX // 128
NG = NT + NCT
DIN = 2576
DFF = 4096
ALPHA = 2.0 ** 0.25
EPS = 1e-5
SEM_LIMIT = 3000
SKEW1 = 0.55
MERGE2 = 2
SKEW2 = 0.55


class Tok:
    __slots__ = ("sem", "val", "key")

    def __init__(self, sem, val, key):
        self.sem, self.val, self.key = sem, val, key


class Buf:
    def __init__(self, t, name, parent=None):
        self.t = t
        self.name = name
        self.w = None
        self.r = {}
        self.dsem = None
        self.dcount = 0
        self.parent = parent
        self.children = {}

    def sub(self, key):
        c = self.children.get(key)
        if c is None:
            c = Buf(self.t, f"{self.name}.{key}", parent=self)
            self.children[key] = c
        return c

    def __getitem__(self, idx):
        return self.t[idx]


class Eng:
    def __init__(self, kb, name, h):
        self.kb, self.name, self.h = kb, name, h
        self.seen = {}
        self.epoch = 0
        self.count = 0
        self.sem = kb.new_sem(f"{name}_e0")
        self.pending = False

    def roll(self):
        if self.count >= SEM_LIMIT and not self.pending:
            self.epoch += 1
            self.count = 0
            self.sem = self.kb.new_sem(f"{self.name}_e{self.epoch}")


class KB:
    def __init__(self, nc, es):
        self.nc, self.es = nc, es
        self.nsem = 0
        self.pe = Eng(self, "pe", nc.tensor)
        self.act = Eng(self, "act", nc.scalar)
        self.dve = Eng(self, "dve", nc.vector)
        self.pool = Eng(self, "pool", nc.gpsimd)
        self.sp = Eng(self, "sp", nc.sync)
        self.engs = [self.pe, self.act, self.dve, self.pool, self.sp]
        self.dma_toks = []

    def new_sem(self, name):
        self.nsem += 1
        s = self.es.enter_context(self.nc.semaphore(name))
        return (s, name)

    def sb(self, name, shape, dt, es=None):
        t = (es or self.es).enter_context(self.nc.sbuf_tensor(name, list(shape), dt))
        return Buf(t, name)

    def ps(self, name, shape, dt, es=None):
        t = (es or self.es).enter_context(self.nc.psum_tensor(name, list(shape), dt))
        b = Buf(t, name)
        b.psum = True
        return b

    def wait(self, eng, tok):
        if tok is None:
            return
        if eng.name == "pe" and tok.key.startswith("pe_e"):
            return
        if eng.seen.get(tok.key, 0) >= tok.val:
            return
        eng.h.wait_ge(tok.sem, tok.val)
        eng.seen[tok.key] = tok.val

    def _deps(self, eng, reads, writes):
        need = {}

        def add(t):
            if t is None:
                return
            o = need.get(t.key)
            if o is None or o.val < t.val:
                need[t.key] = t

        for b in reads:
            add(b.w)
            if getattr(b, "psum", False):
                for k_, t_ in b.r.items():
                    if not k_.startswith(eng.name + "_e"):
                        add(t_)
            if b.parent is not None:
                add(b.parent.w)
            for c in b.children.values():
                add(c.w)
                for c2 in c.children.values():
                    add(c2.w)
        for b in writes:
            add(b.w)
            for t in b.r.values():
                add(t)
            if b.parent is not None:
                add(b.parent.w)
                for t in b.parent.r.values():
                    add(t)
            for c in b.children.values():
                add(c.w)
                for t in c.r.values():
                    add(t)
                for c2 in c.children.values():
                    add(c2.w)
                    for t in c2.r.values():
                        add(t)
        for t in need.values():
            self.wait(eng, t)

    def _mark(self, tok, reads, writes):
        for b in reads:
            old = b.r.get(tok.key)
            if old is None or old.val < tok.val:
                b.r[tok.key] = tok
        for b in writes:
            b.w = tok
            b.r = {}

    def record(self, body, *args):
        outer = getattr(self, "rec", None)
        self.rec = []
        body(*args)
        r = self.rec
        self.rec = outer
        return r

    def op(self, eng, fn, reads=(), writes=(), sig=True):
        if getattr(self, "rec", None) is not None:
            self.rec.append(lambda: self._op(eng, fn, reads, writes, sig))
            return None
        return self._op(eng, fn, reads, writes, sig)

    def _op(self, eng, fn, reads=(), writes=(), sig=True):
        if sig:
            eng.roll()
        self._deps(eng, reads, writes)
        inst = fn(eng.h)
        if sig:
            eng.count += 1
            inst.then_inc(eng.sem[0], 1)
            tok = Tok(eng.sem[0], eng.count, eng.sem[1])
            eng.pending = False
        else:
            tok = Tok(eng.sem[0], eng.count + 1, eng.sem[1])
            eng.pending = True
        self._mark(tok, reads, writes)
        return tok

    def dma(self, out_ap, in_ap, reads=(), writes=(), sembuf=None, eng=None, after=()):
        if getattr(self, "rec", None) is not None:
            self.rec.append(lambda: self._dma(out_ap, in_ap, reads, writes, sembuf, eng, after))
            return None
        return self._dma(out_ap, in_ap, reads, writes, sembuf, eng, after)

    def cast_load(self, dst_buf, pieces, depth=2):
        toks = []
        for piece in pieces:
            out_ap, in_ap, wr = piece[:3]
            semb = wr
            after = [toks[-depth]] if len(toks) >= depth else []
            toks.append(self._dma(out_ap, in_ap, (), [wr], semb, self.pool, after))
        return toks

    def _dma(self, out_ap, in_ap, reads=(), writes=(), sembuf=None, eng=None, after=()):
        eng = eng or self.sp
        for t_ in after:
            self.wait(eng, t_)
        sb_ = sembuf or (writes[0] if writes else reads[0])
        if sb_.dsem is None:
            sb_.dsem = self.new_sem(f"d_{sb_.name}")
        self._deps(eng, reads, writes)
        inst = eng.h.dma_start(out=out_ap, in_=in_ap)
        inst.then_inc(sb_.dsem[0], 16)
        sb_.dcount += 16
        tok = Tok(sb_.dsem[0], sb_.dcount, sb_.dsem[1])
        self._mark(tok, reads, writes)
        self.dma_toks.append(tok)
        return tok

    def barrier(self):
        toks = []
        for e in self.engs:
            if e.count > 0:
                assert not e.pending
                toks.append(Tok(e.sem[0], e.count, e.sem[1]))
        toks += self.dma_toks
        self.dma_toks = []
        for e in self.engs:
            for t in toks:
                self.wait(e, t)


class _Stop(Exception):
    pass


def interleave(step_lists, H):
    n = len(step_lists)
    T = max(i * H + len(sl) for i, sl in enumerate(step_lists))
    lo = 0
    for t in range(T):
        while lo < n and t - lo * H >= len(step_lists[lo]):
            lo += 1
        i = lo
        while i < n and t - i * H >= 0:
            k = t - i * H
            if k < len(step_lists[i]):
                step_lists[i][k]()
            i += 1


def build_program(dbg=False, stage=99):
    try:
        return _build_program(dbg, stage)
    except _Stop as ex:
        return ex.args[0]


def _build_program(dbg, stage):
    nc = bass.Bass("TRN2", target_bir_lowering=False)

    def din(name, shape):
        return nc.dram_tensor(name, list(shape), F32, kind="ExternalInput").ap()

    x_d = din("x", [S, D])
    c_d = din("c", [8, 128])
    ctx_d = din("ctx", [CTX, D])
    cctx_d = din("c_ctx", [8, 128])
    wada_d = din("w_ada", [D, 6 * D])
    bada_d = din("b_ada", [48, 128])
    win_d = din("w_in", [D, DIN])
    ws_d = din("w_s", [4, 128, 128])
    bs_d = din("b_s", [1, 512])
    lnvg_d = din("ln_v_g", [4, 128])
    lnvb_d = din("ln_v_b", [4, 128])
    conv_d = din("conv_qk", [12, 128])
    bg_d = din("b_gates", [1, 16])
    hng_d = din("hn_g", [4, 128])
    wout_d = din("w_out", [D, D])
    ln1g_d = din("ln1_g", [1, D])
    ln1b_d = din("ln1_b", [1, D])
    w1_d = din("w1", [D, DFF])
    b1_d = din("b1", [32, 128])
    w2_d = din("w2", [DFF, D])
    b2_d = din("b2", [1, D])
    ln2g_d = din("ln2_g", [1, D])
    ln2b_d = din("ln2_b", [1, D])
    y_d = nc.dram_tensor("y", [S, D], F32, kind="ExternalOutput").ap()
    dbg_outs = {}

    with ExitStack() as es:
        kb = KB(nc, es)
        pe, act, dve, pool, sp = kb.pe, kb.act, kb.dve, kb.pool, kb.sp

        PB = [kb.ps(f"pb{i}", [128, 512], F32) for i in range(7)]
        PT = kb.ps("pt", [128, 1024], BF16)

        identf = kb.sb("identf", [128, 128], F32)
        identb = kb.sb("identb", [128, 128], BF16)
        LT = kb.sb("LT", [128, 128], F32)
        UT = kb.sb("UT", [128, 128], F32)
        onesf = kb.sb("onesf", [128, 128], F32)
        onesb = kb.sb("onesb", [128, 128], BF16)
        cst = kb.sb("cst", [128, 8], F32)
        modc = kb.sb("modc", [128, 6, 8], F32)
        smallc = kb.sb("smallc", [128, 64], F32)
        bgb = kb.sb("bgb", [128, 16], F32)
        BiasA = kb.sb("BiasA", [128, 4, 128], F32)
        wsT = kb.sb("wsT", [128, 4, 128], BF16)
        setup = Buf(None, "setup")
        ccol = kb.sb("ccol", [128, 2, 8], F32)
        badac = kb.sb("badac", [128, 48], F32)

        def dbg_out(name, buf, ap, shape, dt=F32):
            if not dbg:
                return
            o = nc.dram_tensor("dbg_" + name, list(shape), dt, kind="ExternalOutput").ap()
            dbg_outs[name] = (shape, dt)
            kb.dma(o, ap, reads=[buf], sembuf=buf)

        kb.op(pool, lambda e: e.memset(onesf[:], 1.0), writes=[onesf])
        kb.op(pool, lambda e: e.memset(onesb[:], 1.0), writes=[onesb])
        kb.op(pool, lambda e: e.memset(cst[:, 0:1], -0.5), writes=[cst])
        kb.op(pool, lambda e: e.memset(cst[:, 1:2], EPS), writes=[cst])
        kb.op(pool, lambda e: e.memset(cst[:, 2:3], math.log(8.0)), writes=[cst])
        kb.op(pool, lambda e: e.memset(cst[:, 3:4], 1.0), writes=[cst])
        kb.op(pool, lambda e: e.affine_select(out=identf[:], in_=onesf[:], pattern=[[-1, 128]], compare_op=ALU.is_equal,
                                              fill=0.0, base=0, channel_multiplier=1), reads=[onesf], writes=[identf])
        kb.op(pool, lambda e: e.affine_select(out=LT[:], in_=onesf[:], pattern=[[1, 128]], compare_op=ALU.is_ge,
                                              fill=0.0, base=0, channel_multiplier=-1), reads=[onesf], writes=[LT])
        kb.op(pool, lambda e: e.affine_select(out=UT[:], in_=onesf[:], pattern=[[-1, 128]], compare_op=ALU.is_ge,
                                              fill=0.0, base=0, channel_multiplier=1), reads=[onesf], writes=[UT])
        kb.op(dve, lambda e: e.tensor_copy(out=identb[:], in_=identf[:]), reads=[identf], writes=[identb])

        es_p = es.enter_context(ExitStack())
        qT = kb.sb("qT", [128, 2, S], BF16, es_p)
        kT = kb.sb("kT", [128, 2, NG * 128], BF16, es_p)
        vaug = kb.sb("vaug", [128, NG, 4, 130], BF16, es_p)
        WS = kb.sb("WS", [128, NG, 8], F32, es_p)
        LB = kb.sb("LB", [128, NG, 8], F32, es_p)
        ST = kb.sb("ST", [128, NT, 2, 2, 130], BF16, es_p)
        kb.op(pool, lambda e: e.memset(vaug[:, :, :, 128:130], 1.0), writes=[vaug])

        es0 = es.enter_context(ExitStack())
        ln1gb = kb.sb("ln1gb", [128, D], F32, es0)
        ln1bb = kb.sb("ln1bb", [128, D], F32, es0)
        es_gc = es.enter_context(ExitStack())
        Gt = kb.sb("Gt", [128, NG, 16], F32, es_gc)
        CW = kb.sb("CW", [128, NG, 8], F32, es_gc)
        es_set = es.enter_context(ExitStack())
        rows = kb.sb("rows", [128, 256], F32, es_set)
        bsb = kb.sb("bsb", [128, 512], F32, es_set)
        R_C, R_CC, R_BADA, R_CONV, R_GV, R_BV, R_HNG, R_B1 = 0, 8, 16, 64, 76, 80, 84, 88
        kb.dma(rows[0:8, 0:128], c_d[:, :], writes=[rows])
        kb.dma(rows[0:8, 128:256], cctx_d[:, :], writes=[rows])
        rows2 = kb.sb("rows2", [128, 128], F32, es_set)
        kb.dma(rows2[0:48, :], bada_d[:, :], writes=[rows2])
        rows3 = kb.sb("rows3", [128, 128], F32, es_set)
        kb.dma(rows3[0:12, :], conv_d[:, :], writes=[rows3])
        kb.dma(rows3[32:36, :], lnvg_d[:, :], writes=[rows3])
        kb.dma(rows3[64:68, :], lnvb_d[:, :], writes=[rows3])
        rows4 = kb.sb("rows4", [128, 128], F32, es_set)
        kb.dma(rows4[0:4, :], hng_d[:, :], writes=[rows4])
        kb.dma(rows4[32:64, :], b1_d[:, :], writes=[rows4])
        kb.dma(bgb[:], bg_d[0:1, :].to_broadcast([128, 16]), writes=[bgb])
        kb.dma(bsb[:], bs_d[0:1, :].to_broadcast([128, 512]), writes=[bsb])
        wsr = kb.sb("wsr", [128, 4, 128], F32, es_set)
        kb.dma(wsr[:], ws_d.rearrange("g t s -> t g s"), writes=[wsr])


        def tr_f32(dst_ap, dst_buf, src_ap, src_buf, n, bank, p0=0):
            kb.op(pe, lambda e: e.transpose(out=bank[:, 0:n], in_=src_ap, identity=identf[p0:p0 + n, p0:p0 + n]),
                  reads=[src_buf, identf], writes=[bank])
            kb.op(dve, lambda e: e.tensor_copy(out=dst_ap, in_=bank[:, 0:n]), reads=[bank], writes=[dst_buf])

        craw = kb.sb("craw", [128, 2, 8], F32, es_set)
        tr_f32(craw[:, 0, :], craw, rows[0:8, 0:128], rows, 8, PB[0])
        tr_f32(craw[:, 1, :], craw, rows[0:8, 128:256], rows, 8, PB[1])
        kb.op(act, lambda e: e.activation(out=ccol[:], in_=craw[:], func=AF.Silu), reads=[craw], writes=[ccol])
        tr_f32(badac[:, :], badac, rows2[0:48, :], rows2, 48, PB[2])
        tr_f32(smallc[:, 12:24], smallc, rows3[0:12, :], rows3, 12, PB[3])
        tr_f32(smallc[:, 0:4], smallc, rows3[32:36, :], rows3, 4, PB[4], p0=32)
        tr_f32(smallc[:, 4:8], smallc, rows3[64:68, :], rows3, 4, PB[5], p0=64)
        tr_f32(smallc[:, 8:12], smallc, rows4[0:4, :], rows4, 4, PB[6])
        tr_f32(smallc[:, 24:56], smallc, rows4[32:64, :], rows4, 32, PB[0], p0=32)
        wsTf = kb.sb("wsTf", [128, 4, 128], F32, es_set)
        for g in range(4):
            kb.op(pe, lambda e, g=g: e.transpose(out=PB[1][:, g * 128:(g + 1) * 128], in_=wsr[:, g, :], identity=identf[:]),
                  reads=[wsr, identf], writes=[PB[1]])
        kb.op(dve, lambda e: e.tensor_copy(out=wsTf[:].rearrange("p g t -> p (g t)"), in_=PB[1][:, :]), reads=[PB[1]], writes=[wsTf])
        kb.op(act, lambda e: e.activation(out=wsT[:], in_=wsTf[:], func=AF.Copy), reads=[wsTf], writes=[wsT])
        kb.op(pe, lambda e: e.matmul(PB[2][:, :], lhsT=onesf[:], rhs=wsTf[:].rearrange("p g t -> p (g t)"), start=True, stop=True),
              reads=[onesf, wsTf], writes=[PB[2]])
        for g in range(4):
            kb.op(dve, lambda e, g=g: e.scalar_tensor_tensor(out=BiasA[:, g, :], in0=PB[2][:, g * 128:(g + 1) * 128],
                                                             scalar=smallc[:, 4 + g:5 + g], in1=bsb[:, g * 128:(g + 1) * 128],
                                                             op0=ALU.mult, op1=ALU.add),
                  reads=[PB[2], smallc, bsb], writes=[BiasA])

        if stage == -1:
            dbg_out("smallc", smallc, smallc[:], [128, 64])
            dbg_out("BiasA", BiasA, BiasA[:], [128, 4, 128])
            dbg_out("ccol", ccol, ccol[:], [128, 2, 8])
            dbg_out("LT", LT, LT[:], [128, 128])
            dbg_out("identf", identf, identf[:], [128, 128])
            kb.barrier()
            raise _Stop((nc, dbg_outs))
        kb.barrier()
        es_set.close()
        kb.dma(ln1gb[:], ln1g_d[0:1, :].to_broadcast([128, D]), writes=[ln1gb])
        kb.dma(ln1bb[:], ln1b_d[0:1, :].to_broadcast([128, D]), writes=[ln1bb])
        g2row = nc.dram_tensor("g2scratch", [128, D], F32, kind="Internal").ap()
        g1row = nc.dram_tensor("g1scratch", [128, D], F32, kind="Internal").ap()

        with ExitStack() as esa:
            stg = [kb.sb(f"astg{i}", [128, 8, 512], BF16, esa) for i in range(4)]
            scb = kb.sb("scb", [128, 8, 128], BF16, esa)
            ccolb = kb.sb("ccolb", [128, 2, 8], BF16, esa)
            kb.op(dve, lambda e: e.tensor_copy(out=ccolb[:], in_=ccol[:]), reads=[ccol], writes=[ccolb])
            badab = kb.sb("badab", [128, 2, D], F32, esa)
            g2bc0 = kb.sb("g2bc0", [128, D], F32, esa)
            g1bc = kb.sb("g1bc0", [128, D], F32, esa)
            kb.dma(badab[:, 0, :], bada_d[16:24, :].rearrange("(o a) b -> o (a b)", o=1).to_broadcast([128, D]), writes=[badab])
            kb.dma(badab[:, 1, :], bada_d[40:48, :].rearrange("(o a) b -> o (a b)", o=1).to_broadcast([128, D]), writes=[badab])
            for kc in range(8):
                kb.op(dve, lambda e, kc=kc: e.tensor_copy(out=scb[:, kc, :], in_=ccol[:, 0, kc:kc + 1].to_broadcast([128, 128])),
                      reads=[ccol], writes=[scb])
            wada_v = wada_d.rearrange("(kc p) n -> p kc n", p=128)
            col_kind = {0: 0, 1: 0, 2: 1, 3: 1, 6: 2, 7: 2, 8: 3, 9: 3}
            for blk in range(12):
                sg = stg[blk % 4]
                kb.dma(sg[:, 0:4, :], wada_v[:, 0:4, blk * 512:(blk + 1) * 512], writes=[sg], eng=pool)
                kb.dma(sg[:, 4:8, :], wada_v[:, 4:8, blk * 512:(blk + 1) * 512], writes=[sg], eng=pool)
                if blk in col_kind:
                    mi = col_kind[blk]
                    for jj in range(4):
                        j = blk * 4 + jj
                        fchunk = j % 8
                        bank = PB[jj % 4]
                        for kc in range(8):
                            kb.op(pe, lambda e, kc=kc, jj=jj, bank=bank: e.matmul(
                                bank[:, 0:2], lhsT=sg[:, kc, jj * 128:(jj + 1) * 128], rhs=ccolb[:, :, kc],
                                start=(kc == 0), stop=(kc == 7)), reads=[sg, ccolb], writes=[bank], sig=(kc == 7))
                        kb.op(dve, lambda e, bank=bank, mi=mi, fchunk=fchunk, j=j: e.tensor_tensor(
                            out=modc[:, mi, fchunk:fchunk + 1], in0=bank[:, 0:1], in1=badac[:, j:j + 1], op=ALU.add),
                            reads=[bank, badac], writes=[modc.sub((mi, fchunk))])
                        if mi < 2:
                            kb.op(dve, lambda e, bank=bank, mi=mi, fchunk=fchunk, j=j: e.tensor_tensor(
                                out=modc[:, 4 + mi, fchunk:fchunk + 1], in0=bank[:, 1:2], in1=badac[:, j:j + 1], op=ALU.add),
                                reads=[bank, badac], writes=[modc.sub((4 + mi, fchunk))])
                else:
                    which = 0 if blk in (4, 5) else 1
                    half = blk % 2 if which == 1 else blk - 4
                    bank = PB[4 + (blk % 2)]
                    for kc in range(8):
                        kb.op(pe, lambda e, kc=kc, bank=bank: e.matmul(bank[:, :], lhsT=scb[:, kc, :], rhs=sg[:, kc, :],
                                                                      start=(kc == 0), stop=(kc == 7)),
                              reads=[sg, scb], writes=[bank], sig=(kc == 7))
                    dst = g1bc if which == 0 else g2bc0
                    kb.op(dve, lambda e, bank=bank, dst=dst, half=half, which=which: e.tensor_tensor(
                        out=dst[:, half * 512:(half + 1) * 512], in0=bank[:, :], in1=badab[:, which, half * 512:(half + 1) * 512],
                        op=ALU.add), reads=[bank, badab], writes=[dst])
            for mi in (1, 3, 5):
                kb.op(dve, lambda e, mi=mi: e.tensor_scalar(out=modc[:, mi, :], in0=modc[:, mi, :], scalar1=1.0, scalar2=None,
                                                            op0=ALU.add), reads=[modc], writes=[modc])
            g2st = kb.dma(g2row[:, :], g2bc0[:], reads=[g2bc0])
            kb.dma(g1row[:, :], g1bc[:], reads=[g1bc])
            kb.barrier()
            if stage == 0:
                dbg_out("modc", modc, modc[:], [128, 6, 8])
                dbg_out("g1bc", g1bc, g1bc[:], [128, D])
                dbg_out("smallc", smallc, smallc[:], [128, 64])
                dbg_out("BiasA", BiasA, BiasA[:], [128, 4, 128])
                kb.barrier()
                raise _Stop((nc, dbg_outs))

        def ln_stats(xap, xbuf, width, stats, mv, rstd, nmr=None):
            nchunk = width // 512
            for cidx in range(nchunk):
                kb.op(dve, lambda e, cidx=cidx: e.bn_stats(out=stats[:, cidx, :], in_=xap[:, cidx * 512:(cidx + 1) * 512]),
                      reads=[xbuf], writes=[stats.sub(cidx)])
            kb.op(dve, lambda e: e.bn_aggr(out=mv[:, 0:2], in_=stats[:, 0:nchunk, :].rearrange("p a b -> p (a b)")),
                  reads=[stats], writes=[mv])
            kb.op(pool, lambda e: e.tensor_tensor(out=mv[:, 2:3], in0=mv[:, 1:2], in1=cst[:, 1:2], op=ALU.add),
                  reads=[mv, cst], writes=[mv])
            kb.op(pool, lambda e: e.tensor_tensor(out=rstd[:, 0:1], in0=mv[:, 2:3], in1=cst[:, 0:1], op=ALU.pow),
                  reads=[mv, cst], writes=[rstd])
            if nmr is not None:
                kb.op(dve, lambda e: e.scalar_tensor_tensor(out=rstd[:, 1:2], in0=mv[:, 0:1], scalar=-1.0, in1=rstd[:, 0:1],
                                                            op0=ALU.mult, op1=ALU.mult), reads=[mv, rstd], writes=[rstd])

        def make_hT(xt, hT, xn, stats, mv, rstd, mi_shift, mi_scale, tok0=0):
            make_hT_A(xt, xn, stats, mv, rstd)
            make_hT_B(hT, xn, mi_shift, mi_scale, tok0)

        def make_hT_A(xt, xn, stats, mv, rstd):
            ln_stats(xt[:, :], xt, 1024, stats, mv, rstd, nmr=True)
            kb.op(act, lambda e: e.activation(out=xn[:], in_=xt[:, :], func=AF.Identity, bias=rstd[:, 1:2], scale=rstd[:, 0:1]),
                  reads=[xt, rstd], writes=[xn])

        def make_hT_B(hT, xn, mi_shift, mi_scale, tok0=0):
            for kc in range(8):
                kb.op(pe, lambda e, kc=kc: e.transpose(out=PT[:, kc * 128:(kc + 1) * 128], in_=xn[:, kc * 128:(kc + 1) * 128],
                                                      identity=identb[:]), reads=[xn, identb], writes=[PT], sig=(kc == 7))
            for kc in range(8):
                kb.op(act, lambda e, kc=kc: e.activation(out=hT[:, kc, tok0:tok0 + 128], in_=PT[:, kc * 128:(kc + 1) * 128],
                                                         func=AF.Identity, bias=modc[:, mi_shift, kc:kc + 1],
                                                         scale=modc[:, mi_scale, kc:kc + 1]),
                      reads=[PT, modc], writes=[hT.sub((kc, tok0))])

        def load_weight_block(dst, dst_ap_fn, src_ap, stg_buf, nk, scale_bc=None, scale_cols=None):
            half = nk // 2
            kb.dma(stg_buf[:, 0:half, :], src_ap[:, 0:half, :], writes=[stg_buf])
            kb.dma(stg_buf[:, half:nk, :], src_ap[:, half:nk, :], writes=[stg_buf])
            if scale_bc is None:
                kb.op(act, lambda e: e.activation(out=dst_ap_fn(slice(0, half)), in_=stg_buf[:, 0:half, :], func=AF.Copy),
                      reads=[stg_buf], writes=[dst])
                kb.op(pool, lambda e: e.tensor_copy(out=dst_ap_fn(slice(half, nk)), in_=stg_buf[:, half:nk, :]),
                      reads=[stg_buf], writes=[dst])
            else:
                for k in range(nk):
                    eng = dve if k % 2 == 0 else pool
                    kb.op(eng, lambda e, k=k: e.tensor_tensor(out=dst_ap_fn(k), in0=stg_buf[:, k, :],
                                                              in1=scale_bc[:, scale_cols], op=ALU.mult),
                          reads=[stg_buf, scale_bc], writes=[dst])

        win_v = win_d.rearrange("(kc p) n -> p kc n", p=128)
        with ExitStack() as es1:
            Win1 = kb.sb("Win1", [128, 8, 1040], BF16, es1)
            kb.cast_load(Win1, [(Win1[:, k0:k0 + 4, c0:c0 + 512], win_v[:, k0:k0 + 4, w0:w0 + 512], Win1.sub((k0, c0)))
                                for (c0, w0) in ((0, 1024), (512, 1536)) for k0 in (0, 4)]
                         + [(Win1[:, :, 1024:1040], win_v[:, :, 2560:2576], Win1.sub("g"))])
            xs = [kb.sb(f"xs{i}", [128, D], F32, es1) for i in range(2)]
            xn = kb.sb("xn", [128, D], BF16, es1)
            hT = kb.sb("hT", [128, 8, 128], BF16, es1)
            stats = kb.sb("stats", [128, 2, 6], F32, es1)
            mv = kb.sb("mv", [128, 4], F32, es1)
            rstd = kb.sb("rstd", [128, 2], F32, es1)
            raw = [kb.sb(f"raw{i}", [128, 4, 130], F32, es1) for i in range(3)]
            cvt = kb.sb("cvt", [128, 4, 128], F32, es1)

            def conv_finish(gc_prev, rb):
                for cc in range(4):
                    kb.op(dve, lambda e, cc=cc: e.tensor_scalar(out=cvt[:, cc, :], in0=rb[:, cc, 0:128],
                                                                scalar1=smallc[:, 12 + 0 * 4 + cc:13 + 0 * 4 + cc], scalar2=None,
                                                                op0=ALU.mult), reads=[rb, smallc], writes=[cvt.sub(cc)])
                    kb.op(dve, lambda e, cc=cc: e.scalar_tensor_tensor(out=cvt[:, cc, :], in0=rb[:, cc, 1:129],
                                                                       scalar=smallc[:, 12 + 1 * 4 + cc:13 + 1 * 4 + cc],
                                                                       in1=cvt[:, cc, :], op0=ALU.mult, op1=ALU.add),
                          reads=[rb, smallc, cvt.sub(cc)], writes=[cvt.sub(cc)])
                    kb.op(dve, lambda e, cc=cc: e.scalar_tensor_tensor(out=cvt[:, cc, :], in0=rb[:, cc, 2:130],
                                                                       scalar=smallc[:, 12 + 2 * 4 + cc:13 + 2 * 4 + cc],
                                                                       in1=cvt[:, cc, :], op0=ALU.mult, op1=ALU.add),
                          reads=[rb, smallc, cvt.sub(cc)], writes=[cvt.sub(cc)])
                t0 = gc_prev * 128
                kb.op(act, lambda e: e.activation(out=kT[:, :, t0:t0 + 128], in_=cvt[:, 2:4, :], func=AF.Silu),
                      reads=[cvt.sub(2), cvt.sub(3)], writes=[kT])
                if gc_prev >= NCT:
                    l0 = (gc_prev - NCT) * 128
                    kb.op(act, lambda e: e.activation(out=qT[:, :, l0:l0 + 128], in_=cvt[:, 0:2, :], func=AF.Silu),
                          reads=[cvt.sub(0), cvt.sub(1)], writes=[qT])

            xn_1 = [xn, kb.sb("xn_b", [128, D], BF16, es1)]
            hT_1 = [hT, kb.sb("hT_b", [128, 8, 128], BF16, es1)]
            stats_1 = [stats, kb.sb("stats_b", [128, 2, 6], F32, es1)]
            mv_1 = [mv, kb.sb("mv_b", [128, 4], F32, es1)]
            rstd_1 = [rstd, kb.sb("rstd_b", [128, 2], F32, es1)]

            xs3 = xs + [kb.sb("xs_c", [128, D], F32, es1)]
            xn3 = xn_1 + [kb.sb("xn_c", [128, D], BF16, es1)]
            stA = stats_1 + [kb.sb("stats_c", [128, 2, 6], F32, es1)]
            mvA = mv_1 + [kb.sb("mv_c", [128, 4], F32, es1)]
            rsA = rstd_1 + [kb.sb("rstd_c", [128, 2], F32, es1)]

            def tile1A(gc):
                if gc >= NG:
                    return
                is_ctx = gc < NCT
                src = ctx_d[gc * 128:(gc + 1) * 128, :] if is_ctx else x_d[(gc - NCT) * 128:(gc - NCT + 1) * 128, :]
                xt = xs3[gc % 3]
                kb.dma(xt[:, :], src[:, :], writes=[xt])
                make_hT_A(xt, xn3[gc % 3], stA[gc % 3], mvA[gc % 3], rsA[gc % 3])

            def tile1(gc):
                tile1A(gc + 1)
                sl = gc % 2
                hT = hT_1[sl]
                PBq, PBv, PBg = PB[3 * sl], PB[3 * sl + 1], PB[3 * sl + 2]
                is_ctx = gc < NCT
                make_hT_B(hT, xn3[gc % 3], 4 if is_ctx else 0, 5 if is_ctx else 1)
                for cc in range(4):
                    for kc in range(8):
                        kb.op(pe, lambda e, cc=cc, kc=kc: e.matmul(PBq[:, cc * 128:(cc + 1) * 128],
                                                                   lhsT=Win1[:, kc, cc * 128:(cc + 1) * 128], rhs=hT[:, kc, :],
                                                                   start=(kc == 0), stop=(kc == 7)),
                              reads=[Win1, hT.sub((kc, 0))], writes=[PBq], sig=(kc == 7 and cc == 3))
                for kc in range(8):
                    kb.op(pe, lambda e, kc=kc: e.matmul(PBv[:, :], lhsT=hT[:, kc, :], rhs=Win1[:, kc, 512:1024],
                                                        start=(kc == 0), stop=(kc == 7)),
                          reads=[Win1, hT.sub((kc, 0))], writes=[PBv], sig=(kc == 7))
                for kc in range(8):
                    kb.op(pe, lambda e, kc=kc: e.matmul(PBg[:, 0:16], lhsT=hT[:, kc, :], rhs=Win1[:, kc, 1024:1040],
                                                        start=(kc == 0), stop=(kc == 7)),
                          reads=[Win1, hT.sub((kc, 0))], writes=[PBg], sig=(kc == 7))
                rb = raw[gc % 3]
                first = gc in (0, NCT)
                last = gc in (NCT - 1, NG - 1)
                kb.op(act, lambda e: e.activation(out=rb[:, :, 1:129], in_=PBq[:, :].rearrange("p (c t) -> p c t", c=4), func=AF.Copy),
                      reads=[PBq], writes=[rb])
                kb.op(dve, lambda e: e.tensor_copy(out=vaug[:, gc, :, 0:128], in_=PBv[:, :].rearrange("p (h v) -> p h v", h=4)),
                      reads=[PBv], writes=[vaug])
                kb.op(dve, lambda e: e.tensor_tensor(out=Gt[:, gc, :], in0=PBg[:, 0:16], in1=bgb[:], op=ALU.add),
                      reads=[PBg, bgb], writes=[Gt])
                if first:
                    kb.op(pool, lambda e: e.memset(rb[:, :, 0:1], 0.0), writes=[rb])
                else:
                    rprev = raw[(gc - 1) % 3]
                    kb.op(pool, lambda e: e.tensor_copy(out=rb[:, :, 0:1], in_=rprev[:, :, 128:129]), reads=[rprev], writes=[rb])
                    kb.op(pool, lambda e: e.tensor_copy(out=rprev[:, :, 129:130], in_=rb[:, :, 1:2]), reads=[rb], writes=[rprev])
                    conv_finish(gc - 1, rprev)
                if last:
                    kb.op(pool, lambda e: e.memset(rb[:, :, 129:130], 0.0), writes=[rb])
                    conv_finish(gc, rb)
            tile1A(0)
            lists1 = [kb.record(tile1, gc) for gc in range(NG)]
            interleave(lists1, int(len(lists1[2]) * SKEW1))
            kb.barrier()

        if dbg:
            dbg_out("kT", kT, kT[:], [128, 2, NG * 128], BF16)
            dbg_out("qT", qT, qT[:], [128, 2, S], BF16)
            dbg_out("vaug", vaug, vaug[:], [128, NG, 4, 130], BF16)
            dbg_out("Gt", Gt, Gt[:], [128, NG, 16])
        if stage == 1:
            kb.barrier()
            raise _Stop((nc, dbg_outs))

        with ExitStack() as esg:
            NF = NG * 8
            Gv = Gt[:].rearrange("p c (d t h) -> p c d t h", d=2, t=2, h=4)
            LF = kb.sb("LF", [128, NG, 2, 4], F32, esg)
            T1 = kb.sb("T1", [128, NG, 2, 4], F32, esg)
            T2 = kb.sb("T2", [128, NG, 2, 4], F32, esg)
            Bc = kb.sb("Bc", [128, NG, 2, 4], F32, esg)
            Aa = kb.sb("Aa", [128, NG, 2, 4], F32, esg)
            Mcb = kb.sb("Mcb", [128, NG, 2, 4], F32, esg)
            rowA = kb.sb("rowA", [1, NG, 2, 4], F32, esg)
            rowB = kb.sb("rowB", [1, NG, 2, 4], F32, esg)
            rowM = kb.sb("rowM", [1, NG, 2, 4], F32, esg)
            rowm0 = kb.sb("rowm0", [1, NG, 2, 4], F32, esg)
            colmax = kb.sb("colmax", [128, 3], F32, esg)
            fl = lambda b: b[:].rearrange("p c d h -> p (c d h)")
            FG = Gv[:, :, :, 1, :]
            IG = Gv[:, :, :, 0, :]
            kb.op(dve, lambda e: e.tensor_scalar(out=T1[:], in0=FG, scalar1=-1.0, scalar2=None, op0=ALU.mult), reads=[Gt], writes=[T1])
            kb.op(dve, lambda e: e.tensor_tensor(out=T1[:], in0=T1[:], in1=FG, op=ALU.max), reads=[Gt, T1], writes=[T1])
            kb.op(act, lambda e: e.activation(out=T2[:], in_=T1[:], func=AF.Exp, scale=-1.0), reads=[T1], writes=[T2])
            kb.op(act, lambda e: e.activation(out=T2[:], in_=T2[:], func=AF.Ln, bias=cst[:, 3:4], scale=1.0), reads=[T2, cst], writes=[T2])
            kb.op(dve, lambda e: e.tensor_scalar(out=T1[:], in0=FG, scalar1=0.0, scalar2=None, op0=ALU.min), reads=[Gt], writes=[T1])
            kb.op(dve, lambda e: e.tensor_tensor(out=LF[:], in0=T1[:], in1=T2[:], op=ALU.subtract), reads=[T1, T2], writes=[LF])
            PBv = PB[0][:, 0:NF].rearrange("p (c d h) -> p c d h", c=NG, d=2, h=4)
            kb.op(pe, lambda e: e.matmul(PB[0][:, 0:NF], lhsT=LT[:], rhs=fl(LF), start=True, stop=True),
                  reads=[LT, LF], writes=[PB[0]])
            PBv1 = PB[1][:, 0:NF].rearrange("p (c d h) -> p c d h", c=NG, d=2, h=4)
            kb.op(pe, lambda e: e.matmul(PB[1][:, 0:NF], lhsT=UT[:], rhs=fl(LF), start=True, stop=True),
                  reads=[UT, LF], writes=[PB[1]])
            kb.op(dve, lambda e: e.tensor_copy(out=Bc[:, :, 0, :], in_=PBv[:, :, 0, :]), reads=[PB[0]], writes=[Bc])
            kb.op(dve, lambda e: e.tensor_copy(out=Bc[:, :, 1, :], in_=PBv1[:, :, 1, :]), reads=[PB[1]], writes=[Bc])
            kb.op(dve, lambda e: e.tensor_tensor(out=Aa[:], in0=IG, in1=Bc[:], op=ALU.subtract), reads=[Gt, Bc], writes=[Aa])
            AaF = fl(Aa)
            segs = [(0, 128), (128, 128), (256, NF - 256)]
            for si, (o, n) in enumerate(segs):
                kb.op(pe, lambda e, o=o, n=n: e.transpose(out=PB[2][0:n, 0:128], in_=AaF[:, o:o + n], identity=identf[:]),
                      reads=[Aa, identf], writes=[PB[2]])
                kb.op(dve, lambda e, si=si, n=n: e.reduce_max(out=colmax[0:n, si:si + 1], in_=PB[2][0:n, 0:128], axis=mybir.AxisListType.X),
                      reads=[PB[2]], writes=[colmax])
                kb.op(pe, lambda e, si=si, n=n, o=o: e.matmul(PB[3][0:1, o:o + n], lhsT=colmax[0:n, si:si + 1], rhs=identf[0:n, 0:n],
                                                               start=True, stop=True), reads=[colmax, identf], writes=[PB[3]])
            kb.op(dve, lambda e: e.tensor_copy(out=fl(rowA), in_=PB[3][0:1, 0:NF]), reads=[PB[3]], writes=[rowA])
            kb.op(pe, lambda e: e.matmul(PB[4][0:1, 0:NF], lhsT=onesf[:, 0:1], rhs=fl(LF), start=True, stop=True),
                  reads=[onesf, LF], writes=[PB[4]])
            kb.op(dve, lambda e: e.tensor_copy(out=fl(rowB), in_=PB[4][0:1, 0:NF]), reads=[PB[4]], writes=[rowB])
            order = [list(range(NG)), [1, 0] + list(range(NG - 1, NCT - 1, -1))]
            for d in range(2):
                g0 = order[d][0]
                kb.op(dve, lambda e, d=d, g0=g0: e.memset(rowm0[0:1, g0, d, :], 0.0), writes=[rowm0])
            for j in range(NG):
                for d in range(2):
                    gcur = order[d][j]
                    kb.op(dve, lambda e, d=d, gcur=gcur: e.tensor_tensor(out=rowM[0:1, gcur, d, :], in0=rowm0[0:1, gcur, d, :],
                                                                         in1=rowA[0:1, gcur, d, :], op=ALU.max),
                          reads=[rowm0, rowA], writes=[rowM])
                    if j + 1 < NG:
                        gn = order[d][j + 1]
                        kb.op(dve, lambda e, d=d, gcur=gcur, gn=gn: e.tensor_tensor(out=rowm0[0:1, gn, d, :], in0=rowM[0:1, gcur, d, :],
                                                                                   in1=rowB[0:1, gcur, d, :], op=ALU.add),
                              reads=[rowM, rowB], writes=[rowm0])
            kb.op(pe, lambda e: e.matmul(PB[5][:, 0:NF], lhsT=onesf[0:1, :], rhs=fl(rowM), start=True, stop=True),
                  reads=[onesf, rowM], writes=[PB[5]])
            kb.op(pe, lambda e: e.matmul(PB[6][:, 0:NF], lhsT=onesf[0:1, :], rhs=fl(rowm0), start=True, stop=True),
                  reads=[onesf, rowm0], writes=[PB[6]])
            kb.op(dve, lambda e: e.tensor_copy(out=fl(Mcb), in_=PB[5][:, 0:NF]), reads=[PB[5]], writes=[Mcb])
            kb.op(dve, lambda e: e.tensor_tensor(out=T1[:], in0=Aa[:], in1=Mcb[:], op=ALU.subtract), reads=[Aa, Mcb], writes=[T1])
            kb.op(act, lambda e: e.activation(out=WS[:].rearrange("p c j -> p (c j)"), in_=fl(T1), func=AF.Exp), reads=[T1], writes=[WS])
            kb.op(dve, lambda e: e.tensor_tensor(out=fl(T2), in0=PB[6][:, 0:NF], in1=fl(Mcb), op=ALU.subtract), reads=[PB[6], Mcb], writes=[T2])
            kb.op(act, lambda e: e.activation(out=CW[:].rearrange("p c j -> p (c j)"), in_=fl(T2), func=AF.Exp), reads=[T2], writes=[CW])
            kb.op(dve, lambda e: e.tensor_tensor(out=T1[:], in0=Bc[:], in1=Mcb[:], op=ALU.add), reads=[Bc, Mcb], writes=[T1])
            kb.op(act, lambda e: e.activation(out=LB[:].rearrange("p c j -> p (c j)"), in_=fl(T1), func=AF.Exp, bias=cst[:, 2:3], scale=-1.0),
                  reads=[T1, cst], writes=[LB])
            kb.barrier()

        if dbg:
            dbg_out("WS", WS, WS[:], [128, NG, 8])
            dbg_out("CW", CW, CW[:], [128, NG, 8])
            dbg_out("LB", LB, LB[:], [128, NG, 8])
        if stage == 2:
            kb.barrier()
            raise _Stop((nc, dbg_outs))

        with ExitStack() as ess:
            Cc = [kb.sb(f"Cc{d}", [128, 2, 130], F32, ess) for d in range(2)]
            kp = [kb.sb(f"kp{i}", [128, 4, 64], BF16, ess) for i in range(2)]
            c0b = [kb.sb(f"c0b{i}", [128, 2, 130], BF16, ess) for i in range(2)]
            for d in range(2):
                kb.op(pool, lambda e, d=d: e.memset(Cc[d][:], 0.0), writes=[Cc[d]])
            units = [(j, d) for j in range(NG) for d in range(2)]

            def stepA(u):
                j, d = units[u]
                if j == NG - 1:
                    return
                gcur = order[d][j]
                kpb = kp[u % 2]
                for pr in range(2):
                    kb.op(pe, lambda e, pr=pr: e.transpose(out=PT[:, pr * 128:(pr + 1) * 128],
                                                           in_=kT[:, pr, gcur * 128:(gcur + 1) * 128], identity=identb[:]),
                          reads=[kT, identb], writes=[PT], sig=(pr == 1))
                for h in range(4):
                    kb.op(act, lambda e, h=h: e.activation(
                        out=kpb[:, h, :], in_=PT[:, h * 64:(h + 1) * 64], func=AF.Identity,
                        scale=WS[:, gcur, d * 4 + h:d * 4 + h + 1]), reads=[PT, WS], writes=[kpb.sub(h)])
                bank = PB[(u % 2) * 2:(u % 2) * 2 + 2]
                for h in range(4):
                    bk = bank[h // 2]
                    kb.op(pe, lambda e, h=h, bk=bk: e.matmul(
                        bk[:, (h % 2) * 130:(h % 2) * 130 + 130], lhsT=kpb[:, (h // 2) * 2:(h // 2) * 2 + 2, :].rearrange("p a b -> p (a b)"),
                        rhs=vaug[:, gcur, h, :], start=True, stop=True), reads=[kpb.sub((h // 2) * 2), kpb.sub((h // 2) * 2 + 1), vaug], writes=[bk])

            Cnext = [kb.sb(f"Cn{d}", [128, 2, 130], F32, ess) for d in range(2)]
            for d in range(2):
                kb.op(pool, lambda e, d=d: e.memset(Cnext[d][:], 0.0), writes=[Cnext[d]])
            Cpp = [[Cc[d], Cnext[d]] for d in range(2)]

            def stepB(u):
                j, d = units[u]
                gcur = order[d][j]
                Ccur = Cpp[d][j % 2]
                Cnew = Cpp[d][(j + 1) % 2]
                if gcur >= NCT:
                    for h in range(4):
                        p0 = (h % 2) * 64
                        kb.op(pool, lambda e, h=h, p0=p0: e.tensor_scalar(
                            out=ST[p0:p0 + 64, gcur - NCT, d, h // 2, :], in0=Ccur[p0:p0 + 64, h // 2, :],
                            scalar1=CW[p0:p0 + 64, gcur, d * 4 + h:d * 4 + h + 1], scalar2=1.0, op0=ALU.mult, op1=ALU.mult),
                            reads=[Ccur.sub(h), CW], writes=[ST])
                if j == NG - 1:
                    return
                bank = PB[(u % 2) * 2:(u % 2) * 2 + 2]
                for h in range(4):
                    p0 = (h % 2) * 64
                    bk = bank[h // 2]
                    kb.op(dve, lambda e, h=h, p0=p0, bk=bk: e.scalar_tensor_tensor(
                        out=Cnew[p0:p0 + 64, h // 2, :], in0=Ccur[p0:p0 + 64, h // 2, :],
                        scalar=CW[p0:p0 + 64, gcur, d * 4 + h:d * 4 + h + 1],
                        in1=bk[p0:p0 + 64, (h % 2) * 130:(h % 2) * 130 + 130], op0=ALU.mult, op1=ALU.add),
                        reads=[Ccur.sub(h), CW, bk], writes=[Cnew.sub(h)])

            stepA(0)
            for u in range(len(units)):
                if u + 1 < len(units):
                    stepA(u + 1)
                stepB(u)
            kb.barrier()

        if dbg:
            dbg_out("ST", ST, ST[:], [128, NT, 2, 2, 130], BF16)
        if stage == 3:
            kb.barrier()
            raise _Stop((nc, dbg_outs))

        es_gc.close()
        wout_v = wout_d.rearrange("(kc p) n -> p kc n", p=128)
        with ExitStack() as es2:
            Win2 = kb.sb("Win2", [128, 8, 1536], BF16, es2)
            Wo = kb.sb("Wo", [128, 8, D], BF16, es2)
            with ExitStack() as esw:
                wstg = [kb.sb(f"wstg2{i}", [128, 8, 256], F32, esw) for i in range(2)]
                g1bc = kb.sb("g1bc", [128, D], F32, esw)
                kb.dma(g1bc[:], g1row[:, :], writes=[g1bc])
                nb = 0
                kb.cast_load(Win2, [(Win2[:, k0:k0 + 4, c0:c0 + 512], win_v[:, k0:k0 + 4, w0:w0 + 512], Win2.sub((k0, c0)))
                                    for (c0, w0) in ((0, 0), (512, 512), (1024, 2048)) for k0 in (0, 4)])
                for c0 in (0, 256, 512, 768):
                    load_weight_block(Wo, lambda k, c0=c0: Wo[:, k, c0:c0 + 256], wout_v[:, :, c0:c0 + 256], wstg[nb % 2], 8,
                                      scale_bc=g1bc, scale_cols=slice(c0, c0 + 256))
                    nb += 1
                kb.barrier()
            xs = [kb.sb(f"x2s{i}", [128, D], F32, es2) for i in range(2)]
            def two(name, shape, dt):
                return [kb.sb(f"{name}_{k}", shape, dt, es2) for k in range(2)]
            xn_ = two("xn2", [128, D], BF16)
            hT_ = two("hT2", [128, 8, 128], BF16)
            stats_ = two("stats2", [128, 4, 6], F32)
            mv_ = two("mv2", [128, 4], F32)
            rstd_ = two("rstd2", [128, 2], F32)
            mv4_ = two("mv4", [128, 4, 4], F32)
            rs4_ = two("rs4", [128, 4], F32)
            uT_ = two("uT", [128, 4, 128], BF16)
            sgo_ = two("sgo", [128, 4, 128], BF16)
            vn_ = two("vn", [128, 512], BF16)
            tA_1 = kb.sb("tA", [128, 4, 128], F32, es2)
            tA_ = [tA_1, tA_1]
            yT_ = two("yT", [128, 8, 128], BF16)
            sT_ = two("sT", [128, 8, 128], BF16)
            WM_1 = kb.sb("WM", [128, 8, 128], BF16, es2)
            WM_ = [WM_1, WM_1]
            dn_ = two("dn", [128, 3, 8], F32)
            hs_ = two("hs", [128, 4, 128], F32)
            hn_ = two("hn", [128, 4, 128], BF16)
            Q2_ = [[kb.sb(f"Q2{k}_{i}", [128, 2, 128], BF16, es2) for i in range(2)] for k in range(2)]
            hg = kb.sb("hg", [128, 4], F32, es2)
            for k in range(2):
                for pr in range(2):
                    kb.op(pool, lambda e, k=k, pr=pr: e.memset(Q2_[k][pr][:], 0.0), writes=[Q2_[k][pr]])
            kb.op(dve, lambda e: e.tensor_copy(out=hg[:], in_=smallc[:, 8:12]), reads=[smallc], writes=[hg])

            xs3 = xs + [kb.sb("x2s_c", [128, D], F32, es2)]
            xn3 = xn_ + [kb.sb("xn2_c", [128, D], BF16, es2)]
            stA = [kb.sb(f"stA{k}", [128, 2, 6], F32, es2) for k in range(3)]
            mvA = [kb.sb(f"mvA{k}", [128, 4], F32, es2) for k in range(3)]
            rsA = [kb.sb(f"rsA{k}", [128, 2], F32, es2) for k in range(3)]

            def tile2A(i):
                if i >= NT:
                    return
                xt = xs3[i % 3]
                kb.dma(xt[:, :], x_d[i * 128:(i + 1) * 128, :], writes=[xt])
                make_hT_A(xt, xn3[i % 3], stA[i % 3], mvA[i % 3], rsA[i % 3])

            def tile2(i):
                tile2A(i + 1)
                sl = i % 2
                xn, hT, stats, mv, rstd, mv4, rs4 = xn3[i % 3], hT_[sl], stats_[sl], mv_[sl], rstd_[sl], mv4_[sl], rs4_[sl]
                uT, sgo, vn, tA, yT, sT, dn, hs, hn, Q2, WM = uT_[sl], sgo_[sl], vn_[sl], tA_[sl], yT_[sl], sT_[sl], dn_[sl], hs_[sl], hn_[sl], Q2_[sl], WM_[sl]
                gc = i + NCT
                t0k = gc * 128
                t0q = i * 128
                xt = xs3[i % 3]
                def branchP():
                    make_hT_B(hT, xn, 0, 1)
                    for (bank, c0) in ((PB[0], 0), (PB[1], 1024)):
                        for cc in range(4):
                            for kc in range(8):
                                kb.op(pe, lambda e, cc=cc, kc=kc, bank=bank, c0=c0: e.matmul(
                                    bank[:, cc * 128:(cc + 1) * 128], lhsT=Win2[:, kc, c0 + cc * 128:c0 + (cc + 1) * 128], rhs=hT[:, kc, :],
                                    start=(kc == 0), stop=(kc == 7)), reads=[Win2, hT.sub((kc, 0))], writes=[bank], sig=(kc == 7 and cc == 3))
                    for kc in range(8):
                        kb.op(pe, lambda e, kc=kc: e.matmul(PB[2][:, :], lhsT=hT[:, kc, :], rhs=Win2[:, kc, 512:1024],
                                                            start=(kc == 0), stop=(kc == 7)), reads=[Win2, hT.sub((kc, 0))], writes=[PB[2]], sig=(kc == 7))
                    kb.op(act, lambda e: e.activation(out=uT[:].rearrange("p c t -> p (c t)"), in_=PB[0][:, :], func=AF.Copy),
                          reads=[PB[0]], writes=[uT])
                    kb.op(act, lambda e: e.activation(out=sgo[:].rearrange("p c t -> p (c t)"), in_=PB[1][:, :], func=AF.Sigmoid),
                          reads=[PB[1]], writes=[sgo])
                    ln_stats(PB[2][:, :], PB[2], 512, stA[i % 3], mvA[i % 3], rsA[i % 3])
                    kb.op(dve, lambda e: e.tensor_scalar(out=vn[:], in0=PB[2][:, :], scalar1=mvA[i % 3][:, 0:1], scalar2=rsA[i % 3][:, 0:1],
                                                         op0=ALU.subtract, op1=ALU.mult), reads=[PB[2], mvA[i % 3], rsA[i % 3]], writes=[vn])
                    for g in range(4):
                        kb.op(pe, lambda e, g=g: e.matmul(PB[3][:, g * 128:(g + 1) * 128], lhsT=vn[:, g * 128:(g + 1) * 128], rhs=wsT[:, g, :],
                                                          start=True, stop=True), reads=[vn, wsT], writes=[PB[3]], sig=(g == 3))
                    for g in range(4):
                        kb.op(dve, lambda e, g=g: e.scalar_tensor_tensor(out=tA[:, g, :], in0=PB[3][:, g * 128:(g + 1) * 128],
                                                                         scalar=smallc[:, g:g + 1], in1=BiasA[:, g, :], op0=ALU.mult, op1=ALU.add),
                              reads=[PB[3], smallc, BiasA], writes=[tA.sub(g)])
                    kb.op(pool, lambda e: e.tensor_tensor(out=yT[:, 0:4, :], in0=tA[:], in1=uT[:], op=ALU.mult), reads=[tA, uT], writes=[yT.sub('A')])

                def branchM():
                    for d in range(2):
                        msk = LT if d == 0 else UT
                        for h in range(4):
                            kb.op(pool, lambda e, d=d, h=h, msk=msk: e.tensor_scalar(
                                out=WM[:, d * 4 + h, :], in0=msk[:], scalar1=WS[:, gc, d * 4 + h:d * 4 + h + 1], scalar2=1.0,
                                op0=ALU.mult, op1=ALU.mult), reads=[msk, WS], writes=[WM.sub((d, h))])
                    for pr in range(2):
                        for hh in range(2):
                            kb.op(pool, lambda e, pr=pr, hh=hh: e.tensor_copy(out=Q2[pr][hh * 64:(hh + 1) * 64, hh, :],
                                                                              in_=qT[hh * 64:(hh + 1) * 64, pr, t0q:t0q + 128]),
                                  reads=[qT], writes=[Q2[pr].sub(hh)])
                    for pr in range(2):
                        kb.op(pe, lambda e, pr=pr: e.matmul(PB[4][:, pr * 256:(pr + 1) * 256], lhsT=kT[:, pr, t0k:t0k + 128],
                                                            rhs=Q2[pr][:].rearrange("p a t -> p (a t)"), start=True, stop=True),
                              reads=[kT, Q2[pr]], writes=[PB[4]], sig=(pr == 1))
                    for d in range(2):
                        kb.op(dve, lambda e, d=d: e.tensor_tensor(out=sT[:, d * 4:(d + 1) * 4, :].rearrange("p h t -> p (h t)"), in0=PB[4][:, :],
                                                                  in1=WM[:, d * 4:(d + 1) * 4, :].rearrange("p h t -> p (h t)"), op=ALU.mult),
                              reads=[PB[4]] + [WM.sub((d, hh_)) for hh_ in range(4)], writes=[sT.sub(d)])
                    for d in range(2):
                        bank = PB[5 + d]
                        for h in range(4):
                            kb.op(pe, lambda e, d=d, h=h, bank=bank: e.matmul(bank[:, h * 128:(h + 1) * 128], lhsT=sT[:, d * 4 + h, :],
                                                                              rhs=vaug[:, gc, h, 0:128], start=True, stop=False),
                                  reads=[sT.sub(d), vaug], writes=[bank], sig=False)
                            kb.op(pe, lambda e, d=d, h=h, bank=bank: e.matmul(
                                bank[:, h * 128:(h + 1) * 128], lhsT=Q2[h // 2][:, h % 2, :],
                                rhs=ST[:, i, d, h // 2, 0:128], start=False, stop=True),
                                reads=[Q2[h // 2], ST], writes=[bank], sig=(h == 3))
                    for d in range(2):
                        for h in range(4):
                            jn = d * 4 + h
                            kb.op(pe, lambda e, d=d, h=h, jn=jn: e.matmul(PB[4][:, 2 * jn:2 * jn + 2], lhsT=sT[:, jn, :],
                                                                          rhs=vaug[:, gc, h, 128:130], start=True, stop=False),
                                  reads=[sT.sub(d), vaug], writes=[PB[4]], sig=False)
                            kb.op(pe, lambda e, d=d, h=h, jn=jn: e.matmul(
                                PB[4][:, 2 * jn:2 * jn + 2], lhsT=Q2[h // 2][:, h % 2, :],
                                rhs=ST[:, i, d, h // 2, 128:130], start=False, stop=True),
                                reads=[Q2[h // 2], ST], writes=[PB[4]], sig=(jn == 7))
                    den = PB[4][:, 0:16].rearrange("p (j two) -> p j two", two=2)[:, :, 0]
                    kb.op(dve, lambda e: e.tensor_scalar(out=dn[:, 0, :], in0=den, scalar1=-1.0, scalar2=None, op0=ALU.mult),
                          reads=[PB[4]], writes=[dn])
                    kb.op(dve, lambda e: e.tensor_tensor(out=dn[:, 1, :], in0=dn[:, 0, :], in1=den, op=ALU.max),
                          reads=[PB[4], dn], writes=[dn])
                    kb.op(dve, lambda e: e.tensor_tensor(out=dn[:, 0, :], in0=dn[:, 1, :], in1=LB[:, gc, :], op=ALU.max),
                          reads=[dn, LB], writes=[dn])
                    kb.op(dve, lambda e: e.reciprocal(out=dn[:, 2, :], in_=dn[:, 0, :]), reads=[dn], writes=[dn])
                    for h in range(4):
                        kb.op(act, lambda e, h=h: e.activation(out=hs[:, h, :], in_=PB[5][:, h * 128:(h + 1) * 128], func=AF.Identity,
                                                               scale=dn[:, 2, h:h + 1]), reads=[PB[5], dn], writes=[hs.sub(h)])
                    for h in range(4):
                        kb.op(dve, lambda e, h=h: e.scalar_tensor_tensor(out=hs[:, h, :], in0=PB[6][:, h * 128:(h + 1) * 128],
                                                                         scalar=dn[:, 2, 4 + h:5 + h], in1=hs[:, h, :], op0=ALU.mult, op1=ALU.add),
                              reads=[PB[6], dn, hs.sub(h)], writes=[hs.sub(h)])
                    for h in range(4):
                        kb.op(dve, lambda e, h=h: e.bn_stats(out=stats[:, h, :], in_=hs[:, h, :]), reads=[hs.sub(h)], writes=[stats.sub(h)])
                    for h in range(4):
                        kb.op(dve, lambda e, h=h: e.bn_aggr(out=mv4[:, h, 0:2], in_=stats[:, h, :]), reads=[stats.sub(h)], writes=[mv4.sub(h)])
                    kb.op(pool, lambda e: e.tensor_tensor(out=mv4[:, :, 2], in0=mv4[:, :, 1], in1=cst[:, 1:2].to_broadcast([128, 4]), op=ALU.add),
                          reads=[mv4, cst], writes=[mv4])
                    kb.op(pool, lambda e: e.tensor_tensor(out=rs4[:], in0=mv4[:, :, 2], in1=cst[:, 0:1].to_broadcast([128, 4]), op=ALU.pow),
                          reads=[mv4, cst], writes=[rs4])
                    for h in range(4):
                        kb.op(dve, lambda e, h=h: e.tensor_scalar(out=hn[:, h, :], in0=hs[:, h, :], scalar1=mv4[:, h, 0:1], scalar2=rs4[:, h:h + 1],
                                                                  op0=ALU.subtract, op1=ALU.mult), reads=[hs.sub(h), mv4, rs4], writes=[hn.sub(h)])

                Pl = kb.record(branchP)
                Ml = kb.record(branchM)
                ip = im = 0
                tot = len(Pl) + len(Ml)
                if MERGE2 == 2:
                    kgm = len(Pl) - 9
                    kb.rec.extend(Ml[:12] + Pl[:kgm] + Ml[12:] + Pl[kgm:])
                    tot = 0
                for k in range(tot):
                    if MERGE2 and im * len(Pl) <= ip * len(Ml) and im < len(Ml):
                        kb.rec.append(Ml[im]); im += 1
                    elif ip < len(Pl):
                        kb.rec.append(Pl[ip]); ip += 1
                    else:
                        kb.rec.append(Ml[im]); im += 1
                for h in range(4):
                    kb.op(pe, lambda e, h=h: e.transpose(out=PT[:, h * 128:(h + 1) * 128], in_=hn[:, h, :], identity=identb[:]),
                          reads=[hn.sub(h), identb], writes=[PT], sig=(h == 3))
                for h in range(4):
                    kb.op(dve, lambda e, h=h: e.scalar_tensor_tensor(out=yT[:, 4 + h, :], in0=PT[:, h * 128:(h + 1) * 128],
                                                                     scalar=hg[:, h:h + 1], in1=sgo[:, h, :], op0=ALU.mult, op1=ALU.mult),
                          reads=[PT, hg, sgo], writes=[yT.sub(4 + h)])
                for half in range(2):
                    for kc in range(8):
                        kb.op(pe, lambda e, half=half, kc=kc: e.matmul(PB[3 + half][:, :], lhsT=yT[:, kc, :],
                                                                       rhs=Wo[:, kc, half * 512:(half + 1) * 512],
                                                                       start=(kc == 0), stop=(kc == 7)),
                              reads=[yT, Wo], writes=[PB[3 + half]], sig=(kc == 7))
                for half in range(2):
                    kb.op(dve, lambda e, half=half: e.scalar_tensor_tensor(out=xt[:, half * 512:(half + 1) * 512],
                                                                           in0=xt[:, half * 512:(half + 1) * 512], scalar=ALPHA,
                                                                           in1=PB[3 + half][:, :], op0=ALU.mult, op1=ALU.add),
                          reads=[xt.sub(half), PB[3 + half]], writes=[xt.sub(half)])
                ln_stats(xt[:, :], xt, 1024, stats, mv, rstd)
                kb.op(dve, lambda e: e.scalar_tensor_tensor(out=xt[:, :], in0=xt[:, :], scalar=mv[:, 0:1], in1=ln1gb[:],
                                                            op0=ALU.subtract, op1=ALU.mult), reads=[xt, mv, ln1gb], writes=[xt])
                kb.op(dve, lambda e: e.scalar_tensor_tensor(out=xt[:, :], in0=xt[:, :], scalar=rstd[:, 0:1], in1=ln1bb[:],
                                                            op0=ALU.mult, op1=ALU.add), reads=[xt, rstd, ln1bb], writes=[xt])
                kb.dma(y_d[i * 128:(i + 1) * 128, :], xt[:, :], reads=[xt])
            tile2A(0)
            lists2 = [kb.record(tile2, i) for i in range(NT)]
            interleave(lists2, int(len(lists2[0]) * SKEW2))
            kb.barrier()

        if stage == 4:
            raise _Stop((nc, dbg_outs))
        es0.close()
        es_p.close()

        GRP = 2
        w1_v = w1_d.rearrange("(kc p) n -> p kc n", p=128)
        w2_v = w2_d.rearrange("(j p) n -> p j n", p=128)
        with ExitStack() as es3:
            W1b = kb.sb("W1b", [128, 8, DFF], BF16, es3)
            W2b = kb.sb("W2b", [128, 32, D], BF16, es3)
            ln2gb = kb.sb("ln2gb", [128, D], F32, es3)
            ln2bb = kb.sb("ln2bb", [128, D], F32, es3)
            b2h = kb.sb("b2h", [1, 2, D], BF16, es3)
            kb.dma(ln2gb[:], ln2g_d[0:1, :].to_broadcast([128, D]), writes=[ln2gb])
            kb.dma(ln2bb[:], ln2b_d[0:1, :].to_broadcast([128, D]), writes=[ln2bb])
            g2bc = kb.sb("g2bc", [128, D], F32, es3)
            kb.dma(g2bc[:], g2row[:, :], writes=[g2bc])
            kb.cast_load(W1b, [(W1b[:, k0:k0 + 4, blk * 512:(blk + 1) * 512], w1_v[:, k0:k0 + 4, blk * 512:(blk + 1) * 512], W1b.sub(blk).sub(k0), W1b.sub(blk))
                               for blk in range(8) for k0 in (0, 4)], depth=3)
            kb.cast_load(W2b, [(W2b[:, blk * 4:(blk + 1) * 4, :], w2_v[:, blk * 4:(blk + 1) * 4, :], W2b.sub(blk), W2b.sub(blk))
                               for blk in range(8)], depth=3)
            with ExitStack() as esw:
                b2bc = kb.sb("b2bc", [128, D], F32, esw)
                kb.dma(b2bc[0:1, :], b2_d[0:1, :], writes=[b2bc])
                kb.op(dve, lambda e: e.tensor_copy(out=b2h[0:1, 0, :], in_=b2bc[0:1, :]), reads=[b2bc], writes=[b2h])
                kb.op(dve, lambda e: e.tensor_tensor(out=b2bc[0:1, :], in0=b2bc[0:1, :], in1=b2h[0:1, 0, :], op=ALU.subtract),
                      reads=[b2bc, b2h], writes=[b2bc])
                kb.op(dve, lambda e: e.tensor_copy(out=b2h[0:1, 1, :], in_=b2bc[0:1, :]), reads=[b2bc], writes=[b2h])
                kb.barrier()
            tmpo = [kb.sb(f"tmpo{i}", [128, 512], F32, es3) for i in range(2)]
            xs = [kb.sb(f"x3s{i}", [128, D], F32, es3) for i in range(2 * GRP)]
            xn = kb.sb("xn3", [128, D], BF16, es3)
            h2T_ = [kb.sb(f"h2T{k}", [128, 8, GRP * 128], BF16, es3) for k in range(2)]
            hid = kb.sb("hid", [128, 32, GRP * 128], BF16, es3)
            hidb = [Buf(hid.t, f"hid{j}") for j in range(32)]
            rl = [kb.sb(f"rl{i}", [128, GRP * 128], BF16, es3) for i in range(4)]
            stats_p = kb.sb("stats3p", [128, 2, 6], F32, es3)
            mv_p = kb.sb("mv3p", [128, 4], F32, es3)
            rstd_p = kb.sb("rstd3p", [128, 2], F32, es3)
            stats_e = kb.sb("stats3e", [128, 2, 6], F32, es3)
            mv_e = kb.sb("mv3e", [128, 4], F32, es3)
            rstd_e = kb.sb("rstd3e", [128, 2], F32, es3)
            NGRP = NT // GRP

            xn_3 = [xn] + [kb.sb(f"xn3_{a}", [128, D], BF16, es3) for a in range(1, GRP)]

            def prep3A(gi):
                for a in range(GRP):
                    ti = gi * GRP + a
                    xt = xs[(gi % 2) * GRP + a]
                    kb.dma(xt[:, :], y_d[ti * 128:(ti + 1) * 128, :], writes=[xt])
                    make_hT_A(xt, xn_3[a], stats_p, mv_p, rstd_p)

            def prep3B(gi):
                for a in range(GRP):
                    make_hT_B(h2T_[gi % 2], xn_3[a], 2, 3, tok0=a * 128)

            def main3(gi):
                h2T = h2T_[gi % 2]
                for j in range(32):
                    bank = PB[j % 2]
                    for kc in range(8):
                        kb.op(pe, lambda e, j=j, kc=kc, bank=bank: e.matmul(bank[:, 0:GRP * 128], lhsT=W1b[:, kc, j * 128:(j + 1) * 128],
                                                                            rhs=h2T[:, kc, :], start=(kc == 0), stop=(kc == 7)),
                              reads=[W1b.sub(j // 4)] + [h2T.sub((kc, a_ * 128)) for a_ in range(GRP)], writes=[bank], sig=(kc == 7))
                    rb = rl[j % 4]
                    kb.op(act, lambda e, j=j, bank=bank, rb=rb: e.activation(out=rb[:], in_=bank[:, 0:GRP * 128], func=AF.Relu,
                                                                             bias=smallc[:, 24 + j:25 + j], scale=1.0),
                          reads=[bank, smallc], writes=[rb])
                    eng = pool if j % 4 == 3 else dve
                    kb.op(eng, lambda e, j=j, rb=rb: e.tensor_tensor(out=hid[:, j, :], in0=rb[:], in1=rb[:], op=ALU.mult),
                          reads=[rb], writes=[hidb[j]])
                for a in range(GRP):
                    ti = gi * GRP + a
                    xt = xs[(gi % 2) * GRP + a]
                    for half in range(2):
                        bank = PB[2 + 2 * (a % 2) + half]
                        for j in range(32):
                            kb.op(pe, lambda e, j=j, a=a, half=half, bank=bank: e.matmul(
                                bank[:, :], lhsT=hid[:, j, a * 128:(a + 1) * 128], rhs=W2b[:, j, half * 512:(half + 1) * 512],
                                start=(j == 0), stop=False), reads=[hidb[j], W2b.sub(j // 4)], writes=[bank], sig=False)
                        for hl in range(2):
                            kb.op(pe, lambda e, hl=hl, half=half, bank=bank: e.matmul(
                                bank[:, :], lhsT=onesb[0:1, :], rhs=b2h[0:1, hl, half * 512:(half + 1) * 512],
                                start=False, stop=(hl == 1)), reads=[onesb, b2h], writes=[bank], sig=(hl == 1))
                        tm = tmpo[half]
                        kb.op(dve, lambda e, half=half, bank=bank, tm=tm: e.tensor_tensor(
                            out=tm[:], in0=bank[:, :], in1=g2bc[:, half * 512:(half + 1) * 512], op=ALU.mult),
                            reads=[bank, g2bc], writes=[tm])
                        kb.op(dve, lambda e, half=half, tm=tm, xt=xt: e.scalar_tensor_tensor(
                            out=xt[:, half * 512:(half + 1) * 512], in0=xt[:, half * 512:(half + 1) * 512], scalar=ALPHA,
                            in1=tm[:], op0=ALU.mult, op1=ALU.add), reads=[xt.sub(half), tm], writes=[xt.sub(half)])
                    ln_stats(xt[:, :], xt, 1024, stats_e, mv_e, rstd_e)
                    kb.op(dve, lambda e, xt=xt: e.scalar_tensor_tensor(out=xt[:, :], in0=xt[:, :], scalar=mv_e[:, 0:1], in1=ln2gb[:],
                                                                       op0=ALU.subtract, op1=ALU.mult), reads=[xt, mv_e, ln2gb], writes=[xt])
                    kb.op(dve, lambda e, xt=xt: e.scalar_tensor_tensor(out=xt[:, :], in0=xt[:, :], scalar=rstd_e[:, 0:1], in1=ln2bb[:],
                                                                       op0=ALU.mult, op1=ALU.add), reads=[xt, rstd_e, ln2bb], writes=[xt])
                    kb.dma(y_d[ti * 128:(ti + 1) * 128, :], xt[:, :], reads=[xt])

            for st in kb.record(prep3A, 0) + kb.record(prep3B, 0):
                st()
            for gi in range(NGRP):
                M = kb.record(main3, gi)
                PA = kb.record(prep3A, gi + 1) if gi + 1 < NGRP else []
                PB_ = kb.record(prep3B, gi + 1) if gi + 1 < NGRP else []
                nM = len(M)
                a0, a1 = int(nM * 0.02), int(nM * 0.35)
                b0, b1 = int(nM * 0.55), int(nM * 0.90)
                ia = ib = 0
                for k, st in enumerate(M):
                    st()
                    if k >= a0 and PA:
                        want = min(len(PA), ((k - a0 + 1) * len(PA)) // max(1, a1 - a0))
                        while ia < want:
                            PA[ia]()
                            ia += 1
                    if k >= b0 and PB_:
                        want = min(len(PB_), ((k - b0 + 1) * len(PB_)) // max(1, b1 - b0))
                        while ib < want:
                            PB_[ib]()
                            ib += 1
                for st in PA[ia:] + PB_[ib:]:
                    st()
            kb.barrier()
    return nc, dbg_outs


_CACHE = {}


def make_in_maps(inputs):
    g = lambda k: np.ascontiguousarray(np.asarray(inputs[k], dtype=np.float32))
    shared = {
        "c_ctx": g("c_ctx").reshape(8, 128),
        "w_ada": g("w_ada")[0],
        "b_ada": g("b_ada")[0].reshape(48, 128),
        "w_in": g("w_in")[0],
        "w_s": g("w_s")[0],
        "b_s": g("b_s")[0].reshape(1, 512),
        "ln_v_g": g("ln_v_g")[0].reshape(4, 128),
        "ln_v_b": g("ln_v_b")[0].reshape(4, 128),
        "conv_qk": g("conv_qk")[0].reshape(12, 128),
        "b_gates": g("b_gates")[0].reshape(1, 16),
        "hn_g": g("hn_g")[0].reshape(4, 128),
        "w_out": g("w_out")[0],
        "ln1_g": g("ln1_g")[0].reshape(1, D),
        "ln1_b": g("ln1_b")[0].reshape(1, D),
        "w1": g("w1")[0],
        "b1": g("b1")[0].reshape(32, 128),
        "w2": g("w2")[0],
        "b2": g("b2")[0].reshape(1, D),
        "ln2_g": g("ln2_g")[0].reshape(1, D),
        "ln2_b": g("ln2_b")[0].reshape(1, D),
    }
    x, c, ctx = g("x"), g("c"), g("ctx")
    maps = []
    for b in range(x.shape[0]):
        m = dict(shared)
        m["x"] = x[b]
        m["c"] = c[b].reshape(8, 128)
        m["ctx"] = ctx[b]
        maps.append(m)
    return maps


def kernel(**inputs):
    if "nc" not in _CACHE:
        _CACHE["nc"] = build_program(False)[0]
    nc = _CACHE["nc"]
    maps = make_in_maps(inputs)
    n = len(maps)
    res = run_bass_kernel_spmd(nc, maps, core_ids=list(range(n)))
    out = np.stack([np.asarray(r["y"], dtype=np.float32) for r in res.results], axis=0)
    return out
```

```python
import math
from contextlib import ExitStack
import numpy as np
import concourse.bass as bass
import concourse.mybir as mybir
from concourse.bass_utils import run_bass_kernel_spmd

F32 = mybir.dt.float32
BF16 = mybir.dt.bfloat16
AF = mybir.ActivationFunctionType
ALU = mybir.AluOpType

D = 1024
S = 4096
CTX = 256
NT = S // 128
NCT = CTX // 128
NG = NT + NCT
DIN = 2576
DFF = 4096
ALPHA = 2.0 ** 0.25
EPS = 1e-5
SEM_LIMIT = 3000
SKEW1 = 0.55
MERGE2 = 2
SKEW2 = 0.55


class Tok:
    __slots__ = ("sem", "val", "key")

    def __init__(self, sem, val, key):
        self.sem, self.val, self.key = sem, val, key


class Buf:
    def __init__(self, t, name, parent=None):
        self.t = t
        self.name = name
        self.w = None
        self.r = {}
        self.dsem = None
        self.dcount = 0
        self.parent = parent
        self.children = {}

    def sub(self, key):
        c = self.children.get(key)
        if c is None:
            c = Buf(self.t, f"{self.name}.{key}", parent=self)
            self.children[key] = c
        return c

    def __getitem__(self, idx):
        return self.t[idx]


class Eng:
    def __init__(self, kb, name, h):
        self.kb, self.name, self.h = kb, name, h
        self.seen = {}
        self.epoch = 0
        self.count = 0
        self.sem = kb.new_sem(f"{name}_e0")
        self.pending = False

    def roll(self):
        if self.count >= SEM_LIMIT and not self.pending:
            self.epoch += 1
            self.count = 0
            self.sem = self.kb.new_sem(f"{self.name}_e{self.epoch}")


class KB:
    def __init__(self, nc, es):
        self.nc, self.es = nc, es
        self.nsem = 0
        self.pe = Eng(self, "pe", nc.tensor)
        self.act = Eng(self, "act", nc.scalar)
        self.dve = Eng(self, "dve", nc.vector)
        self.pool = Eng(self, "pool", nc.gpsimd)
        self.sp = Eng(self, "sp", nc.sync)
        self.engs = [self.pe, self.act, self.dve, self.pool, self.sp]
        self.dma_toks = []

    def new_sem(self, name):
        self.nsem += 1
        s = self.es.enter_context(self.nc.semaphore(name))
        return (s, name)

    def sb(self, name, shape, dt, es=None):
        t = (es or self.es).enter_context(self.nc.sbuf_tensor(name, list(shape), dt))
        return Buf(t, name)

    def ps(self, name, shape, dt, es=None):
        t = (es or self.es).enter_context(self.nc.psum_tensor(name, list(shape), dt))
        b = Buf(t, name)
        b.psum = True
        return b

    def wait(self, eng, tok):
        if tok is None:
            return
        if eng.name == "pe" and tok.key.startswith("pe_e"):
            return
        if eng.seen.get(tok.key, 0) >= tok.val:
            return
        eng.h.wait_ge(tok.sem, tok.val)
        eng.seen[tok.key] = tok.val

    def _deps(self, eng, reads, writes):
        need = {}

        def add(t):
            if t is None:
                return
            o = need.get(t.key)
            if o is None or o.val < t.val:
                need[t.key] = t

        for b in reads:
            add(b.w)
            if getattr(b, "psum", False):
                for k_, t_ in b.r.items():
                    if not k_.startswith(eng.name + "_e"):
                        add(t_)
            if b.parent is not None:
                add(b.parent.w)
            for c in b.children.values():
                add(c.w)
                for c2 in c.children.values():
                    add(c2.w)
        for b in writes:
            add(b.w)
            for t in b.r.values():
                add(t)
            if b.parent is not None:
                add(b.parent.w)
                for t in b.parent.r.values():
                    add(t)
            for c in b.children.values():
                add(c.w)
                for t in c.r.values():
                    add(t)
                for c2 in c.children.values():
                    add(c2.w)
                    for t in c2.r.values():
                        add(t)
        for t in need.values():
            self.wait(eng, t)

    def _mark(self, tok, reads, writes):
        for b in reads:
            old = b.r.get(tok.key)
            if old is None or old.val < tok.val:
                b.r[tok.key] = tok
        for b in writes:
            b.w = tok
            b.r = {}

    def record(self, body, *args):
        outer = getattr(self, "rec", None)
        self.rec = []
        body(*args)
        r = self.rec
        self.rec = outer
        return r

    def op(self, eng, fn, reads=(), writes=(), sig=True):
        if getattr(self, "rec", None) is not None:
            self.rec.append(lambda: self._op(eng, fn, reads, writes, sig))
            return None
        return self._op(eng, fn, reads, writes, sig)

    def _op(self, eng, fn, reads=(), writes=(), sig=True):
        if sig:
            eng.roll()
        self._deps(eng, reads, writes)
        inst = fn(eng.h)
        if sig:
            eng.count += 1
            inst.then_inc(eng.sem[0], 1)
            tok = Tok(eng.sem[0], eng.count, eng.sem[1])
            eng.pending = False
        else:
            tok = Tok(eng.sem[0], eng.count + 1, eng.sem[1])
            eng.pending = True
        self._mark(tok, reads, writes)
        return tok

    def dma(self, out_ap, in_ap, reads=(), writes=(), sembuf=None, eng=None, after=()):
        if getattr(self, "rec", None) is not None:
            self.rec.append(lambda: self._dma(out_ap, in_ap, reads, writes, sembuf, eng, after))
            return None
        return self._dma(out_ap, in_ap, reads, writes, sembuf, eng, after)

    def cast_load(self, dst_buf, pieces, depth=2):
        toks = []
        for piece in pieces:
            out_ap, in_ap, wr = piece[:3]
            semb = wr
            after = [toks[-depth]] if len(toks) >= depth else []
            toks.append(self._dma(out_ap, in_ap, (), [wr], semb, self.pool, after))
        return toks

    def _dma(self, out_ap, in_ap, reads=(), writes=(), sembuf=None, eng=None, after=()):
        eng = eng or self.sp
        for t_ in after:
            self.wait(eng, t_)
        sb_ = sembuf or (writes[0] if writes else reads[0])
        if sb_.dsem is None:
            sb_.dsem = self.new_sem(f"d_{sb_.name}")
        self._deps(eng, reads, writes)
        inst = eng.h.dma_start(out=out_ap, in_=in_ap)
        inst.then_inc(sb_.dsem[0], 16)
        sb_.dcount += 16
        tok = Tok(sb_.dsem[0], sb_.dcount, sb_.dsem[1])
        self._mark(tok, reads, writes)
        self.dma_toks.append(tok)
        return tok

    def barrier(self):
        toks = []
        for e in self.engs:
            if e.count > 0:
                assert not e.pending
                toks.append(Tok(e.sem[0], e.count, e.sem[1]))
        toks += self.dma_toks
        self.dma_toks = []
        for e in self.engs:
            for t in toks:
                self.wait(e, t)


class _Stop(Exception):
    pass


def interleave(step_lists, H):
    n = len(step_lists)
    T = max(i * H + len(sl) for i, sl in enumerate(step_lists))
    lo = 0
    for t in range(T):
        while lo < n and t - lo * H >= len(step_lists[lo]):
            lo += 1
        i = lo
        while i < n and t - i * H >= 0:
            k = t - i * H
            if k < len(step_lists[i]):
                step_lists[i][k]()
            i += 1


def build_program(dbg=False, stage=99):
    try:
        return _build_program(dbg, stage)
    except _Stop as ex:
        return ex.args[0]


def _build_program(dbg, stage):
    nc = bass.Bass("TRN2", target_bir_lowering=False)

    def din(name, shape):
        return nc.dram_tensor(name, list(shape), F32, kind="ExternalInput").ap()

    x_d = din("x", [S, D])
    c_d = din("c", [8, 128])
    ctx_d = din("ctx", [CTX, D])
    cctx_d = din("c_ctx", [8, 128])
    wada_d = din("w_ada", [D, 6 * D])
    bada_d = din("b_ada", [48, 128])
    win_d = din("w_in", [D, DIN])
    ws_d = din("w_s", [4, 128, 128])
    bs_d = din("b_s", [1, 512])
    lnvg_d = din("ln_v_g", [4, 128])
    lnvb_d = din("ln_v_b", [4, 128])
    conv_d = din("conv_qk", [12, 128])
    bg_d = din("b_gates", [1, 16])
    hng_d = din("hn_g", [4, 128])
    wout_d = din("w_out", [D, D])
    ln1g_d = din("ln1_g", [1, D])
    ln1b_d = din("ln1_b", [1, D])
    w1_d = din("w1", [D, DFF])
    b1_d = din("b1", [32, 128])
    w2_d = din("w2", [DFF, D])
    b2_d = din("b2", [1, D])
    ln2g_d = din("ln2_g", [1, D])
    ln2b_d = din("ln2_b", [1, D])
    y_d = nc.dram_tensor("y", [S, D], F32, kind="ExternalOutput").ap()
    dbg_outs = {}

    with ExitStack() as es:
        kb = KB(nc, es)
        pe, act, dve, pool, sp = kb.pe, kb.act, kb.dve, kb.pool, kb.sp

        PB = [kb.ps(f"pb{i}", [128, 512], F32) for i in range(7)]
        PT = kb.ps("pt", [128, 1024], BF16)

        identf = kb.sb("identf", [128, 128], F32)
        identb = kb.sb("identb", [128, 128], BF16)
        LT = kb.sb("LT", [128, 128], F32)
        UT = kb.sb("UT", [128, 128], F32)
        onesf = kb.sb("onesf", [128, 128], F32)
        onesb = kb.sb("onesb", [128, 128], BF16)
        cst = kb.sb("cst", [128, 8], F32)
        modc = kb.sb("modc", [128, 6, 8], F32)
        smallc = kb.sb("smallc", [128, 64], F32)
        bgb = kb.sb("bgb", [128, 16], F32)
        BiasA = kb.sb("BiasA", [128, 4, 128], F32)
        wsT = kb.sb("wsT", [128, 4, 128], BF16)
        setup = Buf(None, "setup")
        ccol = kb.sb("ccol", [128, 2, 8], F32)
        badac = kb.sb("badac", [128, 48], F32)

        def dbg_out(name, buf, ap, shape, dt=F32):
            if not dbg:
                return
            o = nc.dram_tensor("dbg_" + name, list(shape), dt, kind="ExternalOutput").ap()
            dbg_outs[name] = (shape, dt)
            kb.dma(o, ap, reads=[buf], sembuf=buf)

        kb.op(pool, lambda e: e.memset(onesf[:], 1.0), writes=[onesf])
        kb.op(pool, lambda e: e.memset(onesb[:], 1.0), writes=[onesb])
        kb.op(pool, lambda e: e.memset(cst[:, 0:1], -0.5), writes=[cst])
        kb.op(pool, lambda e: e.memset(cst[:, 1:2], EPS), writes=[cst])
        kb.op(pool, lambda e: e.memset(cst[:, 2:3], math.log(8.0)), writes=[cst])
        kb.op(pool, lambda e: e.memset(cst[:, 3:4], 1.0), writes=[cst])
        kb.op(pool, lambda e: e.affine_select(out=identf[:], in_=onesf[:], pattern=[[-1, 128]], compare_op=ALU.is_equal,
                                              fill=0.0, base=0, channel_multiplier=1), reads=[onesf], writes=[identf])
        kb.op(pool, lambda e: e.affine_select(out=LT[:], in_=onesf[:], pattern=[[1, 128]], compare_op=ALU.is_ge,
                                              fill=0.0, base=0, channel_multiplier=-1), reads=[onesf], writes=[LT])
        kb.op(pool, lambda e: e.affine_select(out=UT[:], in_=onesf[:], pattern=[[-1, 128]], compare_op=ALU.is_ge,
                                              fill=0.0, base=0, channel_multiplier=1), reads=[onesf], writes=[UT])
        kb.op(dve, lambda e: e.tensor_copy(out=identb[:], in_=identf[:]), reads=[identf], writes=[identb])

        es_p = es.enter_context(ExitStack())
        qT = kb.sb("qT", [128, 2, S], BF16, es_p)
        kT = kb.sb("kT", [128, 2, NG * 128], BF16, es_p)
        vaug = kb.sb("vaug", [128, NG, 4, 130], BF16, es_p)
        WS = kb.sb("WS", [128, NG, 8], F32, es_p)
        LB = kb.sb("LB", [128, NG, 8], F32, es_p)
        ST = kb.sb("ST", [128, NT, 2, 2, 130], BF16, es_p)
        kb.op(pool, lambda e: e.memset(vaug[:, :, :, 128:130], 1.0), writes=[vaug])

        es0 = es.enter_context(ExitStack())
        ln1gb = kb.sb("ln1gb", [128, D], F32, es0)
        ln1bb = kb.sb("ln1bb", [128, D], F32, es0)
        es_gc = es.enter_context(ExitStack())
        Gt = kb.sb("Gt", [128, NG, 16], F32, es_gc)
        CW = kb.sb("CW", [128, NG, 8], F32, es_gc)
        es_set = es.enter_context(ExitStack())
        rows = kb.sb("rows", [128, 256], F32, es_set)
        bsb = kb.sb("bsb", [128, 512], F32, es_set)
        R_C, R_CC, R_BADA, R_CONV, R_GV, R_BV, R_HNG, R_B1 = 0, 8, 16, 64, 76, 80, 84, 88
        kb.dma(rows[0:8, 0:128], c_d[:, :], writes=[rows])
        kb.dma(rows[0:8, 128:256], cctx_d[:, :], writes=[rows])
        rows2 = kb.sb("rows2", [128, 128], F32, es_set)
        kb.dma(rows2[0:48, :], bada_d[:, :], writes=[rows2])
        rows3 = kb.sb("rows3", [128, 128], F32, es_set)
        kb.dma(rows3[0:12, :], conv_d[:, :], writes=[rows3])
        kb.dma(rows3[32:36, :], lnvg_d[:, :], writes=[rows3])
        kb.dma(rows3[64:68, :], lnvb_d[:, :], writes=[rows3])
        rows4 = kb.sb("rows4", [128, 128], F32, es_set)
        kb.dma(rows4[0:4, :], hng_d[:, :], writes=[rows4])
        kb.dma(rows4[32:64, :], b1_d[:, :], writes=[rows4])
        kb.dma(bgb[:], bg_d[0:1, :].to_broadcast([128, 16]), writes=[bgb])
        kb.dma(bsb[:], bs_d[0:1, :].to_broadcast([128, 512]), writes=[bsb])
        wsr = kb.sb("wsr", [128, 4, 128], F32, es_set)
        kb.dma(wsr[:], ws_d.rearrange("g t s -> t g s"), writes=[wsr])


        def tr_f32(dst_ap, dst_buf, src_ap, src_buf, n, bank, p0=0):
            kb.op(pe, lambda e: e.transpose(out=bank[:, 0:n], in_=src_ap, identity=identf[p0:p0 + n, p0:p0 + n]),
                  reads=[src_buf, identf], writes=[bank])
            kb.op(dve, lambda e: e.tensor_copy(out=dst_ap, in_=bank[:, 0:n]), reads=[bank], writes=[dst_buf])

        craw = kb.sb("craw", [128, 2, 8], F32, es_set)
        tr_f32(craw[:, 0, :], craw, rows[0:8, 0:128], rows, 8, PB[0])
        tr_f32(craw[:, 1, :], craw, rows[0:8, 128:256], rows, 8, PB[1])
        kb.op(act, lambda e: e.activation(out=ccol[:], in_=craw[:], func=AF.Silu), reads=[craw], writes=[ccol])
        tr_f32(badac[:, :], badac, rows2[0:48, :], rows2, 48, PB[2])
        tr_f32(smallc[:, 12:24], smallc, rows3[0:12, :], rows3, 12, PB[3])
        tr_f32(smallc[:, 0:4], smallc, rows3[32:36, :], rows3, 4, PB[4], p0=32)
        tr_f32(smallc[:, 4:8], smallc, rows3[64:68, :], rows3, 4, PB[5], p0=64)
        tr_f32(smallc[:, 8:12], smallc, rows4[0:4, :], rows4, 4, PB[6])
        tr_f32(smallc[:, 24:56], smallc, rows4[32:64, :], rows4, 32, PB[0], p0=32)
        wsTf = kb.sb("wsTf", [128, 4, 128], F32, es_set)
        for g in range(4):
            kb.op(pe, lambda e, g=g: e.transpose(out=PB[1][:, g * 128:(g + 1) * 128], in_=wsr[:, g, :], identity=identf[:]),
                  reads=[wsr, identf], writes=[PB[1]])
        kb.op(dve, lambda e: e.tensor_copy(out=wsTf[:].rearrange("p g t -> p (g t)"), in_=PB[1][:, :]), reads=[PB[1]], writes=[wsTf])
        kb.op(act, lambda e: e.activation(out=wsT[:], in_=wsTf[:], func=AF.Copy), reads=[wsTf], writes=[wsT])
        kb.op(pe, lambda e: e.matmul(PB[2][:, :], lhsT=onesf[:], rhs=wsTf[:].rearrange("p g t -> p (g t)"), start=True, stop=True),
              reads=[onesf, wsTf], writes=[PB[2]])
        for g in range(4):
            kb.op(dve, lambda e, g=g: e.scalar_tensor_tensor(out=BiasA[:, g, :], in0=PB[2][:, g * 128:(g + 1) * 128],
                                                             scalar=smallc[:, 4 + g:5 + g], in1=bsb[:, g * 128:(g + 1) * 128],
                                                             op0=ALU.mult, op1=ALU.add),
                  reads=[PB[2], smallc, bsb], writes=[BiasA])

        if stage == -1:
            dbg_out("smallc", smallc, smallc[:], [128, 64])
            dbg_out("BiasA", BiasA, BiasA[:], [128, 4, 128])
            dbg_out("ccol", ccol, ccol[:], [128, 2, 8])
            dbg_out("LT", LT, LT[:], [128, 128])
            dbg_out("identf", identf, identf[:], [128, 128])
            kb.barrier()
            raise _Stop((nc, dbg_outs))
        kb.barrier()
        es_set.close()
        kb.dma(ln1gb[:], ln1g_d[0:1, :].to_broadcast([128, D]), writes=[ln1gb])
        kb.dma(ln1bb[:], ln1b_d[0:1, :].to_broadcast([128, D]), writes=[ln1bb])
        g2row = nc.dram_tensor("g2scratch", [128, D], F32, kind="Internal").ap()
        g1row = nc.dram_tensor("g1scratch", [128, D], F32, kind="Internal").ap()

        with ExitStack() as esa:
            stg = [kb.sb(f"astg{i}", [128, 8, 512], BF16, esa) for i in range(4)]
            scb = kb.sb("scb", [128, 8, 128], BF16, esa)
            ccolb = kb.sb("ccolb", [128, 2, 8], BF16, esa)
            kb.op(dve, lambda e: e.tensor_copy(out=ccolb[:], in_=ccol[:]), reads=[ccol], writes=[ccolb])
            badab = kb.sb("badab", [128, 2, D], F32, esa)
            g2bc0 = kb.sb("g2bc0", [128, D], F32, esa)
            g1bc = kb.sb("g1bc0", [128, D], F32, esa)
            kb.dma(badab[:, 0, :], bada_d[16:24, :].rearrange("(o a) b -> o (a b)", o=1).to_broadcast([128, D]), writes=[badab])
            kb.dma(badab[:, 1, :], bada_d[40:48, :].rearrange("(o a) b -> o (a b)", o=1).to_broadcast([128, D]), writes=[badab])
            for kc in range(8):
                kb.op(dve, lambda e, kc=kc: e.tensor_copy(out=scb[:, kc, :], in_=ccol[:, 0, kc:kc + 1].to_broadcast([128, 128])),
                      reads=[ccol], writes=[scb])
            wada_v = wada_d.rearrange("(kc p) n -> p kc n", p=128)
            col_kind = {0: 0, 1: 0, 2: 1, 3: 1, 6: 2, 7: 2, 8: 3, 9: 3}
            for blk in range(12):
                sg = stg[blk % 4]
                kb.dma(sg[:, 0:4, :], wada_v[:, 0:4, blk * 512:(blk + 1) * 512], writes=[sg], eng=pool)
                kb.dma(sg[:, 4:8, :], wada_v[:, 4:8, blk * 512:(blk + 1) * 512], writes=[sg], eng=pool)
                if blk in col_kind:
                    mi = col_kind[blk]
                    for jj in range(4):
                        j = blk * 4 + jj
                        fchunk = j % 8
                        bank = PB[jj % 4]
                        for kc in range(8):
                            kb.op(pe, lambda e, kc=kc, jj=jj, bank=bank: e.matmul(
                                bank[:, 0:2], lhsT=sg[:, kc, jj * 128:(jj + 1) * 128], rhs=ccolb[:, :, kc],
                                start=(kc == 0), stop=(kc == 7)), reads=[sg, ccolb], writes=[bank], sig=(kc == 7))
                        kb.op(dve, lambda e, bank=bank, mi=mi, fchunk=fchunk, j=j: e.tensor_tensor(
                            out=modc[:, mi, fchunk:fchunk + 1], in0=bank[:, 0:1], in1=badac[:, j:j + 1], op=ALU.add),
                            reads=[bank, badac], writes=[modc.sub((mi, fchunk))])
                        if mi < 2:
                            kb.op(dve, lambda e, bank=bank, mi=mi, fchunk=fchunk, j=j: e.tensor_tensor(
                                out=modc[:, 4 + mi, fchunk:fchunk + 1], in0=bank[:, 1:2], in1=badac[:, j:j + 1], op=ALU.add),
                                reads=[bank, badac], writes=[modc.sub((4 + mi, fchunk))])
                else:
                    which = 0 if blk in (4, 5) else 1
                    half = blk % 2 if which == 1 else blk - 4
                    bank = PB[4 + (blk % 2)]
                    for kc in range(8):
                        kb.op(pe, lambda e, kc=kc, bank=bank: e.matmul(bank[:, :], lhsT=scb[:, kc, :], rhs=sg[:, kc, :],
                                                                      start=(kc == 0), stop=(kc == 7)),
                              reads=[sg, scb], writes=[bank], sig=(kc == 7))
                    dst = g1bc if which == 0 else g2bc0
                    kb.op(dve, lambda e, bank=bank, dst=dst, half=half, which=which: e.tensor_tensor(
                        out=dst[:, half * 512:(half + 1) * 512], in0=bank[:, :], in1=badab[:, which, half * 512:(half + 1) * 512],
                        op=ALU.add), reads=[bank, badab], writes=[dst])
            for mi in (1, 3, 5):
                kb.op(dve, lambda e, mi=mi: e.tensor_scalar(out=modc[:, mi, :], in0=modc[:, mi, :], scalar1=1.0, scalar2=None,
                                                            op0=ALU.add), reads=[modc], writes=[modc])
            g2st = kb.dma(g2row[:, :], g2bc0[:], reads=[g2bc0])
            kb.dma(g1row[:, :], g1bc[:], reads=[g1bc])
            kb.barrier()
            if stage == 0:
                dbg_out("modc", modc, modc[:], [128, 6, 8])
                dbg_out("g1bc", g1bc, g1bc[:], [128, D])
                dbg_out("smallc", smallc, smallc[:], [128, 64])
                dbg_out("BiasA", BiasA, BiasA[:], [128, 4, 128])
                kb.barrier()
                raise _Stop((nc, dbg_outs))

        def ln_stats(xap, xbuf, width, stats, mv, rstd, nmr=None):
            nchunk = width // 512
            for cidx in range(nchunk):
                kb.op(dve, lambda e, cidx=cidx: e.bn_stats(out=stats[:, cidx, :], in_=xap[:, cidx * 512:(cidx + 1) * 512]),
                      reads=[xbuf], writes=[stats.sub(cidx)])
            kb.op(dve, lambda e: e.bn_aggr(out=mv[:, 0:2], in_=stats[:, 0:nchunk, :].rearrange("p a b -> p (a b)")),
                  reads=[stats], writes=[mv])
            kb.op(pool, lambda e: e.tensor_tensor(out=mv[:, 2:3], in0=mv[:, 1:2], in1=cst[:, 1:2], op=ALU.add),
                  reads=[mv, cst], writes=[mv])
            kb.op(pool, lambda e: e.tensor_tensor(out=rstd[:, 0:1], in0=mv[:, 2:3], in1=cst[:, 0:1], op=ALU.pow),
                  reads=[mv, cst], writes=[rstd])
            if nmr is not None:
                kb.op(dve, lambda e: e.scalar_tensor_tensor(out=rstd[:, 1:2], in0=mv[:, 0:1], scalar=-1.0, in1=rstd[:, 0:1],
                                                            op0=ALU.mult, op1=ALU.mult), reads=[mv, rstd], writes=[rstd])

        def make_hT(xt, hT, xn, stats, mv, rstd, mi_shift, mi_scale, tok0=0):
            make_hT_A(xt, xn, stats, mv, rstd)
            make_hT_B(hT, xn, mi_shift, mi_scale, tok0)

        def make_hT_A(xt, xn, stats, mv, rstd):
            ln_stats(xt[:, :], xt, 1024, stats, mv, rstd, nmr=True)
            kb.op(act, lambda e: e.activation(out=xn[:], in_=xt[:, :], func=AF.Identity, bias=rstd[:, 1:2], scale=rstd[:, 0:1]),
                  reads=[xt, rstd], writes=[xn])

        def make_hT_B(hT, xn, mi_shift, mi_scale, tok0=0):
            for kc in range(8):
                kb.op(pe, lambda e, kc=kc: e.transpose(out=PT[:, kc * 128:(kc + 1) * 128], in_=xn[:, kc * 128:(kc + 1) * 128],
                                                      identity=identb[:]), reads=[xn, identb], writes=[PT], sig=(kc == 7))
            for kc in range(8):
                kb.op(act, lambda e, kc=kc: e.activation(out=hT[:, kc, tok0:tok0 + 128], in_=PT[:, kc * 128:(kc + 1) * 128],
                                                         func=AF.Identity, bias=modc[:, mi_shift, kc:kc + 1],
                                                         scale=modc[:, mi_scale, kc:kc + 1]),
                      reads=[PT, modc], writes=[hT.sub((kc, tok0))])

        def load_weight_block(dst, dst_ap_fn, src_ap, stg_buf, nk, scale_bc=None, scale_cols=None):
            half = nk // 2
            kb.dma(stg_buf[:, 0:half, :], src_ap[:, 0:half, :], writes=[stg_buf])
            kb.dma(stg_buf[:, half:nk, :], src_ap[:, half:nk, :], writes=[stg_buf])
            if scale_bc is None:
                kb.op(act, lambda e: e.activation(out=dst_ap_fn(slice(0, half)), in_=stg_buf[:, 0:half, :], func=AF.Copy),
                      reads=[stg_buf], writes=[dst])
                kb.op(pool, lambda e: e.tensor_copy(out=dst_ap_fn(slice(half, nk)), in_=stg_buf[:, half:nk, :]),
                      reads=[stg_buf], writes=[dst])
            else:
                for k in range(nk):
                    eng = dve if k % 2 == 0 else pool
                    kb.op(eng, lambda e, k=k: e.tensor_tensor(out=dst_ap_fn(k), in0=stg_buf[:, k, :],
                                                              in1=scale_bc[:, scale_cols], op=ALU.mult),
                          reads=[stg_buf, scale_bc], writes=[dst])

        win_v = win_d.rearrange("(kc p) n -> p kc n", p=128)
        with ExitStack() as es1:
            Win1 = kb.sb("Win1", [128, 8, 1040], BF16, es1)
            kb.cast_load(Win1, [(Win1[:, k0:k0 + 4, c0:c0 + 512], win_v[:, k0:k0 + 4, w0:w0 + 512], Win1.sub((k0, c0)))
                                for (c0, w0) in ((0, 1024), (512, 1536)) for k0 in (0, 4)]
                         + [(Win1[:, :, 1024:1040], win_v[:, :, 2560:2576], Win1.sub("g"))])
            xs = [kb.sb(f"xs{i}", [128, D], F32, es1) for i in range(2)]
            xn = kb.sb("xn", [128, D], BF16, es1)
            hT = kb.sb("hT", [128, 8, 128], BF16, es1)
            stats = kb.sb("stats", [128, 2, 6], F32, es1)
            mv = kb.sb("mv", [128, 4], F32, es1)
            rstd = kb.sb("rstd", [128, 2], F32, es1)
            raw = [kb.sb(f"raw{i}", [128, 4, 130], F32, es1) for i in range(3)]
            cvt = kb.sb("cvt", [128, 4, 128], F32, es1)

            def conv_finish(gc_prev, rb):
                for cc in range(4):
                    kb.op(dve, lambda e, cc=cc: e.tensor_scalar(out=cvt[:, cc, :], in0=rb[:, cc, 0:128],
                                                                scalar1=smallc[:, 12 + 0 * 4 + cc:13 + 0 * 4 + cc], scalar2=None,
                                                                op0=ALU.mult), reads=[rb, smallc], writes=[cvt.sub(cc)])
                    kb.op(dve, lambda e, cc=cc: e.scalar_tensor_tensor(out=cvt[:, cc, :], in0=rb[:, cc, 1:129],
                                                                       scalar=smallc[:, 12 + 1 * 4 + cc:13 + 1 * 4 + cc],
                                                                       in1=cvt[:, cc, :], op0=ALU.mult, op1=ALU.add),
                          reads=[rb, smallc, cvt.sub(cc)], writes=[cvt.sub(cc)])
                    kb.op(dve, lambda e, cc=cc: e.scalar_tensor_tensor(out=cvt[:, cc, :], in0=rb[:, cc, 2:130],
                                                                       scalar=smallc[:, 12 + 2 * 4 + cc:13 + 2 * 4 + cc],
                                                                       in1=cvt[:, cc, :], op0=ALU.mult, op1=ALU.add),
                          reads=[rb, smallc, cvt.sub(cc)], writes=[cvt.sub(cc)])
                t0 = gc_prev * 128
                kb.op(act, lambda e: e.activation(out=kT[:, :, t0:t0 + 128], in_=cvt[:, 2:4, :], func=AF.Silu),
                      reads=[cvt.sub(2), cvt.sub(3)], writes=[kT])
                if gc_prev >= NCT:
                    l0 = (gc_prev - NCT) * 128
                    kb.op(act, lambda e: e.activation(out=qT[:, :, l0:l0 + 128], in_=cvt[:, 0:2, :], func=AF.Silu),
                          reads=[cvt.sub(0), cvt.sub(1)], writes=[qT])

            xn_1 = [xn, kb.sb("xn_b", [128, D], BF16, es1)]
            hT_1 = [hT, kb.sb("hT_b", [128, 8, 128], BF16, es1)]
            stats_1 = [stats, kb.sb("stats_b", [128, 2, 6], F32, es1)]
            mv_1 = [mv, kb.sb("mv_b", [128, 4], F32, es1)]
            rstd_1 = [rstd, kb.sb("rstd_b", [128, 2], F32, es1)]

            xs3 = xs + [kb.sb("xs_c", [128, D], F32, es1)]
            xn3 = xn_1 + [kb.sb("xn_c", [128, D], BF16, es1)]
            stA = stats_1 + [kb.sb("stats_c", [128, 2, 6], F32, es1)]
            mvA = mv_1 + [kb.sb("mv_c", [128, 4], F32, es1)]
            rsA = rstd_1 + [kb.sb("rstd_c", [128, 2], F32, es1)]

            def tile1A(gc):
                if gc >= NG:
                    return
                is_ctx = gc < NCT
                src = ctx_d[gc * 128:(gc + 1) * 128, :] if is_ctx else x_d[(gc - NCT) * 128:(gc - NCT + 1) * 128, :]
                xt = xs3[gc % 3]
                kb.dma(xt[:, :], src[:, :], writes=[xt])
                make_hT_A(xt, xn3[gc % 3], stA[gc % 3], mvA[gc % 3], rsA[gc % 3])

            def tile1(gc):
                tile1A(gc + 1)
                sl = gc % 2
                hT = hT_1[sl]
                PBq, PBv, PBg = PB[3 * sl], PB[3 * sl + 1], PB[3 * sl + 2]
                is_ctx = gc < NCT
                make_hT_B(hT, xn3[gc % 3], 4 if is_ctx else 0, 5 if is_ctx else 1)
                for cc in range(4):
                    for kc in range(8):
                        kb.op(pe, lambda e, cc=cc, kc=kc: e.matmul(PBq[:, cc * 128:(cc + 1) * 128],
                                                                   lhsT=Win1[:, kc, cc * 128:(cc + 1) * 128], rhs=hT[:, kc, :],
                                                                   start=(kc == 0), stop=(kc == 7)),
                              reads=[Win1, hT.sub((kc, 0))], writes=[PBq], sig=(kc == 7 and cc == 3))
                for kc in range(8):
                    kb.op(pe, lambda e, kc=kc: e.matmul(PBv[:, :], lhsT=hT[:, kc, :], rhs=Win1[:, kc, 512:1024],
                                                        start=(kc == 0), stop=(kc == 7)),
                          reads=[Win1, hT.sub((kc, 0))], writes=[PBv], sig=(kc == 7))
                for kc in range(8):
                    kb.op(pe, lambda e, kc=kc: e.matmul(PBg[:, 0:16], lhsT=hT[:, kc, :], rhs=Win1[:, kc, 1024:1040],
                                                        start=(kc == 0), stop=(kc == 7)),
                          reads=[Win1, hT.sub((kc, 0))], writes=[PBg], sig=(kc == 7))
                rb = raw[gc % 3]
                first = gc in (0, NCT)
                last = gc in (NCT - 1, NG - 1)
                kb.op(act, lambda e: e.activation(out=rb[:, :, 1:129], in_=PBq[:, :].rearrange("p (c t) -> p c t", c=4), func=AF.Copy),
                      reads=[PBq], writes=[rb])
                kb.op(dve, lambda e: e.tensor_copy(out=vaug[:, gc, :, 0:128], in_=PBv[:, :].rearrange("p (h v) -> p h v", h=4)),
                      reads=[PBv], writes=[vaug])
                kb.op(dve, lambda e: e.tensor_tensor(out=Gt[:, gc, :], in0=PBg[:, 0:16], in1=bgb[:], op=ALU.add),
                      reads=[PBg, bgb], writes=[Gt])
                if first:
                    kb.op(pool, lambda e: e.memset(rb[:, :, 0:1], 0.0), writes=[rb])
                else:
                    rprev = raw[(gc - 1) % 3]
                    kb.op(pool, lambda e: e.tensor_copy(out=rb[:, :, 0:1], in_=rprev[:, :, 128:129]), reads=[rprev], writes=[rb])
                    kb.op(pool, lambda e: e.tensor_copy(out=rprev[:, :, 129:130], in_=rb[:, :, 1:2]), reads=[rb], writes=[rprev])
                    conv_finish(gc - 1, rprev)
                if last:
                    kb.op(pool, lambda e: e.memset(rb[:, :, 129:130], 0.0), writes=[rb])
                    conv_finish(gc, rb)
            tile1A(0)
            lists1 = [kb.record(tile1, gc) for gc in range(NG)]
            interleave(lists1, int(len(lists1[2]) * SKEW1))
            kb.barrier()

        if dbg:
            dbg_out("kT", kT, kT[:], [128, 2, NG * 128], BF16)
            dbg_out("qT", qT, qT[:], [128, 2, S], BF16)
            dbg_out("vaug", vaug, vaug[:], [128, NG, 4, 130], BF16)
            dbg_out("Gt", Gt, Gt[:], [128, NG, 16])
        if stage == 1:
            kb.barrier()
            raise _Stop((nc, dbg_outs))

        with ExitStack() as esg:
            NF = NG * 8
            Gv = Gt[:].rearrange("p c (d t h) -> p c d t h", d=2, t=2, h=4)
            LF = kb.sb("LF", [128, NG, 2, 4], F32, esg)
            T1 = kb.sb("T1", [128, NG, 2, 4], F32, esg)
            T2 = kb.sb("T2", [128, NG, 2, 4], F32, esg)
            Bc = kb.sb("Bc", [128, NG, 2, 4], F32, esg)
            Aa = kb.sb("Aa", [128, NG, 2, 4], F32, esg)
            Mcb = kb.sb("Mcb", [128, NG, 2, 4], F32, esg)
            rowA = kb.sb("rowA", [1, NG, 2, 4], F32, esg)
            rowB = kb.sb("rowB", [1, NG, 2, 4], F32, esg)
            rowM = kb.sb("rowM", [1, NG, 2, 4], F32, esg)
            rowm0 = kb.sb("rowm0", [1, NG, 2, 4], F32, esg)
            colmax = kb.sb("colmax", [128, 3], F32, esg)
            fl = lambda b: b[:].rearrange("p c d h -> p (c d h)")
            FG = Gv[:, :, :, 1, :]
            IG = Gv[:, :, :, 0, :]
            kb.op(dve, lambda e: e.tensor_scalar(out=T1[:], in0=FG, scalar1=-1.0, scalar2=None, op0=ALU.mult), reads=[Gt], writes=[T1])
            kb.op(dve, lambda e: e.tensor_tensor(out=T1[:], in0=T1[:], in1=FG, op=ALU.max), reads=[Gt, T1], writes=[T1])
            kb.op(act, lambda e: e.activation(out=T2[:], in_=T1[:], func=AF.Exp, scale=-1.0), reads=[T1], writes=[T2])
            kb.op(act, lambda e: e.activation(out=T2[:], in_=T2[:], func=AF.Ln, bias=cst[:, 3:4], scale=1.0), reads=[T2, cst], writes=[T2])
            kb.op(dve, lambda e: e.tensor_scalar(out=T1[:], in0=FG, scalar1=0.0, scalar2=None, op0=ALU.min), reads=[Gt], writes=[T1])
            kb.op(dve, lambda e: e.tensor_tensor(out=LF[:], in0=T1[:], in1=T2[:], op=ALU.subtract), reads=[T1, T2], writes=[LF])
            PBv = PB[0][:, 0:NF].rearrange("p (c d h) -> p c d h", c=NG, d=2, h=4)
            kb.op(pe, lambda e: e.matmul(PB[0][:, 0:NF], lhsT=LT[:], rhs=fl(LF), start=True, stop=True),
                  reads=[LT, LF], writes=[PB[0]])
            PBv1 = PB[1][:, 0:NF].rearrange("p (c d h) -> p c d h", c=NG, d=2, h=4)
            kb.op(pe, lambda e: e.matmul(PB[1][:, 0:NF], lhsT=UT[:], rhs=fl(LF), start=True, stop=True),
                  reads=[UT, LF], writes=[PB[1]])
            kb.op(dve, lambda e: e.tensor_copy(out=Bc[:, :, 0, :], in_=PBv[:, :, 0, :]), reads=[PB[0]], writes=[Bc])
            kb.op(dve, lambda e: e.tensor_copy(out=Bc[:, :, 1, :], in_=PBv1[:, :, 1, :]), reads=[PB[1]], writes=[Bc])
            kb.op(dve, lambda e: e.tensor_tensor(out=Aa[:], in0=IG, in1=Bc[:], op=ALU.subtract), reads=[Gt, Bc], writes=[Aa])
            AaF = fl(Aa)
            segs = [(0, 128), (128, 128), (256, NF - 256)]
            for si, (o, n) in enumerate(segs):
                kb.op(pe, lambda e, o=o, n=n: e.transpose(out=PB[2][0:n, 0:128], in_=AaF[:, o:o + n], identity=identf[:]),
                      reads=[Aa, identf], writes=[PB[2]])
                kb.op(dve, lambda e, si=si, n=n: e.reduce_max(out=colmax[0:n, si:si + 1], in_=PB[2][0:n, 0:128], axis=mybir.AxisListType.X),
                      reads=[PB[2]], writes=[colmax])
                kb.op(pe, lambda e, si=si, n=n, o=o: e.matmul(PB[3][0:1, o:o + n], lhsT=colmax[0:n, si:si + 1], rhs=identf[0:n, 0:n],
                                                               start=True, stop=True), reads=[colmax, identf], writes=[PB[3]])
            kb.op(dve, lambda e: e.tensor_copy(out=fl(rowA), in_=PB[3][0:1, 0:NF]), reads=[PB[3]], writes=[rowA])
            kb.op(pe, lambda e: e.matmul(PB[4][0:1, 0:NF], lhsT=onesf[:, 0:1], rhs=fl(LF), start=True, stop=True),
                  reads=[onesf, LF], writes=[PB[4]])
            kb.op(dve, lambda e: e.tensor_copy(out=fl(rowB), in_=PB[4][0:1, 0:NF]), reads=[PB[4]], writes=[rowB])
            order = [list(range(NG)), [1, 0] + list(range(NG - 1, NCT - 1, -1))]
            for d in range(2):
                g0 = order[d][0]
                kb.op(dve, lambda e, d=d, g0=g0: e.memset(rowm0[0:1, g0, d, :], 0.0), writes=[rowm0])
            for j in range(NG):
                for d in range(2):
                    gcur = order[d][j]
                    kb.op(dve, lambda e, d=d, gcur=gcur: e.tensor_tensor(out=rowM[0:1, gcur, d, :], in0=rowm0[0:1, gcur, d, :],
                                                                         in1=rowA[0:1, gcur, d, :], op=ALU.max),
                          reads=[rowm0, rowA], writes=[rowM])
                    if j + 1 < NG:
                        gn = order[d][j + 1]
                        kb.op(dve, lambda e, d=d, gcur=gcur, gn=gn: e.tensor_tensor(out=rowm0[0:1, gn, d, :], in0=rowM[0:1, gcur, d, :],
                                                                                   in1=rowB[0:1, gcur, d, :], op=ALU.add),
                              reads=[rowM, rowB], writes=[rowm0])
            kb.op(pe, lambda e: e.matmul(PB[5][:, 0:NF], lhsT=onesf[0:1, :], rhs=fl(rowM), start=True, stop=True),
                  reads=[onesf, rowM], writes=[PB[5]])
            kb.op(pe, lambda e: e.matmul(PB[6][:, 0:NF], lhsT=onesf[0:1, :], rhs=fl(rowm0), start=True, stop=True),
                  reads=[onesf, rowm0], writes=[PB[6]])
            kb.op(dve, lambda e: e.tensor_copy(out=fl(Mcb), in_=PB[5][:, 0:NF]), reads=[PB[5]], writes=[Mcb])
            kb.op(dve, lambda e: e.tensor_tensor(out=T1[:], in0=Aa[:], in1=Mcb[:], op=ALU.subtract), reads=[Aa, Mcb], writes=[T1])
            kb.op(act, lambda e: e.activation(out=WS[:].rearrange("p c j -> p (c j)"), in_=fl(T1), func=AF.Exp), reads=[T1], writes=[WS])
            kb.op(dve, lambda e: e.tensor_tensor(out=fl(T2), in0=PB[6][:, 0:NF], in1=fl(Mcb), op=ALU.subtract), reads=[PB[6], Mcb], writes=[T2])
            kb.op(act, lambda e: e.activation(out=CW[:].rearrange("p c j -> p (c j)"), in_=fl(T2), func=AF.Exp), reads=[T2], writes=[CW])
            kb.op(dve, lambda e: e.tensor_tensor(out=T1[:], in0=Bc[:], in1=Mcb[:], op=ALU.add), reads=[Bc, Mcb], writes=[T1])
            kb.op(act, lambda e: e.activation(out=LB[:].rearrange("p c j -> p (c j)"), in_=fl(T1), func=AF.Exp, bias=cst[:, 2:3], scale=-1.0),
                  reads=[T1, cst], writes=[LB])
            kb.barrier()

        if dbg:
            dbg_out("WS", WS, WS[:], [128, NG, 8])
            dbg_out("CW", CW, CW[:], [128, NG, 8])
            dbg_out("LB", LB, LB[:], [128, NG, 8])
        if stage == 2:
            kb.barrier()
            raise _Stop((nc, dbg_outs))

        with ExitStack() as ess:
            Cc = [kb.sb(f"Cc{d}", [128, 2, 130], F32, ess) for d in range(2)]
            kp = [kb.sb(f"kp{i}", [128, 4, 64], BF16, ess) for i in range(2)]
            c0b = [kb.sb(f"c0b{i}", [128, 2, 130], BF16, ess) for i in range(2)]
            for d in range(2):
                kb.op(pool, lambda e, d=d: e.memset(Cc[d][:], 0.0), writes=[Cc[d]])
            units = [(j, d) for j in range(NG) for d in range(2)]

            def stepA(u):
                j, d = units[u]
                if j == NG - 1:
                    return
                gcur = order[d][j]
                kpb = kp[u % 2]
                for pr in range(2):
                    kb.op(pe, lambda e, pr=pr: e.transpose(out=PT[:, pr * 128:(pr + 1) * 128],
                                                           in_=kT[:, pr, gcur * 128:(gcur + 1) * 128], identity=identb[:]),
                          reads=[kT, identb], writes=[PT], sig=(pr == 1))
                for h in range(4):
                    kb.op(act, lambda e, h=h: e.activation(
                        out=kpb[:, h, :], in_=PT[:, h * 64:(h + 1) * 64], func=AF.Identity,
                        scale=WS[:, gcur, d * 4 + h:d * 4 + h + 1]), reads=[PT, WS], writes=[kpb.sub(h)])
                bank = PB[(u % 2) * 2:(u % 2) * 2 + 2]
                for h in range(4):
                    bk = bank[h // 2]
                    kb.op(pe, lambda e, h=h, bk=bk: e.matmul(
                        bk[:, (h % 2) * 130:(h % 2) * 130 + 130], lhsT=kpb[:, (h // 2) * 2:(h // 2) * 2 + 2, :].rearrange("p a b -> p (a b)"),
                        rhs=vaug[:, gcur, h, :], start=True, stop=True), reads=[kpb.sub((h // 2) * 2), kpb.sub((h // 2) * 2 + 1), vaug], writes=[bk])

            Cnext = [kb.sb(f"Cn{d}", [128, 2, 130], F32, ess) for d in range(2)]
            for d in range(2):
                kb.op(pool, lambda e, d=d: e.memset(Cnext[d][:], 0.0), writes=[Cnext[d]])
            Cpp = [[Cc[d], Cnext[d]] for d in range(2)]

            def stepB(u):
                j, d = units[u]
                gcur = order[d][j]
                Ccur = Cpp[d][j % 2]
                Cnew = Cpp[d][(j + 1) % 2]
                if gcur >= NCT:
                    for h in range(4):
                        p0 = (h % 2) * 64
                        kb.op(pool, lambda e, h=h, p0=p0: e.tensor_scalar(
                            out=ST[p0:p0 + 64, gcur - NCT, d, h // 2, :], in0=Ccur[p0:p0 + 64, h // 2, :],
                            scalar1=CW[p0:p0 + 64, gcur, d * 4 + h:d * 4 + h + 1], scalar2=1.0, op0=ALU.mult, op1=ALU.mult),
                            reads=[Ccur.sub(h), CW], writes=[ST])
                if j == NG - 1:
                    return
                bank = PB[(u % 2) * 2:(u % 2) * 2 + 2]
                for h in range(4):
                    p0 = (h % 2) * 64
                    bk = bank[h // 2]
                    kb.op(dve, lambda e, h=h, p0=p0, bk=bk: e.scalar_tensor_tensor(
                        out=Cnew[p0:p0 + 64, h // 2, :], in0=Ccur[p0:p0 + 64, h // 2, :],
                        scalar=CW[p0:p0 + 64, gcur, d * 4 + h:d * 4 + h + 1],
                        in1=bk[p0:p0 + 64, (h % 2) * 130:(h % 2) * 130 + 130], op0=ALU.mult, op1=ALU.add),
                        reads=[Ccur.sub(h), CW, bk], writes=[Cnew.sub(h)])

            stepA(0)
            for u in range(len(units)):
                if u + 1 < len(units):
                    stepA(u + 1)
                stepB(u)
            kb.barrier()

        if dbg:
            dbg_out("ST", ST, ST[:], [128, NT, 2, 2, 130], BF16)
        if stage == 3:
            kb.barrier()
            raise _Stop((nc, dbg_outs))

        es_gc.close()
        wout_v = wout_d.rearrange("(kc p) n -> p kc n", p=128)
        with ExitStack() as es2:
            Win2 = kb.sb("Win2", [128, 8, 1536], BF16, es2)
            Wo = kb.sb("Wo", [128, 8, D], BF16, es2)
            with ExitStack() as esw:
                wstg = [kb.sb(f"wstg2{i}", [128, 8, 256], F32, esw) for i in range(2)]
                g1bc = kb.sb("g1bc", [128, D], F32, esw)
                kb.dma(g1bc[:], g1row[:, :], writes=[g1bc])
                nb = 0
                kb.cast_load(Win2, [(Win2[:, k0:k0 + 4, c0:c0 + 512], win_v[:, k0:k0 + 4, w0:w0 + 512], Win2.sub((k0, c0)))
                                    for (c0, w0) in ((0, 0), (512, 512), (1024, 2048)) for k0 in (0, 4)])
                for c0 in (0, 256, 512, 768):
                    load_weight_block(Wo, lambda k, c0=c0: Wo[:, k, c0:c0 + 256], wout_v[:, :, c0:c0 + 256], wstg[nb % 2], 8,
                                      scale_bc=g1bc, scale_cols=slice(c0, c0 + 256))
                    nb += 1
                kb.barrier()
            xs = [kb.sb(f"x2s{i}", [128, D], F32, es2) for i in range(2)]
            def two(name, shape, dt):
                return [kb.sb(f"{name}_{k}", shape, dt, es2) for k in range(2)]
            xn_ = two("xn2", [128, D], BF16)
            hT_ = two("hT2", [128, 8, 128], BF16)
            stats_ = two("stats2", [128, 4, 6], F32)
            mv_ = two("mv2", [128, 4], F32)
            rstd_ = two("rstd2", [128, 2], F32)
            mv4_ = two("mv4", [128, 4, 4], F32)
            rs4_ = two("rs4", [128, 4], F32)
            uT_ = two("uT", [128, 4, 128], BF16)
            sgo_ = two("sgo", [128, 4, 128], BF16)
            vn_ = two("vn", [128, 512], BF16)
            tA_1 = kb.sb("tA", [128, 4, 128], F32, es2)
            tA_ = [tA_1, tA_1]
            yT_ = two("yT", [128, 8, 128], BF16)
            sT_ = two("sT", [128, 8, 128], BF16)
            WM_1 = kb.sb("WM", [128, 8, 128], BF16, es2)
            WM_ = [WM_1, WM_1]
            dn_ = two("dn", [128, 3, 8], F32)
            hs_ = two("hs", [128, 4, 128], F32)
            hn_ = two("hn", [128, 4, 128], BF16)
            Q2_ = [[kb.sb(f"Q2{k}_{i}", [128, 2, 128], BF16, es2) for i in range(2)] for k in range(2)]
            hg = kb.sb("hg", [128, 4], F32, es2)
            for k in range(2):
                for pr in range(2):
                    kb.op(pool, lambda e, k=k, pr=pr: e.memset(Q2_[k][pr][:], 0.0), writes=[Q2_[k][pr]])
            kb.op(dve, lambda e: e.tensor_copy(out=hg[:], in_=smallc[:, 8:12]), reads=[smallc], writes=[hg])

            xs3 = xs + [kb.sb("x2s_c", [128, D], F32, es2)]
            xn3 = xn_ + [kb.sb("xn2_c", [128, D], BF16, es2)]
            stA = [kb.sb(f"stA{k}", [128, 2, 6], F32, es2) for k in range(3)]
            mvA = [kb.sb(f"mvA{k}", [128, 4], F32, es2) for k in range(3)]
            rsA = [kb.sb(f"rsA{k}", [128, 2], F32, es2) for k in range(3)]

            def tile2A(i):
                if i >= NT:
                    return
                xt = xs3[i % 3]
                kb.dma(xt[:, :], x_d[i * 128:(i + 1) * 128, :], writes=[xt])
                make_hT_A(xt, xn3[i % 3], stA[i % 3], mvA[i % 3], rsA[i % 3])

            def tile2(i):
                tile2A(i + 1)
                sl = i % 2
                xn, hT, stats, mv, rstd, mv4, rs4 = xn3[i % 3], hT_[sl], stats_[sl], mv_[sl], rstd_[sl], mv4_[sl], rs4_[sl]
                uT, sgo, vn, tA, yT, sT, dn, hs, hn, Q2, WM = uT_[sl], sgo_[sl], vn_[sl], tA_[sl], yT_[sl], sT_[sl], dn_[sl], hs_[sl], hn_[sl], Q2_[sl], WM_[sl]
                gc = i + NCT
                t0k = gc * 128
                t0q = i * 128
                xt = xs3[i % 3]
                def branchP():
                    make_hT_B(hT, xn, 0, 1)
                    for (bank, c0) in ((PB[0], 0), (PB[1], 1024)):
                        for cc in range(4):
                            for kc in range(8):
                                kb.op(pe, lambda e, cc=cc, kc=kc, bank=bank, c0=c0: e.matmul(
                                    bank[:, cc * 128:(cc + 1) * 128], lhsT=Win2[:, kc, c0 + cc * 128:c0 + (cc + 1) * 128], rhs=hT[:, kc, :],
                                    start=(kc == 0), stop=(kc == 7)), reads=[Win2, hT.sub((kc, 0))], writes=[bank], sig=(kc == 7 and cc == 3))
                    for kc in range(8):
                        kb.op(pe, lambda e, kc=kc: e.matmul(PB[2][:, :], lhsT=hT[:, kc, :], rhs=Win2[:, kc, 512:1024],
                                                            start=(kc == 0), stop=(kc == 7)), reads=[Win2, hT.sub((kc, 0))], writes=[PB[2]], sig=(kc == 7))
                    kb.op(act, lambda e: e.activation(out=uT[:].rearrange("p c t -> p (c t)"), in_=PB[0][:, :], func=AF.Copy),
                          reads=[PB[0]], writes=[uT])
                    kb.op(act, lambda e: e.activation(out=sgo[:].rearrange("p c t -> p (c t)"), in_=PB[1][:, :], func=AF.Sigmoid),
                          reads=[PB[1]], writes=[sgo])
                    ln_stats(PB[2][:, :], PB[2], 512, stA[i % 3], mvA[i % 3], rsA[i % 3], nmr=True)
                    kb.op(act, lambda e: e.activation(out=vn[:], in_=PB[2][:, :], func=AF.Identity, bias=rsA[i % 3][:, 1:2],
                                                      scale=rsA[i % 3][:, 0:1]), reads=[PB[2], rsA[i % 3]], writes=[vn])
                    for g in range(4):
                        kb.op(pe, lambda e, g=g: e.matmul(PB[3][:, g * 128:(g + 1) * 128], lhsT=vn[:, g * 128:(g + 1) * 128], rhs=wsT[:, g, :],
                                                          start=True, stop=True), reads=[vn, wsT], writes=[PB[3]], sig=(g == 3))
                    for g in range(4):
                        kb.op(dve, lambda e, g=g: e.scalar_tensor_tensor(out=tA[:, g, :], in0=PB[3][:, g * 128:(g + 1) * 128],
                                                                         scalar=smallc[:, g:g + 1], in1=BiasA[:, g, :], op0=ALU.mult, op1=ALU.add),
                              reads=[PB[3], smallc, BiasA], writes=[tA.sub(g)])
                    kb.op(pool, lambda e: e.tensor_tensor(out=yT[:, 0:4, :], in0=tA[:], in1=uT[:], op=ALU.mult), reads=[tA, uT], writes=[yT.sub('A')])

                def branchM():
                    for d in range(2):
                        msk = LT if d == 0 else UT
                        for h in range(4):
                            kb.op(pool, lambda e, d=d, h=h, msk=msk: e.tensor_scalar(
                                out=WM[:, d * 4 + h, :], in0=msk[:], scalar1=WS[:, gc, d * 4 + h:d * 4 + h + 1], scalar2=1.0,
                                op0=ALU.mult, op1=ALU.mult), reads=[msk, WS], writes=[WM.sub((d, h))])
                    for pr in range(2):
                        for hh in range(2):
                            kb.op(pool, lambda e, pr=pr, hh=hh: e.tensor_copy(out=Q2[pr][hh * 64:(hh + 1) * 64, hh, :],
                                                                              in_=qT[hh * 64:(hh + 1) * 64, pr, t0q:t0q + 128]),
                                  reads=[qT], writes=[Q2[pr].sub(hh)])
                    for pr in range(2):
                        kb.op(pe, lambda e, pr=pr: e.matmul(PB[4][:, pr * 256:(pr + 1) * 256], lhsT=kT[:, pr, t0k:t0k + 128],
                                                            rhs=Q2[pr][:].rearrange("p a t -> p (a t)"), start=True, stop=True),
                              reads=[kT, Q2[pr]], writes=[PB[4]], sig=(pr == 1))
                    for d in range(2):
                        kb.op(dve, lambda e, d=d: e.tensor_tensor(out=sT[:, d * 4:(d + 1) * 4, :].rearrange("p h t -> p (h t)"), in0=PB[4][:, :],
                                                                  in1=WM[:, d * 4:(d + 1) * 4, :].rearrange("p h t -> p (h t)"), op=ALU.mult),
                              reads=[PB[4]] + [WM.sub((d, hh_)) for hh_ in range(4)], writes=[sT.sub(d)])
                    for d in range(2):
                        bank = PB[5 + d]
                        for h in range(4):
                            kb.op(pe, lambda e, d=d, h=h, bank=bank: e.matmul(bank[:, h * 128:(h + 1) * 128], lhsT=sT[:, d * 4 + h, :],
                                                                              rhs=vaug[:, gc, h, 0:128], start=True, stop=False),
                                  reads=[sT.sub(d), vaug], writes=[bank], sig=False)
                            kb.op(pe, lambda e, d=d, h=h, bank=bank: e.matmul(
                                bank[:, h * 128:(h + 1) * 128], lhsT=Q2[h // 2][:, h % 2, :],
                                rhs=ST[:, i, d, h // 2, 0:128], start=False, stop=True),
                                reads=[Q2[h // 2], ST], writes=[bank], sig=(h == 3))
                    for d in range(2):
                        for h in range(4):
                            jn = d * 4 + h
                            kb.op(pe, lambda e, d=d, h=h, jn=jn: e.matmul(PB[4][:, 2 * jn:2 * jn + 2], lhsT=sT[:, jn, :],
                                                                          rhs=vaug[:, gc, h, 128:130], start=True, stop=False),
                                  reads=[sT.sub(d), vaug], writes=[PB[4]], sig=False)
                            kb.op(pe, lambda e, d=d, h=h, jn=jn: e.matmul(
                                PB[4][:, 2 * jn:2 * jn + 2], lhsT=Q2[h // 2][:, h % 2, :],
                                rhs=ST[:, i, d, h // 2, 128:130], start=False, stop=True),
                                reads=[Q2[h // 2], ST], writes=[PB[4]], sig=(jn == 7))
                    den = PB[4][:, 0:16].rearrange("p (j two) -> p j two", two=2)[:, :, 0]
                    kb.op(dve, lambda e: e.tensor_scalar(out=dn[:, 0, :], in0=den, scalar1=-1.0, scalar2=None, op0=ALU.mult),
                          reads=[PB[4]], writes=[dn])
                    kb.op(dve, lambda e: e.tensor_tensor(out=dn[:, 1, :], in0=dn[:, 0, :], in1=den, op=ALU.max),
                          reads=[PB[4], dn], writes=[dn])
                    kb.op(dve, lambda e: e.tensor_tensor(out=dn[:, 0, :], in0=dn[:, 1, :], in1=LB[:, gc, :], op=ALU.max),
                          reads=[dn, LB], writes=[dn])
                    kb.op(dve, lambda e: e.reciprocal(out=dn[:, 2, :], in_=dn[:, 0, :]), reads=[dn], writes=[dn])
                    for h in range(4):
                        kb.op(act, lambda e, h=h: e.activation(out=hs[:, h, :], in_=PB[5][:, h * 128:(h + 1) * 128], func=AF.Identity,
                                                               scale=dn[:, 2, h:h + 1]), reads=[PB[5], dn], writes=[hs.sub(h)])
                    for h in range(4):
                        kb.op(dve, lambda e, h=h: e.scalar_tensor_tensor(out=hs[:, h, :], in0=PB[6][:, h * 128:(h + 1) * 128],
                                                                         scalar=dn[:, 2, 4 + h:5 + h], in1=hs[:, h, :], op0=ALU.mult, op1=ALU.add),
                              reads=[PB[6], dn, hs.sub(h)], writes=[hs.sub(h)])
                    for h in range(4):
                        kb.op(dve, lambda e, h=h: e.bn_stats(out=stats[:, h, :], in_=hs[:, h, :]), reads=[hs.sub(h)], writes=[stats.sub(h)])
                    for h in range(4):
                        kb.op(dve, lambda e, h=h: e.bn_aggr(out=mv4[:, h, 0:2], in_=stats[:, h, :]), reads=[stats.sub(h)], writes=[mv4.sub(h)])
                    kb.op(pool, lambda e: e.tensor_tensor(out=mv4[:, :, 2], in0=mv4[:, :, 1], in1=cst[:, 1:2].to_broadcast([128, 4]), op=ALU.add),
                          reads=[mv4, cst], writes=[mv4])
                    kb.op(pool, lambda e: e.tensor_tensor(out=rs4[:], in0=mv4[:, :, 2], in1=cst[:, 0:1].to_broadcast([128, 4]), op=ALU.pow),
                          reads=[mv4, cst], writes=[rs4])
                    kb.op(dve, lambda e: e.scalar_tensor_tensor(out=mv4[:, :, 3], in0=mv4[:, :, 0], scalar=-1.0, in1=rs4[:],
                                                                op0=ALU.mult, op1=ALU.mult), reads=[mv4, rs4], writes=[mv4])
                    for h in range(4):
                        kb.op(act, lambda e, h=h: e.activation(out=hn[:, h, :], in_=hs[:, h, :], func=AF.Identity,
                                                               bias=mv4[:, h, 3:4], scale=rs4[:, h:h + 1]),
                              reads=[hs.sub(h), mv4, rs4], writes=[hn.sub(h)])

                Pl = kb.record(branchP)
                Ml = kb.record(branchM)
                ip = im = 0
                tot = len(Pl) + len(Ml)
                if MERGE2 == 2:
                    kgm = len(Pl) - 9
                    kb.rec.extend(Ml[:12] + Pl[:kgm] + Ml[12:] + Pl[kgm:])
                    tot = 0
                for k in range(tot):
                    if MERGE2 and im * len(Pl) <= ip * len(Ml) and im < len(Ml):
                        kb.rec.append(Ml[im]); im += 1
                    elif ip < len(Pl):
                        kb.rec.append(Pl[ip]); ip += 1
                    else:
                        kb.rec.append(Ml[im]); im += 1
                for h in range(4):
                    kb.op(pe, lambda e, h=h: e.transpose(out=PT[:, h * 128:(h + 1) * 128], in_=hn[:, h, :], identity=identb[:]),
                          reads=[hn.sub(h), identb], writes=[PT], sig=(h == 3))
                for h in range(4):
                    kb.op(dve, lambda e, h=h: e.scalar_tensor_tensor(out=yT[:, 4 + h, :], in0=PT[:, h * 128:(h + 1) * 128],
                                                                     scalar=hg[:, h:h + 1], in1=sgo[:, h, :], op0=ALU.mult, op1=ALU.mult),
                          reads=[PT, hg, sgo], writes=[yT.sub(4 + h)])
                for half in range(2):
                    for kc in range(8):
                        kb.op(pe, lambda e, half=half, kc=kc: e.matmul(PB[3 + half][:, :], lhsT=yT[:, kc, :],
                                                                       rhs=Wo[:, kc, half * 512:(half + 1) * 512],
                                                                       start=(kc == 0), stop=(kc == 7)),
                              reads=[yT, Wo], writes=[PB[3 + half]], sig=(kc == 7))
                for half in range(2):
                    kb.op(dve, lambda e, half=half: e.scalar_tensor_tensor(out=xt[:, half * 512:(half + 1) * 512],
                                                                           in0=xt[:, half * 512:(half + 1) * 512], scalar=ALPHA,
                                                                           in1=PB[3 + half][:, :], op0=ALU.mult, op1=ALU.add),
                          reads=[xt.sub(half), PB[3 + half]], writes=[xt.sub(half)])
                ln_stats(xt[:, :], xt, 1024, stats, mv, rstd)
                kb.op(dve, lambda e: e.scalar_tensor_tensor(out=xt[:, :], in0=xt[:, :], scalar=mv[:, 0:1], in1=ln1gb[:],
                                                            op0=ALU.subtract, op1=ALU.mult), reads=[xt, mv, ln1gb], writes=[xt])
                kb.op(dve, lambda e: e.scalar_tensor_tensor(out=xt[:, :], in0=xt[:, :], scalar=rstd[:, 0:1], in1=ln1bb[:],
                                                            op0=ALU.mult, op1=ALU.add), reads=[xt, rstd, ln1bb], writes=[xt])
                kb.dma(y_d[i * 128:(i + 1) * 128, :], xt[:, :], reads=[xt])
            tile2A(0)
            lists2 = [kb.record(tile2, i) for i in range(NT)]
            interleave(lists2, int(len(lists2[0]) * SKEW2))
            kb.barrier()

        if stage == 4:
            raise _Stop((nc, dbg_outs))
        es0.close()
        es_p.close()

        GRP = 2
        w1_v = w1_d.rearrange("(kc p) n -> p kc n", p=128)
        w2_v = w2_d.rearrange("(j p) n -> p j n", p=128)
        with ExitStack() as es3:
            W1b = kb.sb("W1b", [128, 8, DFF], BF16, es3)
            W2b = kb.sb("W2b", [128, 32, D], BF16, es3)
            ln2gb = kb.sb("ln2gb", [128, D], F32, es3)
            ln2bb = kb.sb("ln2bb", [128, D], F32, es3)
            b2h = kb.sb("b2h", [1, 2, D], BF16, es3)
            kb.dma(ln2gb[:], ln2g_d[0:1, :].to_broadcast([128, D]), writes=[ln2gb])
            kb.dma(ln2bb[:], ln2b_d[0:1, :].to_broadcast([128, D]), writes=[ln2bb])
            g2bc = kb.sb("g2bc", [128, D], F32, es3)
            kb.dma(g2bc[:], g2row[:, :], writes=[g2bc])
            kb.cast_load(W1b, [(W1b[:, k0:k0 + 4, blk * 512:(blk + 1) * 512], w1_v[:, k0:k0 + 4, blk * 512:(blk + 1) * 512], W1b.sub(blk).sub(k0), W1b.sub(blk))
                               for blk in range(8) for k0 in (0, 4)], depth=3)
            kb.cast_load(W2b, [(W2b[:, blk * 4:(blk + 1) * 4, :], w2_v[:, blk * 4:(blk + 1) * 4, :], W2b.sub(blk), W2b.sub(blk))
                               for blk in range(8)], depth=3)
            with ExitStack() as esw:
                b2bc = kb.sb("b2bc", [128, D], F32, esw)
                kb.dma(b2bc[0:1, :], b2_d[0:1, :], writes=[b2bc])
                kb.op(dve, lambda e: e.tensor_copy(out=b2h[0:1, 0, :], in_=b2bc[0:1, :]), reads=[b2bc], writes=[b2h])
                kb.op(dve, lambda e: e.tensor_tensor(out=b2bc[0:1, :], in0=b2bc[0:1, :], in1=b2h[0:1, 0, :], op=ALU.subtract),
                      reads=[b2bc, b2h], writes=[b2bc])
                kb.op(dve, lambda e: e.tensor_copy(out=b2h[0:1, 1, :], in_=b2bc[0:1, :]), reads=[b2bc], writes=[b2h])
                kb.barrier()
            tmpo = [kb.sb(f"tmpo{i}", [128, 512], F32, es3) for i in range(2)]
            xs = [kb.sb(f"x3s{i}", [128, D], F32, es3) for i in range(2 * GRP)]
            xn = kb.sb("xn3", [128, D], BF16, es3)
            h2T_ = [kb.sb(f"h2T{k}", [128, 8, GRP * 128], BF16, es3) for k in range(2)]
            hid = kb.sb("hid", [128, 32, GRP * 128], BF16, es3)
            hidb = [Buf(hid.t, f"hid{j}") for j in range(32)]
            rl = [kb.sb(f"rl{i}", [128, GRP * 128], BF16, es3) for i in range(4)]
            stats_p = kb.sb("stats3p", [128, 2, 6], F32, es3)
            mv_p = kb.sb("mv3p", [128, 4], F32, es3)
            rstd_p = kb.sb("rstd3p", [128, 2], F32, es3)
            stats_e = kb.sb("stats3e", [128, 2, 6], F32, es3)
            mv_e = kb.sb("mv3e", [128, 4], F32, es3)
            rstd_e = kb.sb("rstd3e", [128, 2], F32, es3)
            NGRP = NT // GRP

            xn_3 = [xn] + [kb.sb(f"xn3_{a}", [128, D], BF16, es3) for a in range(1, GRP)]

            def prep3A(gi):
                for a in range(GRP):
                    ti = gi * GRP + a
                    xt = xs[(gi % 2) * GRP + a]
                    kb.dma(xt[:, :], y_d[ti * 128:(ti + 1) * 128, :], writes=[xt])
                    make_hT_A(xt, xn_3[a], stats_p, mv_p, rstd_p)

            def prep3B(gi):
                for a in range(GRP):
                    make_hT_B(h2T_[gi % 2], xn_3[a], 2, 3, tok0=a * 128)

            def main3(gi):
                h2T = h2T_[gi % 2]
                for j in range(32):
                    bank = PB[j % 2]
                    for kc in range(8):
                        kb.op(pe, lambda e, j=j, kc=kc, bank=bank: e.matmul(bank[:, 0:GRP * 128], lhsT=W1b[:, kc, j * 128:(j + 1) * 128],
                                                                            rhs=h2T[:, kc, :], start=(kc == 0), stop=(kc == 7)),
                              reads=[W1b.sub(j // 4)] + [h2T.sub((kc, a_ * 128)) for a_ in range(GRP)], writes=[bank], sig=(kc == 7))
                    rb = rl[j % 4]
                    kb.op(act, lambda e, j=j, bank=bank, rb=rb: e.activation(out=rb[:], in_=bank[:, 0:GRP * 128], func=AF.Relu,
                                                                             bias=smallc[:, 24 + j:25 + j], scale=1.0),
                          reads=[bank, smallc], writes=[rb])
                    eng = pool if j % 4 == 3 else dve
                    kb.op(eng, lambda e, j=j, rb=rb: e.tensor_tensor(out=hid[:, j, :], in0=rb[:], in1=rb[:], op=ALU.mult),
                          reads=[rb], writes=[hidb[j]])
                for a in range(GRP):
                    ti = gi * GRP + a
                    xt = xs[(gi % 2) * GRP + a]
                    for half in range(2):
                        bank = PB[2 + 2 * (a % 2) + half]
                        for j in range(32):
                            kb.op(pe, lambda e, j=j, a=a, half=half, bank=bank: e.matmul(
                                bank[:, :], lhsT=hid[:, j, a * 128:(a + 1) * 128], rhs=W2b[:, j, half * 512:(half + 1) * 512],
                                start=(j == 0), stop=False), reads=[hidb[j], W2b.sub(j // 4)], writes=[bank], sig=False)
                        for hl in range(2):
                            kb.op(pe, lambda e, hl=hl, half=half, bank=bank: e.matmul(
                                bank[:, :], lhsT=onesb[0:1, :], rhs=b2h[0:1, hl, half * 512:(half + 1) * 512],
                                start=False, stop=(hl == 1)), reads=[onesb, b2h], writes=[bank], sig=(hl == 1))
                        tm = tmpo[half]
                        kb.op(dve, lambda e, half=half, bank=bank, tm=tm: e.tensor_tensor(
                            out=tm[:], in0=bank[:, :], in1=g2bc[:, half * 512:(half + 1) * 512], op=ALU.mult),
                            reads=[bank, g2bc], writes=[tm])
                        kb.op(dve, lambda e, half=half, tm=tm, xt=xt: e.scalar_tensor_tensor(
                            out=xt[:, half * 512:(half + 1) * 512], in0=xt[:, half * 512:(half + 1) * 512], scalar=ALPHA,
                            in1=tm[:], op0=ALU.mult, op1=ALU.add), reads=[xt.sub(half), tm], writes=[xt.sub(half)])
                    ln_stats(xt[:, :], xt, 1024, stats_e, mv_e, rstd_e)
                    kb.op(dve, lambda e, xt=xt: e.scalar_tensor_tensor(out=xt[:, :], in0=xt[:, :], scalar=mv_e[:, 0:1], in1=ln2gb[:],
                                                                       op0=ALU.subtract, op1=ALU.mult), reads=[xt, mv_e, ln2gb], writes=[xt])
                    kb.op(dve, lambda e, xt=xt: e.scalar_tensor_tensor(out=xt[:, :], in0=xt[:, :], scalar=rstd_e[:, 0:1], in1=ln2bb[:],
                                                                       op0=ALU.mult, op1=ALU.add), reads=[xt, rstd_e, ln2bb], writes=[xt])
                    kb.dma(y_d[ti * 128:(ti + 1) * 128, :], xt[:, :], reads=[xt])

            for st in kb.record(prep3A, 0) + kb.record(prep3B, 0):
                st()
            for gi in range(NGRP):
                M = kb.record(main3, gi)
                PA = kb.record(prep3A, gi + 1) if gi + 1 < NGRP else []
                PB_ = kb.record(prep3B, gi + 1) if gi + 1 < NGRP else []
                nM = len(M)
                a0, a1 = int(nM * 0.02), int(nM * 0.35)
                b0, b1 = int(nM * 0.55), int(nM * 0.90)
                ia = ib = 0
                for k, st in enumerate(M):
                    st()
                    if k >= a0 and PA:
                        want = min(len(PA), ((k - a0 + 1) * len(PA)) // max(1, a1 - a0))
                        while ia < want:
                            PA[ia]()
                            ia += 1
                    if k >= b0 and PB_:
                        want = min(len(PB_), ((k - b0 + 1) * len(PB_)) // max(1, b1 - b0))
                        while ib < want:
                            PB_[ib]()
                            ib += 1
                for st in PA[ia:] + PB_[ib:]:
                    st()
            kb.barrier()
    return nc, dbg_outs


_CACHE = {}


def make_in_maps(inputs):
    g = lambda k: np.ascontiguousarray(np.asarray(inputs[k], dtype=np.float32))
    shared = {
        "c_ctx": g("c_ctx").reshape(8, 128),
        "w_ada": g("w_ada")[0],
        "b_ada": g("b_ada")[0].reshape(48, 128),
        "w_in": g("w_in")[0],
        "w_s": g("w_s")[0],
        "b_s": g("b_s")[0].reshape(1, 512),
        "ln_v_g": g("ln_v_g")[0].reshape(4, 128),
        "ln_v_b": g("ln_v_b")[0].reshape(4, 128),
        "conv_qk": g("conv_qk")[0].reshape(12, 128),
        "b_gates": g("b_gates")[0].reshape(1, 16),
        "hn_g": g("hn_g")[0].reshape(4, 128),
        "w_out": g("w_out")[0],
        "ln1_g": g("ln1_g")[0].reshape(1, D),
        "ln1_b": g("ln1_b")[0].reshape(1, D),
        "w1": g("w1")[0],
        "b1": g("b1")[0].reshape(32, 128),
        "w2": g("w2")[0],
        "b2": g("b2")[0].reshape(1, D),
        "ln2_g": g("ln2_g")[0].reshape(1, D),
        "ln2_b": g("ln2_b")[0].reshape(1, D),
    }
    x, c, ctx = g("x"), g("c"), g("ctx")
    maps = []
    for b in range(x.shape[0]):
        m = dict(shared)
        m["x"] = x[b]
        m["c"] = c[b].reshape(8, 128)
        m["ctx"] = ctx[b]
        maps.append(m)
    return maps


def kernel(**inputs):
    if "nc" not in _CACHE:
        _CACHE["nc"] = build_program(False)[0]
    nc = _CACHE["nc"]
    maps = make_in_maps(inputs)
    n = len(maps)
    res = run_bass_kernel_spmd(nc, maps, core_ids=list(range(n)))
    out = np.stack([np.asarray(r["y"], dtype=np.float32) for r in res.results], axis=0)
    return out
```
